# Optimizing a Trainium2 kernel written in Bass

```python
import math
import jax, jax.numpy as jnp
from jax import lax
import numpy as np

D_MODEL = 1024
BATCH = 4
SEQ = 4096
DEPTH = 4

CHUNK = 64
N_MIXERS = 3
N_RET = (DEPTH + 2) // 3
N_CONV = (DEPTH + 1) // 3
N_SB = DEPTH // 3
RET_HEADS = 4
RET_DK = D_MODEL // RET_HEADS
RET_DV = 2 * D_MODEL // RET_HEADS
ROPE_BASE = 10000.0
CONV_WIDTH = 31
SB_HEADS = 16
SB_DH = D_MODEL // SB_HEADS
SB_BLOCK = 128
D_FF = 4 * D_MODEL
EPS = 1e-6

kernel_name = "hybrid_ret_conv_stickbreak_trunk"


def rms_norm(x, w):
    xf = x.astype(jnp.float32)
    y = xf * lax.rsqrt(jnp.mean(xf * xf, axis=-1, keepdims=True) + EPS)
    return (y * w.astype(jnp.float32)).astype(x.dtype)


def layer_norm(x, w, b):
    xf = x.astype(jnp.float32)
    mu = jnp.mean(xf, axis=-1, keepdims=True)
    var = jnp.mean(jnp.square(xf - mu), axis=-1, keepdims=True)
    y = (xf - mu) * lax.rsqrt(var + EPS)
    return y * w.astype(jnp.float32) + b.astype(jnp.float32)


def apply_rotary(t, seq_len):
    half = t.shape[-1] // 2
    inv_freq = ROPE_BASE ** (-jnp.arange(half, dtype=jnp.float32) / half)
    ang = jnp.arange(seq_len, dtype=jnp.float32)[:, None] * inv_freq[None, :]
    cos = jnp.cos(ang)[:, None, :]
    sin = jnp.sin(ang)[:, None, :]
    t1, t2 = t[..., :half], t[..., half:]
    return jnp.concatenate([t1 * cos - t2 * sin, t1 * sin + t2 * cos], axis=-1)


def retention_chunkwise(q, k, v):
    b, s, h, _ = q.shape
    n = s // CHUNK

    def to_chunks(t):
        return t.reshape(b, n, CHUNK, h, -1).transpose(1, 0, 3, 2, 4)

    log_g = jnp.log(1.0 - jnp.exp2(-5.0 - jnp.arange(h, dtype=jnp.float32)))
    idx = jnp.arange(CHUNK, dtype=jnp.float32)
    intra_decay = jnp.exp(log_g[:, None, None] * jnp.abs(idx[:, None] - idx[None, :]))
    q_decay = jnp.exp(log_g[:, None] * (idx + 1.0))[..., None]
    k_decay = jnp.exp(log_g[:, None] * (CHUNK - 1.0 - idx))[..., None]
    chunk_decay = jnp.exp(log_g * CHUNK)[:, None, None]

    def step(state, inp):
        qc, kc, vc = inp
        scores = jnp.einsum('bhcd,bhsd->bhcs', qc, kc) * intra_decay
        out = (jnp.einsum('bhcs,bhse->bhce', scores, vc)
               + jnp.einsum('bhcd,bhde->bhce', qc * q_decay, state))
        state = state * chunk_decay + jnp.einsum('bhsd,bhse->bhde', kc * k_decay, vc)
        return state, out

    state0 = jnp.zeros((b, h, q.shape[-1], v.shape[-1]), jnp.float32)
    _, out = lax.scan(step, state0, (to_chunks(q), to_chunks(k), to_chunks(v)))
    return out.transpose(1, 0, 3, 2, 4).reshape(b, s, h, -1)


def retention_mixer(xn, w_in, q_gain, k_gain, gn_w, gn_b, w_out):
    b, s, _ = xn.shape
    proj = xn @ w_in
    q, k, v, g = jnp.split(proj, [D_MODEL, 2 * D_MODEL, 4 * D_MODEL], axis=-1)
    q = rms_norm(q.reshape(b, s, RET_HEADS, RET_DK), q_gain).astype(jnp.float32)
    k = rms_norm(k.reshape(b, s, RET_HEADS, RET_DK), k_gain).astype(jnp.float32)
    v = v.reshape(b, s, RET_HEADS, RET_DV).astype(jnp.float32)
    q = apply_rotary(q, s)
    k = apply_rotary(k, s) * (RET_DK ** -0.5)
    y = retention_chunkwise(q, k, v)
    mu = jnp.mean(y, axis=-1, keepdims=True)
    var = jnp.mean(jnp.square(y - mu), axis=-1, keepdims=True)
    y = (y - mu) * lax.rsqrt(var + EPS)
    y = y * gn_w.astype(jnp.float32).reshape(RET_HEADS, RET_DV) + gn_b.astype(jnp.float32).reshape(RET_HEADS, RET_DV)
    y = y.reshape(b, s, 2 * D_MODEL).astype(xn.dtype)
    return (jax.nn.silu(g) * y) @ w_out


def conformer_conv_mixer(xn, pw1_w, pw1_b, dw_w, dw_b, ln_w, ln_b, pw2_w, pw2_b):
    h = xn @ pw1_w + pw1_b
    a, gate = jnp.split(h, 2, axis=-1)
    h = a * jax.nn.sigmoid(gate)
    h = lax.conv_general_dilated(
        h, dw_w[:, None, :].astype(h.dtype), window_strides=(1,),
        padding=[(CONV_WIDTH - 1, 0)],
        dimension_numbers=('NWC', 'WIO', 'NWC'),
        feature_group_count=D_MODEL) + dw_b
    h = layer_norm(h, ln_w, ln_b)
    h = jax.nn.silu(h).astype(xn.dtype)
    return h @ pw2_w + pw2_b


def stick_breaking_mixer(xn, w_in, q_gain, k_gain, w_out):
    b, s, _ = xn.shape
    proj = xn @ w_in
    q, k, v = jnp.split(proj, 3, axis=-1)
    q = rms_norm(q.reshape(b, s, SB_HEADS, SB_DH), q_gain).astype(jnp.float32).transpose(0, 2, 1, 3)
    k = rms_norm(k.reshape(b, s, SB_HEADS, SB_DH), k_gain).astype(jnp.float32).transpose(0, 2, 1, 3)
    v = v.reshape(b, s, SB_HEADS, SB_DH).astype(jnp.float32).transpose(0, 2, 1, 3)
    scale = SB_DH ** -0.5
    outs = []
    for b0 in range(0, s, SB_BLOCK):
        kend = b0 + SB_BLOCK
        z = jnp.einsum('bhqd,bhkd->bhqk', q[:, :, b0:kend], k[:, :, :kend]) * scale
        t_idx = b0 + jnp.arange(SB_BLOCK)
        s_idx = jnp.arange(kend)
        mask = s_idx[None, :] < t_idx[:, None]
        log_1mb = jnp.where(mask, jax.nn.log_sigmoid(-z), 0.0)
        log_w = jax.nn.log_sigmoid(z) + lax.cumsum(log_1mb, axis=3, reverse=True) - log_1mb
        a = jnp.where(mask, jnp.exp(log_w), 0.0)
        outs.append(jnp.einsum('bhqk,bhkd->bhqd', a, v[:, :, :kend]))
    y = jnp.concatenate(outs, axis=2).transpose(0, 2, 1, 3).reshape(b, s, D_MODEL).astype(xn.dtype)
    return y @ w_out


def squared_relu_mlp(xn, w1, w2):
    h = jax.nn.relu(xn @ w1)
    return (h * h) @ w2


def setup_inputs(seed: int = 0) -> dict:
    key = jax.random.key(seed)
    ks = iter(jax.random.split(key, 32))
    f32 = jnp.float32

    def nrm(shape, scale):
        return jax.random.normal(next(ks), shape, f32) * scale

    def gain(shape):
        return 1.0 + 0.02 * jax.random.normal(next(ks), shape, f32)

    D = D_MODEL
    return {
        "x": jax.random.normal(next(ks), (BATCH, SEQ, D), f32),
        "norm_mix": gain((DEPTH, D)),
        "norm_ffn": gain((DEPTH, D)),
        "ret_w_in": nrm((N_RET, D, 6 * D), D ** -0.5),
        "ret_q_norm": gain((N_RET, RET_DK)),
        "ret_k_norm": gain((N_RET, RET_DK)),
        "ret_gn_w": gain((N_RET, 2 * D)),
        "ret_gn_b": nrm((N_RET, 2 * D), 0.01),
        "ret_w_out": nrm((N_RET, 2 * D, D), (2 * D) ** -0.5),
        "conv_pw1_w": nrm((N_CONV, D, 2 * D), D ** -0.5),
        "conv_pw1_b": nrm((N_CONV, 2 * D), 0.01),
        "conv_dw_w": nrm((N_CONV, CONV_WIDTH, D), CONV_WIDTH ** -0.5),
        "conv_dw_b": nrm((N_CONV, D), 0.01),
        "conv_ln_w": gain((N_CONV, D)),
        "conv_ln_b": nrm((N_CONV, D), 0.01),
        "conv_pw2_w": nrm((N_CONV, D, D), D ** -0.5),
        "conv_pw2_b": nrm((N_CONV, D), 0.01),
        "sb_w_in": nrm((N_SB, D, 3 * D), D ** -0.5),
        "sb_q_norm": gain((N_SB, SB_DH)),
        "sb_k_norm": gain((N_SB, SB_DH)),
        "sb_w_out": nrm((N_SB, D, D), D ** -0.5),
        "ffn_w1": nrm((DEPTH, D, D_FF), D ** -0.5),
        "ffn_w2": nrm((DEPTH, D_FF, D), D_FF ** -0.5),
        "final_norm": gain((D,)),
    }


def reference(x, norm_mix, norm_ffn,
              ret_w_in, ret_q_norm, ret_k_norm, ret_gn_w, ret_gn_b, ret_w_out,
              conv_pw1_w, conv_pw1_b, conv_dw_w, conv_dw_b, conv_ln_w, conv_ln_b, conv_pw2_w, conv_pw2_b,
              sb_w_in, sb_q_norm, sb_k_norm, sb_w_out,
              ffn_w1, ffn_w2, final_norm):
    for i in range(DEPTH):
        kind = i % N_MIXERS
        j = i // N_MIXERS
        h = rms_norm(x, norm_mix[i])
        if kind == 0:
            m = retention_mixer(h, ret_w_in[j], ret_q_norm[j], ret_k_norm[j],
                                ret_gn_w[j], ret_gn_b[j], ret_w_out[j])
        elif kind == 1:
            m = conformer_conv_mixer(h, conv_pw1_w[j], conv_pw1_b[j], conv_dw_w[j], conv_dw_b[j],
                                     conv_ln_w[j], conv_ln_b[j], conv_pw2_w[j], conv_pw2_b[j])
        else:
            m = stick_breaking_mixer(h, sb_w_in[j], sb_q_norm[j], sb_k_norm[j], sb_w_out[j])
        x = x + m
        x = x + squared_relu_mlp(rms_norm(x, norm_ffn[i]), ffn_w1[i], ffn_w2[i])
    return rms_norm(x, final_norm)
```

```python
import math
from contextlib import ExitStack

import numpy as np
import ml_dtypes
import concourse.bass as bass
import concourse.mybir as mybir
from concourse.bass_utils import run_bass_kernel_spmd

F32 = mybir.dt.float32
BF16 = mybir.dt.bfloat16
AF = mybir.ActivationFunctionType
ALU = mybir.AluOpType

D = 1024
S = 4096
B = 4
DFF = 4096
EPS = 1e-6
NCORES = 8
TOK = 2048
TT = 512


class Ev:
    __slots__ = ("sem", "val")

    def __init__(self, sem, val):
        self.sem = sem
        self.val = val


class Buf:
    __slots__ = ("name", "w", "r", "sem", "wlist")

    def __init__(self, name, sem=None):
        self.name = name
        self.w = None
        self.r = []
        self.sem = sem


class Prog:
    ENGS = ("pe", "act", "dve", "pool", "sp")

    def __init__(self, nc, stack):
        self.nc = nc
        self.stack = stack
        self.q = {e: [] for e in self.ENGS}
        self.sems = {}
        self.cnt = {}
        self.waited = {}
        self.nsem = 0
        self.arena = None
        self.arena_off = 0
        self.psbanks = None
        self.ps_i = 0
        self.sempool = None
        self.sem_i = 0
        for e in self.ENGS:
            self.newsem("c_" + e)

    def newsem(self, name, kind="hw"):
        if self.sempool is not None:
            pool = self.sempool[kind]
            if self.sem_i[kind] < len(pool):
                nm = pool[self.sem_i[kind]]
            else:
                nm = f"q{kind}{len(pool)}"
                self.sems[nm] = self.stack.enter_context(self.nc.semaphore(nm))
                self.cnt[nm] = 0
                pool.append(nm)
            self.sem_i[kind] += 1
            return nm
        s = self.stack.enter_context(self.nc.semaphore(name))
        self.sems[name] = s
        self.cnt[name] = 0
        self.nsem += 1
        return name

    def buf(self, name, dma=False):
        return Buf(name, "LAZY" if dma else None)

    def semof(self, b, eng):
        if b.sem == "LAZY":
            b.sem = self.newsem("d_" + b.name, kind=("sw" if eng == "pool" else "hw"))
        return b.sem

    def enable_phases(self, arena_bytes=206 * 1024):
        self.arena = self.stack.enter_context(self.nc.sbuf_tensor("arena", [128, arena_bytes // 2], BF16))
        self.arena_bytes = arena_bytes
        self.psbanks = [self.stack.enter_context(self.nc.psum_tensor(f"pbank{i}", [128, 512], F32)) for i in range(8)]
        self.sempool = {"hw": [], "sw": []}
        self.sem_i = {"hw": 0, "sw": 0}

    def next_phase(self):
        for eng in self.ENGS:
            for name, c in self.cnt.items():
                if c > 0:
                    self._wait(eng, Ev(name, c))
        self.arena_off = 0
        self.ps_i = 0
        self.sem_i = {"hw": 0, "sw": 0}

    def sb(self, name, shape, dt):
        if self.arena is None:
            return self.stack.enter_context(self.nc.sbuf_tensor("s_" + name, list(shape), dt))
        shape = list(shape)
        n = 1
        for d in shape[1:]:
            n *= d
        esz = 4 if dt == F32 else 2
        nb = (n * esz + 31) // 32 * 32
        off = self.arena_off
        assert off + nb <= self.arena_bytes, f"arena overflow allocating {name}: {off}+{nb}"
        self.arena_off += nb
        v = self.arena[0:shape[0], off // 2: off // 2 + n * esz // 2]
        if dt == F32:
            v = v.bitcast(F32)
        if len(shape) == 3:
            v = v.rearrange("p (a b) -> p a b", a=shape[1])
        elif len(shape) == 4:
            v = v.rearrange("p (a b c) -> p a b c", a=shape[1], b=shape[2])
        return v

    def ps(self, name, shape, dt=F32):
        if self.psbanks is None:
            return self.stack.enter_context(self.nc.psum_tensor("p_" + name, list(shape), dt))
        shape = list(shape)
        bank = self.psbanks[self.ps_i]
        self.ps_i += 1
        n = 1
        for d in shape[1:]:
            n *= d
        if dt == F32:
            v = bank[0:shape[0], 0:n]
        else:
            v = bank[0:shape[0], 0:n // 2].bitcast(dt)
        if len(shape) == 3:
            v = v.rearrange("p (a b) -> p a b", a=shape[1])
        return v

    def collective(self, kind, op, groups, in_ap, out_ap, extra=()):
        if "cc" not in self.sems:
            self.sems["cc"] = self.stack.enter_context(self.nc.semaphore("cc"))
            self.cnt["cc"] = 0
        return self.op("pool", lambda e: e.collective_compute(kind, op, replica_groups=groups, ins=[in_ap], outs=[out_ap]),
                       sem="cc", inc=1, extra=extra)

    def _wait(self, eng, ev):
        if ev is None:
            return
        if eng == "pe" and ev.sem == "c_pe":
            return
        key = (eng, ev.sem)
        if self.waited.get(key, 0) >= ev.val:
            return
        self.waited[key] = ev.val
        s = self.sems[ev.sem]
        v = ev.val
        self.q[eng].append(lambda e, s=s, v=v: e.wait_ge(s, v))

    def op(self, eng, fn, reads=(), writes=(), signal=True, sem=None, inc=1, extra=()):
        for b in reads:
            self._wait(eng, b.w)
        for b in writes:
            self._wait(eng, b.w)
            for ev in b.r:
                self._wait(eng, ev)
        for ev in extra:
            self._wait(eng, ev)
        name = sem or ("c_" + eng)
        if signal:
            self.cnt[name] += inc
            ev = Ev(name, self.cnt[name])
            s = self.sems[name]
            self.q[eng].append(lambda e, fn=fn, s=s, inc=inc: fn(e).then_inc(s, inc))
        else:
            ev = Ev(name, self.cnt[name] + inc)
            self.q[eng].append(lambda e, fn=fn: fn(e))
        for b in reads:
            b.r.append(ev)
        for b in writes:
            b.w = ev
            b.r = []
        return ev

    def dma(self, eng, out, in_, dst=None, src=None):
        reads = [src] if src is not None else []
        writes = [dst] if dst is not None else []
        semname = self.semof(dst, eng) if (dst is not None and dst.sem) else None
        if semname is None:
            semname = self.out_sem
        return self.op(eng, lambda e: e.dma_start(out=out, in_=in_), reads, writes,
                       sem=semname, inc=16)

    def finish(self, final_evs):
        for ev in final_evs:
            self._wait("sp", ev)
        nc = self.nc
        with nc.Block() as block:
            def mk(name):
                def body(e):
                    for f in self.q[name]:
                        f(e)
                return body
            block.tensor(mk("pe"))
            block.scalar(mk("act"))
            block.vector(mk("dve"))
            block.gpsimd(mk("pool"))
            block.sync(mk("sp"))


def mm_group(P, out_ap, out_buf, terms):
    n = len(terms)
    ev = None
    for i, (l, r, rb) in enumerate(terms):
        ev = P.op("pe",
                  lambda e, l=l, r=r, i=i: e.matmul(out_ap, lhsT=l, rhs=r, start=(i == 0), stop=(i == n - 1)),
                  reads=rb, writes=[out_buf] if i == 0 else [], signal=(i == n - 1))
        if i > 0:
            pass
    out_buf.w = ev
    out_buf.r = []
    return ev


def pcol(v):
    v = np.asarray(v, dtype=np.float32)
    return np.ascontiguousarray(v.reshape(-1, 128).T)


class Rot:
    def __init__(self, items):
        self.items = items
        self.i = 0

    def next(self):
        it = self.items[self.i % len(self.items)]
        self.i += 1
        return it


class Ctx:
    pass


def setup_common(P, nc):
    C = Ctx()
    C.ones = P.sb("ones32", [128, 128], F32)
    C.ones_b = P.buf("ones")
    P.op("pool", lambda e: e.memset(C.ones[:], 1.0), writes=[C.ones_b])
    C.psA = Rot([(P.ps(f"psA{i}", [128, 512]), P.buf(f"psA{i}")) for i in range(3)])
    C.psB = Rot([(P.ps(f"psB{i}", [128, 512]), P.buf(f"psB{i}")) for i in range(3)])
    C.psS = Rot([(P.ps(f"psS{i}", [128, 512]), P.buf(f"psS{i}")) for i in range(2)])
    return C


def load_vec(P, name, dram_ap, nchunk):
    t = P.sb(name, [128, nchunk], F32)
    b = P.buf(name, dma=True)
    P.dma("sp", t[:], dram_ap, dst=b)
    return t, b


def rmsnorm_tile(P, C, x_sb, x_bufs, tsl, gain, gain_b, out_fn, scr, scr_b, rs, rs_b, nfeat=1024):
    nchunk = nfeat // 128
    for c in range(nchunk):
        P.op("act", lambda e, c=c: e.activation(out=scr[:, c, :], in_=x_sb[:, c, tsl], func=AF.Square),
             reads=[x_bufs[c]], writes=[scr_b[c]])
    ps, psb = C.psS.next()
    mm_group(P, ps[:], psb, [(C.ones[:], scr[:, c, :], [scr_b[c], C.ones_b]) for c in range(nchunk)])
    P.op("act", lambda e: e.activation(out=rs[:], in_=ps[:], func=AF.Sqrt, bias=EPS, scale=1.0 / nfeat),
         reads=[psb], writes=[rs_b])
    P.op("dve", lambda e: e.reciprocal(rs[:], rs[:]), reads=[rs_b], writes=[rs_b])
    for c in range(nchunk):
        oap, ob = out_fn(c)
        P.op("dve", lambda e, c=c, oap=oap: e.scalar_tensor_tensor(
            out=oap, in0=x_sb[:, c, tsl], scalar=gain[:, c:c + 1], in1=rs[:], op0=ALU.mult, op1=ALU.mult),
            reads=[x_bufs[c], gain_b, rs_b], writes=[ob])


def build_tok(mode, T=TOK, fm=0, env=None, final=True):
    nc = env.nc if env else bass.Bass("TRN2", target_bir_lowering=False)
    NT = T // TT
    dr = env.dr if env else (lambda n, s, dt, kind: nc.dram_tensor(n, list(s), dt, kind=kind).ap())
    xn_dt = F32 if (env is None or final) else BF16
    xT_d = dr("xT", [D, T], F32, "ExternalInput")
    gn_d = dr("g_next", [128, 8], F32, "ExternalInput")
    xn_o = dr("xn_out", [D, T], xn_dt, "ExternalOutput")
    if mode == "p2":
        if fm:
            mT_d = dr("mT", [fm, T], F32, "ExternalInput")
            wo_d = dr("w_out", [fm, D], F32, "ExternalInput")
        gf_d = dr("g_ffn", [128, 8], F32, "ExternalInput")
        w1_d = dr("w1", [D, DFF], F32, "ExternalInput")
        w2_d = dr("w2", [DFF, D], F32, "ExternalInput")
        x_o = dr("x_out", [D, T], F32, "ExternalOutput")
    with ExitStack() as st:
        if env:
            P = env.P
        else:
            P = Prog(nc, st)
            P.out_sem = P.newsem("outs")
        C = setup_common(P, nc)
        x_sb = P.sb("x_sb", [128, 8, T], F32)
        x_b = [[P.buf(f"x{c}_{t}") for t in range(NT)] for c in range(8)]
        xld = [P.buf(f"xld{t}", dma=True) for t in range(NT)]
        xT_v = xT_d.rearrange("(c p) t -> p c t", p=128)
        for t in range(NT):
            ev = P.dma("sp", x_sb[:, :, t * TT:(t + 1) * TT], xT_v[:, :, t * TT:(t + 1) * TT], dst=xld[t])
            for c in range(8):
                x_b[c][t].w = ev
        gnext, gnext_b = load_vec(P, "gnext", gn_d, 8)
        scr = P.sb("scr", [128, 8, TT], F32)
        scr_b = [P.buf(f"scr{c}") for c in range(8)]
        rs = P.sb("rs", [128, TT], F32)
        rs_b = P.buf("rs")
        outs = []
        xno = Rot([(P.sb(f"xno{i}", [128, 8, TT], xn_dt), [P.buf(f"xno{i}_{c}") for c in range(8)]) for i in range(2)])
        xn_ov = xn_o.rearrange("(c p) t -> p c t", p=128) if (env is None or final) else None
        xno_sems = [P.newsem(f"st_xno{i}") for i in range(2)]
        x_ov = x_o.rearrange("(c p) t -> p c t", p=128) if mode == "p2" else None
        tail_i = [0]

        def tile_tail(t):
            tsl = slice(t * TT, (t + 1) * TT)
            if mode == "p2" and (env is None or not final):
                for c in range(8):
                    outs.append(P.op("sp", lambda e, c=c, tsl=tsl: e.dma_start(out=x_ov[:, c, tsl], in_=x_sb[:, c, tsl]),
                                     reads=[x_b[c][t]], sem=P.out_sem, inc=16))
            o_sb, o_b = xno.next()
            osem = xno_sems[tail_i[0] % 2]
            tail_i[0] += 1
            rmsnorm_tile(P, C, x_sb, [x_b[c][t] for c in range(8)], tsl, gnext, gnext_b,
                         lambda c, o_sb=o_sb, o_b=o_b: (o_sb[:, c, :], o_b[c]), scr, scr_b, rs, rs_b)
            xn_dst = env.xn_dst(t) if (env and not final) else xn_ov[:, :, tsl]
            ev = P.op("sp", lambda e, o_sb=o_sb, xn_dst=xn_dst: e.dma_start(out=xn_dst, in_=o_sb[:]),
                      reads=o_b, sem=osem, inc=16)
            outs.append(ev)
            if env and not final:
                env.gather_tile(P, t, [ev])

        if mode == "p2":
            gffn, gffn_b = load_vec(P, "gffn", gf_d, 8)
            KM = fm // 128
            if fm:
              pass
            R1 = P.sb("R1", [128, 32768], BF16)

            def view(off, d0, d1):
                return R1[:, off:off + d0 * d1].rearrange("p (a b) -> p a b", a=d0)
            lastA = None
            if fm:
                wo_sb = view(0, KM, D)
                wo_b = [P.buf(f"wo{k}", dma=True) for k in range(KM // 4)]
                wo_v = wo_d.rearrange("(c p) f -> p c f", p=128)
                for k4 in range(KM // 4):
                    P.dma("pool", wo_sb[:, 4 * k4:4 * k4 + 4, :], wo_v[:, 4 * k4:4 * k4 + 4, :], dst=wo_b[k4])
                mr = Rot([(view(16384 + i * 8192, KM, TT), P.buf(f"mt{i}", dma=True)) for i in range(2)])
                if env:
                    m_src = env.m_src
                else:
                    mT_v = mT_d.rearrange("(c p) t -> p c t", p=128)
                    m_src = lambda t: mT_v[:, :, t * TT:(t + 1) * TT]
                for t in range(NT):
                    tsl = slice(t * TT, (t + 1) * TT)
                    m_t, m_tb = mr.next()
                    P.dma("pool", m_t, m_src(t), dst=m_tb)
                    for fo in range(8):
                        ps, psb = C.psB.next()
                        lastA = mm_group(P, ps[:], psb,
                                         [(wo_sb[:, k, fo * 128:(fo + 1) * 128], m_t[:, k, :], [wo_b[k // 4], m_tb])
                                          for k in range(KM)])
                        P.op("dve", lambda e, ps=ps, fo=fo, tsl=tsl: e.tensor_tensor(
                            out=x_sb[:, fo, tsl], in0=x_sb[:, fo, tsl], in1=ps[:], op=ALU.add),
                            reads=[psb, x_b[fo][t]], writes=[x_b[fo][t]])
            xn_sb = view(0, 8, T)
            xn_b = [[P.buf(f"xn{c}_{t}") for t in range(NT)] for c in range(8)]
            for c in range(8):
                for t in range(NT):
                    xn_b[c][t].w = lastA
            for t in range(NT):
                tsl = slice(t * TT, (t + 1) * TT)
                rmsnorm_tile(P, C, x_sb, [x_b[c][t] for c in range(8)], tsl, gffn, gffn_b,
                             lambda c, t=t, tsl=tsl: (xn_sb[:, c, tsl], xn_b[c][t]), scr, scr_b, rs, rs_b)
            NG = DFF // 512
            w1r = Rot([(view(16384 + i * 4096, 8, 512), P.buf(f"w1g{i}", dma=True)) for i in range(2)])
            w2r = Rot([(view(24576 + i * 4096, 4, D), P.buf(f"w2g{i}", dma=True)) for i in range(2)])
            for (_, wb_) in w1r.items + w2r.items:
                wb_.w = lastA
            hr = Rot([(P.sb(f"h{i}", [128, 4, TT], BF16), [P.buf(f"h{i}_{j}") for j in range(4)]) for i in range(2)])
            sqr = Rot([(P.sb(f"sq{i}", [128, TT], F32), P.buf(f"sq{i}")) for i in range(2)])
            w1_v = w1_d.rearrange("(c p) f -> p c f", p=128)
            w2_v = w2_d.rearrange("(c p) f -> p c f", p=128)
            for g in range(NG):
                w1g, w1b = w1r.next()
                w2g, w2b = w2r.next()
                P.dma("pool", w1g, w1_v[:, :, g * 512:(g + 1) * 512], dst=w1b)
                P.dma("pool", w2g, w2_v[:, 4 * g:4 * g + 4, :], dst=w2b)
                for t in range(NT):
                    tsl = slice(t * TT, (t + 1) * TT)
                    h_sb, h_b = hr.next()
                    for j in range(4):
                        ps, psb = C.psA.next()
                        mm_group(P, ps[:], psb,
                                 [(w1g[:, k, j * 128:(j + 1) * 128], xn_sb[:, k, tsl], [w1b, xn_b[k][t]])
                                  for k in range(8)])
                        sq, sqb = sqr.next()
                        P.op("act", lambda e, sq=sq, ps=ps: e.activation(out=sq[:], in_=ps[:], func=AF.Square),
                             reads=[psb], writes=[sqb])
                        P.op("dve", lambda e, sq=sq, ps=ps, h_sb=h_sb, j=j: e.scalar_tensor_tensor(
                            out=h_sb[:, j, :], in0=ps[:], scalar=0.0, in1=sq[:], op0=ALU.is_gt, op1=ALU.mult),
                            reads=[psb, sqb], writes=[h_b[j]])
                    for fo in range(8):
                        ps, psb = C.psB.next()
                        mm_group(P, ps[:], psb,
                                 [(w2g[:, j, fo * 128:(fo + 1) * 128], h_sb[:, j, :], [w2b, h_b[j]])
                                  for j in range(4)])
                        P.op("dve", lambda e, ps=ps, fo=fo, tsl=tsl: e.tensor_tensor(
                            out=x_sb[:, fo, tsl], in0=x_sb[:, fo, tsl], in1=ps[:], op=ALU.add),
                            reads=[psb, x_b[fo][t]], writes=[x_b[fo][t]])
                    if g == NG - 1:
                        tile_tail(t)
        if mode != "p2":
            for t in range(NT):
                tile_tail(t)
        if env:
            env.outs = outs
        else:
            P.finish(outs)
    return nc


RET_G = [1.0 - 2.0 ** (-5.0 - h) for h in range(4)]


def ret_tables(heads):
    half = 128
    inv_freq = (10000.0 ** (-np.arange(half, dtype=np.float32) / np.float32(half))).astype(np.float32)
    ang = (np.arange(S, dtype=np.float32)[None, :] * inv_freq[:, None]).astype(np.float32)
    cos = np.cos(ang.astype(np.float64)).astype(np.float32)
    sin = np.sin(ang.astype(np.float64)).astype(np.float32)
    idx = np.arange(128)
    dt = np.zeros((128, 2, 128), np.float32)
    qdec = np.zeros((128, 2, 512), np.float32)
    kdec = np.zeros((128, 2), np.float32)
    g128 = np.zeros((128, 2), np.float32)
    for i, h in enumerate(heads):
        lg = math.log(RET_G[h])
        t = idx[None, :]
        s = idx[:, None]
        same = (t // 64) == (s // 64)
        later = (t // 64) > (s // 64)
        dmat = np.where(same, np.exp(lg * np.abs(t - s)), np.where(later, np.exp(lg * (t - s)), 0.0))
        dt[:, i, :] = dmat.astype(np.float32)
        qdec[:, i, :] = np.tile(np.exp(lg * (idx + 1.0)), 4)[None, :]
        kdec[:, i] = np.exp(lg * (127.0 - idx))
        g128[:, i] = math.exp(lg * 128.0)
    return dict(cos=cos, sin=sin, dt=dt, qdec=qdec, kdec=kdec, g128=g128)


def build_ret(SEQ=S, env=None):
    nc = env.nc if env else bass.Bass("TRN2", target_bir_lowering=False)
    NTI = SEQ // TT
    dr = env.dr if env else (lambda n, s, dt, kind: nc.dram_tensor(n, list(s), dt, kind=kind).ap())
    xn_d = dr("xnT", [D, SEQ], F32, "ExternalInput")
    wq_d = dr("wq", [D, 512], F32, "ExternalInput")
    wk_d = dr("wk", [D, 512], F32, "ExternalInput")
    wv_d = dr("wv", [D, 1024], F32, "ExternalInput")
    wg_d = dr("wg", [D, 1024], F32, "ExternalInput")
    qg_d = dr("qg", [128, 2], F32, "ExternalInput")
    kg_d = dr("kg", [128, 2], F32, "ExternalInput")
    gnw_d = dr("gnw", [128, 8], F32, "ExternalInput")
    gnb_d = dr("gnb", [128, 8], F32, "ExternalInput")
    cos_d = dr("cos", [128, SEQ], F32, "ExternalInput")
    sin_d = dr("sin", [128, SEQ], F32, "ExternalInput")
    dt_d = dr("dt", [128, 2, 128], F32, "ExternalInput")
    qdec_d = dr("qdec", [128, 2, 512], F32, "ExternalInput")
    kdec_d = dr("kdec", [128, 2], F32, "ExternalInput")
    g128_d = dr("g128", [128, 2], F32, "ExternalInput")
    m_o = dr("mT_out", [D, SEQ], F32, "ExternalOutput")
    with ExitStack() as st:
        if env:
            P = env.P
            env.phase_setup(P)
        else:
            P = Prog(nc, st)
            P.out_sem = P.newsem("outs")
        ones = P.sb("ones32", [128, 128], F32)
        ones_b = P.buf("ones")
        P.op("pool", lambda e: e.memset(ones[:], 1.0), writes=[ones_b])
        ident = P.sb("ident", [128, 128], BF16)
        ident_b = P.buf("ident")
        P.op("pool", lambda e: e.memset(ident[:], 1.0), writes=[ident_b])
        P.op("pool", lambda e: e.affine_select(out=ident[:], in_=ident[:], pattern=[[-1, 128]],
                                               compare_op=ALU.is_equal, fill=0.0, base=0, channel_multiplier=1),
             reads=[ident_b], writes=[ident_b])
        psA = Rot([(P.ps(f"psA{i}", [128, 512]), P.buf(f"psA{i}")) for i in range(2)])
        psS = Rot([(P.ps(f"psS{i}", [128, 512]), P.buf(f"psS{i}")) for i in range(1)])
        psT = Rot([(P.ps(f"psT{i}", [128, 4, 128], BF16), P.buf(f"psT{i}")) for i in range(1)])
        psSc = Rot([(P.ps(f"psSc{i}", [128, 128]), P.buf(f"psSc{i}")) for i in range(1)])
        psY = Rot([(P.ps(f"psY{i}", [128, 4, 128]), P.buf(f"psY{i}")) for i in range(1)])
        psSt = Rot([(P.ps(f"psSt{i}", [128, 512]), P.buf(f"psSt{i}")) for i in range(2)])

        def small(name, d_ap, shape):
            t = P.sb(name, shape, F32)
            b = P.buf(name, dma=True)
            P.dma("sp", t[:], d_ap, dst=b)
            return t, b
        qg, qg_b = small("qg", qg_d, [128, 2])
        kg, kg_b = small("kg", kg_d, [128, 2])
        gnw, gnw_b = small("gnw", gnw_d, [128, 8])
        gnb, gnb_b = small("gnb", gnb_d, [128, 8])
        dtt, dtt_b = small("dtt", dt_d, [128, 2, 128])
        qdec, qdec_b = small("qdec", qdec_d, [128, 2, 512])
        kdec, kdec_b = small("kdec", kdec_d, [128, 2])
        g128, g128_b = small("g128", g128_d, [128, 2])

        def wload(name, d_ap, ncol):
            t = P.sb(name, [128, 8, ncol], BF16)
            bs = []
            v = d_ap.rearrange("(c p) f -> p c f", p=128)
            for k2 in range(4):
                b = P.buf(f"{name}{k2}", dma=True)
                P.dma("pool", t[:, 2 * k2:2 * k2 + 2, :], v[:, 2 * k2:2 * k2 + 2, :], dst=b)
                bs.append(b)
            return t, bs
        wq, wq_b = wload("wq", wq_d, 512)
        wk, wk_b = wload("wk", wk_d, 512)
        wv, wv_b = wload("wv", wv_d, 1024)
        wg, wg_b = wload("wg", wg_d, 1024)

        xnr = Rot([(P.sb(f"xn{i}", [128, 8, TT], BF16), P.buf(f"xn{i}", dma=True)) for i in range(2)])
        csr = Rot([(P.sb(f"cs{i}", [128, 2, TT], F32), P.buf(f"cs{i}", dma=True)) for i in range(2)])
        raw = P.sb("raw", [128, 4, TT], F32)
        raw_b = [P.buf(f"raw{c}") for c in range(4)]
        rot = P.sb("rot", [128, 4, TT], F32)
        rot_b = [P.buf(f"rot{c}") for c in range(4)]
        scr = P.sb("scr", [128, 4, TT], F32)
        scr_b = [P.buf(f"scr{c}") for c in range(4)]
        tmpA = P.sb("tmpA", [128, TT], F32); tmpA_b = P.buf("tmpA")
        tmpB = P.sb("tmpB", [128, TT], F32); tmpB_b = P.buf("tmpB")
        rs = P.sb("rs", [128, TT], F32); rs_b = P.buf("rs")
        mu = P.sb("mu", [128, TT], F32); mu_b = P.buf("mu")
        def mkset(i):
            d = Ctx()
            d.QT = P.sb(f"QT{i}", [128, 4, TT], BF16); d.QT_b = [P.buf(f"QT{i}_{c}") for c in range(4)]
            d.QdT = P.sb(f"QdT{i}", [128, 4, TT], BF16); d.QdT_b = [P.buf(f"QdT{i}_{c}") for c in range(4)]
            d.KT = P.sb(f"KT{i}", [128, 4, TT], BF16); d.KT_b = [P.buf(f"KT{i}_{c}") for c in range(4)]
            d.Kd = P.sb(f"Kd{i}", [128, 4, 4, 128], BF16); d.Kd_b = [[P.buf(f"Kd{i}_{b}_{h}") for h in range(2)] for b in range(4)]
            d.Vt = P.sb(f"Vt{i}", [128, 4, 2, 512], BF16); d.Vt_b = [[P.buf(f"Vt{i}_{b}_{h}") for h in range(2)] for b in range(4)]
            return d
        sets = [mkset(0), mkset(1)]
        sg = P.sb("sg", [128, 8, TT], BF16); sg_b = [P.buf(f"sg{c}") for c in range(8)]
        y32 = P.sb("y32", [128, 2, 4, TT], F32)
        y_b = [[[P.buf(f"y{h}_{ec}_{b}") for b in range(4)] for ec in range(4)] for h in range(2)]
        PT = P.sb("PT", [128, 2, 128], BF16); PT_b = [P.buf(f"PT{h}") for h in range(2)]
        S32 = P.sb("S32", [128, 2, 2, 512], F32)
        Sbf = P.sb("Sbf", [128, 2, 2, 512], BF16)
        S_b = [[P.buf(f"S32_{h}_{d}") for d in range(2)] for h in range(2)]
        Sbf_b = [[P.buf(f"Sbf_{h}_{d}") for d in range(2)] for h in range(2)]
        for h in range(2):
            for d in range(2):
                P.op("pool", lambda e, h=h, d=d: e.memset(S32[:, h, d, :], 0.0), writes=[S_b[h][d]])
                P.op("pool", lambda e, h=h, d=d: e.memset(Sbf[:, h, d, :], 0.0), writes=[Sbf_b[h][d]])

        if env:
            xn_src = env.xn_src
        else:
            xn_v = xn_d.rearrange("(c p) t -> p c t", p=128)
            xn_src = lambda ti: xn_v[:, :, ti * TT:(ti + 1) * TT]
            m_ov = m_o.rearrange("(c p) t -> p c t", p=128)
            st_sem = [[P.newsem(f"st_y{h}_{ec}") for ec in range(4)] for h in range(2)]
        outs = []

        def qk_path(w, w_b, gain, gain_b, xn, xn_b, cs, cs_b, is_k, bs):
            QT, QT_b, QdT, QdT_b, KT, KT_b = bs.QT, bs.QT_b, bs.QdT, bs.QdT_b, bs.KT, bs.KT_b
            for c in range(4):
                ps, psb = psA.next()
                mm_group(P, ps[:], psb, [(w[:, k, c * 128:(c + 1) * 128], xn[:, k, :], [w_b[k // 2], xn_b]) for k in range(8)])
                P.op("act", lambda e, c=c, ps=ps: e.copy(raw[:, c, :], ps[:]), reads=[psb], writes=[raw_b[c]])
            for h in range(2):
                for dc in range(2):
                    c = 2 * h + dc
                    P.op("act", lambda e, c=c: e.activation(out=scr[:, c, :], in_=raw[:, c, :], func=AF.Square),
                         reads=[raw_b[c]], writes=[scr_b[c]])
                ps, psb = psS.next()
                mm_group(P, ps[:], psb, [(ones[:], scr[:, 2 * h + dc, :], [scr_b[2 * h + dc], ones_b]) for dc in range(2)])
                if is_k:
                    P.op("act", lambda e, ps=ps: e.activation(out=rs[:], in_=ps[:], func=AF.Sqrt, bias=256.0 * EPS, scale=1.0),
                         reads=[psb], writes=[rs_b])
                else:
                    P.op("act", lambda e, ps=ps: e.activation(out=rs[:], in_=ps[:], func=AF.Sqrt, bias=EPS, scale=1.0 / 256.0),
                         reads=[psb], writes=[rs_b])
                P.op("dve", lambda e: e.reciprocal(rs[:], rs[:]), reads=[rs_b], writes=[rs_b])
                for dc in range(2):
                    c = 2 * h + dc
                    P.op("dve", lambda e, c=c, dc=dc: e.scalar_tensor_tensor(
                        out=raw[:, c, :], in0=raw[:, c, :], scalar=gain[:, dc:dc + 1], in1=rs[:], op0=ALU.mult, op1=ALU.mult),
                        reads=[raw_b[c], gain_b, rs_b], writes=[raw_b[c]])
                c1, c2 = 2 * h, 2 * h + 1
                P.op("dve", lambda e, c1=c1: e.tensor_tensor(out=tmpA[:], in0=raw[:, c1, :], in1=cs[:, 0, :], op=ALU.mult),
                     reads=[raw_b[c1], cs_b], writes=[tmpA_b])
                P.op("dve", lambda e, c2=c2: e.tensor_tensor(out=tmpB[:], in0=raw[:, c2, :], in1=cs[:, 1, :], op=ALU.mult),
                     reads=[raw_b[c2], cs_b], writes=[tmpB_b])
                P.op("dve", lambda e, c1=c1: e.tensor_tensor(out=rot[:, c1, :], in0=tmpA[:], in1=tmpB[:], op=ALU.subtract),
                     reads=[tmpA_b, tmpB_b], writes=[rot_b[c1]])
                P.op("dve", lambda e, c1=c1: e.tensor_tensor(out=tmpA[:], in0=raw[:, c1, :], in1=cs[:, 1, :], op=ALU.mult),
                     reads=[raw_b[c1], cs_b], writes=[tmpA_b])
                P.op("dve", lambda e, c2=c2: e.tensor_tensor(out=tmpB[:], in0=raw[:, c2, :], in1=cs[:, 0, :], op=ALU.mult),
                     reads=[raw_b[c2], cs_b], writes=[tmpB_b])
                P.op("dve", lambda e, c2=c2: e.tensor_tensor(out=rot[:, c2, :], in0=tmpA[:], in1=tmpB[:], op=ALU.add),
                     reads=[tmpA_b, tmpB_b], writes=[rot_b[c2]])
                for c in (c1, c2):
                    if is_k:
                        P.op("act", lambda e, c=c: e.copy(KT[:, c, :], rot[:, c, :]), reads=[rot_b[c]], writes=[KT_b[c]])
                    else:
                        P.op("act", lambda e, c=c: e.copy(QT[:, c, :], rot[:, c, :]), reads=[rot_b[c]], writes=[QT_b[c]])
                        P.op("dve", lambda e, c=c, h=h: e.tensor_tensor(out=QdT[:, c, :], in0=rot[:, c, :], in1=qdec[:, h, :], op=ALU.mult),
                             reads=[rot_b[c], qdec_b], writes=[QdT_b[c]])

        tiles = {}

        def load(ti):
            tsl = slice(ti * TT, (ti + 1) * TT)
            xn, xn_b = xnr.next()
            P.dma("pool", xn[:], xn_src(ti), dst=xn_b)
            cs, cs_b = csr.next()
            P.dma("sp", cs[:, 0, :], cos_d[:, tsl], dst=cs_b)
            P.dma("sp", cs[:, 1, :], sin_d[:, tsl], dst=cs_b)
            tiles[ti] = dict(xn=xn, xn_b=xn_b, cs=cs, cs_b=cs_b, bs=sets[ti % 2])

        def qpath(ti):
            t = tiles[ti]
            qk_path(wq, wq_b, qg, qg_b, t["xn"], t["xn_b"], t["cs"], t["cs_b"], False, t["bs"])

        def kpath(ti):
            t = tiles[ti]
            bs = t["bs"]
            qk_path(wk, wk_b, kg, kg_b, t["xn"], t["xn_b"], t["cs"], t["cs_b"], True, bs)
            for b in range(4):
                bsl = slice(b * 128, (b + 1) * 128)
                pt, ptb = psT.next()
                ev = None
                for c in range(4):
                    ev = P.op("pe", lambda e, c=c, pt=pt, bsl=bsl, bs=bs: e.transpose(pt[:, c, :], bs.KT[:, c, bsl], ident[:]),
                              reads=[bs.KT_b[c], ident_b], writes=[ptb] if c == 0 else [], signal=(c == 3))
                ptb.w = ev
                ptb.r = []
                for h in range(2):
                    P.op("dve", lambda e, b=b, h=h, pt=pt, bs=bs: e.tensor_scalar(
                        out=bs.Kd[:, b, 2 * h:2 * h + 2, :], in0=pt[:, 2 * h:2 * h + 2, :], scalar1=kdec[:, h:h + 1], scalar2=None, op0=ALU.mult),
                        reads=[ptb, kdec_b], writes=[bs.Kd_b[b][h]])

        def vproj(ti):
            t = tiles[ti]
            xn, xn_b, bs = t["xn"], t["xn_b"], t["bs"]
            for b in range(4):
                for h in range(2):
                    ps, psb = psA.next()
                    mm_group(P, ps[:], psb, [(xn[:, k, b * 128:(b + 1) * 128], wv[:, k, h * 512:(h + 1) * 512], [xn_b, wv_b[k // 2]]) for k in range(8)])
                    P.op("act", lambda e, b=b, h=h, ps=ps, bs=bs: e.copy(bs.Vt[:, b, h, :], ps[:]), reads=[psb], writes=[bs.Vt_b[b][h]])

        def gproj(ti):
            t = tiles[ti]
            xn, xn_b = t["xn"], t["xn_b"]
            for c in range(8):
                ps, psb = psA.next()
                mm_group(P, ps[:], psb, [(wg[:, k, c * 128:(c + 1) * 128], xn[:, k, :], [wg_b[k // 2], xn_b]) for k in range(8)])
                P.op("act", lambda e, c=c, ps=ps: e.activation(out=sg[:, c, :], in_=ps[:], func=AF.Silu), reads=[psb], writes=[sg_b[c]])

        def block(ti, b):
            bs = tiles[ti]["bs"]
            KT, KT_b, QT, QT_b, QdT, QdT_b, Kd, Kd_b, Vt, Vt_b = bs.KT, bs.KT_b, bs.QT, bs.QT_b, bs.QdT, bs.QdT_b, bs.Kd, bs.Kd_b, bs.Vt, bs.Vt_b
            bsl = slice(b * 128, (b + 1) * 128)
            for h in range(2):
                sc, scb = psSc.next()
                mm_group(P, sc[:], scb, [(KT[:, 2 * h + dc, bsl], QT[:, 2 * h + dc, bsl], [KT_b[2 * h + dc], QT_b[2 * h + dc]]) for dc in range(2)])
                P.op("dve", lambda e, h=h, sc=sc: e.tensor_tensor(out=PT[:, h, :], in0=sc[:], in1=dtt[:, h, :], op=ALU.mult),
                     reads=[scb, dtt_b], writes=[PT_b[h]])
                py, pyb = psY.next()
                first = True
                ev = None
                for ec in range(4):
                    esl = slice(ec * 128, (ec + 1) * 128)
                    terms = [(Vt[:, b, h, esl], PT[:, h, :], [Vt_b[b][h], PT_b[h]])]
                    terms += [(Sbf[:, h, dc, esl], QdT[:, 2 * h + dc, bsl], [Sbf_b[h][dc], QdT_b[2 * h + dc]]) for dc in range(2)]
                    for i, (l, r, rb) in enumerate(terms):
                        ev = P.op("pe", lambda e, l=l, r=r, i=i, ec=ec, py=py: e.matmul(py[:, ec, :], lhsT=l, rhs=r, start=(i == 0), stop=(i == 2)),
                                  reads=rb, writes=[pyb] if first else [], signal=(ec == 3 and i == 2))
                        first = False
                pyb.w = ev
                pyb.r = []
                P.op("act", lambda e, h=h, bsl=bsl, py=py: e.copy(y32[:, h, :, bsl], py[:]),
                     reads=[pyb], writes=[y_b[h][ec][b] for ec in range(4)])
                for dc in range(2):
                    pst, pstb = psSt.next()
                    mm_group(P, pst[:], pstb, [(Kd[:, b, 2 * h + dc, :], Vt[:, b, h, :], [Kd_b[b][h], Vt_b[b][h]])])
                    P.op("dve", lambda e, h=h, dc=dc, pst=pst: e.scalar_tensor_tensor(
                        out=S32[:, h, dc, :], in0=S32[:, h, dc, :], scalar=g128[:, h:h + 1], in1=pst[:], op0=ALU.mult, op1=ALU.add),
                        reads=[pstb, g128_b, S_b[h][dc]], writes=[S_b[h][dc]])
                    P.op("act", lambda e, h=h, dc=dc: e.copy(Sbf[:, h, dc, :], S32[:, h, dc, :]),
                         reads=[S_b[h][dc]], writes=[Sbf_b[h][dc]])

        def gn_norm(ti):
            for h in range(2):
                for ec in range(4):
                    P.op("act", lambda e, h=h, ec=ec: e.activation(out=scr[:, ec, :], in_=y32[:, h, ec, :], func=AF.Square),
                         reads=[y_b[h][ec][b] for b in range(4)], writes=[scr_b[ec]])
                ps1, ps1b = psA.next()
                mm_group(P, ps1[:], ps1b, [(ones[:], y32[:, h, ec, :], [y_b[h][ec][b] for b in range(4)] + [ones_b]) for ec in range(4)])
                ps2, ps2b = psS.next()
                mm_group(P, ps2[:], ps2b, [(ones[:], scr[:, ec, :], [scr_b[ec], ones_b]) for ec in range(4)])
                P.op("act", lambda e, ps1=ps1: e.activation(out=mu[:], in_=ps1[:], func=AF.Copy, scale=1.0 / 512.0),
                     reads=[ps1b], writes=[mu_b])
                P.op("dve", lambda e: e.tensor_tensor(out=tmpA[:], in0=mu[:], in1=mu[:], op=ALU.mult), reads=[mu_b], writes=[tmpA_b])
                P.op("dve", lambda e, ps2=ps2: e.scalar_tensor_tensor(out=rs[:], in0=ps2[:], scalar=1.0 / 512.0, in1=tmpA[:], op0=ALU.mult, op1=ALU.subtract),
                     reads=[ps2b, tmpA_b], writes=[rs_b])
                P.op("act", lambda e: e.activation(out=rs[:], in_=rs[:], func=AF.Sqrt, bias=EPS, scale=1.0), reads=[rs_b], writes=[rs_b])
                P.op("dve", lambda e: e.reciprocal(rs[:], rs[:]), reads=[rs_b], writes=[rs_b])
                for ec in range(4):
                    c = 4 * h + ec
                    yb = [y_b[h][ec][b] for b in range(4)]
                    P.op("dve", lambda e, h=h, ec=ec: e.tensor_tensor(out=y32[:, h, ec, :], in0=y32[:, h, ec, :], in1=mu[:], op=ALU.subtract),
                         reads=yb + [mu_b], writes=yb)
                    P.op("dve", lambda e, h=h, ec=ec: e.tensor_tensor(out=y32[:, h, ec, :], in0=y32[:, h, ec, :], in1=rs[:], op=ALU.mult),
                         reads=yb + [rs_b], writes=yb)
                    P.op("act", lambda e, h=h, ec=ec, c=c: e.activation(out=y32[:, h, ec, :], in_=y32[:, h, ec, :], func=AF.Identity,
                                                                      bias=gnb[:, c:c + 1], scale=gnw[:, c:c + 1]),
                         reads=yb + [gnw_b, gnb_b], writes=yb)

        def gn_out(ti):
            tsl = slice(ti * TT, (ti + 1) * TT)
            for h in range(2):
                for ec in range(4):
                    c = 4 * h + ec
                    yb = [y_b[h][ec][b] for b in range(4)]
                    P.op("dve", lambda e, h=h, ec=ec, c=c: e.tensor_tensor(out=y32[:, h, ec, :], in0=y32[:, h, ec, :], in1=sg[:, c, :], op=ALU.mult),
                         reads=yb + [sg_b[c]], writes=yb)
                    if env:
                        outs.extend(env.emit_m(P, c * 128, 128, ti, y32[:, h, ec, :], yb))
                    else:
                        outs.append(P.op("sp", lambda e, h=h, ec=ec, c=c, tsl=tsl: e.dma_start(out=m_ov[:, c, tsl], in_=y32[:, h, ec, :]),
                                         reads=yb, sem=st_sem[h][ec], inc=16))

        load(0)
        qpath(0)
        kpath(0)
        vproj(0)
        for ti in range(NTI):
            nxt = ti + 1 < NTI
            if nxt:
                load(ti + 1)
            block(ti, 0)
            if nxt:
                qpath(ti + 1)
            block(ti, 1)
            if nxt:
                kpath(ti + 1)
            block(ti, 2)
            if nxt:
                vproj(ti + 1)
            block(ti, 3)
            gn_norm(ti)
            gproj(ti)
            gn_out(ti)
            if env:
                env.m_done(P, ti)
        if env:
            env.outs = outs
        else:
            P.finish(outs)
    return nc


def sb_tables():
    j = np.arange(128)
    ltri = (j[:, None] >= j[None, :]).astype(np.float32)
    ustr = (j[:, None] < j[None, :]).astype(np.float32)
    oblk = np.zeros((128, 128), np.float32)
    oblk[:64, :64] = 1.0
    oblk[64:, 64:] = 1.0
    t = np.arange(512)
    maskd = np.zeros((128, 4, 512), np.float32)
    for r in range(4):
        maskd[:, r, :] = ((128 * r + j)[:, None] < t[None, :]).astype(np.float32)
    return dict(ltri=ltri, ustr=ustr, oblk=oblk, maskd=maskd)


def build_sb(SEQ=S, env=None):
    nc = env.nc if env else bass.Bass("TRN2", target_bir_lowering=False)
    NTI = SEQ // TT
    NKB = SEQ // 128
    dr = env.dr if env else (lambda n, s, dt, kind: nc.dram_tensor(n, list(s), dt, kind=kind).ap())
    xn_d = dr("xnT", [D, SEQ], F32, "ExternalInput")
    wq_d = dr("wq", [D, 512], F32, "ExternalInput")
    wk_d = dr("wk", [D, 512], F32, "ExternalInput")
    wv_d = dr("wv", [D, 512], F32, "ExternalInput")
    qg_d = dr("qg", [128, 1], F32, "ExternalInput")
    kg_d = dr("kg", [128, 1], F32, "ExternalInput")
    ltri_d = dr("ltri", [128, 128], F32, "ExternalInput")
    ustr_d = dr("ustr", [128, 128], F32, "ExternalInput")
    oblk_d = dr("oblk", [128, 128], F32, "ExternalInput")
    maskd_d = dr("maskd", [128, 4, 512], F32, "ExternalInput")
    y_o = dr("yT_out", [512, SEQ], F32, "ExternalOutput")
    with ExitStack() as st:
        if env:
            P = env.P
            env.phase_setup(P)
        else:
            P = Prog(nc, st)
            P.out_sem = P.newsem("outs")
        psZ = Rot([(P.ps(f"psZ{i}", [128, 512]), P.buf(f"psZ{i}")) for i in range(2)])
        psAcc = Rot([(P.ps(f"psAcc{i}", [128, 512]), P.buf(f"psAcc{i}")) for i in range(2)])
        psY = Rot([(P.ps(f"psY{i}", [64, 512]), P.buf(f"psY{i}")) for i in range(4)])
        psS = psAcc

        def small(name, d_ap, shape, dt=F32, eng="sp"):
            t = P.sb(name, shape, dt)
            b = P.buf(name, dma=True)
            P.dma(eng, t[:], d_ap, dst=b)
            return t, b
        qg, qg_b = small("qg", qg_d, [128, 1])
        kg, kg_b = small("kg", kg_d, [128, 1])
        oblk, oblk_b = small("oblk", oblk_d, [128, 128])
        maskd, maskd_b = small("maskd", maskd_d, [128, 4, 512])
        ltri, ltri_b = small("ltri", ltri_d, [128, 128], BF16, "pool")
        ustr, ustr_b = small("ustr", ustr_d, [128, 128], BF16, "pool")

        def wload(name, d_ap, ncol):
            t = P.sb(name, [128, 8, ncol], BF16)
            bs = []
            v = d_ap.rearrange("(c p) f -> p c f", p=128)
            for k2 in range(4):
                b = P.buf(f"{name}{k2}", dma=True)
                P.dma("pool", t[:, 2 * k2:2 * k2 + 2, :], v[:, 2 * k2:2 * k2 + 2, :], dst=b)
                bs.append(b)
            return t, bs
        wq, wq_b = wload("wq", wq_d, 512)
        wk, wk_b = wload("wk", wk_d, 512)
        wv, wv_b = wload("wv", wv_d, 512)

        xnr = Rot([(P.sb(f"xn{i}", [128, 8, TT], BF16), P.buf(f"xn{i}", dma=True)) for i in range(2)])
        raw = P.sb("raw", [128, 4, TT], F32); raw_b = [P.buf(f"raw{c}") for c in range(4)]
        scr = P.sb("scr", [128, 4, TT], F32); scr_b = [P.buf(f"scr{c}") for c in range(4)]
        rs = P.sb("rs", [128, TT], F32); rs_b = P.buf("rs")
        QT = P.sb("QT", [128, 4, TT], BF16); QT_b = [P.buf(f"QT{c}") for c in range(4)]
        KT = P.sb("KT", [128, 4, SEQ], BF16); KT_b = [[P.buf(f"KT{c}_{t}") for t in range(NTI)] for c in range(4)]
        V = P.sb("V", [128, NKB, 512], BF16); V_b = [P.buf(f"V{kb}") for kb in range(NKB)]
        er = Rot([(P.sb(f"e{i}", [128, TT], F32), P.buf(f"e{i}")) for i in range(8)])
        wr = Rot([(P.sb(f"w{i}", [128, TT], F32), P.buf(f"w{i}")) for i in range(3)])
        spr = Rot([(P.sb(f"sp{i}", [128, TT], BF16), P.buf(f"sp{i}")) for i in range(5)])
        Ar = Rot([(P.sb(f"A{i}", [128, TT], BF16), P.buf(f"A{i}")) for i in range(4)])
        srun_rots = [Rot([(P.sb(f"srun{j}_{i}", [128, TT], BF16), P.buf(f"srun{j}_{i}")) for i in range(3)]) for j in range(2)]
        ones_bf = P.sb("ones_bf", [128, 128], BF16); ones_bf_b = P.buf("ones_bf")
        P.op("pool", lambda e: e.memset(ones_bf[:], 1.0), writes=[ones_bf_b])
        yr = [(P.sb(f"yo{i}", [64, TT], F32), P.buf(f"yo{i}"), P.newsem(f"st_yo{i}")) for i in range(4)]
        yri = [0]

        if env:
            xn_src = env.xn_src
        else:
            xn_v = xn_d.rearrange("(c p) t -> p c t", p=128)
            xn_src = lambda ti: xn_v[:, :, ti * TT:(ti + 1) * TT]
        outs = []

        def qk_proj(w, w_b, gain, gain_b, xn, xn_b, out_fn):
            for c in range(4):
                ps, psb = psZ.next()
                mm_group(P, ps[:], psb, [(w[:, k, c * 128:(c + 1) * 128], xn[:, k, :], [w_b[k // 2], xn_b]) for k in range(8)])
                P.op("act", lambda e, c=c, ps=ps: e.copy(raw[:, c, :], ps[:]), reads=[psb], writes=[raw_b[c]])
                P.op("act", lambda e, c=c: e.activation(out=scr[:, c, :], in_=raw[:, c, :], func=AF.Square),
                     reads=[raw_b[c]], writes=[scr_b[c]])
                ps2, ps2b = psS.next()
                mm_group(P, ps2[:], ps2b, [(oblk[:], scr[:, c, :], [scr_b[c], oblk_b])])
                P.op("act", lambda e, ps2=ps2: e.activation(out=rs[:], in_=ps2[:], func=AF.Sqrt, bias=EPS, scale=1.0 / 64.0),
                     reads=[ps2b], writes=[rs_b])
                P.op("dve", lambda e: e.reciprocal(rs[:], rs[:]), reads=[rs_b], writes=[rs_b])
                oap, ob = out_fn(c)
                P.op("dve", lambda e, c=c, oap=oap: e.scalar_tensor_tensor(
                    out=oap, in0=raw[:, c, :], scalar=gain[:, 0:1], in1=rs[:], op0=ALU.mult, op1=ALU.mult),
                    reads=[raw_b[c], gain_b, rs_b], writes=[ob])

        for ti in range(NTI):
            tsl = slice(ti * TT, (ti + 1) * TT)
            xn, xn_b = xnr.next()
            P.dma("pool", xn[:], xn_src(ti), dst=xn_b)
            qk_proj(wq, wq_b, qg, qg_b, xn, xn_b, lambda c: (QT[:, c, :], QT_b[c]))
            qk_proj(wk, wk_b, kg, kg_b, xn, xn_b, lambda c, ti=ti, tsl=tsl: (KT[:, c, tsl], KT_b[c][ti]))
            for b in range(4):
                kb = 4 * ti + b
                ps, psb = psZ.next()
                mm_group(P, ps[:], psb, [(xn[:, k, b * 128:(b + 1) * 128], wv[:, k, :], [xn_b, wv_b[k // 2]]) for k in range(8)])
                P.op("act", lambda e, kb=kb, ps=ps: e.copy(V[:, kb, :], ps[:]), reads=[psb], writes=[V_b[kb]])
            nkb = 4 * ti + 4
            units = []
            for c in range(4):
                pys = [psY.next() for _ in range(2)]
                for step in range(nkb):
                    for j in range(2):
                        units.append(dict(c=c, j=j, step=step, kb=nkb - 1 - step, psl=slice(64 * j, 64 * j + 64),
                                          py=pys[j][0], pyb=pys[j][1]))
            srun_state = {}

            def stage(k, u):
                c, j, step, kb = u["c"], u["j"], u["step"], u["kb"]
                ksl = slice(kb * 128, (kb + 1) * 128)
                r = kb - 4 * ti
                if k == 0:
                    z, zb = psZ.next()
                    mm_group(P, z[:], zb, [(KT[u["psl"], c, ksl], QT[u["psl"], c, :], [KT_b[c][kb // 4], QT_b[c]])])
                    u["z"], u["zb"] = z, zb
                elif k == 1:
                    e_sb, e_b = er.next()
                    P.op("act", lambda e, e_sb=e_sb, z=u["z"]: e.activation(out=e_sb[:], in_=z[:], func=AF.Exp, scale=0.125),
                         reads=[u["zb"]], writes=[e_b])
                    if r >= 0:
                        P.op("dve", lambda e, e_sb=e_sb, r=r: e.tensor_tensor(out=e_sb[:], in0=e_sb[:], in1=maskd[:, r, :], op=ALU.mult),
                             reads=[e_b, maskd_b], writes=[e_b])
                    u["e"], u["eb"] = e_sb, e_b
                elif k == 2:
                    sp_sb, sp_b = spr.next()
                    P.op("act", lambda e, sp_sb=sp_sb, e_sb=u["e"]: e.activation(out=sp_sb[:], in_=e_sb[:], func=AF.Ln, bias=1.0, scale=1.0),
                         reads=[u["eb"]], writes=[sp_b])
                    u["sp"], u["spb"] = sp_sb, sp_b
                elif k == 3:
                    acc, accb = psAcc.next()
                    terms = [(ltri[:], u["sp"][:], [u["spb"], ltri_b])]
                    if step > 0:
                        srun, srunb = srun_state[(c, j)]
                        terms.append((ones_bf[:], srun[:], [srunb, ones_bf_b]))
                    mm_group(P, acc[:], accb, terms)
                    u["acc"], u["accb"] = acc, accb
                elif k == 4:
                    if kb > 0:
                        nsr, nsrb = srun_rots[j].next()
                        if step == 0:
                            P.op("dve", lambda e, nsr=nsr, sp_sb=u["sp"]: e.tensor_copy(nsr[:], sp_sb[:]),
                                 reads=[u["spb"]], writes=[nsrb])
                        else:
                            old, oldb = srun_state[(c, j)]
                            P.op("dve", lambda e, nsr=nsr, sp_sb=u["sp"], old=old: e.tensor_tensor(out=nsr[:], in0=old[:], in1=sp_sb[:], op=ALU.add),
                                 reads=[u["spb"], oldb], writes=[nsrb])
                        srun_state[(c, j)] = (nsr, nsrb)
                    w_sb, w_b = wr.next()
                    P.op("act", lambda e, w_sb=w_sb, acc=u["acc"]: e.activation(out=w_sb[:], in_=acc[:], func=AF.Exp, scale=-1.0),
                         reads=[u["accb"]], writes=[w_b])
                    u["w"], u["wb"] = w_sb, w_b
                elif k == 5:
                    A_sb, A_b = Ar.next()
                    P.op("dve", lambda e, A_sb=A_sb, e_sb=u["e"], w_sb=u["w"]: e.tensor_tensor(out=A_sb[:], in0=e_sb[:], in1=w_sb[:], op=ALU.mult),
                         reads=[u["eb"], u["wb"]], writes=[A_b])
                    u["A"], u["Ab"] = A_sb, A_b
                elif k == 6:
                    py = u["py"]
                    P.op("pe", lambda e, py=py, A_sb=u["A"], kb=kb, j=j, c=c, step=step: e.matmul(
                        py[:], lhsT=V[:, kb, c * 128 + 64 * j: c * 128 + 64 * j + 64], rhs=A_sb[:], start=(step == 0), stop=(kb == 0)),
                        reads=[u["Ab"], V_b[kb]], writes=[u["pyb"]])
                    if kb == 0:
                        yo, yo_b, yo_sem = yr[yri[0] % 4]
                        yri[0] += 1
                        P.op("act", lambda e, yo=yo, py=py: e.copy(yo[:], py[:]), reads=[u["pyb"]], writes=[yo_b])
                        row = c * 128 + 64 * j
                        if env:
                            outs.extend(env.emit_m(P, row, 64, ti, yo[:], [yo_b]))
                        else:
                            outs.append(P.op("sp", lambda e, yo=yo, row=row, tsl=tsl: e.dma_start(out=y_o[row:row + 64, tsl], in_=yo[:]),
                                             reads=[yo_b], sem=yo_sem, inc=16))

            NS = 7
            SK = [0, 2, 3, 4, 6, 7, 8]
            for slot in range(len(units) + SK[-1]):
                for k in reversed(range(NS)):
                    ui = slot - SK[k]
                    if 0 <= ui < len(units):
                        stage(k, units[ui])
            if env:
                env.m_done(P, ti)
        if env:
            env.outs = outs
        else:
            P.finish(outs)
    return nc


CW = 31
HALO = 32


def build_conv(T=TOK, env=None):
    nc = env.nc if env else bass.Bass("TRN2", target_bir_lowering=False)
    NT = T // TT
    dr = env.dr if env else (lambda n, s, dt, kind: nc.dram_tensor(n, list(s), dt, kind=kind).ap())
    xT_d = dr("xT", [D, T], F32, "ExternalInput")
    xnh_d = dr("xnhT", [D, HALO + T], F32, "ExternalInput")
    flag_d = dr("flag", [128, 1], F32, "ExternalInput")
    pw1_d = dr("pw1_w", [D, 2 * D], F32, "ExternalInput")
    pw1b_d = dr("pw1_b", [128, 16], F32, "ExternalInput")
    dww_d = dr("dw_w", [128, CW * 8], F32, "ExternalInput")
    dwb_d = dr("dw_b", [128, 8], F32, "ExternalInput")
    lnw_d = dr("ln_w", [128, 8], F32, "ExternalInput")
    lnb_d = dr("ln_b", [128, 8], F32, "ExternalInput")
    pw2_d = dr("pw2_w", [D, D], F32, "ExternalInput")
    pw2b_d = dr("pw2_b", [128, 8], F32, "ExternalInput")
    x_o = dr("x_out", [D, T], F32, "ExternalOutput")
    with ExitStack() as st:
        if env:
            P = env.P
        else:
            P = Prog(nc, st)
            P.out_sem = P.newsem("outs")
        ones = P.sb("ones32", [128, 128], F32); ones_b = P.buf("ones")
        P.op("pool", lambda e: e.memset(ones[:], 1.0), writes=[ones_b])
        ident = P.sb("ident", [128, 128], F32); ident_b = P.buf("ident")
        P.op("pool", lambda e: e.memset(ident[:], 1.0), writes=[ident_b])
        P.op("pool", lambda e: e.affine_select(out=ident[:], in_=ident[:], pattern=[[-1, 128]],
                                               compare_op=ALU.is_equal, fill=0.0, base=0, channel_multiplier=1),
             reads=[ident_b], writes=[ident_b])
        psA = Rot([(P.ps(f"psA{i}", [128, 512]), P.buf(f"psA{i}")) for i in range(2)])
        psG = Rot([(P.ps(f"psG{i}", [128, 512]), P.buf(f"psG{i}")) for i in range(2)])
        psC = Rot([(P.ps(f"psC{i}", [128, 512]), P.buf(f"psC{i}")) for i in range(2)])
        psS = Rot([(P.ps(f"psS{i}", [128, 512]), P.buf(f"psS{i}")) for i in range(2)])

        def small(name, d_ap, shape):
            t = P.sb(name, shape, F32)
            b = P.buf(name, dma=True)
            P.dma("sp", t[:], d_ap, dst=b)
            return t, b
        flag, flag_b = small("flag", flag_d, [128, 1])
        pw1b, pw1b_b = small("pw1b", pw1b_d, [128, 16])
        dww, dww_b = small("dww", dww_d, [128, CW * 8])
        dwb, dwb_b = small("dwb", dwb_d, [128, 8])
        lnw, lnw_b = small("lnw", lnw_d, [128, 8])
        lnb, lnb_b = small("lnb", lnb_d, [128, 8])
        pw2b, pw2b_b = small("pw2b", pw2b_d, [128, 8])

        WB = P.sb("WB", [128, 32768], BF16)
        pw1 = WB[:, 0:16384].rearrange("p (a b) -> p a b", a=8)
        pw1_b = [P.buf(f"pw1_{k}", dma=True) for k in range(8)]
        pw1_v = pw1_d.rearrange("(c p) f -> p c f", p=128)
        for k in range(8):
            P.dma("pool", pw1[:, k, :], pw1_v[:, k, :], dst=pw1_b[k])
        pw2 = P.sb("pw2", [128, 8, D], BF16)
        pw2_b = [P.buf(f"pw2_{k}", dma=True) for k in range(4)]
        pw2_v = pw2_d.rearrange("(c p) f -> p c f", p=128)
        for k2 in range(4):
            P.dma("pool", pw2[:, 2 * k2:2 * k2 + 2, :], pw2_v[:, 2 * k2:2 * k2 + 2, :], dst=pw2_b[k2])

        h = P.sb("h", [128, 8, HALO + T], BF16)
        h_b = [[P.buf(f"h{c}_{t}") for t in range(NT + 1)] for c in range(8)]
        xnt = P.sb("xnt", [128, 8, HALO + TT], BF16); xnt_b = P.buf("xnt", dma=True)
        sgr = Rot([(P.sb(f"sgm{i}", [128, TT], F32), P.buf(f"sgm{i}")) for i in range(2)])
        if env:
            xn_halo = env.xn_halo
            xn_main = env.xn_main
        else:
            xnh_v = xnh_d.rearrange("(c p) t -> p c t", p=128)
            xn_halo = lambda: xnh_v[:, :, 0:HALO]
            xn_main = lambda t: xnh_v[:, :, HALO + t * TT:HALO + (t + 1) * TT]
        last_pw1 = None
        for t in range(NT):
            if t == 0:
                P.dma("pool", xnt[:, :, 0:HALO], xn_halo(), dst=xnt_b)
                P.dma("pool", xnt[:, :, HALO:HALO + TT], xn_main(0), dst=xnt_b)
                segs = [(0, HALO, 0), (HALO, TT, 1)]
            else:
                P.dma("pool", xnt[:, :, HALO:HALO + TT], xn_main(t), dst=xnt_b)
                segs = [(HALO, TT, t + 1)]
            for (off, n, hidx) in segs:
                col0 = 0 if hidx == 0 else HALO + (hidx - 1) * TT
                for c in range(8):
                    pa, pab = psA.next()
                    mm_group(P, pa[:, 0:n], pab, [(pw1[:, k, c * 128:(c + 1) * 128], xnt[:, k, off:off + n], [pw1_b[k], xnt_b]) for k in range(8)])
                    pg, pgb = psG.next()
                    last_pw1 = mm_group(P, pg[:, 0:n], pgb, [(pw1[:, k, D + c * 128:D + (c + 1) * 128], xnt[:, k, off:off + n], [pw1_b[k], xnt_b]) for k in range(8)])
                    sg_, sg_b = sgr.next()
                    P.op("act", lambda e, sg_=sg_, pg=pg, n=n, c=c: e.activation(out=sg_[:, 0:n], in_=pg[:, 0:n], func=AF.Sigmoid,
                                                                                bias=pw1b[:, 8 + c:9 + c], scale=1.0),
                         reads=[pgb, pw1b_b], writes=[sg_b])
                    P.op("dve", lambda e, sg_=sg_, pa=pa, n=n, c=c, col0=col0: e.scalar_tensor_tensor(
                        out=h[:, c, col0:col0 + n], in0=pa[:, 0:n], scalar=pw1b[:, c:c + 1], in1=sg_[:, 0:n], op0=ALU.add, op1=ALU.mult),
                        reads=[pab, sg_b, pw1b_b], writes=[h_b[c][hidx]])
                    if hidx == 0:
                        P.op("dve", lambda e, c=c: e.tensor_scalar(out=h[:, c, 0:HALO], in0=h[:, c, 0:HALO], scalar1=flag[:, 0:1], scalar2=None, op0=ALU.mult),
                             reads=[h_b[c][0], flag_b], writes=[h_b[c][0]])
        diag = WB[:, 0:CW * 8 * 128].rearrange("p (a b) -> p a b", a=CW * 8)
        diag_b = [P.buf(f"diag{c}") for c in range(8)]
        for c in range(8):
            diag_b[c].w = last_pw1
            ev = None
            for j in range(CW):
                eng = "dve"
                ev = P.op(eng, lambda e, c=c, j=j: e.tensor_scalar(out=diag[:, c * CW + j, :], in0=ident[:], scalar1=dww[:, j * 8 + c:j * 8 + c + 1], scalar2=None, op0=ALU.mult),
                          reads=[ident_b, dww_b], writes=[], extra=[last_pw1])
                diag_b[c].r.append(ev)
            diag_b[c].w = None
            diag_b[c].wlist = list(diag_b[c].r)
            diag_b[c].r = []
        cv = P.sb("cv", [128, 8, TT], F32); cv_b = [P.buf(f"cv{c}") for c in range(8)]
        scr = P.sb("scr", [128, 8, TT], F32); scr_b = [P.buf(f"scr{c}") for c in range(8)]
        u = P.sb("u", [128, 8, TT], BF16); u_b = [P.buf(f"u{c}") for c in range(8)]
        xt = P.sb("xt", [128, 8, TT], F32); xt_b = P.buf("xt", dma=True)
        xt_cb = [P.buf(f"xt{c}") for c in range(8)]
        st_sem = [P.newsem(f"st_x{c}") for c in range(8)]
        mu = P.sb("mu", [128, TT], F32); mu_b = P.buf("mu")
        rs = P.sb("rs", [128, TT], F32); rs_b = P.buf("rs")
        tmp = P.sb("tmp", [128, TT], F32); tmp_b = P.buf("tmp")
        xT_v = xT_d.rearrange("(c p) t -> p c t", p=128)
        x_ov = x_o.rearrange("(c p) t -> p c t", p=128)
        outs = []
        for t in range(NT):
            tsl = slice(t * TT, (t + 1) * TT)
            evl = P.op("sp", lambda e, tsl=tsl: e.dma_start(out=xt[:], in_=xT_v[:, :, tsl]), reads=[], writes=xt_cb, sem=P.semof(xt_b, "sp"), inc=16)
            for c in range(8):
                pc, pcb = psC.next()
                hb = [h_b[c][t], h_b[c][t + 1]]
                n = CW
                evm = None
                for j in range(CW):
                    evm = P.op("pe", lambda e, pc=pc, c=c, j=j, t=t: e.matmul(pc[:], lhsT=diag[:, c * CW + j, :], rhs=h[:, c, t * TT + 2 + j:t * TT + 2 + j + TT],
                                                                        start=(j == 0), stop=(j == CW - 1)),
                               reads=hb, writes=[pcb] if j == 0 else [], signal=(j == CW - 1), extra=diag_b[c].wlist)
                pcb.w = evm
                pcb.r = []
                P.op("act", lambda e, pc=pc, c=c: e.activation(out=cv[:, c, :], in_=pc[:], func=AF.Identity, bias=dwb[:, c:c + 1], scale=1.0),
                     reads=[pcb, dwb_b], writes=[cv_b[c]])
                P.op("act", lambda e, c=c: e.activation(out=scr[:, c, :], in_=cv[:, c, :], func=AF.Square), reads=[cv_b[c]], writes=[scr_b[c]])
            ps1, ps1b = psS.next()
            mm_group(P, ps1[:], ps1b, [(ones[:], cv[:, c, :], [cv_b[c], ones_b]) for c in range(8)])
            ps2, ps2b = psS.next()
            mm_group(P, ps2[:], ps2b, [(ones[:], scr[:, c, :], [scr_b[c], ones_b]) for c in range(8)])
            P.op("act", lambda e, ps1=ps1: e.activation(out=mu[:], in_=ps1[:], func=AF.Copy, scale=1.0 / D), reads=[ps1b], writes=[mu_b])
            P.op("dve", lambda e: e.tensor_tensor(out=tmp[:], in0=mu[:], in1=mu[:], op=ALU.mult), reads=[mu_b], writes=[tmp_b])
            P.op("dve", lambda e, ps2=ps2: e.scalar_tensor_tensor(out=rs[:], in0=ps2[:], scalar=1.0 / D, in1=tmp[:], op0=ALU.mult, op1=ALU.subtract),
                 reads=[ps2b, tmp_b], writes=[rs_b])
            P.op("act", lambda e: e.activation(out=rs[:], in_=rs[:], func=AF.Sqrt, bias=EPS, scale=1.0), reads=[rs_b], writes=[rs_b])
            P.op("dve", lambda e: e.reciprocal(rs[:], rs[:]), reads=[rs_b], writes=[rs_b])
            for c in range(8):
                P.op("dve", lambda e, c=c: e.tensor_tensor(out=cv[:, c, :], in0=cv[:, c, :], in1=mu[:], op=ALU.subtract), reads=[cv_b[c], mu_b], writes=[cv_b[c]])
                P.op("dve", lambda e, c=c: e.tensor_tensor(out=cv[:, c, :], in0=cv[:, c, :], in1=rs[:], op=ALU.mult), reads=[cv_b[c], rs_b], writes=[cv_b[c]])
                P.op("act", lambda e, c=c: e.activation(out=u[:, c, :], in_=cv[:, c, :], func=AF.Silu, bias=lnb[:, c:c + 1], scale=lnw[:, c:c + 1]),
                     reads=[cv_b[c], lnw_b, lnb_b], writes=[u_b[c]])
            for fo in range(8):
                po, pob = psA.next()
                mm_group(P, po[:], pob, [(pw2[:, k, fo * 128:(fo + 1) * 128], u[:, k, :], [pw2_b[k // 2], u_b[k]]) for k in range(8)])
                P.op("dve", lambda e, po=po, fo=fo: e.scalar_tensor_tensor(out=xt[:, fo, :], in0=po[:], scalar=pw2b[:, fo:fo + 1], in1=xt[:, fo, :], op0=ALU.add, op1=ALU.add),
                     reads=[pob, pw2b_b, xt_cb[fo]], writes=[xt_cb[fo]])
                outs.append(P.op("sp", lambda e, fo=fo, tsl=tsl: e.dma_start(out=x_ov[:, fo, tsl], in_=xt[:, fo, :]), reads=[xt_cb[fo]], sem=st_sem[fo], inc=16))
        if env:
            env.outs = outs
        else:
            P.finish(outs)
    return nc


def conv_inputs(xT, xnhT, flagv, pw1_w, pw1_b, dw_w, dw_b, ln_w, ln_b, pw2_w, pw2_b):
    f = lambda a: np.ascontiguousarray(a, dtype=np.float32)
    dww = np.asarray(dw_w, np.float32).reshape(CW, 8, 128).transpose(2, 0, 1).reshape(128, CW * 8)
    return {"xT": f(xT), "xnhT": f(xnhT), "flag": np.full((128, 1), flagv, np.float32),
            "pw1_w": f(pw1_w), "pw1_b": pcol(pw1_b), "dw_w": f(dww), "dw_b": pcol(dw_b),
            "ln_w": pcol(ln_w), "ln_b": pcol(ln_b), "pw2_w": f(pw2_w), "pw2_b": pcol(pw2_b)}


class Env:
    def __init__(self, nc, P, T):
        self.nc = nc
        self.P = P
        self.T = T
        self.io = {}
        self.outs = []
        self.flags_d = None
        self.mz2d = None
        self.fmy = 0
        self.m_pending = []
        self.m_evs = {}
        self.nocc = False

    def dr(self, name, shape, dt, kind):
        return self.io.get(name)

    def phase_setup(self, P):
        self.flag = P.sb("flags", [128, 2], F32)
        self.flag_b = P.buf("flags", dma=True)
        P.dma("sp", self.flag[:], self.flags_d, dst=self.flag_b)
        self.stages = Rot([(P.sb(f"stg{i}", [128, TT], BF16), P.buf(f"stg{i}"), P.newsem(f"st_stg{i}")) for i in range(4)])

    def gather_tile(self, P, t, evs):
        if self.nocc:
            return
        P.collective("AllGather", ALU.bypass, self.groups, self.xn_my[t], self.xn_full[t], extra=evs)

    def scatter_tile(self, P, tl, evs):
        if self.nocc:
            return
        P.collective("ReduceScatter", ALU.add, self.groups, self.mz2d[tl], self.mrs_out[tl], extra=evs)

    def m_done(self, P, ti):
        nt = self.T // TT
        h, tl = ti // nt, ti % nt
        self.m_evs.setdefault(tl, []).extend(self.m_pending)
        self.m_pending = []
        if h == 1:
            self.scatter_tile(P, tl, self.m_evs.pop(tl))

    def xn_src(self, ti):
        nt = self.T // TT
        rank, tl = ti // nt, ti % nt
        return self.xn_full[tl][rank * D:(rank + 1) * D, :].rearrange("(c p) t -> p c t", p=128)

    def xn_halo(self):
        nt = self.T // TT
        return self.xn_full[nt - 1][0:D, TT - HALO:TT].rearrange("(c p) t -> p c t", p=128)

    def xn_main(self, t):
        return self.xn_my[t].rearrange("(c p) t -> p c t", p=128)

    def xn_dst(self, t):
        return self.xn_my[t].rearrange("(c p) t -> p c t", p=128)

    def m_src(self, t):
        return self.mrs[t].rearrange("(c p) t -> p c t", p=128)

    def emit_m(self, P, row0, nrows, ti, src_ap, src_bufs):
        nt = self.T // TT
        h, tl = ti // nt, ti % nt
        evs = []
        for j in range(2):
            stg, stg_b, stg_sem = self.stages.next()
            P.op("act", lambda e, stg=stg, j=j: e.activation(out=stg[0:nrows, :], in_=src_ap, func=AF.Identity,
                                                            bias=0.0, scale=self.flag[0:nrows, j:j + 1]),
                 reads=list(src_bufs) + [self.flag_b], writes=[stg_b])
            r0 = (h * 2 + j) * self.fmy + row0
            mz = self.mz2d[tl]
            evs.append(P.op("sp", lambda e, stg=stg, r0=r0, mz=mz: e.dma_start(out=mz[r0:r0 + nrows, :], in_=stg[0:nrows, :]),
                            reads=[stg_b], sem=stg_sem, inc=16))
        self.m_pending.extend(evs)
        return evs


FUSED_INPUTS = None


def build_fused(T=TOK, groups=None):
    SEQ = 2 * T
    if groups is None:
        groups = [[2 * i, 2 * i + 1] for i in range(NCORES // 2)]
    nc = bass.Bass("TRN2", target_bir_lowering=False)
    ext = {}

    def inp(name, shape):
        ext[name] = nc.dram_tensor(name, list(shape), F32, kind="ExternalInput").ap()
        return ext[name]

    def internal(name, shape, dt):
        return nc.dram_tensor(name, list(shape), dt).ap()

    xT_in = inp("xT", [D, T])
    flags = inp("flags", [128, 2])
    flagc = inp("flagc", [128, 1])
    g_mix = [inp(f"g_mix{i}", [128, 8]) for i in range(4)]
    g_ffn = [inp(f"g_ffn{i}", [128, 8]) for i in range(4)]
    g_fin = inp("g_final", [128, 8])
    w1 = [inp(f"w1_{i}", [D, DFF]) for i in range(4)]
    w2 = [inp(f"w2_{i}", [DFF, D]) for i in range(4)]
    ret = []
    for j in range(2):
        ret.append(dict(wq=inp(f"r{j}_wq", [D, 512]), wk=inp(f"r{j}_wk", [D, 512]), wv=inp(f"r{j}_wv", [D, 1024]),
                        wg=inp(f"r{j}_wg", [D, 1024]), qg=inp(f"r{j}_qg", [128, 2]), kg=inp(f"r{j}_kg", [128, 2]),
                        gnw=inp(f"r{j}_gnw", [128, 8]), gnb=inp(f"r{j}_gnb", [128, 8]), w_out=inp(f"r{j}_wout", [2 * D, D])))
    rtab = dict(cos=inp("cos", [128, SEQ]), sin=inp("sin", [128, SEQ]), dt=inp("dt", [128, 2, 128]),
                qdec=inp("qdec", [128, 2, 512]), kdec=inp("kdec", [128, 2]), g128=inp("g128", [128, 2]))
    cv = dict(pw1_w=inp("pw1_w", [D, 2 * D]), pw1_b=inp("pw1_b", [128, 16]), dw_w=inp("dw_w", [128, CW * 8]),
              dw_b=inp("dw_b", [128, 8]), ln_w=inp("ln_w", [128, 8]), ln_b=inp("ln_b", [128, 8]),
              pw2_w=inp("pw2_w", [D, D]), pw2_b=inp("pw2_b", [128, 8]))
    sbw = dict(wq=inp("s_wq", [D, 512]), wk=inp("s_wk", [D, 512]), wv=inp("s_wv", [D, 512]),
               qg=inp("s_qg", [128, 1]), kg=inp("s_kg", [128, 1]), ltri=inp("ltri", [128, 128]), ustr=inp("ustr", [128, 128]),
               oblk=inp("oblk", [128, 128]), maskd=inp("maskd", [128, 4, 512]), w_out=inp("s_wout", [D, D]))
    out_d = nc.dram_tensor("out", [D, T], F32, kind="ExternalOutput").ap()
    xsp = internal("xsp", [D, T], F32)
    NT = T // TT
    xn_my = [internal(f"xn_my{t}", [D, TT], BF16) for t in range(NT)]
    xn_full = [internal(f"xn_full{t}", [2 * D, TT], BF16) for t in range(NT)]
    mz_ret = [internal(f"mz_ret{t}", [4 * 1024, TT], BF16) for t in range(NT)]
    mrs_ret = [internal(f"mrs_ret{t}", [2 * 1024, TT], BF16) for t in range(NT)]
    mz_sb = [internal(f"mz_sb{t}", [4 * 512, TT], BF16) for t in range(NT)]
    mrs_sb = [internal(f"mrs_sb{t}", [2 * 512, TT], BF16) for t in range(NT)]

    global FUSED_INPUTS
    FUSED_INPUTS = list(ext.keys())
    with ExitStack() as st:
        P = Prog(nc, st)
        P.out_sem = P.newsem("outs")
        P.enable_phases()
        env = Env(nc, P, T)
        env.flags_d = flags
        env.xn_full = xn_full
        env.xn_my = xn_my

        import os as _os
        nocc = bool(_os.environ.get("NOCC"))

        env.nocc = nocc
        env.groups = groups

        def gather():
            P.next_phase()

        def scatter(mz, mrs):
            P.next_phase()

        def tok_phase(i, x_src, m_src, w_out, fm, final=False):
            env.mrs = m_src
            env.io = {"xT": x_src, "mT": m_src, "w_out": w_out, "g_ffn": g_ffn[i], "w1": w1[i], "w2": w2[i],
                      "g_next": (g_fin if final else g_mix[i + 1]), "x_out": xsp, "xn_out": (out_d if final else xn_my)}
            build_tok("p2", T=T, fm=fm, env=env, final=final)

        def ret_phase(j):
            env.io = dict(xnT=None, **{k: ret[j][k] for k in ("wq", "wk", "wv", "wg", "qg", "kg", "gnw", "gnb")}, **rtab)
            env.mz2d, env.fmy, env.mrs_out = mz_ret, 1024, mrs_ret
            build_ret(SEQ=SEQ, env=env)
            scatter(mz_ret, mrs_ret)

        env.io = {"xT": xT_in, "g_next": g_mix[0], "xn_out": xn_my}
        build_tok("norm0", T=T, env=env, final=False)
        gather()
        ret_phase(0)
        tok_phase(0, xT_in, mrs_ret, ret[0]["w_out"], 2 * D)
        gather()
        env.io = dict(xT=xsp, x_out=xsp, flag=flagc, **cv)
        build_conv(T=T, env=env)
        P.next_phase()
        tok_phase(1, xsp, None, None, 0)
        gather()
        env.io = dict(xnT=None, **{k: sbw[k] for k in ("wq", "wk", "wv", "qg", "kg", "ltri", "ustr", "oblk", "maskd")})
        env.mz2d, env.fmy, env.mrs_out = mz_sb, 512, mrs_sb
        build_sb(SEQ=SEQ, env=env)
        scatter(mz_sb, mrs_sb)
        tok_phase(2, xsp, mrs_sb, sbw["w_out"], D)
        gather()
        ret_phase(1)
        tok_phase(3, xsp, mrs_ret, ret[1]["w_out"], 2 * D, final=True)
        P.next_phase()
        P.finish(env.outs)
    return nc


_PROGS = {}


def fused_inputs(T, x_b, rank, prm):
    A = lambda a: np.ascontiguousarray(a, dtype=np.float32)
    hp = rank
    m = {"xT": A(x_b[rank * T:(rank + 1) * T].T)}
    fl = np.zeros((128, 2), np.float32)
    fl[:, rank] = 1.0
    m["flags"] = fl
    m["flagc"] = np.full((128, 1), float(rank), np.float32)
    for i in range(4):
        m[f"g_mix{i}"] = pcol(prm["norm_mix"][i])
        m[f"g_ffn{i}"] = pcol(prm["norm_ffn"][i])
        m[f"w1_{i}"] = A(prm["ffn_w1"][i])
        m[f"w2_{i}"] = A(prm["ffn_w2"][i])
    m["g_final"] = pcol(prm["final_norm"])
    for j in range(2):
        w_in = np.asarray(prm["ret_w_in"][j], np.float32)
        m[f"r{j}_wq"] = A(w_in[:, hp * 512:(hp + 1) * 512])
        m[f"r{j}_wk"] = A(w_in[:, 1024 + hp * 512:1024 + (hp + 1) * 512])
        m[f"r{j}_wv"] = A(w_in[:, 2048 + hp * 1024:2048 + (hp + 1) * 1024])
        m[f"r{j}_wg"] = A(w_in[:, 4096 + hp * 1024:4096 + (hp + 1) * 1024])
        m[f"r{j}_qg"] = pcol(prm["ret_q_norm"][j])
        m[f"r{j}_kg"] = pcol(prm["ret_k_norm"][j])
        m[f"r{j}_gnw"] = pcol(np.asarray(prm["ret_gn_w"][j], np.float32)[hp * 1024:(hp + 1) * 1024])
        m[f"r{j}_gnb"] = pcol(np.asarray(prm["ret_gn_b"][j], np.float32)[hp * 1024:(hp + 1) * 1024])
        m[f"r{j}_wout"] = A(prm["ret_w_out"][j])
    tabs = ret_tables((2 * hp, 2 * hp + 1))
    m["cos"] = A(tabs["cos"][:, :2 * T]); m["sin"] = A(tabs["sin"][:, :2 * T])
    for k in ("dt", "qdec", "kdec", "g128"):
        m[k] = A(tabs[k])
    ci = conv_inputs(np.zeros((1, 1)), np.zeros((1, 1)), 0.0, prm["conv_pw1_w"][0], prm["conv_pw1_b"][0], prm["conv_dw_w"][0],
                     prm["conv_dw_b"][0], prm["conv_ln_w"][0], prm["conv_ln_b"][0], prm["conv_pw2_w"][0], prm["conv_pw2_b"][0])
    for k in ("pw1_w", "pw1_b", "dw_w", "dw_b", "ln_w", "ln_b", "pw2_w", "pw2_b"):
        m[k] = ci[k]
    sw = np.asarray(prm["sb_w_in"][0], np.float32)
    m["s_wq"] = A(sw[:, hp * 512:(hp + 1) * 512])
    m["s_wk"] = A(sw[:, 1024 + hp * 512:1024 + (hp + 1) * 512])
    m["s_wv"] = A(sw[:, 2048 + hp * 512:2048 + (hp + 1) * 512])
    m["s_qg"] = A(np.tile(np.asarray(prm["sb_q_norm"][0], np.float32), 2)[:, None])
    m["s_kg"] = A(np.tile(np.asarray(prm["sb_k_norm"][0], np.float32), 2)[:, None])
    m["s_wout"] = A(prm["sb_w_out"][0])
    for k, v in sb_tables().items():
        m[k] = A(v)
    return m


def kernel(**prm):
    x = np.asarray(prm["x"], np.float32)
    if "fused" not in _PROGS:
        _PROGS["fused"] = build_fused()
    nc = _PROGS["fused"]
    in_maps = [fused_inputs(TOK, x[c // 2], c % 2, prm) for c in range(NCORES)]
    res = run_bass_kernel_spmd(nc, in_maps, core_ids=list(range(NCORES)))
    out = np.empty((B, S, D), np.float32)
    for c in range(NCORES):
        out[c // 2, (c % 2) * TOK:(c % 2 + 1) * TOK] = res.results[c]["out"].T
    return out
```

```python
import math
from contextlib import ExitStack

import numpy as np
import ml_dtypes
import concourse.bass as bass
import concourse.mybir as mybir
from concourse.bass_utils import run_bass_kernel_spmd

F32 = mybir.dt.float32
BF16 = mybir.dt.bfloat16
AF = mybir.ActivationFunctionType
ALU = mybir.AluOpType

D = 1024
S = 4096
B = 4
DFF = 4096
EPS = 1e-6
NCORES = 8
TOK = 2048
TT = 512


class Ev:
    __slots__ = ("sem", "val")

    def __init__(self, sem, val):
        self.sem = sem
        self.val = val


class Buf:
    __slots__ = ("name", "w", "r", "sem", "wlist")

    def __init__(self, name, sem=None):
        self.name = name
        self.w = None
        self.r = []
        self.sem = sem


class Prog:
    ENGS = ("pe", "act", "dve", "pool", "sp")

    def __init__(self, nc, stack):
        self.nc = nc
        self.stack = stack
        self.q = {e: [] for e in self.ENGS}
        self.sems = {}
        self.cnt = {}
        self.waited = {}
        self.nsem = 0
        self.arena = None
        self.arena_off = 0
        self.psbanks = None
        self.ps_i = 0
        self.sempool = None
        self.sem_i = 0
        for e in self.ENGS:
            self.newsem("c_" + e)

    def newsem(self, name, kind="hw"):
        if self.sempool is not None:
            pool = self.sempool[kind]
            if self.sem_i[kind] < len(pool):
                nm = pool[self.sem_i[kind]]
            else:
                nm = f"q{kind}{len(pool)}"
                self.sems[nm] = self.stack.enter_context(self.nc.semaphore(nm))
                self.cnt[nm] = 0
                pool.append(nm)
            self.sem_i[kind] += 1
            return nm
        s = self.stack.enter_context(self.nc.semaphore(name))
        self.sems[name] = s
        self.cnt[name] = 0
        self.nsem += 1
        return name

    def buf(self, name, dma=False):
        return Buf(name, "LAZY" if dma else None)

    def semof(self, b, eng):
        if b.sem == "LAZY":
            b.sem = self.newsem("d_" + b.name, kind=("sw" if eng == "pool" else "hw"))
        return b.sem

    def enable_phases(self, arena_bytes=206 * 1024):
        self.arena = self.stack.enter_context(self.nc.sbuf_tensor("arena", [128, arena_bytes // 2], BF16))
        self.arena_bytes = arena_bytes
        self.psbanks = [self.stack.enter_context(self.nc.psum_tensor(f"pbank{i}", [128, 512], F32)) for i in range(8)]
        self.sempool = {"hw": [], "sw": []}
        self.sem_i = {"hw": 0, "sw": 0}

    def next_phase(self):
        for eng in self.ENGS:
            for name, c in self.cnt.items():
                if c > 0:
                    self._wait(eng, Ev(name, c))
        self.arena_off = 0
        self.ps_i = 0
        self.sem_i = {"hw": 0, "sw": 0}

    def sb(self, name, shape, dt):
        if self.arena is None:
            return self.stack.enter_context(self.nc.sbuf_tensor("s_" + name, list(shape), dt))
        shape = list(shape)
        n = 1
        for d in shape[1:]:
            n *= d
        esz = 4 if dt == F32 else 2
        nb = (n * esz + 31) // 32 * 32
        off = self.arena_off
        assert off + nb <= self.arena_bytes, f"arena overflow allocating {name}: {off}+{nb}"
        self.arena_off += nb
        self.arena_max = max(getattr(self, "arena_max", 0), self.arena_off)
        v = self.arena[0:shape[0], off // 2: off // 2 + n * esz // 2]
        if dt == F32:
            v = v.bitcast(F32)
        if len(shape) == 3:
            v = v.rearrange("p (a b) -> p a b", a=shape[1])
        elif len(shape) == 4:
            v = v.rearrange("p (a b c) -> p a b c", a=shape[1], b=shape[2])
        return v

    def ps(self, name, shape, dt=F32):
        if self.psbanks is None:
            return self.stack.enter_context(self.nc.psum_tensor("p_" + name, list(shape), dt))
        shape = list(shape)
        bank = self.psbanks[self.ps_i]
        self.ps_i += 1
        n = 1
        for d in shape[1:]:
            n *= d
        if dt == F32:
            v = bank[0:shape[0], 0:n]
        else:
            v = bank[0:shape[0], 0:n // 2].bitcast(dt)
        if len(shape) == 3:
            v = v.rearrange("p (a b) -> p a b", a=shape[1])
        return v

    def collective(self, kind, op, groups, in_ap, out_ap, extra=()):
        if "cc" not in self.sems:
            self.sems["cc"] = self.stack.enter_context(self.nc.semaphore("cc"))
            self.cnt["cc"] = 0
        return self.op("pool", lambda e: e.collective_compute(kind, op, replica_groups=groups, ins=[in_ap], outs=[out_ap]),
                       sem="cc", inc=1, extra=extra)

    def _wait(self, eng, ev):
        if ev is None:
            return
        if eng == "pe" and ev.sem == "c_pe":
            return
        key = (eng, ev.sem)
        if self.waited.get(key, 0) >= ev.val:
            return
        self.waited[key] = ev.val
        s = self.sems[ev.sem]
        v = ev.val
        self.q[eng].append(lambda e, s=s, v=v: e.wait_ge(s, v))

    def op(self, eng, fn, reads=(), writes=(), signal=True, sem=None, inc=1, extra=()):
        for b in reads:
            self._wait(eng, b.w)
        for b in writes:
            self._wait(eng, b.w)
            for ev in b.r:
                self._wait(eng, ev)
        for ev in extra:
            self._wait(eng, ev)
        name = sem or ("c_" + eng)
        if signal:
            self.cnt[name] += inc
            ev = Ev(name, self.cnt[name])
            s = self.sems[name]
            self.q[eng].append(lambda e, fn=fn, s=s, inc=inc: fn(e).then_inc(s, inc))
        else:
            ev = Ev(name, self.cnt[name] + inc)
            self.q[eng].append(lambda e, fn=fn: fn(e))
        for b in reads:
            b.r.append(ev)
        for b in writes:
            b.w = ev
            b.r = []
        return ev

    def dma(self, eng, out, in_, dst=None, src=None):
        reads = [src] if src is not None else []
        writes = [dst] if dst is not None else []
        semname = self.semof(dst, eng) if (dst is not None and dst.sem) else None
        if semname is None:
            semname = self.out_sem
        return self.op(eng, lambda e: e.dma_start(out=out, in_=in_), reads, writes,
                       sem=semname, inc=16)

    def finish(self, final_evs):
        for ev in final_evs:
            self._wait("sp", ev)
        nc = self.nc
        with nc.Block() as block:
            def mk(name):
                def body(e):
                    for f in self.q[name]:
                        f(e)
                return body
            block.tensor(mk("pe"))
            block.scalar(mk("act"))
            block.vector(mk("dve"))
            block.gpsimd(mk("pool"))
            block.sync(mk("sp"))


def mm_group(P, out_ap, out_buf, terms):
    n = len(terms)
    ev = None
    for i, (l, r, rb) in enumerate(terms):
        ev = P.op("pe",
                  lambda e, l=l, r=r, i=i: e.matmul(out_ap, lhsT=l, rhs=r, start=(i == 0), stop=(i == n - 1)),
                  reads=rb, writes=[out_buf] if i == 0 else [], signal=(i == n - 1))
        if i > 0:
            pass
    out_buf.w = ev
    out_buf.r = []
    return ev


def pcol(v):
    v = np.asarray(v, dtype=np.float32)
    return np.ascontiguousarray(v.reshape(-1, 128).T)


class Rot:
    def __init__(self, items):
        self.items = items
        self.i = 0

    def next(self):
        it = self.items[self.i % len(self.items)]
        self.i += 1
        return it


class Ctx:
    pass


def setup_common(P, nc):
    C = Ctx()
    C.ones = P.sb("ones32", [128, 128], F32)
    C.ones_b = P.buf("ones")
    P.op("pool", lambda e: e.memset(C.ones[:], 1.0), writes=[C.ones_b])
    C.psA = Rot([(P.ps(f"psA{i}", [128, 512]), P.buf(f"psA{i}")) for i in range(3)])
    C.psB = Rot([(P.ps(f"psB{i}", [128, 512]), P.buf(f"psB{i}")) for i in range(3)])
    C.psS = Rot([(P.ps(f"psS{i}", [128, 512]), P.buf(f"psS{i}")) for i in range(2)])
    return C


def load_vec(P, name, dram_ap, nchunk):
    t = P.sb(name, [128, nchunk], F32)
    b = P.buf(name, dma=True)
    P.dma("sp", t[:], dram_ap, dst=b)
    return t, b


def rmsnorm_tile(P, C, x_sb, x_bufs, tsl, gain, gain_b, out_fn, scr, scr_b, rs, rs_b, nfeat=1024):
    nchunk = nfeat // 128
    for c in range(nchunk):
        P.op("act", lambda e, c=c: e.activation(out=scr[:, c, :], in_=x_sb[:, c, tsl], func=AF.Square),
             reads=[x_bufs[c]], writes=[scr_b[c]])
    ps, psb = C.psS.next()
    mm_group(P, ps[:], psb, [(C.ones[:], scr[:, c, :], [scr_b[c], C.ones_b]) for c in range(nchunk)])
    P.op("act", lambda e: e.activation(out=rs[:], in_=ps[:], func=AF.Sqrt, bias=EPS, scale=1.0 / nfeat),
         reads=[psb], writes=[rs_b])
    P.op("dve", lambda e: e.reciprocal(rs[:], rs[:]), reads=[rs_b], writes=[rs_b])
    for c in range(nchunk):
        oap, ob = out_fn(c)
        P.op("dve", lambda e, c=c, oap=oap: e.scalar_tensor_tensor(
            out=oap, in0=x_sb[:, c, tsl], scalar=gain[:, c:c + 1], in1=rs[:], op0=ALU.mult, op1=ALU.mult),
            reads=[x_bufs[c], gain_b, rs_b], writes=[ob])


def build_tok(mode, T=TOK, fm=0, env=None, final=True):
    nc = env.nc if env else bass.Bass("TRN2", target_bir_lowering=False)
    NT = T // TT
    dr = env.dr if env else (lambda n, s, dt, kind: nc.dram_tensor(n, list(s), dt, kind=kind).ap())
    xn_dt = F32 if (env is None or final) else BF16
    xT_d = dr("xT", [D, T], F32, "ExternalInput")
    gn_d = dr("g_next", [128, 8], F32, "ExternalInput")
    xn_o = dr("xn_out", [D, T], xn_dt, "ExternalOutput")
    if mode == "p2":
        if fm:
            mT_d = dr("mT", [fm, T], F32, "ExternalInput")
            wo_d = dr("w_out", [fm, D], F32, "ExternalInput")
        gf_d = dr("g_ffn", [128, 8], F32, "ExternalInput")
        w1_d = dr("w1", [D, DFF], F32, "ExternalInput")
        w2_d = dr("w2", [DFF, D], F32, "ExternalInput")
        x_o = dr("x_out", [D, T], F32, "ExternalOutput")
    with ExitStack() as st:
        if env:
            P = env.P
        else:
            P = Prog(nc, st)
            P.out_sem = P.newsem("outs")
        C = setup_common(P, nc)
        x_sb = P.sb("x_sb", [128, 8, T], F32)
        x_b = [[P.buf(f"x{c}_{t}") for t in range(NT)] for c in range(8)]
        xld = [P.buf(f"xld{t}", dma=True) for t in range(NT)]
        xT_v = xT_d.rearrange("(c p) t -> p c t", p=128)
        for t in range(NT):
            ev = P.dma("sp", x_sb[:, :, t * TT:(t + 1) * TT], xT_v[:, :, t * TT:(t + 1) * TT], dst=xld[t])
            for c in range(8):
                x_b[c][t].w = ev
        gnext, gnext_b = load_vec(P, "gnext", gn_d, 8)
        scr = P.sb("scr", [128, 8, TT], F32)
        scr_b = [P.buf(f"scr{c}") for c in range(8)]
        rs = P.sb("rs", [128, TT], F32)
        rs_b = P.buf("rs")
        outs = []
        xno = Rot([(P.sb(f"xno{i}", [128, 8, TT], xn_dt), [P.buf(f"xno{i}_{c}") for c in range(8)]) for i in range(2)])
        xn_ov = xn_o.rearrange("(c p) t -> p c t", p=128) if (env is None or final) else None
        xno_sems = [P.newsem(f"st_xno{i}") for i in range(2)]
        x_ov = x_o.rearrange("(c p) t -> p c t", p=128) if mode == "p2" else None
        tail_i = [0]

        def tile_tail(t):
            tsl = slice(t * TT, (t + 1) * TT)
            if mode == "p2" and (env is None or not final):
                for c in range(8):
                    outs.append(P.op("sp", lambda e, c=c, tsl=tsl: e.dma_start(out=x_ov[:, c, tsl], in_=x_sb[:, c, tsl]),
                                     reads=[x_b[c][t]], sem=P.out_sem, inc=16))
            o_sb, o_b = xno.next()
            osem = xno_sems[tail_i[0] % 2]
            tail_i[0] += 1
            rmsnorm_tile(P, C, x_sb, [x_b[c][t] for c in range(8)], tsl, gnext, gnext_b,
                         lambda c, o_sb=o_sb, o_b=o_b: (o_sb[:, c, :], o_b[c]), scr, scr_b, rs, rs_b)
            xn_dst = env.xn_dst(t) if (env and not final) else xn_ov[:, :, tsl]
            ev = P.op("sp", lambda e, o_sb=o_sb, xn_dst=xn_dst: e.dma_start(out=xn_dst, in_=o_sb[:]),
                      reads=o_b, sem=osem, inc=16)
            outs.append(ev)
            if env and not final:
                env.gather_tile(P, t, [ev])

        if mode == "p2":
            gffn, gffn_b = load_vec(P, "gffn", gf_d, 8)
            KM = fm // 128
            if fm:
              pass
            R1 = P.sb("R1", [128, 32768], BF16)

            def view(off, d0, d1):
                return R1[:, off:off + d0 * d1].rearrange("p (a b) -> p a b", a=d0)
            lastA = None
            if fm:
                wo_sb = view(0, KM, D)
                wo_b = [P.buf(f"wo{k}", dma=True) for k in range(KM // 4)]
                wo_v = wo_d.rearrange("(c p) f -> p c f", p=128)
                for k4 in range(KM // 4):
                    P.dma("pool", wo_sb[:, 4 * k4:4 * k4 + 4, :], wo_v[:, 4 * k4:4 * k4 + 4, :], dst=wo_b[k4])
                mr = Rot([(view(16384 + i * 8192, KM, TT), P.buf(f"mt{i}", dma=True)) for i in range(2)])
                if env:
                    m_src = env.m_src
                else:
                    mT_v = mT_d.rearrange("(c p) t -> p c t", p=128)
                    m_src = lambda t: mT_v[:, :, t * TT:(t + 1) * TT]
                for t in range(NT):
                    tsl = slice(t * TT, (t + 1) * TT)
                    m_t, m_tb = mr.next()
                    P.dma("pool", m_t, m_src(t), dst=m_tb)
                    for fo in range(8):
                        ps, psb = C.psB.next()
                        lastA = mm_group(P, ps[:], psb,
                                         [(wo_sb[:, k, fo * 128:(fo + 1) * 128], m_t[:, k, :], [wo_b[k // 4], m_tb])
                                          for k in range(KM)])
                        P.op("dve", lambda e, ps=ps, fo=fo, tsl=tsl: e.tensor_tensor(
                            out=x_sb[:, fo, tsl], in0=x_sb[:, fo, tsl], in1=ps[:], op=ALU.add),
                            reads=[psb, x_b[fo][t]], writes=[x_b[fo][t]])
            xn_sb = view(0, 8, T)
            xn_b = [[P.buf(f"xn{c}_{t}") for t in range(NT)] for c in range(8)]
            for c in range(8):
                for t in range(NT):
                    xn_b[c][t].w = lastA
            for t in range(NT):
                tsl = slice(t * TT, (t + 1) * TT)
                rmsnorm_tile(P, C, x_sb, [x_b[c][t] for c in range(8)], tsl, gffn, gffn_b,
                             lambda c, t=t, tsl=tsl: (xn_sb[:, c, tsl], xn_b[c][t]), scr, scr_b, rs, rs_b)
            NG = DFF // 512
            w1r = Rot([(view(16384 + i * 4096, 8, 512), P.buf(f"w1g{i}", dma=True)) for i in range(2)])
            w2r = Rot([(view(24576 + i * 4096, 4, D), P.buf(f"w2g{i}", dma=True)) for i in range(2)])
            for (_, wb_) in w1r.items + w2r.items:
                wb_.w = lastA
            hr = Rot([(P.sb(f"h{i}", [128, 4, TT], BF16), [P.buf(f"h{i}_{j}") for j in range(4)]) for i in range(2)])
            sqr = Rot([(P.sb(f"sq{i}", [128, TT], F32), P.buf(f"sq{i}")) for i in range(2)])
            w1_v = w1_d.rearrange("(c p) f -> p c f", p=128)
            w2_v = w2_d.rearrange("(c p) f -> p c f", p=128)
            for g in range(NG):
                w1g, w1b = w1r.next()
                w2g, w2b = w2r.next()
                P.dma("pool", w1g, w1_v[:, :, g * 512:(g + 1) * 512], dst=w1b)
                P.dma("pool", w2g, w2_v[:, 4 * g:4 * g + 4, :], dst=w2b)
                for t in range(NT):
                    tsl = slice(t * TT, (t + 1) * TT)
                    h_sb, h_b = hr.next()
                    for j in range(4):
                        ps, psb = C.psA.next()
                        mm_group(P, ps[:], psb,
                                 [(w1g[:, k, j * 128:(j + 1) * 128], xn_sb[:, k, tsl], [w1b, xn_b[k][t]])
                                  for k in range(8)])
                        sq, sqb = sqr.next()
                        P.op("act", lambda e, sq=sq, ps=ps: e.activation(out=sq[:], in_=ps[:], func=AF.Square),
                             reads=[psb], writes=[sqb])
                        P.op("dve", lambda e, sq=sq, ps=ps, h_sb=h_sb, j=j: e.scalar_tensor_tensor(
                            out=h_sb[:, j, :], in0=ps[:], scalar=0.0, in1=sq[:], op0=ALU.is_gt, op1=ALU.mult),
                            reads=[psb, sqb], writes=[h_b[j]])
                    for fo in range(8):
                        ps, psb = C.psB.next()
                        mm_group(P, ps[:], psb,
                                 [(w2g[:, j, fo * 128:(fo + 1) * 128], h_sb[:, j, :], [w2b, h_b[j]])
                                  for j in range(4)])
                        P.op("dve", lambda e, ps=ps, fo=fo, tsl=tsl: e.tensor_tensor(
                            out=x_sb[:, fo, tsl], in0=x_sb[:, fo, tsl], in1=ps[:], op=ALU.add),
                            reads=[psb, x_b[fo][t]], writes=[x_b[fo][t]])
                    if g == NG - 1:
                        tile_tail(t)
        if mode != "p2":
            for t in range(NT):
                tile_tail(t)
        if env:
            env.outs = outs
        else:
            P.finish(outs)
    return nc


RET_G = [1.0 - 2.0 ** (-5.0 - h) for h in range(4)]


def ret_tables(heads):
    half = 128
    inv_freq = (10000.0 ** (-np.arange(half, dtype=np.float32) / np.float32(half))).astype(np.float32)
    ang = (np.arange(S, dtype=np.float32)[None, :] * inv_freq[:, None]).astype(np.float32)
    cos = np.cos(ang.astype(np.float64)).astype(np.float32)
    sin = np.sin(ang.astype(np.float64)).astype(np.float32)
    idx = np.arange(128)
    dt = np.zeros((128, 2, 128), np.float32)
    qdec = np.zeros((128, 2, 512), np.float32)
    kdec = np.zeros((128, 2), np.float32)
    g128 = np.zeros((128, 2), np.float32)
    for i, h in enumerate(heads):
        lg = math.log(RET_G[h])
        t = idx[None, :]
        s = idx[:, None]
        same = (t // 64) == (s // 64)
        later = (t // 64) > (s // 64)
        dmat = np.where(same, np.exp(lg * np.abs(t - s)), np.where(later, np.exp(lg * (t - s)), 0.0))
        dt[:, i, :] = dmat.astype(np.float32)
        qdec[:, i, :] = np.tile(np.exp(lg * (idx + 1.0)), 4)[None, :]
        kdec[:, i] = np.exp(lg * (127.0 - idx))
        g128[:, i] = math.exp(lg * 128.0)
    return dict(cos=cos, sin=sin, dt=dt, qdec=qdec, kdec=kdec, g128=g128)


def build_ret(SEQ=S, env=None):
    nc = env.nc if env else bass.Bass("TRN2", target_bir_lowering=False)
    NTI = SEQ // TT
    dr = env.dr if env else (lambda n, s, dt, kind: nc.dram_tensor(n, list(s), dt, kind=kind).ap())
    xn_d = dr("xnT", [D, SEQ], F32, "ExternalInput")
    wq_d = dr("wq", [D, 512], F32, "ExternalInput")
    wk_d = dr("wk", [D, 512], F32, "ExternalInput")
    wv_d = dr("wv", [D, 1024], F32, "ExternalInput")
    wg_d = dr("wg", [D, 1024], F32, "ExternalInput")
    qg_d = dr("qg", [128, 2], F32, "ExternalInput")
    kg_d = dr("kg", [128, 2], F32, "ExternalInput")
    gnw_d = dr("gnw", [128, 8], F32, "ExternalInput")
    gnb_d = dr("gnb", [128, 8], F32, "ExternalInput")
    cos_d = dr("cos", [128, SEQ], F32, "ExternalInput")
    sin_d = dr("sin", [128, SEQ], F32, "ExternalInput")
    dt_d = dr("dt", [128, 2, 128], F32, "ExternalInput")
    qdec_d = dr("qdec", [128, 2, 512], F32, "ExternalInput")
    kdec_d = dr("kdec", [128, 2], F32, "ExternalInput")
    g128_d = dr("g128", [128, 2], F32, "ExternalInput")
    m_o = dr("mT_out", [D, SEQ], F32, "ExternalOutput")
    with ExitStack() as st:
        if env:
            P = env.P
            env.phase_setup(P)
        else:
            P = Prog(nc, st)
            P.out_sem = P.newsem("outs")
        ones = P.sb("ones32", [128, 128], F32)
        ones_b = P.buf("ones")
        P.op("pool", lambda e: e.memset(ones[:], 1.0), writes=[ones_b])
        ident = P.sb("ident", [128, 128], BF16)
        ident_b = P.buf("ident")
        P.op("pool", lambda e: e.memset(ident[:], 1.0), writes=[ident_b])
        P.op("pool", lambda e: e.affine_select(out=ident[:], in_=ident[:], pattern=[[-1, 128]],
                                               compare_op=ALU.is_equal, fill=0.0, base=0, channel_multiplier=1),
             reads=[ident_b], writes=[ident_b])
        psA = Rot([(P.ps(f"psA{i}", [128, 512]), P.buf(f"psA{i}")) for i in range(2)])
        psS = Rot([(P.ps(f"psS{i}", [128, 512]), P.buf(f"psS{i}")) for i in range(1)])
        psT = Rot([(P.ps(f"psT{i}", [128, 4, 128], BF16), P.buf(f"psT{i}")) for i in range(1)])
        psSc = Rot([(P.ps(f"psSc{i}", [128, 128]), P.buf(f"psSc{i}")) for i in range(1)])
        psY = Rot([(P.ps(f"psY{i}", [128, 4, 128]), P.buf(f"psY{i}")) for i in range(1)])
        psSt = Rot([(P.ps(f"psSt{i}", [128, 512]), P.buf(f"psSt{i}")) for i in range(2)])

        def small(name, d_ap, shape):
            t = P.sb(name, shape, F32)
            b = P.buf(name, dma=True)
            P.dma("sp", t[:], d_ap, dst=b)
            return t, b
        qg, qg_b = small("qg", qg_d, [128, 2])
        kg, kg_b = small("kg", kg_d, [128, 2])
        gnw, gnw_b = small("gnw", gnw_d, [128, 8])
        gnb, gnb_b = small("gnb", gnb_d, [128, 8])
        dtt, dtt_b = small("dtt", dt_d, [128, 2, 128])
        qdec, qdec_b = small("qdec", qdec_d, [128, 2, 512])
        kdec, kdec_b = small("kdec", kdec_d, [128, 2])
        g128, g128_b = small("g128", g128_d, [128, 2])

        def wload(name, d_ap, ncol):
            t = P.sb(name, [128, 8, ncol], BF16)
            bs = []
            v = d_ap.rearrange("(c p) f -> p c f", p=128)
            for k2 in range(4):
                b = P.buf(f"{name}{k2}", dma=True)
                P.dma("pool", t[:, 2 * k2:2 * k2 + 2, :], v[:, 2 * k2:2 * k2 + 2, :], dst=b)
                bs.append(b)
            return t, bs
        wq, wq_b = wload("wq", wq_d, 512)
        wk, wk_b = wload("wk", wk_d, 512)
        wv, wv_b = wload("wv", wv_d, 1024)
        wg, wg_b = wload("wg", wg_d, 1024)

        xnr = Rot([(P.sb(f"xn{i}", [128, 8, TT], BF16), P.buf(f"xn{i}", dma=True)) for i in range(2)])
        csr = Rot([(P.sb(f"cs{i}", [128, 2, TT], F32), P.buf(f"cs{i}", dma=True)) for i in range(2)])
        raw = P.sb("raw", [128, 4, TT], F32)
        raw_b = [P.buf(f"raw{c}") for c in range(4)]
        rs2 = P.sb("rs2", [128, 2, TT], F32); rs2_b = [P.buf(f"rs2_{h}") for h in range(2)]
        mu2 = P.sb("mu2", [128, 2, TT], F32); mu2_b = [P.buf(f"mu2_{h}") for h in range(2)]
        scr = P.sb("scr", [128, 4, TT], F32)
        scr_b = [P.buf(f"scr{c}") for c in range(4)]
        tmpA = P.sb("tmpA", [128, TT], F32); tmpA_b = P.buf("tmpA")
        tmpB = P.sb("tmpB", [128, TT], F32); tmpB_b = P.buf("tmpB")
        def mkset(i):
            d = Ctx()
            d.QT = P.sb(f"QT{i}", [128, 4, TT], BF16); d.QT_b = [P.buf(f"QT{i}_{c}") for c in range(4)]
            d.QdT = P.sb(f"QdT{i}", [128, 4, TT], BF16); d.QdT_b = [P.buf(f"QdT{i}_{c}") for c in range(4)]
            d.KT = P.sb(f"KT{i}", [128, 4, TT], BF16); d.KT_b = [P.buf(f"KT{i}_{c}") for c in range(4)]
            d.Kd = P.sb(f"Kd{i}", [128, 4, 4, 128], BF16); d.Kd_b = [[P.buf(f"Kd{i}_{b}_{h}") for h in range(2)] for b in range(4)]
            d.Vt = P.sb(f"Vt{i}", [128, 4, 2, 512], BF16); d.Vt_b = [[P.buf(f"Vt{i}_{b}_{h}") for h in range(2)] for b in range(4)]
            return d
        sets = [mkset(0), mkset(1)]
        sg = P.sb("sg", [128, 8, TT], BF16); sg_b = [P.buf(f"sg{c}") for c in range(8)]
        y32 = P.sb("y32", [128, 2, 4, TT], F32)
        y_b = [[[P.buf(f"y{h}_{ec}_{b}") for b in range(4)] for ec in range(4)] for h in range(2)]
        PT = P.sb("PT", [128, 2, 128], BF16); PT_b = [P.buf(f"PT{h}") for h in range(2)]
        S32 = P.sb("S32", [128, 2, 2, 512], F32)
        Sbf = P.sb("Sbf", [128, 2, 2, 512], BF16)
        S_b = [[P.buf(f"S32_{h}_{d}") for d in range(2)] for h in range(2)]
        Sbf_b = [[P.buf(f"Sbf_{h}_{d}") for d in range(2)] for h in range(2)]
        for h in range(2):
            for d in range(2):
                P.op("pool", lambda e, h=h, d=d: e.memset(S32[:, h, d, :], 0.0), writes=[S_b[h][d]])
                P.op("pool", lambda e, h=h, d=d: e.memset(Sbf[:, h, d, :], 0.0), writes=[Sbf_b[h][d]])

        if env:
            xn_src = env.xn_src
        else:
            xn_v = xn_d.rearrange("(c p) t -> p c t", p=128)
            xn_src = lambda ti: xn_v[:, :, ti * TT:(ti + 1) * TT]
            m_ov = m_o.rearrange("(c p) t -> p c t", p=128)
            st_sem = [[P.newsem(f"st_y{h}_{ec}") for ec in range(4)] for h in range(2)]
        outs = []

        def qk_path(w, w_b, gain, gain_b, xn, xn_b, cs, cs_b, is_k, bs):
            QT, QT_b, QdT, QdT_b, KT, KT_b = bs.QT, bs.QT_b, bs.QdT, bs.QdT_b, bs.KT, bs.KT_b
            OT, OT_b = (KT, KT_b) if is_k else (QT, QT_b)
            for c in range(4):
                ps, psb = psA.next()
                mm_group(P, ps[:], psb, [(w[:, k, c * 128:(c + 1) * 128], xn[:, k, :], [w_b[k // 2], xn_b]) for k in range(8)])
                P.op("act", lambda e, c=c, ps=ps: e.copy(raw[:, c, :], ps[:]), reads=[psb], writes=[raw_b[c]])
                P.op("act", lambda e, c=c: e.activation(out=scr[:, c, :], in_=raw[:, c, :], func=AF.Square),
                     reads=[raw_b[c]], writes=[scr_b[c]])
            for h in range(2):
                ps, psb = psSt.next()
                mm_group(P, ps[:], psb, [(ones[:], scr[:, 2 * h + dc, :], [scr_b[2 * h + dc], ones_b]) for dc in range(2)])
                if is_k:
                    P.op("act", lambda e, ps=ps, h=h: e.activation(out=rs2[:, h, :], in_=ps[:], func=AF.Sqrt, bias=256.0 * EPS, scale=1.0),
                         reads=[psb], writes=[rs2_b[h]])
                else:
                    P.op("act", lambda e, ps=ps, h=h: e.activation(out=rs2[:, h, :], in_=ps[:], func=AF.Sqrt, bias=EPS, scale=1.0 / 256.0),
                         reads=[psb], writes=[rs2_b[h]])
            for h in range(2):
                P.op("dve", lambda e, h=h: e.reciprocal(rs2[:, h, :], rs2[:, h, :]), reads=[rs2_b[h]], writes=[rs2_b[h]])
                for dc in range(2):
                    c = 2 * h + dc
                    P.op("dve", lambda e, c=c, dc=dc, h=h: e.scalar_tensor_tensor(
                        out=raw[:, c, :], in0=raw[:, c, :], scalar=gain[:, dc:dc + 1], in1=rs2[:, h, :], op0=ALU.mult, op1=ALU.mult),
                        reads=[raw_b[c], gain_b, rs2_b[h]], writes=[raw_b[c]])
                c1, c2 = 2 * h, 2 * h + 1
                P.op("dve", lambda e, c1=c1: e.tensor_tensor(out=tmpA[:], in0=raw[:, c1, :], in1=cs[:, 0, :], op=ALU.mult),
                     reads=[raw_b[c1], cs_b], writes=[tmpA_b])
                P.op("dve", lambda e, c2=c2: e.tensor_tensor(out=tmpB[:], in0=raw[:, c2, :], in1=cs[:, 1, :], op=ALU.mult),
                     reads=[raw_b[c2], cs_b], writes=[tmpB_b])
                P.op("dve", lambda e, c1=c1: e.tensor_tensor(out=OT[:, c1, :], in0=tmpA[:], in1=tmpB[:], op=ALU.subtract),
                     reads=[tmpA_b, tmpB_b], writes=[OT_b[c1]])
                P.op("dve", lambda e, c1=c1: e.tensor_tensor(out=tmpA[:], in0=raw[:, c1, :], in1=cs[:, 1, :], op=ALU.mult),
                     reads=[raw_b[c1], cs_b], writes=[tmpA_b])
                P.op("dve", lambda e, c2=c2: e.tensor_tensor(out=tmpB[:], in0=raw[:, c2, :], in1=cs[:, 0, :], op=ALU.mult),
                     reads=[raw_b[c2], cs_b], writes=[tmpB_b])
                P.op("dve", lambda e, c2=c2: e.tensor_tensor(out=OT[:, c2, :], in0=tmpA[:], in1=tmpB[:], op=ALU.add),
                     reads=[tmpA_b, tmpB_b], writes=[OT_b[c2]])
                if not is_k:
                    for c in (c1, c2):
                        P.op("dve", lambda e, c=c, h=h: e.tensor_tensor(out=QdT[:, c, :], in0=QT[:, c, :], in1=qdec[:, h, :], op=ALU.mult),
                             reads=[QT_b[c], qdec_b], writes=[QdT_b[c]])

        tiles = {}

        def load(ti):
            tsl = slice(ti * TT, (ti + 1) * TT)
            xn, xn_b = xnr.next()
            P.dma("pool", xn[:], xn_src(ti), dst=xn_b)
            cs, cs_b = csr.next()
            P.dma("sp", cs[:, 0, :], cos_d[:, tsl], dst=cs_b)
            P.dma("sp", cs[:, 1, :], sin_d[:, tsl], dst=cs_b)
            tiles[ti] = dict(xn=xn, xn_b=xn_b, cs=cs, cs_b=cs_b, bs=sets[ti % 2])

        def qpath(ti):
            t = tiles[ti]
            qk_path(wq, wq_b, qg, qg_b, t["xn"], t["xn_b"], t["cs"], t["cs_b"], False, t["bs"])

        def kpath(ti):
            t = tiles[ti]
            bs = t["bs"]
            qk_path(wk, wk_b, kg, kg_b, t["xn"], t["xn_b"], t["cs"], t["cs_b"], True, bs)
            for b in range(4):
                bsl = slice(b * 128, (b + 1) * 128)
                pt, ptb = psT.next()
                ev = None
                for c in range(4):
                    ev = P.op("pe", lambda e, c=c, pt=pt, bsl=bsl, bs=bs: e.transpose(pt[:, c, :], bs.KT[:, c, bsl], ident[:]),
                              reads=[bs.KT_b[c], ident_b], writes=[ptb] if c == 0 else [], signal=(c == 3))
                ptb.w = ev
                ptb.r = []
                for h in range(2):
                    P.op("dve", lambda e, b=b, h=h, pt=pt, bs=bs: e.tensor_scalar(
                        out=bs.Kd[:, b, 2 * h:2 * h + 2, :], in0=pt[:, 2 * h:2 * h + 2, :], scalar1=kdec[:, h:h + 1], scalar2=None, op0=ALU.mult),
                        reads=[ptb, kdec_b], writes=[bs.Kd_b[b][h]])

        def vproj(ti):
            t = tiles[ti]
            xn, xn_b, bs = t["xn"], t["xn_b"], t["bs"]
            for b in range(4):
                for h in range(2):
                    ps, psb = psA.next()
                    mm_group(P, ps[:], psb, [(xn[:, k, b * 128:(b + 1) * 128], wv[:, k, h * 512:(h + 1) * 512], [xn_b, wv_b[k // 2]]) for k in range(8)])
                    P.op("act", lambda e, b=b, h=h, ps=ps, bs=bs: e.copy(bs.Vt[:, b, h, :], ps[:]), reads=[psb], writes=[bs.Vt_b[b][h]])

        def gproj(ti):
            t = tiles[ti]
            xn, xn_b = t["xn"], t["xn_b"]
            for c in range(8):
                ps, psb = psA.next()
                mm_group(P, ps[:], psb, [(wg[:, k, c * 128:(c + 1) * 128], xn[:, k, :], [wg_b[k // 2], xn_b]) for k in range(8)])
                P.op("act", lambda e, c=c, ps=ps: e.activation(out=sg[:, c, :], in_=ps[:], func=AF.Silu), reads=[psb], writes=[sg_b[c]])

        def block(ti, b):
            bs = tiles[ti]["bs"]
            KT, KT_b, QT, QT_b, QdT, QdT_b, Kd, Kd_b, Vt, Vt_b = bs.KT, bs.KT_b, bs.QT, bs.QT_b, bs.QdT, bs.QdT_b, bs.Kd, bs.Kd_b, bs.Vt, bs.Vt_b
            bsl = slice(b * 128, (b + 1) * 128)
            for h in range(2):
                sc, scb = psSc.next()
                mm_group(P, sc[:], scb, [(KT[:, 2 * h + dc, bsl], QT[:, 2 * h + dc, bsl], [KT_b[2 * h + dc], QT_b[2 * h + dc]]) for dc in range(2)])
                P.op("dve", lambda e, h=h, sc=sc: e.tensor_tensor(out=PT[:, h, :], in0=sc[:], in1=dtt[:, h, :], op=ALU.mult),
                     reads=[scb, dtt_b], writes=[PT_b[h]])
                py, pyb = psY.next()
                first = True
                ev = None
                for ec in range(4):
                    esl = slice(ec * 128, (ec + 1) * 128)
                    terms = [(Vt[:, b, h, esl], PT[:, h, :], [Vt_b[b][h], PT_b[h]])]
                    terms += [(Sbf[:, h, dc, esl], QdT[:, 2 * h + dc, bsl], [Sbf_b[h][dc], QdT_b[2 * h + dc]]) for dc in range(2)]
                    for i, (l, r, rb) in enumerate(terms):
                        ev = P.op("pe", lambda e, l=l, r=r, i=i, ec=ec, py=py: e.matmul(py[:, ec, :], lhsT=l, rhs=r, start=(i == 0), stop=(i == 2)),
                                  reads=rb, writes=[pyb] if first else [], signal=(ec == 3 and i == 2))
                        first = False
                pyb.w = ev
                pyb.r = []
                P.op("act", lambda e, h=h, bsl=bsl, py=py: e.copy(y32[:, h, :, bsl], py[:]),
                     reads=[pyb], writes=[y_b[h][ec][b] for ec in range(4)])
                for dc in range(2):
                    pst, pstb = psSt.next()
                    mm_group(P, pst[:], pstb, [(Kd[:, b, 2 * h + dc, :], Vt[:, b, h, :], [Kd_b[b][h], Vt_b[b][h]])])
                    P.op("dve", lambda e, h=h, dc=dc, pst=pst: e.scalar_tensor_tensor(
                        out=S32[:, h, dc, :], in0=S32[:, h, dc, :], scalar=g128[:, h:h + 1], in1=pst[:], op0=ALU.mult, op1=ALU.add),
                        reads=[pstb, g128_b, S_b[h][dc]], writes=[S_b[h][dc]])
                    P.op("act", lambda e, h=h, dc=dc: e.copy(Sbf[:, h, dc, :], S32[:, h, dc, :]),
                         reads=[S_b[h][dc]], writes=[Sbf_b[h][dc]])

        def gn_norm(ti):
            sq = [(scr, scr_b), (raw, raw_b)]
            for h in range(2):
                for ec in range(4):
                    P.op("act", lambda e, h=h, ec=ec: e.activation(out=sq[h][0][:, ec, :], in_=y32[:, h, ec, :], func=AF.Square),
                         reads=[y_b[h][ec][b] for b in range(4)], writes=[sq[h][1][ec]])
            pss = []
            for h in range(2):
                ps1, ps1b = psA.next()
                mm_group(P, ps1[:], ps1b, [(ones[:], y32[:, h, ec, :], [y_b[h][ec][b] for b in range(4)] + [ones_b]) for ec in range(4)])
                ps2, ps2b = psSt.next()
                mm_group(P, ps2[:], ps2b, [(ones[:], sq[h][0][:, ec, :], [sq[h][1][ec], ones_b]) for ec in range(4)])
                pss.append((ps1, ps1b, ps2, ps2b))
            for h in range(2):
                ps1, ps1b, ps2, ps2b = pss[h]
                P.op("act", lambda e, ps1=ps1, h=h: e.activation(out=mu2[:, h, :], in_=ps1[:], func=AF.Copy, scale=1.0 / 512.0),
                     reads=[ps1b], writes=[mu2_b[h]])
                P.op("dve", lambda e, h=h: e.tensor_tensor(out=tmpA[:], in0=mu2[:, h, :], in1=mu2[:, h, :], op=ALU.mult), reads=[mu2_b[h]], writes=[tmpA_b])
                P.op("dve", lambda e, ps2=ps2, h=h: e.scalar_tensor_tensor(out=rs2[:, h, :], in0=ps2[:], scalar=1.0 / 512.0, in1=tmpA[:], op0=ALU.mult, op1=ALU.subtract),
                     reads=[ps2b, tmpA_b], writes=[rs2_b[h]])
                P.op("act", lambda e, h=h: e.activation(out=rs2[:, h, :], in_=rs2[:, h, :], func=AF.Sqrt, bias=EPS, scale=1.0), reads=[rs2_b[h]], writes=[rs2_b[h]])
            for h in range(2):
                P.op("dve", lambda e, h=h: e.reciprocal(rs2[:, h, :], rs2[:, h, :]), reads=[rs2_b[h]], writes=[rs2_b[h]])
                for ec in range(4):
                    c = 4 * h + ec
                    yb = [y_b[h][ec][b] for b in range(4)]
                    P.op("dve", lambda e, h=h, ec=ec: e.tensor_tensor(out=y32[:, h, ec, :], in0=y32[:, h, ec, :], in1=mu2[:, h, :], op=ALU.subtract),
                         reads=yb + [mu2_b[h]], writes=yb)
                    P.op("dve", lambda e, h=h, ec=ec: e.tensor_tensor(out=y32[:, h, ec, :], in0=y32[:, h, ec, :], in1=rs2[:, h, :], op=ALU.mult),
                         reads=yb + [rs2_b[h]], writes=yb)

        def gn_out(ti):
            tsl = slice(ti * TT, (ti + 1) * TT)
            for h in range(2):
                for ec in range(4):
                    c = 4 * h + ec
                    yb = [y_b[h][ec][b] for b in range(4)]
                    P.op("act", lambda e, h=h, ec=ec, c=c: e.activation(out=y32[:, h, ec, :], in_=y32[:, h, ec, :], func=AF.Identity,
                                                                      bias=gnb[:, c:c + 1], scale=gnw[:, c:c + 1]),
                         reads=yb + [gnw_b, gnb_b], writes=yb)
                    P.op("dve", lambda e, h=h, ec=ec, c=c: e.tensor_tensor(out=y32[:, h, ec, :], in0=y32[:, h, ec, :], in1=sg[:, c, :], op=ALU.mult),
                         reads=yb + [sg_b[c]], writes=yb)
                    if env:
                        outs.extend(env.emit_m(P, c * 128, 128, ti, y32[:, h, ec, :], yb))
                    else:
                        outs.append(P.op("sp", lambda e, h=h, ec=ec, c=c, tsl=tsl: e.dma_start(out=m_ov[:, c, tsl], in_=y32[:, h, ec, :]),
                                         reads=yb, sem=st_sem[h][ec], inc=16))

        load(0)
        qpath(0)
        kpath(0)
        vproj(0)
        for ti in range(NTI):
            nxt = ti + 1 < NTI
            if nxt:
                load(ti + 1)
            block(ti, 0)
            if nxt:
                qpath(ti + 1)
            block(ti, 1)
            if nxt:
                kpath(ti + 1)
            block(ti, 2)
            if nxt:
                vproj(ti + 1)
            block(ti, 3)
            gn_norm(ti)
            gproj(ti)
            gn_out(ti)
            if env:
                env.m_done(P, ti)
        if env:
            env.outs = outs
        else:
            P.finish(outs)
    return nc


def sb_tables():
    j = np.arange(128)
    ltri = (j[:, None] >= j[None, :]).astype(np.float32)
    ustr = (j[:, None] < j[None, :]).astype(np.float32)
    oblk = np.zeros((128, 128), np.float32)
    oblk[:64, :64] = 1.0
    oblk[64:, 64:] = 1.0
    t = np.arange(512)
    maskd = np.zeros((128, 4, 512), np.float32)
    for r in range(4):
        maskd[:, r, :] = ((128 * r + j)[:, None] < t[None, :]).astype(np.float32)
    return dict(ltri=ltri, ustr=ustr, oblk=oblk, maskd=maskd)


def build_sb(SEQ=S, env=None):
    nc = env.nc if env else bass.Bass("TRN2", target_bir_lowering=False)
    NTI = SEQ // TT
    NKB = SEQ // 128
    dr = env.dr if env else (lambda n, s, dt, kind: nc.dram_tensor(n, list(s), dt, kind=kind).ap())
    xn_d = dr("xnT", [D, SEQ], F32, "ExternalInput")
    wq_d = dr("wq", [D, 512], F32, "ExternalInput")
    wk_d = dr("wk", [D, 512], F32, "ExternalInput")
    wv_d = dr("wv", [D, 512], F32, "ExternalInput")
    qg_d = dr("qg", [128, 1], F32, "ExternalInput")
    kg_d = dr("kg", [128, 1], F32, "ExternalInput")
    ltri_d = dr("ltri", [128, 128], F32, "ExternalInput")
    ustr_d = dr("ustr", [128, 128], F32, "ExternalInput")
    oblk_d = dr("oblk", [128, 128], F32, "ExternalInput")
    maskd_d = dr("maskd", [128, 4, 512], F32, "ExternalInput")
    y_o = dr("yT_out", [512, SEQ], F32, "ExternalOutput")
    with ExitStack() as st:
        if env:
            P = env.P
            env.phase_setup(P)
        else:
            P = Prog(nc, st)
            P.out_sem = P.newsem("outs")
        psZ = Rot([(P.ps(f"psZ{i}", [128, 512]), P.buf(f"psZ{i}")) for i in range(2)])
        psAcc = Rot([(P.ps(f"psAcc{i}", [128, 512]), P.buf(f"psAcc{i}")) for i in range(2)])
        psY = Rot([(P.ps(f"psY{i}", [64, 512]), P.buf(f"psY{i}")) for i in range(4)])
        psS = psAcc

        def small(name, d_ap, shape, dt=F32, eng="sp"):
            t = P.sb(name, shape, dt)
            b = P.buf(name, dma=True)
            P.dma(eng, t[:], d_ap, dst=b)
            return t, b
        qg, qg_b = small("qg", qg_d, [128, 1])
        kg, kg_b = small("kg", kg_d, [128, 1])
        oblk, oblk_b = small("oblk", oblk_d, [128, 128])
        maskd, maskd_b = small("maskd", maskd_d, [128, 4, 512])
        ltri, ltri_b = small("ltri", ltri_d, [128, 128], BF16, "pool")
        ustr, ustr_b = small("ustr", ustr_d, [128, 128], BF16, "pool")

        def wload(name, d_ap, ncol):
            t = P.sb(name, [128, 8, ncol], BF16)
            bs = []
            v = d_ap.rearrange("(c p) f -> p c f", p=128)
            for k2 in range(4):
                b = P.buf(f"{name}{k2}", dma=True)
                P.dma("pool", t[:, 2 * k2:2 * k2 + 2, :], v[:, 2 * k2:2 * k2 + 2, :], dst=b)
                bs.append(b)
            return t, bs
        wq, wq_b = wload("wq", wq_d, 512)
        wk, wk_b = wload("wk", wk_d, 512)
        wv, wv_b = wload("wv", wv_d, 512)

        xnr = Rot([(P.sb(f"xn{i}", [128, 8, TT], BF16), P.buf(f"xn{i}", dma=True)) for i in range(2)])
        raw = P.sb("raw", [128, 4, TT], F32); raw_b = [P.buf(f"raw{c}") for c in range(4)]
        scr = P.sb("scr", [128, 4, TT], F32); scr_b = [P.buf(f"scr{c}") for c in range(4)]
        rs = P.sb("rs", [128, TT], F32); rs_b = P.buf("rs")
        QT = P.sb("QT", [128, 4, TT], BF16); QT_b = [P.buf(f"QT{c}") for c in range(4)]
        KT = P.sb("KT", [128, 4, SEQ], BF16); KT_b = [[P.buf(f"KT{c}_{t}") for t in range(NTI)] for c in range(4)]
        V = P.sb("V", [128, NKB, 512], BF16); V_b = [P.buf(f"V{kb}") for kb in range(NKB)]
        er = Rot([(P.sb(f"e{i}", [128, TT], F32), P.buf(f"e{i}")) for i in range(8)])
        wr = Rot([(P.sb(f"w{i}", [128, TT], F32), P.buf(f"w{i}")) for i in range(3)])
        spr = Rot([(P.sb(f"sp{i}", [128, TT], BF16), P.buf(f"sp{i}")) for i in range(5)])
        Ar = Rot([(P.sb(f"A{i}", [128, TT], BF16), P.buf(f"A{i}")) for i in range(4)])
        srun_rots = [Rot([(P.sb(f"srun{j}_{i}", [128, TT], BF16), P.buf(f"srun{j}_{i}")) for i in range(3)]) for j in range(2)]
        ones_bf = P.sb("ones_bf", [128, 128], BF16); ones_bf_b = P.buf("ones_bf")
        P.op("pool", lambda e: e.memset(ones_bf[:], 1.0), writes=[ones_bf_b])
        yr = [(P.sb(f"yo{i}", [64, TT], F32), P.buf(f"yo{i}"), P.newsem(f"st_yo{i}")) for i in range(4)]
        yri = [0]

        if env:
            xn_src = env.xn_src
        else:
            xn_v = xn_d.rearrange("(c p) t -> p c t", p=128)
            xn_src = lambda ti: xn_v[:, :, ti * TT:(ti + 1) * TT]
        outs = []

        def qk_proj(w, w_b, gain, gain_b, xn, xn_b, out_fn):
            for c in range(4):
                ps, psb = psZ.next()
                mm_group(P, ps[:], psb, [(w[:, k, c * 128:(c + 1) * 128], xn[:, k, :], [w_b[k // 2], xn_b]) for k in range(8)])
                P.op("act", lambda e, c=c, ps=ps: e.copy(raw[:, c, :], ps[:]), reads=[psb], writes=[raw_b[c]])
                P.op("act", lambda e, c=c: e.activation(out=scr[:, c, :], in_=raw[:, c, :], func=AF.Square),
                     reads=[raw_b[c]], writes=[scr_b[c]])
                ps2, ps2b = psS.next()
                mm_group(P, ps2[:], ps2b, [(oblk[:], scr[:, c, :], [scr_b[c], oblk_b])])
                P.op("act", lambda e, ps2=ps2: e.activation(out=rs[:], in_=ps2[:], func=AF.Sqrt, bias=EPS, scale=1.0 / 64.0),
                     reads=[ps2b], writes=[rs_b])
                P.op("dve", lambda e: e.reciprocal(rs[:], rs[:]), reads=[rs_b], writes=[rs_b])
                oap, ob = out_fn(c)
                P.op("dve", lambda e, c=c, oap=oap: e.scalar_tensor_tensor(
                    out=oap, in0=raw[:, c, :], scalar=gain[:, 0:1], in1=rs[:], op0=ALU.mult, op1=ALU.mult),
                    reads=[raw_b[c], gain_b, rs_b], writes=[ob])

        for ti in range(NTI):
            tsl = slice(ti * TT, (ti + 1) * TT)
            xn, xn_b = xnr.next()
            P.dma("pool", xn[:], xn_src(ti), dst=xn_b)
            qk_proj(wq, wq_b, qg, qg_b, xn, xn_b, lambda c: (QT[:, c, :], QT_b[c]))
            qk_proj(wk, wk_b, kg, kg_b, xn, xn_b, lambda c, ti=ti, tsl=tsl: (KT[:, c, tsl], KT_b[c][ti]))
            for b in range(4):
                kb = 4 * ti + b
                ps, psb = psZ.next()
                mm_group(P, ps[:], psb, [(xn[:, k, b * 128:(b + 1) * 128], wv[:, k, :], [xn_b, wv_b[k // 2]]) for k in range(8)])
                P.op("act", lambda e, kb=kb, ps=ps: e.copy(V[:, kb, :], ps[:]), reads=[psb], writes=[V_b[kb]])
            nkb = 4 * ti + 4
            units = []
            for c in range(4):
                pys = [psY.next() for _ in range(2)]
                for step in range(nkb):
                    for j in range(2):
                        units.append(dict(c=c, j=j, step=step, kb=nkb - 1 - step, psl=slice(64 * j, 64 * j + 64),
                                          py=pys[j][0], pyb=pys[j][1]))
            srun_state = {}

            def stage(k, u):
                c, j, step, kb = u["c"], u["j"], u["step"], u["kb"]
                ksl = slice(kb * 128, (kb + 1) * 128)
                r = kb - 4 * ti
                if k == 0:
                    z, zb = psZ.next()
                    mm_group(P, z[:], zb, [(KT[u["psl"], c, ksl], QT[u["psl"], c, :], [KT_b[c][kb // 4], QT_b[c]])])
                    u["z"], u["zb"] = z, zb
                elif k == 1:
                    e_sb, e_b = er.next()
                    P.op("act", lambda e, e_sb=e_sb, z=u["z"]: e.activation(out=e_sb[:], in_=z[:], func=AF.Exp, scale=0.125),
                         reads=[u["zb"]], writes=[e_b])
                    if r >= 0:
                        P.op("dve", lambda e, e_sb=e_sb, r=r: e.tensor_tensor(out=e_sb[:], in0=e_sb[:], in1=maskd[:, r, :], op=ALU.mult),
                             reads=[e_b, maskd_b], writes=[e_b])
                    u["e"], u["eb"] = e_sb, e_b
                elif k == 2:
                    sp_sb, sp_b = spr.next()
                    P.op("act", lambda e, sp_sb=sp_sb, e_sb=u["e"]: e.activation(out=sp_sb[:], in_=e_sb[:], func=AF.Ln, bias=1.0, scale=1.0),
                         reads=[u["eb"]], writes=[sp_b])
                    u["sp"], u["spb"] = sp_sb, sp_b
                elif k == 3:
                    acc, accb = psAcc.next()
                    terms = [(ltri[:], u["sp"][:], [u["spb"], ltri_b])]
                    if step > 0:
                        srun, srunb = srun_state[(c, j)]
                        terms.append((ones_bf[:], srun[:], [srunb, ones_bf_b]))
                    mm_group(P, acc[:], accb, terms)
                    u["acc"], u["accb"] = acc, accb
                elif k == 4:
                    if kb > 0:
                        nsr, nsrb = srun_rots[j].next()
                        if step == 0:
                            P.op("dve", lambda e, nsr=nsr, sp_sb=u["sp"]: e.tensor_copy(nsr[:], sp_sb[:]),
                                 reads=[u["spb"]], writes=[nsrb])
                        else:
                            old, oldb = srun_state[(c, j)]
                            P.op("dve", lambda e, nsr=nsr, sp_sb=u["sp"], old=old: e.tensor_tensor(out=nsr[:], in0=old[:], in1=sp_sb[:], op=ALU.add),
                                 reads=[u["spb"], oldb], writes=[nsrb])
                        srun_state[(c, j)] = (nsr, nsrb)
                    w_sb, w_b = wr.next()
                    P.op("act", lambda e, w_sb=w_sb, acc=u["acc"]: e.activation(out=w_sb[:], in_=acc[:], func=AF.Exp, scale=-1.0),
                         reads=[u["accb"]], writes=[w_b])
                    u["w"], u["wb"] = w_sb, w_b
                elif k == 5:
                    A_sb, A_b = Ar.next()
                    P.op("dve", lambda e, A_sb=A_sb, e_sb=u["e"], w_sb=u["w"]: e.tensor_tensor(out=A_sb[:], in0=e_sb[:], in1=w_sb[:], op=ALU.mult),
                         reads=[u["eb"], u["wb"]], writes=[A_b])
                    u["A"], u["Ab"] = A_sb, A_b
                elif k == 6:
                    py = u["py"]
                    P.op("pe", lambda e, py=py, A_sb=u["A"], kb=kb, j=j, c=c, step=step: e.matmul(
                        py[:], lhsT=V[:, kb, c * 128 + 64 * j: c * 128 + 64 * j + 64], rhs=A_sb[:], start=(step == 0), stop=(kb == 0)),
                        reads=[u["Ab"], V_b[kb]], writes=[u["pyb"]])
                    if kb == 0:
                        yo, yo_b, yo_sem = yr[yri[0] % 4]
                        yri[0] += 1
                        P.op("act", lambda e, yo=yo, py=py: e.copy(yo[:], py[:]), reads=[u["pyb"]], writes=[yo_b])
                        row = c * 128 + 64 * j
                        if env:
                            outs.extend(env.emit_m(P, row, 64, ti, yo[:], [yo_b]))
                        else:
                            outs.append(P.op("sp", lambda e, yo=yo, row=row, tsl=tsl: e.dma_start(out=y_o[row:row + 64, tsl], in_=yo[:]),
                                             reads=[yo_b], sem=yo_sem, inc=16))

            NS = 7
            SK = [0, 2, 3, 4, 6, 7, 8]
            for slot in range(len(units) + SK[-1]):
                for k in reversed(range(NS)):
                    ui = slot - SK[k]
                    if 0 <= ui < len(units):
                        stage(k, units[ui])
            if env:
                env.m_done(P, ti)
        if env:
            env.outs = outs
        else:
            P.finish(outs)
    return nc


CW = 31
HALO = 32


def build_conv(T=TOK, env=None):
    nc = env.nc if env else bass.Bass("TRN2", target_bir_lowering=False)
    NT = T // TT
    dr = env.dr if env else (lambda n, s, dt, kind: nc.dram_tensor(n, list(s), dt, kind=kind).ap())
    xT_d = dr("xT", [D, T], F32, "ExternalInput")
    xnh_d = dr("xnhT", [D, HALO + T], F32, "ExternalInput")
    flag_d = dr("flag", [128, 1], F32, "ExternalInput")
    pw1_d = dr("pw1_w", [D, 2 * D], F32, "ExternalInput")
    pw1b_d = dr("pw1_b", [128, 16], F32, "ExternalInput")
    dww_d = dr("dw_w", [128, CW * 8], F32, "ExternalInput")
    dwb_d = dr("dw_b", [128, 8], F32, "ExternalInput")
    lnw_d = dr("ln_w", [128, 8], F32, "ExternalInput")
    lnb_d = dr("ln_b", [128, 8], F32, "ExternalInput")
    pw2_d = dr("pw2_w", [D, D], F32, "ExternalInput")
    pw2b_d = dr("pw2_b", [128, 8], F32, "ExternalInput")
    x_o = dr("x_out", [D, T], F32, "ExternalOutput")
    with ExitStack() as st:
        if env:
            P = env.P
        else:
            P = Prog(nc, st)
            P.out_sem = P.newsem("outs")
        ones = P.sb("ones32", [128, 128], F32); ones_b = P.buf("ones")
        P.op("pool", lambda e: e.memset(ones[:], 1.0), writes=[ones_b])
        ident = P.sb("ident", [128, 128], F32); ident_b = P.buf("ident")
        P.op("pool", lambda e: e.memset(ident[:], 1.0), writes=[ident_b])
        P.op("pool", lambda e: e.affine_select(out=ident[:], in_=ident[:], pattern=[[-1, 128]],
                                               compare_op=ALU.is_equal, fill=0.0, base=0, channel_multiplier=1),
             reads=[ident_b], writes=[ident_b])
        psA = Rot([(P.ps(f"psA{i}", [128, 512]), P.buf(f"psA{i}")) for i in range(2)])
        psG = Rot([(P.ps(f"psG{i}", [128, 512]), P.buf(f"psG{i}")) for i in range(2)])
        psC = Rot([(P.ps(f"psC{i}", [128, 512]), P.buf(f"psC{i}")) for i in range(2)])
        psS = Rot([(P.ps(f"psS{i}", [128, 512]), P.buf(f"psS{i}")) for i in range(2)])

        def small(name, d_ap, shape):
            t = P.sb(name, shape, F32)
            b = P.buf(name, dma=True)
            P.dma("sp", t[:], d_ap, dst=b)
            return t, b
        flag, flag_b = small("flag", flag_d, [128, 1])
        pw1b, pw1b_b = small("pw1b", pw1b_d, [128, 16])
        dww, dww_b = small("dww", dww_d, [128, CW * 8])
        dwb, dwb_b = small("dwb", dwb_d, [128, 8])
        lnw, lnw_b = small("lnw", lnw_d, [128, 8])
        lnb, lnb_b = small("lnb", lnb_d, [128, 8])
        pw2b, pw2b_b = small("pw2b", pw2b_d, [128, 8])

        WB = P.sb("WB", [128, 32768], BF16)
        pw1 = WB[:, 0:16384].rearrange("p (a b) -> p a b", a=8)
        pw1_b = [P.buf(f"pw1_{k}", dma=True) for k in range(8)]
        pw1_v = pw1_d.rearrange("(c p) f -> p c f", p=128)
        for k in range(8):
            P.dma("pool", pw1[:, k, :], pw1_v[:, k, :], dst=pw1_b[k])
        pw2 = P.sb("pw2", [128, 8, D], BF16)
        pw2_b = [P.buf(f"pw2_{k}", dma=True) for k in range(4)]
        pw2_v = pw2_d.rearrange("(c p) f -> p c f", p=128)
        for k2 in range(4):
            P.dma("pool", pw2[:, 2 * k2:2 * k2 + 2, :], pw2_v[:, 2 * k2:2 * k2 + 2, :], dst=pw2_b[k2])

        h = P.sb("h", [128, 8, HALO + T], BF16)
        h_b = [[P.buf(f"h{c}_{t}") for t in range(NT + 1)] for c in range(8)]
        xnt = P.sb("xnt", [128, 8, HALO + TT], BF16); xnt_b = P.buf("xnt", dma=True)
        sgr = Rot([(P.sb(f"sgm{i}", [128, TT], F32), P.buf(f"sgm{i}")) for i in range(2)])
        if env:
            xn_halo = env.xn_halo
            xn_main = env.xn_main
        else:
            xnh_v = xnh_d.rearrange("(c p) t -> p c t", p=128)
            xn_halo = lambda: xnh_v[:, :, 0:HALO]
            xn_main = lambda t: xnh_v[:, :, HALO + t * TT:HALO + (t + 1) * TT]
        last_pw1 = None
        for t in range(NT):
            if t == 0:
                P.dma("pool", xnt[:, :, 0:HALO], xn_halo(), dst=xnt_b)
                P.dma("pool", xnt[:, :, HALO:HALO + TT], xn_main(0), dst=xnt_b)
                segs = [(0, HALO, 0), (HALO, TT, 1)]
            else:
                P.dma("pool", xnt[:, :, HALO:HALO + TT], xn_main(t), dst=xnt_b)
                segs = [(HALO, TT, t + 1)]
            for (off, n, hidx) in segs:
                col0 = 0 if hidx == 0 else HALO + (hidx - 1) * TT
                for c in range(8):
                    pa, pab = psA.next()
                    mm_group(P, pa[:, 0:n], pab, [(pw1[:, k, c * 128:(c + 1) * 128], xnt[:, k, off:off + n], [pw1_b[k], xnt_b]) for k in range(8)])
                    pg, pgb = psG.next()
                    last_pw1 = mm_group(P, pg[:, 0:n], pgb, [(pw1[:, k, D + c * 128:D + (c + 1) * 128], xnt[:, k, off:off + n], [pw1_b[k], xnt_b]) for k in range(8)])
                    sg_, sg_b = sgr.next()
                    P.op("act", lambda e, sg_=sg_, pg=pg, n=n, c=c: e.activation(out=sg_[:, 0:n], in_=pg[:, 0:n], func=AF.Sigmoid,
                                                                                bias=pw1b[:, 8 + c:9 + c], scale=1.0),
                         reads=[pgb, pw1b_b], writes=[sg_b])
                    P.op("dve", lambda e, sg_=sg_, pa=pa, n=n, c=c, col0=col0: e.scalar_tensor_tensor(
                        out=h[:, c, col0:col0 + n], in0=pa[:, 0:n], scalar=pw1b[:, c:c + 1], in1=sg_[:, 0:n], op0=ALU.add, op1=ALU.mult),
                        reads=[pab, sg_b, pw1b_b], writes=[h_b[c][hidx]])
                    if hidx == 0:
                        P.op("dve", lambda e, c=c: e.tensor_scalar(out=h[:, c, 0:HALO], in0=h[:, c, 0:HALO], scalar1=flag[:, 0:1], scalar2=None, op0=ALU.mult),
                             reads=[h_b[c][0], flag_b], writes=[h_b[c][0]])
        diag = WB[:, 0:CW * 8 * 128].rearrange("p (a b) -> p a b", a=CW * 8)
        diag_b = [P.buf(f"diag{c}") for c in range(8)]
        for c in range(8):
            diag_b[c].w = last_pw1
            ev = None
            for j in range(CW):
                eng = "dve"
                ev = P.op(eng, lambda e, c=c, j=j: e.tensor_scalar(out=diag[:, c * CW + j, :], in0=ident[:], scalar1=dww[:, j * 8 + c:j * 8 + c + 1], scalar2=None, op0=ALU.mult),
                          reads=[ident_b, dww_b], writes=[], extra=[last_pw1])
                diag_b[c].r.append(ev)
            diag_b[c].w = None
            diag_b[c].wlist = list(diag_b[c].r)
            diag_b[c].r = []
        cv = P.sb("cv", [128, 8, TT], F32); cv_b = [P.buf(f"cv{c}") for c in range(8)]
        scr = P.sb("scr", [128, 8, TT], F32); scr_b = [P.buf(f"scr{c}") for c in range(8)]
        u = P.sb("u", [128, 8, TT], BF16); u_b = [P.buf(f"u{c}") for c in range(8)]
        xt = P.sb("xt", [128, 8, TT], F32); xt_b = P.buf("xt", dma=True)
        xt_cb = [P.buf(f"xt{c}") for c in range(8)]
        st_sem = [P.newsem(f"st_x{c}") for c in range(8)]
        mu = P.sb("mu", [128, TT], F32); mu_b = P.buf("mu")
        rs = P.sb("rs", [128, TT], F32); rs_b = P.buf("rs")
        tmp = P.sb("tmp", [128, TT], F32); tmp_b = P.buf("tmp")
        xT_v = xT_d.rearrange("(c p) t -> p c t", p=128)
        x_ov = x_o.rearrange("(c p) t -> p c t", p=128)
        outs = []
        for t in range(NT):
            tsl = slice(t * TT, (t + 1) * TT)
            evl = P.op("sp", lambda e, tsl=tsl: e.dma_start(out=xt[:], in_=xT_v[:, :, tsl]), reads=[], writes=xt_cb, sem=P.semof(xt_b, "sp"), inc=16)
            for c in range(8):
                pc, pcb = psC.next()
                hb = [h_b[c][t], h_b[c][t + 1]]
                n = CW
                evm = None
                for j in range(CW):
                    evm = P.op("pe", lambda e, pc=pc, c=c, j=j, t=t: e.matmul(pc[:], lhsT=diag[:, c * CW + j, :], rhs=h[:, c, t * TT + 2 + j:t * TT + 2 + j + TT],
                                                                        start=(j == 0), stop=(j == CW - 1)),
                               reads=hb, writes=[pcb] if j == 0 else [], signal=(j == CW - 1), extra=diag_b[c].wlist)
                pcb.w = evm
                pcb.r = []
                P.op("act", lambda e, pc=pc, c=c: e.activation(out=cv[:, c, :], in_=pc[:], func=AF.Identity, bias=dwb[:, c:c + 1], scale=1.0),
                     reads=[pcb, dwb_b], writes=[cv_b[c]])
                P.op("act", lambda e, c=c: e.activation(out=scr[:, c, :], in_=cv[:, c, :], func=AF.Square), reads=[cv_b[c]], writes=[scr_b[c]])
            ps1, ps1b = psS.next()
            mm_group(P, ps1[:], ps1b, [(ones[:], cv[:, c, :], [cv_b[c], ones_b]) for c in range(8)])
            ps2, ps2b = psS.next()
            mm_group(P, ps2[:], ps2b, [(ones[:], scr[:, c, :], [scr_b[c], ones_b]) for c in range(8)])
            P.op("act", lambda e, ps1=ps1: e.activation(out=mu[:], in_=ps1[:], func=AF.Copy, scale=1.0 / D), reads=[ps1b], writes=[mu_b])
            P.op("dve", lambda e: e.tensor_tensor(out=tmp[:], in0=mu[:], in1=mu[:], op=ALU.mult), reads=[mu_b], writes=[tmp_b])
            P.op("dve", lambda e, ps2=ps2: e.scalar_tensor_tensor(out=rs[:], in0=ps2[:], scalar=1.0 / D, in1=tmp[:], op0=ALU.mult, op1=ALU.subtract),
                 reads=[ps2b, tmp_b], writes=[rs_b])
            P.op("act", lambda e: e.activation(out=rs[:], in_=rs[:], func=AF.Sqrt, bias=EPS, scale=1.0), reads=[rs_b], writes=[rs_b])
            P.op("dve", lambda e: e.reciprocal(rs[:], rs[:]), reads=[rs_b], writes=[rs_b])
            for c in range(8):
                P.op("dve", lambda e, c=c: e.tensor_tensor(out=cv[:, c, :], in0=cv[:, c, :], in1=mu[:], op=ALU.subtract), reads=[cv_b[c], mu_b], writes=[cv_b[c]])
                P.op("dve", lambda e, c=c: e.tensor_tensor(out=cv[:, c, :], in0=cv[:, c, :], in1=rs[:], op=ALU.mult), reads=[cv_b[c], rs_b], writes=[cv_b[c]])
                P.op("act", lambda e, c=c: e.activation(out=u[:, c, :], in_=cv[:, c, :], func=AF.Silu, bias=lnb[:, c:c + 1], scale=lnw[:, c:c + 1]),
                     reads=[cv_b[c], lnw_b, lnb_b], writes=[u_b[c]])
            for fo in range(8):
                po, pob = psA.next()
                mm_group(P, po[:], pob, [(pw2[:, k, fo * 128:(fo + 1) * 128], u[:, k, :], [pw2_b[k // 2], u_b[k]]) for k in range(8)])
                P.op("dve", lambda e, po=po, fo=fo: e.scalar_tensor_tensor(out=xt[:, fo, :], in0=po[:], scalar=pw2b[:, fo:fo + 1], in1=xt[:, fo, :], op0=ALU.add, op1=ALU.add),
                     reads=[pob, pw2b_b, xt_cb[fo]], writes=[xt_cb[fo]])
                outs.append(P.op("sp", lambda e, fo=fo, tsl=tsl: e.dma_start(out=x_ov[:, fo, tsl], in_=xt[:, fo, :]), reads=[xt_cb[fo]], sem=st_sem[fo], inc=16))
        if env:
            env.outs = outs
        else:
            P.finish(outs)
    return nc


def conv_inputs(xT, xnhT, flagv, pw1_w, pw1_b, dw_w, dw_b, ln_w, ln_b, pw2_w, pw2_b):
    f = lambda a: np.ascontiguousarray(a, dtype=np.float32)
    dww = np.asarray(dw_w, np.float32).reshape(CW, 8, 128).transpose(2, 0, 1).reshape(128, CW * 8)
    return {"xT": f(xT), "xnhT": f(xnhT), "flag": np.full((128, 1), flagv, np.float32),
            "pw1_w": f(pw1_w), "pw1_b": pcol(pw1_b), "dw_w": f(dww), "dw_b": pcol(dw_b),
            "ln_w": pcol(ln_w), "ln_b": pcol(ln_b), "pw2_w": f(pw2_w), "pw2_b": pcol(pw2_b)}


class Env:
    def __init__(self, nc, P, T):
        self.nc = nc
        self.P = P
        self.T = T
        self.io = {}
        self.outs = []
        self.flags_d = None
        self.mz2d = None
        self.fmy = 0
        self.m_pending = []
        self.m_evs = {}
        self.nocc = False

    def dr(self, name, shape, dt, kind):
        return self.io.get(name)

    def phase_setup(self, P):
        self.flag = P.sb("flags", [128, 2], F32)
        self.flag_b = P.buf("flags", dma=True)
        P.dma("sp", self.flag[:], self.flags_d, dst=self.flag_b)
        self.stages = Rot([(P.sb(f"stg{i}", [128, TT], BF16), P.buf(f"stg{i}"), P.newsem(f"st_stg{i}")) for i in range(4)])

    def gather_tile(self, P, t, evs):
        if self.nocc:
            return
        P.collective("AllGather", ALU.bypass, self.groups, self.xn_my[t], self.xn_full[t], extra=evs)

    def scatter_tile(self, P, tl, evs):
        if self.nocc:
            return
        P.collective("ReduceScatter", ALU.add, self.groups, self.mz2d[tl], self.mrs_out[tl], extra=evs)

    def m_done(self, P, ti):
        nt = self.T // TT
        h, tl = ti // nt, ti % nt
        self.m_evs.setdefault(tl, []).extend(self.m_pending)
        self.m_pending = []
        if h == 1:
            self.scatter_tile(P, tl, self.m_evs.pop(tl))

    def xn_src(self, ti):
        nt = self.T // TT
        rank, tl = ti // nt, ti % nt
        return self.xn_full[tl][rank * D:(rank + 1) * D, :].rearrange("(c p) t -> p c t", p=128)

    def xn_halo(self):
        nt = self.T // TT
        return self.xn_full[nt - 1][0:D, TT - HALO:TT].rearrange("(c p) t -> p c t", p=128)

    def xn_main(self, t):
        return self.xn_my[t].rearrange("(c p) t -> p c t", p=128)

    def xn_dst(self, t):
        return self.xn_my[t].rearrange("(c p) t -> p c t", p=128)

    def m_src(self, t):
        return self.mrs[t].rearrange("(c p) t -> p c t", p=128)

    def emit_m(self, P, row0, nrows, ti, src_ap, src_bufs):
        nt = self.T // TT
        h, tl = ti // nt, ti % nt
        evs = []
        for j in range(2):
            stg, stg_b, stg_sem = self.stages.next()
            P.op("act", lambda e, stg=stg, j=j: e.activation(out=stg[0:nrows, :], in_=src_ap, func=AF.Identity,
                                                            bias=0.0, scale=self.flag[0:nrows, j:j + 1]),
                 reads=list(src_bufs) + [self.flag_b], writes=[stg_b])
            r0 = (h * 2 + j) * self.fmy + row0
            mz = self.mz2d[tl]
            evs.append(P.op("sp", lambda e, stg=stg, r0=r0, mz=mz: e.dma_start(out=mz[r0:r0 + nrows, :], in_=stg[0:nrows, :]),
                            reads=[stg_b], sem=stg_sem, inc=16))
        self.m_pending.extend(evs)
        return evs


FUSED_INPUTS = None


def build_fused(T=TOK, groups=None):
    SEQ = 2 * T
    if groups is None:
        groups = [[2 * i, 2 * i + 1] for i in range(NCORES // 2)]
    nc = bass.Bass("TRN2", target_bir_lowering=False)
    ext = {}

    def inp(name, shape):
        ext[name] = nc.dram_tensor(name, list(shape), F32, kind="ExternalInput").ap()
        return ext[name]

    def internal(name, shape, dt):
        return nc.dram_tensor(name, list(shape), dt).ap()

    xT_in = inp("xT", [D, T])
    flags = inp("flags", [128, 2])
    flagc = inp("flagc", [128, 1])
    g_mix = [inp(f"g_mix{i}", [128, 8]) for i in range(4)]
    g_ffn = [inp(f"g_ffn{i}", [128, 8]) for i in range(4)]
    g_fin = inp("g_final", [128, 8])
    w1 = [inp(f"w1_{i}", [D, DFF]) for i in range(4)]
    w2 = [inp(f"w2_{i}", [DFF, D]) for i in range(4)]
    ret = []
    for j in range(2):
        ret.append(dict(wq=inp(f"r{j}_wq", [D, 512]), wk=inp(f"r{j}_wk", [D, 512]), wv=inp(f"r{j}_wv", [D, 1024]),
                        wg=inp(f"r{j}_wg", [D, 1024]), qg=inp(f"r{j}_qg", [128, 2]), kg=inp(f"r{j}_kg", [128, 2]),
                        gnw=inp(f"r{j}_gnw", [128, 8]), gnb=inp(f"r{j}_gnb", [128, 8]), w_out=inp(f"r{j}_wout", [2 * D, D])))
    rtab = dict(cos=inp("cos", [128, SEQ]), sin=inp("sin", [128, SEQ]), dt=inp("dt", [128, 2, 128]),
                qdec=inp("qdec", [128, 2, 512]), kdec=inp("kdec", [128, 2]), g128=inp("g128", [128, 2]))
    cv = dict(pw1_w=inp("pw1_w", [D, 2 * D]), pw1_b=inp("pw1_b", [128, 16]), dw_w=inp("dw_w", [128, CW * 8]),
              dw_b=inp("dw_b", [128, 8]), ln_w=inp("ln_w", [128, 8]), ln_b=inp("ln_b", [128, 8]),
              pw2_w=inp("pw2_w", [D, D]), pw2_b=inp("pw2_b", [128, 8]))
    sbw = dict(wq=inp("s_wq", [D, 512]), wk=inp("s_wk", [D, 512]), wv=inp("s_wv", [D, 512]),
               qg=inp("s_qg", [128, 1]), kg=inp("s_kg", [128, 1]), ltri=inp("ltri", [128, 128]), ustr=inp("ustr", [128, 128]),
               oblk=inp("oblk", [128, 128]), maskd=inp("maskd", [128, 4, 512]), w_out=inp("s_wout", [D, D]))
    out_d = nc.dram_tensor("out", [D, T], F32, kind="ExternalOutput").ap()
    xsp = internal("xsp", [D, T], F32)
    NT = T // TT
    xn_my = [internal(f"xn_my{t}", [D, TT], BF16) for t in range(NT)]
    xn_full = [internal(f"xn_full{t}", [2 * D, TT], BF16) for t in range(NT)]
    mz_ret = [internal(f"mz_ret{t}", [4 * 1024, TT], BF16) for t in range(NT)]
    mrs_ret = [internal(f"mrs_ret{t}", [2 * 1024, TT], BF16) for t in range(NT)]
    mz_sb = [internal(f"mz_sb{t}", [4 * 512, TT], BF16) for t in range(NT)]
    mrs_sb = [internal(f"mrs_sb{t}", [2 * 512, TT], BF16) for t in range(NT)]

    global FUSED_INPUTS
    FUSED_INPUTS = list(ext.keys())
    with ExitStack() as st:
        P = Prog(nc, st)
        P.out_sem = P.newsem("outs")
        P.enable_phases()
        env = Env(nc, P, T)
        env.flags_d = flags
        env.xn_full = xn_full
        env.xn_my = xn_my

        import os as _os
        nocc = bool(_os.environ.get("NOCC"))

        env.nocc = nocc
        env.groups = groups

        def gather():
            P.next_phase()

        def scatter(mz, mrs):
            P.next_phase()

        def tok_phase(i, x_src, m_src, w_out, fm, final=False):
            env.mrs = m_src
            env.io = {"xT": x_src, "mT": m_src, "w_out": w_out, "g_ffn": g_ffn[i], "w1": w1[i], "w2": w2[i],
                      "g_next": (g_fin if final else g_mix[i + 1]), "x_out": xsp, "xn_out": (out_d if final else xn_my)}
            build_tok("p2", T=T, fm=fm, env=env, final=final)

        def ret_phase(j):
            env.io = dict(xnT=None, **{k: ret[j][k] for k in ("wq", "wk", "wv", "wg", "qg", "kg", "gnw", "gnb")}, **rtab)
            env.mz2d, env.fmy, env.mrs_out = mz_ret, 1024, mrs_ret
            build_ret(SEQ=SEQ, env=env)
            scatter(mz_ret, mrs_ret)

        env.io = {"xT": xT_in, "g_next": g_mix[0], "xn_out": xn_my}
        build_tok("norm0", T=T, env=env, final=False)
        gather()
        ret_phase(0)
        tok_phase(0, xT_in, mrs_ret, ret[0]["w_out"], 2 * D)
        gather()
        env.io = dict(xT=xsp, x_out=xsp, flag=flagc, **cv)
        build_conv(T=T, env=env)
        P.next_phase()
        tok_phase(1, xsp, None, None, 0)
        gather()
        env.io = dict(xnT=None, **{k: sbw[k] for k in ("wq", "wk", "wv", "qg", "kg", "ltri", "ustr", "oblk", "maskd")})
        env.mz2d, env.fmy, env.mrs_out = mz_sb, 512, mrs_sb
        build_sb(SEQ=SEQ, env=env)
        scatter(mz_sb, mrs_sb)
        tok_phase(2, xsp, mrs_sb, sbw["w_out"], D)
        gather()
        ret_phase(1)
        tok_phase(3, xsp, mrs_ret, ret[1]["w_out"], 2 * D, final=True)
        P.next_phase()
        P.finish(env.outs)
    return nc


_PROGS = {}


def fused_inputs(T, x_b, rank, prm):
    A = lambda a: np.ascontiguousarray(a, dtype=np.float32)
    hp = rank
    m = {"xT": A(x_b[rank * T:(rank + 1) * T].T)}
    fl = np.zeros((128, 2), np.float32)
    fl[:, rank] = 1.0
    m["flags"] = fl
    m["flagc"] = np.full((128, 1), float(rank), np.float32)
    for i in range(4):
        m[f"g_mix{i}"] = pcol(prm["norm_mix"][i])
        m[f"g_ffn{i}"] = pcol(prm["norm_ffn"][i])
        m[f"w1_{i}"] = A(prm["ffn_w1"][i])
        m[f"w2_{i}"] = A(prm["ffn_w2"][i])
    m["g_final"] = pcol(prm["final_norm"])
    for j in range(2):
        w_in = np.asarray(prm["ret_w_in"][j], np.float32)
        m[f"r{j}_wq"] = A(w_in[:, hp * 512:(hp + 1) * 512])
        m[f"r{j}_wk"] = A(w_in[:, 1024 + hp * 512:1024 + (hp + 1) * 512])
        m[f"r{j}_wv"] = A(w_in[:, 2048 + hp * 1024:2048 + (hp + 1) * 1024])
        m[f"r{j}_wg"] = A(w_in[:, 4096 + hp * 1024:4096 + (hp + 1) * 1024])
        m[f"r{j}_qg"] = pcol(prm["ret_q_norm"][j])
        m[f"r{j}_kg"] = pcol(prm["ret_k_norm"][j])
        m[f"r{j}_gnw"] = pcol(np.asarray(prm["ret_gn_w"][j], np.float32)[hp * 1024:(hp + 1) * 1024])
        m[f"r{j}_gnb"] = pcol(np.asarray(prm["ret_gn_b"][j], np.float32)[hp * 1024:(hp + 1) * 1024])
        m[f"r{j}_wout"] = A(prm["ret_w_out"][j])
    tabs = ret_tables((2 * hp, 2 * hp + 1))
    m["cos"] = A(tabs["cos"][:, :2 * T]); m["sin"] = A(tabs["sin"][:, :2 * T])
    for k in ("dt", "qdec", "kdec", "g128"):
        m[k] = A(tabs[k])
    ci = conv_inputs(np.zeros((1, 1)), np.zeros((1, 1)), 0.0, prm["conv_pw1_w"][0], prm["conv_pw1_b"][0], prm["conv_dw_w"][0],
                     prm["conv_dw_b"][0], prm["conv_ln_w"][0], prm["conv_ln_b"][0], prm["conv_pw2_w"][0], prm["conv_pw2_b"][0])
    for k in ("pw1_w", "pw1_b", "dw_w", "dw_b", "ln_w", "ln_b", "pw2_w", "pw2_b"):
        m[k] = ci[k]
    sw = np.asarray(prm["sb_w_in"][0], np.float32)
    m["s_wq"] = A(sw[:, hp * 512:(hp + 1) * 512])
    m["s_wk"] = A(sw[:, 1024 + hp * 512:1024 + (hp + 1) * 512])
    m["s_wv"] = A(sw[:, 2048 + hp * 512:2048 + (hp + 1) * 512])
    m["s_qg"] = A(np.tile(np.asarray(prm["sb_q_norm"][0], np.float32), 2)[:, None])
    m["s_kg"] = A(np.tile(np.asarray(prm["sb_k_norm"][0], np.float32), 2)[:, None])
    m["s_wout"] = A(prm["sb_w_out"][0])
    for k, v in sb_tables().items():
        m[k] = A(v)
    return m


def kernel(**prm):
    x = np.asarray(prm["x"], np.float32)
    if "fused" not in _PROGS:
        _PROGS["fused"] = build_fused()
    nc = _PROGS["fused"]
    in_maps = [fused_inputs(TOK, x[c // 2], c % 2, prm) for c in range(NCORES)]
    res = run_bass_kernel_spmd(nc, in_maps, core_ids=list(range(NCORES)))
    out = np.empty((B, S, D), np.float32)
    for c in range(NCORES):
        out[c // 2, (c % 2) * TOK:(c % 2 + 1) * TOK] = res.results[c]["out"].T
    return out
```

```python
import math
from contextlib import ExitStack

import numpy as np
import ml_dtypes
import concourse.bass as bass
import concourse.mybir as mybir
from concourse.bass_utils import run_bass_kernel_spmd

F32 = mybir.dt.float32
BF16 = mybir.dt.bfloat16
AF = mybir.ActivationFunctionType
ALU = mybir.AluOpType

D = 1024
S = 4096
B = 4
DFF = 4096
EPS = 1e-6
NCORES = 8
TOK = 2048
TT = 512


class Ev:
    __slots__ = ("sem", "val")

    def __init__(self, sem, val):
        self.sem = sem
        self.val = val


class Buf:
    __slots__ = ("name", "w", "r", "sem", "wlist")

    def __init__(self, name, sem=None):
        self.name = name
        self.w = None
        self.r = []
        self.sem = sem


class Prog:
    ENGS = ("pe", "act", "dve", "pool", "sp")

    def __init__(self, nc, stack):
        self.nc = nc
        self.stack = stack
        self.q = {e: [] for e in self.ENGS}
        self.sems = {}
        self.cnt = {}
        self.waited = {}
        self.nsem = 0
        self.arena = None
        self.arena_off = 0
        self.psbanks = None
        self.ps_i = 0
        self.sempool = None
        self.sem_i = 0
        for e in self.ENGS:
            self.newsem("c_" + e)

    def newsem(self, name, kind="hw"):
        if self.sempool is not None:
            pool = self.sempool[kind]
            if self.sem_i[kind] < len(pool):
                nm = pool[self.sem_i[kind]]
            else:
                nm = f"q{kind}{len(pool)}"
                self.sems[nm] = self.stack.enter_context(self.nc.semaphore(nm))
                self.cnt[nm] = 0
                pool.append(nm)
            self.sem_i[kind] += 1
            return nm
        s = self.stack.enter_context(self.nc.semaphore(name))
        self.sems[name] = s
        self.cnt[name] = 0
        self.nsem += 1
        return name

    def buf(self, name, dma=False):
        return Buf(name, "LAZY" if dma else None)

    def semof(self, b, eng):
        if b.sem == "LAZY":
            b.sem = self.newsem("d_" + b.name, kind=("sw" if eng == "pool" else "hw"))
        return b.sem

    def enable_phases(self, arena_bytes=206 * 1024):
        self.arena = self.stack.enter_context(self.nc.sbuf_tensor("arena", [128, arena_bytes // 2], BF16))
        self.arena_bytes = arena_bytes
        self.psbanks = [self.stack.enter_context(self.nc.psum_tensor(f"pbank{i}", [128, 512], F32)) for i in range(8)]
        self.sempool = {"hw": [], "sw": []}
        self.sem_i = {"hw": 0, "sw": 0}

    def next_phase(self):
        for eng in self.ENGS:
            for name, c in self.cnt.items():
                if c > 0 and name != "cc":
                    self._wait(eng, Ev(name, c))
        self.arena_off = 0
        self.ps_i = 0
        self.sem_i = {"hw": 0, "sw": 0}

    def sb(self, name, shape, dt):
        if self.arena is None:
            return self.stack.enter_context(self.nc.sbuf_tensor("s_" + name, list(shape), dt))
        shape = list(shape)
        n = 1
        for d in shape[1:]:
            n *= d
        esz = 4 if dt == F32 else 2
        nb = (n * esz + 31) // 32 * 32
        off = self.arena_off
        assert off + nb <= self.arena_bytes, f"arena overflow allocating {name}: {off}+{nb}"
        self.arena_off += nb
        self.arena_max = max(getattr(self, "arena_max", 0), self.arena_off)
        v = self.arena[0:shape[0], off // 2: off // 2 + n * esz // 2]
        if dt == F32:
            v = v.bitcast(F32)
        if len(shape) == 3:
            v = v.rearrange("p (a b) -> p a b", a=shape[1])
        elif len(shape) == 4:
            v = v.rearrange("p (a b c) -> p a b c", a=shape[1], b=shape[2])
        return v

    def ps(self, name, shape, dt=F32):
        if self.psbanks is None:
            return self.stack.enter_context(self.nc.psum_tensor("p_" + name, list(shape), dt))
        shape = list(shape)
        bank = self.psbanks[self.ps_i]
        self.ps_i += 1
        n = 1
        for d in shape[1:]:
            n *= d
        if dt == F32:
            v = bank[0:shape[0], 0:n]
        else:
            v = bank[0:shape[0], 0:n // 2].bitcast(dt)
        if len(shape) == 3:
            v = v.rearrange("p (a b) -> p a b", a=shape[1])
        return v

    def collective(self, kind, op, groups, in_ap, out_ap, extra=()):
        if "cc" not in self.sems:
            self.sems["cc"] = self.stack.enter_context(self.nc.semaphore("cc"))
            self.cnt["cc"] = 0
        return self.op("pool", lambda e: e.collective_compute(kind, op, replica_groups=groups, ins=[in_ap], outs=[out_ap]),
                       sem="cc", inc=1, extra=extra)

    def _wait(self, eng, ev):
        if ev is None:
            return
        if eng == "pe" and ev.sem == "c_pe":
            return
        key = (eng, ev.sem)
        if self.waited.get(key, 0) >= ev.val:
            return
        self.waited[key] = ev.val
        s = self.sems[ev.sem]
        v = ev.val
        self.q[eng].append(lambda e, s=s, v=v: e.wait_ge(s, v))

    def op(self, eng, fn, reads=(), writes=(), signal=True, sem=None, inc=1, extra=()):
        for b in reads:
            self._wait(eng, b.w)
        for b in writes:
            self._wait(eng, b.w)
            for ev in b.r:
                self._wait(eng, ev)
        for ev in extra:
            self._wait(eng, ev)
        name = sem or ("c_" + eng)
        if signal:
            self.cnt[name] += inc
            ev = Ev(name, self.cnt[name])
            s = self.sems[name]
            self.q[eng].append(lambda e, fn=fn, s=s, inc=inc: fn(e).then_inc(s, inc))
        else:
            ev = Ev(name, self.cnt[name] + inc)
            self.q[eng].append(lambda e, fn=fn: fn(e))
        for b in reads:
            b.r.append(ev)
        for b in writes:
            b.w = ev
            b.r = []
        return ev

    def dma(self, eng, out, in_, dst=None, src=None, extra=()):
        reads = [src] if src is not None else []
        writes = [dst] if dst is not None else []
        semname = self.semof(dst, eng) if (dst is not None and dst.sem) else None
        if semname is None:
            semname = self.out_sem
        return self.op(eng, lambda e: e.dma_start(out=out, in_=in_), reads, writes,
                       sem=semname, inc=16, extra=extra)

    def finish(self, final_evs):
        for ev in final_evs:
            self._wait("sp", ev)
        nc = self.nc
        with nc.Block() as block:
            def mk(name):
                def body(e):
                    for f in self.q[name]:
                        f(e)
                return body
            block.tensor(mk("pe"))
            block.scalar(mk("act"))
            block.vector(mk("dve"))
            block.gpsimd(mk("pool"))
            block.sync(mk("sp"))


def mm_group(P, out_ap, out_buf, terms):
    n = len(terms)
    ev = None
    for i, (l, r, rb) in enumerate(terms):
        ev = P.op("pe",
                  lambda e, l=l, r=r, i=i: e.matmul(out_ap, lhsT=l, rhs=r, start=(i == 0), stop=(i == n - 1)),
                  reads=rb, writes=[out_buf] if i == 0 else [], signal=(i == n - 1))
        if i > 0:
            pass
    out_buf.w = ev
    out_buf.r = []
    return ev


def pcol(v):
    v = np.asarray(v, dtype=np.float32)
    return np.ascontiguousarray(v.reshape(-1, 128).T)


class Rot:
    def __init__(self, items):
        self.items = items
        self.i = 0

    def next(self):
        it = self.items[self.i % len(self.items)]
        self.i += 1
        return it


class Ctx:
    pass


def setup_common(P, nc):
    C = Ctx()
    C.ones = P.sb("ones32", [128, 128], F32)
    C.ones_b = P.buf("ones")
    P.op("pool", lambda e: e.memset(C.ones[:], 1.0), writes=[C.ones_b])
    C.psA = Rot([(P.ps(f"psA{i}", [128, 512]), P.buf(f"psA{i}")) for i in range(3)])
    C.psB = Rot([(P.ps(f"psB{i}", [128, 512]), P.buf(f"psB{i}")) for i in range(3)])
    C.psS = Rot([(P.ps(f"psS{i}", [128, 512]), P.buf(f"psS{i}")) for i in range(2)])
    return C


def load_vec(P, name, dram_ap, nchunk):
    t = P.sb(name, [128, nchunk], F32)
    b = P.buf(name, dma=True)
    P.dma("sp", t[:], dram_ap, dst=b)
    return t, b


def rmsnorm_tile(P, C, x_sb, x_bufs, tsl, gain, gain_b, out_fn, scr, scr_b, rs, rs_b, nfeat=1024):
    nchunk = nfeat // 128
    for c in range(nchunk):
        P.op("act", lambda e, c=c: e.activation(out=scr[:, c, :], in_=x_sb[:, c, tsl], func=AF.Square),
             reads=[x_bufs[c]], writes=[scr_b[c]])
    ps, psb = C.psS.next()
    mm_group(P, ps[:], psb, [(C.ones[:], scr[:, c, :], [scr_b[c], C.ones_b]) for c in range(nchunk)])
    P.op("act", lambda e: e.activation(out=rs[:], in_=ps[:], func=AF.Sqrt, bias=EPS, scale=1.0 / nfeat),
         reads=[psb], writes=[rs_b])
    P.op("dve", lambda e: e.reciprocal(rs[:], rs[:]), reads=[rs_b], writes=[rs_b])
    for c in range(nchunk):
        oap, ob = out_fn(c)
        P.op("dve", lambda e, c=c, oap=oap: e.scalar_tensor_tensor(
            out=oap, in0=x_sb[:, c, tsl], scalar=gain[:, c:c + 1], in1=rs[:], op0=ALU.mult, op1=ALU.mult),
            reads=[x_bufs[c], gain_b, rs_b], writes=[ob])


def build_tok(mode, T=TOK, fm=0, env=None, final=True):
    nc = env.nc if env else bass.Bass("TRN2", target_bir_lowering=False)
    NT = T // TT
    dr = env.dr if env else (lambda n, s, dt, kind: nc.dram_tensor(n, list(s), dt, kind=kind).ap())
    xn_dt = F32 if (env is None or final) else BF16
    xT_d = dr("xT", [D, T], F32, "ExternalInput")
    gn_d = dr("g_next", [128, 8], F32, "ExternalInput")
    xn_o = dr("xn_out", [D, T], xn_dt, "ExternalOutput")
    if mode == "p2":
        if fm:
            mT_d = dr("mT", [fm, T], F32, "ExternalInput")
            wo_d = dr("w_out", [fm, D], F32, "ExternalInput")
        gf_d = dr("g_ffn", [128, 8], F32, "ExternalInput")
        w1_d = dr("w1", [D, DFF], F32, "ExternalInput")
        w2_d = dr("w2", [DFF, D], F32, "ExternalInput")
        x_o = dr("x_out", [D, T], F32, "ExternalOutput")
    with ExitStack() as st:
        if env:
            P = env.P
        else:
            P = Prog(nc, st)
            P.out_sem = P.newsem("outs")
        C = setup_common(P, nc)
        x_sb = P.sb("x_sb", [128, 8, T], F32)
        x_b = [[P.buf(f"x{c}_{t}") for t in range(NT)] for c in range(8)]
        xld = [P.buf(f"xld{t}", dma=True) for t in range(NT)]
        xT_v = xT_d.rearrange("(c p) t -> p c t", p=128)
        for t in range(NT):
            ev = P.dma("sp", x_sb[:, :, t * TT:(t + 1) * TT], xT_v[:, :, t * TT:(t + 1) * TT], dst=xld[t])
            for c in range(8):
                x_b[c][t].w = ev
        gnext, gnext_b = load_vec(P, "gnext", gn_d, 8)
        scr = P.sb("scr", [128, 8, TT], F32)
        scr_b = [P.buf(f"scr{c}") for c in range(8)]
        rs = P.sb("rs", [128, TT], F32)
        rs_b = P.buf("rs")
        outs = []
        xno = Rot([(P.sb(f"xno{i}", [128, 8, TT], xn_dt), [P.buf(f"xno{i}_{c}") for c in range(8)]) for i in range(2)])
        xn_ov = xn_o.rearrange("(c p) t -> p c t", p=128) if (env is None or final) else None
        xno_sems = [P.newsem(f"st_xno{i}") for i in range(2)]
        x_ov = x_o.rearrange("(c p) t -> p c t", p=128) if mode == "p2" else None
        tail_i = [0]

        def tile_tail(t):
            tsl = slice(t * TT, (t + 1) * TT)
            if mode == "p2" and (env is None or not final):
                for c in range(8):
                    outs.append(P.op("sp", lambda e, c=c, tsl=tsl: e.dma_start(out=x_ov[:, c, tsl], in_=x_sb[:, c, tsl]),
                                     reads=[x_b[c][t]], sem=P.out_sem, inc=16))
            o_sb, o_b = xno.next()
            osem = xno_sems[tail_i[0] % 2]
            tail_i[0] += 1
            rmsnorm_tile(P, C, x_sb, [x_b[c][t] for c in range(8)], tsl, gnext, gnext_b,
                         lambda c, o_sb=o_sb, o_b=o_b: (o_sb[:, c, :], o_b[c]), scr, scr_b, rs, rs_b)
            xn_dst = env.xn_dst(t) if (env and not final) else xn_ov[:, :, tsl]
            ev = P.op("sp", lambda e, o_sb=o_sb, xn_dst=xn_dst: e.dma_start(out=xn_dst, in_=o_sb[:]),
                      reads=o_b, sem=osem, inc=16)
            outs.append(ev)
            if env and not final:
                env.gather_tile(P, t, [ev])

        if mode == "p2":
            gffn, gffn_b = load_vec(P, "gffn", gf_d, 8)
            KM = fm // 128
            if fm:
              pass
            R1 = P.sb("R1", [128, 32768], BF16)

            def view(off, d0, d1):
                return R1[:, off:off + d0 * d1].rearrange("p (a b) -> p a b", a=d0)
            lastA = None
            if fm:
                wo_sb = view(0, KM, D)
                wo_b = [P.buf(f"wo{k}", dma=True) for k in range(KM // 4)]
                wo_v = wo_d.rearrange("(c p) f -> p c f", p=128)
                for k4 in range(KM // 4):
                    P.dma("pool", wo_sb[:, 4 * k4:4 * k4 + 4, :], wo_v[:, 4 * k4:4 * k4 + 4, :], dst=wo_b[k4])
                mr = Rot([(view(16384 + i * 8192, KM, TT), P.buf(f"mt{i}", dma=True)) for i in range(2)])
                if env:
                    m_src = env.m_src
                else:
                    mT_v = mT_d.rearrange("(c p) t -> p c t", p=128)
                    m_src = lambda t: mT_v[:, :, t * TT:(t + 1) * TT]
                for t in range(NT):
                    tsl = slice(t * TT, (t + 1) * TT)
                    m_t, m_tb = mr.next()
                    P.dma("pool", m_t, m_src(t), dst=m_tb, extra=(env.m_wait(t) if env else ()))
                    for fo in range(8):
                        ps, psb = C.psB.next()
                        lastA = mm_group(P, ps[:], psb,
                                         [(wo_sb[:, k, fo * 128:(fo + 1) * 128], m_t[:, k, :], [wo_b[k // 4], m_tb])
                                          for k in range(KM)])
                        P.op("dve", lambda e, ps=ps, fo=fo, tsl=tsl: e.tensor_tensor(
                            out=x_sb[:, fo, tsl], in0=x_sb[:, fo, tsl], in1=ps[:], op=ALU.add),
                            reads=[psb, x_b[fo][t]], writes=[x_b[fo][t]])
            xn_sb = view(0, 8, T)
            xn_b = [[P.buf(f"xn{c}_{t}") for t in range(NT)] for c in range(8)]
            for c in range(8):
                for t in range(NT):
                    xn_b[c][t].w = lastA
            for t in range(NT):
                tsl = slice(t * TT, (t + 1) * TT)
                rmsnorm_tile(P, C, x_sb, [x_b[c][t] for c in range(8)], tsl, gffn, gffn_b,
                             lambda c, t=t, tsl=tsl: (xn_sb[:, c, tsl], xn_b[c][t]), scr, scr_b, rs, rs_b)
            NG = DFF // 512
            w1r = Rot([(view(16384 + i * 4096, 8, 512), P.buf(f"w1g{i}", dma=True)) for i in range(2)])
            w2r = Rot([(view(24576 + i * 4096, 4, D), P.buf(f"w2g{i}", dma=True)) for i in range(2)])
            for (_, wb_) in w1r.items + w2r.items:
                wb_.w = lastA
            hr = Rot([(P.sb(f"h{i}", [128, 4, TT], BF16), [P.buf(f"h{i}_{j}") for j in range(4)]) for i in range(2)])
            sqr = Rot([(P.sb(f"sq{i}", [128, TT], F32), P.buf(f"sq{i}")) for i in range(2)])
            w1_v = w1_d.rearrange("(c p) f -> p c f", p=128)
            w2_v = w2_d.rearrange("(c p) f -> p c f", p=128)
            for g in range(NG):
                w1g, w1b = w1r.next()
                w2g, w2b = w2r.next()
                P.dma("pool", w1g, w1_v[:, :, g * 512:(g + 1) * 512], dst=w1b)
                P.dma("pool", w2g, w2_v[:, 4 * g:4 * g + 4, :], dst=w2b)
                for t in range(NT):
                    tsl = slice(t * TT, (t + 1) * TT)
                    h_sb, h_b = hr.next()
                    for j in range(4):
                        ps, psb = C.psA.next()
                        mm_group(P, ps[:], psb,
                                 [(w1g[:, k, j * 128:(j + 1) * 128], xn_sb[:, k, tsl], [w1b, xn_b[k][t]])
                                  for k in range(8)])
                        sq, sqb = sqr.next()
                        P.op("act", lambda e, sq=sq, ps=ps: e.activation(out=sq[:], in_=ps[:], func=AF.Square),
                             reads=[psb], writes=[sqb])
                        P.op("dve", lambda e, sq=sq, ps=ps, h_sb=h_sb, j=j: e.scalar_tensor_tensor(
                            out=h_sb[:, j, :], in0=ps[:], scalar=0.0, in1=sq[:], op0=ALU.is_gt, op1=ALU.mult),
                            reads=[psb, sqb], writes=[h_b[j]])
                    for fo in range(8):
                        ps, psb = C.psB.next()
                        mm_group(P, ps[:], psb,
                                 [(w2g[:, j, fo * 128:(fo + 1) * 128], h_sb[:, j, :], [w2b, h_b[j]])
                                  for j in range(4)])
                        P.op("dve", lambda e, ps=ps, fo=fo, tsl=tsl: e.tensor_tensor(
                            out=x_sb[:, fo, tsl], in0=x_sb[:, fo, tsl], in1=ps[:], op=ALU.add),
                            reads=[psb, x_b[fo][t]], writes=[x_b[fo][t]])
                    if g == NG - 1:
                        tile_tail(t)
        if mode != "p2":
            for t in range(NT):
                tile_tail(t)
        if env:
            env.outs = outs
        else:
            P.finish(outs)
    return nc


RET_G = [1.0 - 2.0 ** (-5.0 - h) for h in range(4)]


def ret_tables(heads):
    half = 128
    inv_freq = (10000.0 ** (-np.arange(half, dtype=np.float32) / np.float32(half))).astype(np.float32)
    ang = (np.arange(S, dtype=np.float32)[None, :] * inv_freq[:, None]).astype(np.float32)
    cos = np.cos(ang.astype(np.float64)).astype(np.float32)
    sin = np.sin(ang.astype(np.float64)).astype(np.float32)
    idx = np.arange(128)
    dt = np.zeros((128, 2, 128), np.float32)
    qdec = np.zeros((128, 2, 512), np.float32)
    kdec = np.zeros((128, 2), np.float32)
    g128 = np.zeros((128, 2), np.float32)
    for i, h in enumerate(heads):
        lg = math.log(RET_G[h])
        t = idx[None, :]
        s = idx[:, None]
        same = (t // 64) == (s // 64)
        later = (t // 64) > (s // 64)
        dmat = np.where(same, np.exp(lg * np.abs(t - s)), np.where(later, np.exp(lg * (t - s)), 0.0))
        dt[:, i, :] = dmat.astype(np.float32)
        qdec[:, i, :] = np.tile(np.exp(lg * (idx + 1.0)), 4)[None, :]
        kdec[:, i] = np.exp(lg * (127.0 - idx))
        g128[:, i] = math.exp(lg * 128.0)
    return dict(cos=cos, sin=sin, dt=dt, qdec=qdec, kdec=kdec, g128=g128)


def build_ret(SEQ=S, env=None):
    nc = env.nc if env else bass.Bass("TRN2", target_bir_lowering=False)
    NTI = SEQ // TT
    dr = env.dr if env else (lambda n, s, dt, kind: nc.dram_tensor(n, list(s), dt, kind=kind).ap())
    xn_d = dr("xnT", [D, SEQ], F32, "ExternalInput")
    wq_d = dr("wq", [D, 512], F32, "ExternalInput")
    wk_d = dr("wk", [D, 512], F32, "ExternalInput")
    wv_d = dr("wv", [D, 1024], F32, "ExternalInput")
    wg_d = dr("wg", [D, 1024], F32, "ExternalInput")
    qg_d = dr("qg", [128, 2], F32, "ExternalInput")
    kg_d = dr("kg", [128, 2], F32, "ExternalInput")
    gnw_d = dr("gnw", [128, 8], F32, "ExternalInput")
    gnb_d = dr("gnb", [128, 8], F32, "ExternalInput")
    cos_d = dr("cos", [128, SEQ], F32, "ExternalInput")
    sin_d = dr("sin", [128, SEQ], F32, "ExternalInput")
    dt_d = dr("dt", [128, 2, 128], F32, "ExternalInput")
    qdec_d = dr("qdec", [128, 2, 512], F32, "ExternalInput")
    kdec_d = dr("kdec", [128, 2], F32, "ExternalInput")
    g128_d = dr("g128", [128, 2], F32, "ExternalInput")
    m_o = dr("mT_out", [D, SEQ], F32, "ExternalOutput")
    with ExitStack() as st:
        if env:
            P = env.P
            env.phase_setup(P)
        else:
            P = Prog(nc, st)
            P.out_sem = P.newsem("outs")
        ones = P.sb("ones32", [128, 128], F32)
        ones_b = P.buf("ones")
        P.op("pool", lambda e: e.memset(ones[:], 1.0), writes=[ones_b])
        ident = P.sb("ident", [128, 128], BF16)
        ident_b = P.buf("ident")
        P.op("pool", lambda e: e.memset(ident[:], 1.0), writes=[ident_b])
        P.op("pool", lambda e: e.affine_select(out=ident[:], in_=ident[:], pattern=[[-1, 128]],
                                               compare_op=ALU.is_equal, fill=0.0, base=0, channel_multiplier=1),
             reads=[ident_b], writes=[ident_b])
        psA = Rot([(P.ps(f"psA{i}", [128, 512]), P.buf(f"psA{i}")) for i in range(2)])
        psS = Rot([(P.ps(f"psS{i}", [128, 512]), P.buf(f"psS{i}")) for i in range(1)])
        psT = Rot([(P.ps(f"psT{i}", [128, 4, 128], BF16), P.buf(f"psT{i}")) for i in range(1)])
        psSc = Rot([(P.ps(f"psSc{i}", [128, 128]), P.buf(f"psSc{i}")) for i in range(1)])
        psY = Rot([(P.ps(f"psY{i}", [128, 4, 128]), P.buf(f"psY{i}")) for i in range(1)])
        psSt = Rot([(P.ps(f"psSt{i}", [128, 512]), P.buf(f"psSt{i}")) for i in range(2)])

        def small(name, d_ap, shape):
            t = P.sb(name, shape, F32)
            b = P.buf(name, dma=True)
            P.dma("sp", t[:], d_ap, dst=b)
            return t, b
        qg, qg_b = small("qg", qg_d, [128, 2])
        kg, kg_b = small("kg", kg_d, [128, 2])
        gnw, gnw_b = small("gnw", gnw_d, [128, 8])
        gnb, gnb_b = small("gnb", gnb_d, [128, 8])
        if env:
            env.flagged_affine(P, gnw, gnw_b, gnb, gnb_b)
        dtt, dtt_b = small("dtt", dt_d, [128, 2, 128])
        qdec, qdec_b = small("qdec", qdec_d, [128, 2, 512])
        kdec, kdec_b = small("kdec", kdec_d, [128, 2])
        g128, g128_b = small("g128", g128_d, [128, 2])

        def wload(name, d_ap, ncol):
            t = P.sb(name, [128, 8, ncol], BF16)
            bs = []
            v = d_ap.rearrange("(c p) f -> p c f", p=128)
            for k2 in range(4):
                b = P.buf(f"{name}{k2}", dma=True)
                P.dma("pool", t[:, 2 * k2:2 * k2 + 2, :], v[:, 2 * k2:2 * k2 + 2, :], dst=b)
                bs.append(b)
            return t, bs
        wq, wq_b = wload("wq", wq_d, 512)
        wk, wk_b = wload("wk", wk_d, 512)
        wv, wv_b = wload("wv", wv_d, 1024)
        wg, wg_b = wload("wg", wg_d, 1024)

        xnr = Rot([(P.sb(f"xn{i}", [128, 8, TT], BF16), P.buf(f"xn{i}", dma=True)) for i in range(2)])
        csr = Rot([(P.sb(f"cs{i}", [128, 2, TT], F32), P.buf(f"cs{i}", dma=True)) for i in range(2)])
        raw = P.sb("raw", [128, 4, TT], F32)
        raw_b = [P.buf(f"raw{c}") for c in range(4)]
        rs2 = P.sb("rs2", [128, 2, TT], F32); rs2_b = [P.buf(f"rs2_{h}") for h in range(2)]
        mu2 = P.sb("mu2", [128, 2, TT], F32); mu2_b = [P.buf(f"mu2_{h}") for h in range(2)]
        scr = P.sb("scr", [128, 4, TT], F32)
        scr_b = [P.buf(f"scr{c}") for c in range(4)]
        tmpA = P.sb("tmpA", [128, TT], F32); tmpA_b = P.buf("tmpA")
        tmpB = P.sb("tmpB", [128, TT], F32); tmpB_b = P.buf("tmpB")
        def mkset(i):
            d = Ctx()
            d.QT = P.sb(f"QT{i}", [128, 4, TT], BF16); d.QT_b = [P.buf(f"QT{i}_{c}") for c in range(4)]
            d.QdT = P.sb(f"QdT{i}", [128, 4, TT], BF16); d.QdT_b = [P.buf(f"QdT{i}_{c}") for c in range(4)]
            d.KT = P.sb(f"KT{i}", [128, 4, TT], BF16); d.KT_b = [P.buf(f"KT{i}_{c}") for c in range(4)]
            d.Kd = P.sb(f"Kd{i}", [128, 4, 4, 128], BF16); d.Kd_b = [[P.buf(f"Kd{i}_{b}_{h}") for h in range(2)] for b in range(4)]
            d.Vt = P.sb(f"Vt{i}", [128, 4, 2, 512], BF16); d.Vt_b = [[P.buf(f"Vt{i}_{b}_{h}") for h in range(2)] for b in range(4)]
            return d
        sets = [mkset(0), mkset(1)]
        sg = P.sb("sg", [128, 8, TT], BF16); sg_b = [P.buf(f"sg{c}") for c in range(8)]
        y32 = P.sb("y32", [128, 2, 4, TT], F32)
        y_b = [[[P.buf(f"y{h}_{ec}_{b}") for b in range(4)] for ec in range(4)] for h in range(2)]
        PT = P.sb("PT", [128, 2, 128], BF16); PT_b = [P.buf(f"PT{h}") for h in range(2)]
        S32 = P.sb("S32", [128, 2, 2, 512], F32)
        Sbf = P.sb("Sbf", [128, 2, 2, 512], BF16)
        S_b = [[P.buf(f"S32_{h}_{d}") for d in range(2)] for h in range(2)]
        Sbf_b = [[P.buf(f"Sbf_{h}_{d}") for d in range(2)] for h in range(2)]
        for h in range(2):
            for d in range(2):
                P.op("pool", lambda e, h=h, d=d: e.memset(S32[:, h, d, :], 0.0), writes=[S_b[h][d]])
                P.op("pool", lambda e, h=h, d=d: e.memset(Sbf[:, h, d, :], 0.0), writes=[Sbf_b[h][d]])

        if env:
            xn_src = env.xn_src
        else:
            xn_v = xn_d.rearrange("(c p) t -> p c t", p=128)
            xn_src = lambda ti: xn_v[:, :, ti * TT:(ti + 1) * TT]
            m_ov = m_o.rearrange("(c p) t -> p c t", p=128)
            st_sem = [[P.newsem(f"st_y{h}_{ec}") for ec in range(4)] for h in range(2)]
        outs = []

        def qk_path(w, w_b, gain, gain_b, xn, xn_b, cs, cs_b, is_k, bs):
            QT, QT_b, QdT, QdT_b, KT, KT_b = bs.QT, bs.QT_b, bs.QdT, bs.QdT_b, bs.KT, bs.KT_b
            OT, OT_b = (KT, KT_b) if is_k else (QT, QT_b)
            for c in range(4):
                ps, psb = psA.next()
                mm_group(P, ps[:], psb, [(w[:, k, c * 128:(c + 1) * 128], xn[:, k, :], [w_b[k // 2], xn_b]) for k in range(8)])
                P.op("act", lambda e, c=c, ps=ps: e.copy(raw[:, c, :], ps[:]), reads=[psb], writes=[raw_b[c]])
                P.op("act", lambda e, c=c: e.activation(out=scr[:, c, :], in_=raw[:, c, :], func=AF.Square),
                     reads=[raw_b[c]], writes=[scr_b[c]])
            for h in range(2):
                ps, psb = psSt.next()
                mm_group(P, ps[:], psb, [(ones[:], scr[:, 2 * h + dc, :], [scr_b[2 * h + dc], ones_b]) for dc in range(2)])
                if is_k:
                    P.op("act", lambda e, ps=ps, h=h: e.activation(out=rs2[:, h, :], in_=ps[:], func=AF.Sqrt, bias=256.0 * EPS, scale=1.0),
                         reads=[psb], writes=[rs2_b[h]])
                else:
                    P.op("act", lambda e, ps=ps, h=h: e.activation(out=rs2[:, h, :], in_=ps[:], func=AF.Sqrt, bias=EPS, scale=1.0 / 256.0),
                         reads=[psb], writes=[rs2_b[h]])
            for h in range(2):
                P.op("dve", lambda e, h=h: e.reciprocal(rs2[:, h, :], rs2[:, h, :]), reads=[rs2_b[h]], writes=[rs2_b[h]])
                for dc in range(2):
                    c = 2 * h + dc
                    P.op("dve", lambda e, c=c, dc=dc, h=h: e.scalar_tensor_tensor(
                        out=raw[:, c, :], in0=raw[:, c, :], scalar=gain[:, dc:dc + 1], in1=rs2[:, h, :], op0=ALU.mult, op1=ALU.mult),
                        reads=[raw_b[c], gain_b, rs2_b[h]], writes=[raw_b[c]])
                c1, c2 = 2 * h, 2 * h + 1
                P.op("dve", lambda e, c1=c1: e.tensor_tensor(out=tmpA[:], in0=raw[:, c1, :], in1=cs[:, 0, :], op=ALU.mult),
                     reads=[raw_b[c1], cs_b], writes=[tmpA_b])
                P.op("dve", lambda e, c2=c2: e.tensor_tensor(out=tmpB[:], in0=raw[:, c2, :], in1=cs[:, 1, :], op=ALU.mult),
                     reads=[raw_b[c2], cs_b], writes=[tmpB_b])
                P.op("dve", lambda e, c1=c1: e.tensor_tensor(out=OT[:, c1, :], in0=tmpA[:], in1=tmpB[:], op=ALU.subtract),
                     reads=[tmpA_b, tmpB_b], writes=[OT_b[c1]])
                P.op("dve", lambda e, c1=c1: e.tensor_tensor(out=tmpA[:], in0=raw[:, c1, :], in1=cs[:, 1, :], op=ALU.mult),
                     reads=[raw_b[c1], cs_b], writes=[tmpA_b])
                P.op("dve", lambda e, c2=c2: e.tensor_tensor(out=tmpB[:], in0=raw[:, c2, :], in1=cs[:, 0, :], op=ALU.mult),
                     reads=[raw_b[c2], cs_b], writes=[tmpB_b])
                P.op("dve", lambda e, c2=c2: e.tensor_tensor(out=OT[:, c2, :], in0=tmpA[:], in1=tmpB[:], op=ALU.add),
                     reads=[tmpA_b, tmpB_b], writes=[OT_b[c2]])
                if not is_k:
                    for c in (c1, c2):
                        P.op("dve", lambda e, c=c, h=h: e.tensor_tensor(out=QdT[:, c, :], in0=QT[:, c, :], in1=qdec[:, h, :], op=ALU.mult),
                             reads=[QT_b[c], qdec_b], writes=[QdT_b[c]])

        tiles = {}

        def load(ti):
            tsl = slice(ti * TT, (ti + 1) * TT)
            xn, xn_b = xnr.next()
            P.dma("pool", xn[:], xn_src(ti), dst=xn_b, extra=(env.xn_wait(ti) if env else ()))
            cs, cs_b = csr.next()
            P.dma("sp", cs[:, 0, :], cos_d[:, tsl], dst=cs_b)
            P.dma("sp", cs[:, 1, :], sin_d[:, tsl], dst=cs_b)
            tiles[ti] = dict(xn=xn, xn_b=xn_b, cs=cs, cs_b=cs_b, bs=sets[ti % 2])

        def qpath(ti):
            t = tiles[ti]
            qk_path(wq, wq_b, qg, qg_b, t["xn"], t["xn_b"], t["cs"], t["cs_b"], False, t["bs"])

        def kpath(ti):
            t = tiles[ti]
            bs = t["bs"]
            qk_path(wk, wk_b, kg, kg_b, t["xn"], t["xn_b"], t["cs"], t["cs_b"], True, bs)
            for b in range(4):
                bsl = slice(b * 128, (b + 1) * 128)
                pt, ptb = psT.next()
                ev = None
                for c in range(4):
                    ev = P.op("pe", lambda e, c=c, pt=pt, bsl=bsl, bs=bs: e.transpose(pt[:, c, :], bs.KT[:, c, bsl], ident[:]),
                              reads=[bs.KT_b[c], ident_b], writes=[ptb] if c == 0 else [], signal=(c == 3))
                ptb.w = ev
                ptb.r = []
                for h in range(2):
                    P.op("dve", lambda e, b=b, h=h, pt=pt, bs=bs: e.tensor_scalar(
                        out=bs.Kd[:, b, 2 * h:2 * h + 2, :], in0=pt[:, 2 * h:2 * h + 2, :], scalar1=kdec[:, h:h + 1], scalar2=None, op0=ALU.mult),
                        reads=[ptb, kdec_b], writes=[bs.Kd_b[b][h]])

        def vproj(ti):
            t = tiles[ti]
            xn, xn_b, bs = t["xn"], t["xn_b"], t["bs"]
            for b in range(4):
                for h in range(2):
                    ps, psb = psA.next()
                    mm_group(P, ps[:], psb, [(xn[:, k, b * 128:(b + 1) * 128], wv[:, k, h * 512:(h + 1) * 512], [xn_b, wv_b[k // 2]]) for k in range(8)])
                    P.op("act", lambda e, b=b, h=h, ps=ps, bs=bs: e.copy(bs.Vt[:, b, h, :], ps[:]), reads=[psb], writes=[bs.Vt_b[b][h]])

        def gproj(ti):
            t = tiles[ti]
            xn, xn_b = t["xn"], t["xn_b"]
            for c in range(8):
                ps, psb = psA.next()
                mm_group(P, ps[:], psb, [(wg[:, k, c * 128:(c + 1) * 128], xn[:, k, :], [wg_b[k // 2], xn_b]) for k in range(8)])
                P.op("act", lambda e, c=c, ps=ps: e.activation(out=sg[:, c, :], in_=ps[:], func=AF.Silu), reads=[psb], writes=[sg_b[c]])

        def block(ti, b):
            bs = tiles[ti]["bs"]
            KT, KT_b, QT, QT_b, QdT, QdT_b, Kd, Kd_b, Vt, Vt_b = bs.KT, bs.KT_b, bs.QT, bs.QT_b, bs.QdT, bs.QdT_b, bs.Kd, bs.Kd_b, bs.Vt, bs.Vt_b
            bsl = slice(b * 128, (b + 1) * 128)
            for h in range(2):
                sc, scb = psSc.next()
                mm_group(P, sc[:], scb, [(KT[:, 2 * h + dc, bsl], QT[:, 2 * h + dc, bsl], [KT_b[2 * h + dc], QT_b[2 * h + dc]]) for dc in range(2)])
                P.op("dve", lambda e, h=h, sc=sc: e.tensor_tensor(out=PT[:, h, :], in0=sc[:], in1=dtt[:, h, :], op=ALU.mult),
                     reads=[scb, dtt_b], writes=[PT_b[h]])
                py, pyb = psY.next()
                first = True
                ev = None
                for ec in range(4):
                    esl = slice(ec * 128, (ec + 1) * 128)
                    terms = [(Vt[:, b, h, esl], PT[:, h, :], [Vt_b[b][h], PT_b[h]])]
                    terms += [(Sbf[:, h, dc, esl], QdT[:, 2 * h + dc, bsl], [Sbf_b[h][dc], QdT_b[2 * h + dc]]) for dc in range(2)]
                    for i, (l, r, rb) in enumerate(terms):
                        ev = P.op("pe", lambda e, l=l, r=r, i=i, ec=ec, py=py: e.matmul(py[:, ec, :], lhsT=l, rhs=r, start=(i == 0), stop=(i == 2)),
                                  reads=rb, writes=[pyb] if first else [], signal=(ec == 3 and i == 2))
                        first = False
                pyb.w = ev
                pyb.r = []
                P.op("act", lambda e, h=h, bsl=bsl, py=py: e.copy(y32[:, h, :, bsl], py[:]),
                     reads=[pyb], writes=[y_b[h][ec][b] for ec in range(4)])
                for dc in range(2):
                    pst, pstb = psSt.next()
                    mm_group(P, pst[:], pstb, [(Kd[:, b, 2 * h + dc, :], Vt[:, b, h, :], [Kd_b[b][h], Vt_b[b][h]])])
                    P.op("dve", lambda e, h=h, dc=dc, pst=pst: e.scalar_tensor_tensor(
                        out=S32[:, h, dc, :], in0=S32[:, h, dc, :], scalar=g128[:, h:h + 1], in1=pst[:], op0=ALU.mult, op1=ALU.add),
                        reads=[pstb, g128_b, S_b[h][dc]], writes=[S_b[h][dc]])
                    P.op("act", lambda e, h=h, dc=dc: e.copy(Sbf[:, h, dc, :], S32[:, h, dc, :]),
                         reads=[S_b[h][dc]], writes=[Sbf_b[h][dc]])

        def gn_norm(ti):
            sq = [(scr, scr_b), (raw, raw_b)]
            for h in range(2):
                for ec in range(4):
                    P.op("act", lambda e, h=h, ec=ec: e.activation(out=sq[h][0][:, ec, :], in_=y32[:, h, ec, :], func=AF.Square),
                         reads=[y_b[h][ec][b] for b in range(4)], writes=[sq[h][1][ec]])
            pss = []
            for h in range(2):
                ps1, ps1b = psA.next()
                mm_group(P, ps1[:], ps1b, [(ones[:], y32[:, h, ec, :], [y_b[h][ec][b] for b in range(4)] + [ones_b]) for ec in range(4)])
                ps2, ps2b = psSt.next()
                mm_group(P, ps2[:], ps2b, [(ones[:], sq[h][0][:, ec, :], [sq[h][1][ec], ones_b]) for ec in range(4)])
                pss.append((ps1, ps1b, ps2, ps2b))
            for h in range(2):
                ps1, ps1b, ps2, ps2b = pss[h]
                P.op("act", lambda e, ps1=ps1, h=h: e.activation(out=mu2[:, h, :], in_=ps1[:], func=AF.Copy, scale=1.0 / 512.0),
                     reads=[ps1b], writes=[mu2_b[h]])
                P.op("dve", lambda e, h=h: e.tensor_tensor(out=tmpA[:], in0=mu2[:, h, :], in1=mu2[:, h, :], op=ALU.mult), reads=[mu2_b[h]], writes=[tmpA_b])
                P.op("dve", lambda e, ps2=ps2, h=h: e.scalar_tensor_tensor(out=rs2[:, h, :], in0=ps2[:], scalar=1.0 / 512.0, in1=tmpA[:], op0=ALU.mult, op1=ALU.subtract),
                     reads=[ps2b, tmpA_b], writes=[rs2_b[h]])
                P.op("act", lambda e, h=h: e.activation(out=rs2[:, h, :], in_=rs2[:, h, :], func=AF.Sqrt, bias=EPS, scale=1.0), reads=[rs2_b[h]], writes=[rs2_b[h]])
            for h in range(2):
                P.op("dve", lambda e, h=h: e.reciprocal(rs2[:, h, :], rs2[:, h, :]), reads=[rs2_b[h]], writes=[rs2_b[h]])
                for ec in range(4):
                    c = 4 * h + ec
                    yb = [y_b[h][ec][b] for b in range(4)]
                    P.op("dve", lambda e, h=h, ec=ec: e.tensor_tensor(out=y32[:, h, ec, :], in0=y32[:, h, ec, :], in1=mu2[:, h, :], op=ALU.subtract),
                         reads=yb + [mu2_b[h]], writes=yb)
                    P.op("dve", lambda e, h=h, ec=ec: e.tensor_tensor(out=y32[:, h, ec, :], in0=y32[:, h, ec, :], in1=rs2[:, h, :], op=ALU.mult),
                         reads=yb + [rs2_b[h]], writes=yb)

        def gn_out(ti):
            tsl = slice(ti * TT, (ti + 1) * TT)
            for h in range(2):
                for ec in range(4):
                    c = 4 * h + ec
                    yb = [y_b[h][ec][b] for b in range(4)]
                    if env:
                        outs.extend(env.emit_gn(P, c, ti, y32[:, h, ec, :], yb, sg[:, c, :], sg_b[c]))
                        continue
                    P.op("act", lambda e, h=h, ec=ec, c=c: e.activation(out=y32[:, h, ec, :], in_=y32[:, h, ec, :], func=AF.Identity,
                                                                      bias=gnb[:, c:c + 1], scale=gnw[:, c:c + 1]),
                         reads=yb + [gnw_b, gnb_b], writes=yb)
                    P.op("dve", lambda e, h=h, ec=ec, c=c: e.tensor_tensor(out=y32[:, h, ec, :], in0=y32[:, h, ec, :], in1=sg[:, c, :], op=ALU.mult),
                         reads=yb + [sg_b[c]], writes=yb)
                    if env:
                        pass
                    else:
                        outs.append(P.op("sp", lambda e, h=h, ec=ec, c=c, tsl=tsl: e.dma_start(out=m_ov[:, c, tsl], in_=y32[:, h, ec, :]),
                                         reads=yb, sem=st_sem[h][ec], inc=16))

        load(0)
        qpath(0)
        kpath(0)
        vproj(0)
        for ti in range(NTI):
            nxt = ti + 1 < NTI
            if nxt:
                load(ti + 1)
            block(ti, 0)
            if nxt:
                qpath(ti + 1)
            block(ti, 1)
            if nxt:
                kpath(ti + 1)
            block(ti, 2)
            if nxt:
                vproj(ti + 1)
            block(ti, 3)
            gn_norm(ti)
            gproj(ti)
            gn_out(ti)
            if env:
                env.m_done(P, ti)
        if env:
            env.outs = outs
        else:
            P.finish(outs)
    return nc


def sb_tables():
    j = np.arange(128)
    ltri = (j[:, None] >= j[None, :]).astype(np.float32)
    ustr = (j[:, None] < j[None, :]).astype(np.float32)
    oblk = np.zeros((128, 128), np.float32)
    oblk[:64, :64] = 1.0
    oblk[64:, 64:] = 1.0
    t = np.arange(512)
    maskd = np.zeros((128, 4, 512), np.float32)
    for r in range(4):
        maskd[:, r, :] = ((128 * r + j)[:, None] < t[None, :]).astype(np.float32)
    return dict(ltri=ltri, ustr=ustr, oblk=oblk, maskd=maskd)


def build_sb(SEQ=S, env=None):
    nc = env.nc if env else bass.Bass("TRN2", target_bir_lowering=False)
    NTI = SEQ // TT
    NKB = SEQ // 128
    dr = env.dr if env else (lambda n, s, dt, kind: nc.dram_tensor(n, list(s), dt, kind=kind).ap())
    xn_d = dr("xnT", [D, SEQ], F32, "ExternalInput")
    wq_d = dr("wq", [D, 512], F32, "ExternalInput")
    wk_d = dr("wk", [D, 512], F32, "ExternalInput")
    wv_d = dr("wv", [D, 512], F32, "ExternalInput")
    qg_d = dr("qg", [128, 1], F32, "ExternalInput")
    kg_d = dr("kg", [128, 1], F32, "ExternalInput")
    ltri_d = dr("ltri", [128, 128], F32, "ExternalInput")
    ustr_d = dr("ustr", [128, 128], F32, "ExternalInput")
    oblk_d = dr("oblk", [128, 128], F32, "ExternalInput")
    maskd_d = dr("maskd", [128, 4, 512], F32, "ExternalInput")
    y_o = dr("yT_out", [512, SEQ], F32, "ExternalOutput")
    with ExitStack() as st:
        if env:
            P = env.P
            env.phase_setup(P)
        else:
            P = Prog(nc, st)
            P.out_sem = P.newsem("outs")
        psZ = Rot([(P.ps(f"psZ{i}", [128, 512]), P.buf(f"psZ{i}")) for i in range(2)])
        psAcc = Rot([(P.ps(f"psAcc{i}", [128, 512]), P.buf(f"psAcc{i}")) for i in range(2)])
        psY = Rot([(P.ps(f"psY{i}", [64, 512]), P.buf(f"psY{i}")) for i in range(4)])
        psS = psAcc

        def small(name, d_ap, shape, dt=F32, eng="sp"):
            t = P.sb(name, shape, dt)
            b = P.buf(name, dma=True)
            P.dma(eng, t[:], d_ap, dst=b)
            return t, b
        qg, qg_b = small("qg", qg_d, [128, 1])
        kg, kg_b = small("kg", kg_d, [128, 1])
        oblk, oblk_b = small("oblk", oblk_d, [128, 128])
        maskd, maskd_b = small("maskd", maskd_d, [128, 4, 512])
        ltri, ltri_b = small("ltri", ltri_d, [128, 128], BF16, "pool")
        ustr, ustr_b = small("ustr", ustr_d, [128, 128], BF16, "pool")

        def wload(name, d_ap, ncol):
            t = P.sb(name, [128, 8, ncol], BF16)
            bs = []
            v = d_ap.rearrange("(c p) f -> p c f", p=128)
            for k2 in range(4):
                b = P.buf(f"{name}{k2}", dma=True)
                P.dma("pool", t[:, 2 * k2:2 * k2 + 2, :], v[:, 2 * k2:2 * k2 + 2, :], dst=b)
                bs.append(b)
            return t, bs
        wq, wq_b = wload("wq", wq_d, 512)
        wk, wk_b = wload("wk", wk_d, 512)
        wv, wv_b = wload("wv", wv_d, 512)

        xnr = Rot([(P.sb(f"xn{i}", [128, 8, TT], BF16), P.buf(f"xn{i}", dma=True)) for i in range(2)])
        raw = P.sb("raw", [128, 4, TT], F32); raw_b = [P.buf(f"raw{c}") for c in range(4)]
        scr = P.sb("scr", [128, 4, TT], F32); scr_b = [P.buf(f"scr{c}") for c in range(4)]
        rs = P.sb("rs", [128, TT], F32); rs_b = P.buf("rs")
        QT = P.sb("QT", [128, 4, TT], BF16); QT_b = [P.buf(f"QT{c}") for c in range(4)]
        KT = P.sb("KT", [128, 4, SEQ], BF16); KT_b = [[P.buf(f"KT{c}_{t}") for t in range(NTI)] for c in range(4)]
        V = P.sb("V", [128, NKB, 512], BF16); V_b = [P.buf(f"V{kb}") for kb in range(NKB)]
        er = Rot([(P.sb(f"e{i}", [128, TT], F32), P.buf(f"e{i}")) for i in range(8)])
        wr = Rot([(P.sb(f"w{i}", [128, TT], F32), P.buf(f"w{i}")) for i in range(3)])
        spr = Rot([(P.sb(f"sp{i}", [128, TT], BF16), P.buf(f"sp{i}")) for i in range(5)])
        Ar = Rot([(P.sb(f"A{i}", [128, TT], BF16), P.buf(f"A{i}")) for i in range(4)])
        srun_rots = [Rot([(P.sb(f"srun{j}_{i}", [128, TT], BF16), P.buf(f"srun{j}_{i}")) for i in range(3)]) for j in range(2)]
        ones_bf = P.sb("ones_bf", [128, 128], BF16); ones_bf_b = P.buf("ones_bf")
        P.op("pool", lambda e: e.memset(ones_bf[:], 1.0), writes=[ones_bf_b])
        yr = [(P.sb(f"yo{i}", [64, TT], F32), P.buf(f"yo{i}"), P.newsem(f"st_yo{i}")) for i in range(4)]
        yri = [0]

        if env:
            xn_src = env.xn_src
        else:
            xn_v = xn_d.rearrange("(c p) t -> p c t", p=128)
            xn_src = lambda ti: xn_v[:, :, ti * TT:(ti + 1) * TT]
        outs = []

        def qk_proj(w, w_b, gain, gain_b, xn, xn_b, out_fn):
            for c in range(4):
                ps, psb = psZ.next()
                mm_group(P, ps[:], psb, [(w[:, k, c * 128:(c + 1) * 128], xn[:, k, :], [w_b[k // 2], xn_b]) for k in range(8)])
                P.op("act", lambda e, c=c, ps=ps: e.copy(raw[:, c, :], ps[:]), reads=[psb], writes=[raw_b[c]])
                P.op("act", lambda e, c=c: e.activation(out=scr[:, c, :], in_=raw[:, c, :], func=AF.Square),
                     reads=[raw_b[c]], writes=[scr_b[c]])
                ps2, ps2b = psS.next()
                mm_group(P, ps2[:], ps2b, [(oblk[:], scr[:, c, :], [scr_b[c], oblk_b])])
                P.op("act", lambda e, ps2=ps2: e.activation(out=rs[:], in_=ps2[:], func=AF.Sqrt, bias=EPS, scale=1.0 / 64.0),
                     reads=[ps2b], writes=[rs_b])
                P.op("dve", lambda e: e.reciprocal(rs[:], rs[:]), reads=[rs_b], writes=[rs_b])
                oap, ob = out_fn(c)
                P.op("dve", lambda e, c=c, oap=oap: e.scalar_tensor_tensor(
                    out=oap, in0=raw[:, c, :], scalar=gain[:, 0:1], in1=rs[:], op0=ALU.mult, op1=ALU.mult),
                    reads=[raw_b[c], gain_b, rs_b], writes=[ob])

        for ti in range(NTI):
            tsl = slice(ti * TT, (ti + 1) * TT)
            xn, xn_b = xnr.next()
            P.dma("pool", xn[:], xn_src(ti), dst=xn_b, extra=(env.xn_wait(ti) if env else ()))
            qk_proj(wq, wq_b, qg, qg_b, xn, xn_b, lambda c: (QT[:, c, :], QT_b[c]))
            qk_proj(wk, wk_b, kg, kg_b, xn, xn_b, lambda c, ti=ti, tsl=tsl: (KT[:, c, tsl], KT_b[c][ti]))
            for b in range(4):
                kb = 4 * ti + b
                ps, psb = psZ.next()
                mm_group(P, ps[:], psb, [(xn[:, k, b * 128:(b + 1) * 128], wv[:, k, :], [xn_b, wv_b[k // 2]]) for k in range(8)])
                P.op("act", lambda e, kb=kb, ps=ps: e.copy(V[:, kb, :], ps[:]), reads=[psb], writes=[V_b[kb]])
            nkb = 4 * ti + 4
            units = []
            for c in range(4):
                pys = [psY.next() for _ in range(2)]
                for step in range(nkb):
                    for j in range(2):
                        units.append(dict(c=c, j=j, step=step, kb=nkb - 1 - step, psl=slice(64 * j, 64 * j + 64),
                                          py=pys[j][0], pyb=pys[j][1]))
            srun_state = {}

            def stage(k, u):
                c, j, step, kb = u["c"], u["j"], u["step"], u["kb"]
                ksl = slice(kb * 128, (kb + 1) * 128)
                r = kb - 4 * ti
                if k == 0:
                    z, zb = psZ.next()
                    mm_group(P, z[:], zb, [(KT[u["psl"], c, ksl], QT[u["psl"], c, :], [KT_b[c][kb // 4], QT_b[c]])])
                    u["z"], u["zb"] = z, zb
                elif k == 1:
                    e_sb, e_b = er.next()
                    P.op("act", lambda e, e_sb=e_sb, z=u["z"]: e.activation(out=e_sb[:], in_=z[:], func=AF.Exp, scale=0.125),
                         reads=[u["zb"]], writes=[e_b])
                    if r >= 0:
                        P.op("dve", lambda e, e_sb=e_sb, r=r: e.tensor_tensor(out=e_sb[:], in0=e_sb[:], in1=maskd[:, r, :], op=ALU.mult),
                             reads=[e_b, maskd_b], writes=[e_b])
                    u["e"], u["eb"] = e_sb, e_b
                elif k == 2:
                    sp_sb, sp_b = spr.next()
                    P.op("act", lambda e, sp_sb=sp_sb, e_sb=u["e"]: e.activation(out=sp_sb[:], in_=e_sb[:], func=AF.Ln, bias=1.0, scale=1.0),
                         reads=[u["eb"]], writes=[sp_b])
                    u["sp"], u["spb"] = sp_sb, sp_b
                elif k == 3:
                    acc, accb = psAcc.next()
                    terms = [(ltri[:], u["sp"][:], [u["spb"], ltri_b])]
                    if step > 0:
                        srun, srunb = srun_state[(c, j)]
                        terms.append((ones_bf[:], srun[:], [srunb, ones_bf_b]))
                    mm_group(P, acc[:], accb, terms)
                    u["acc"], u["accb"] = acc, accb
                elif k == 4:
                    if kb > 0:
                        nsr, nsrb = srun_rots[j].next()
                        if step == 0:
                            P.op("dve", lambda e, nsr=nsr, sp_sb=u["sp"]: e.tensor_copy(nsr[:], sp_sb[:]),
                                 reads=[u["spb"]], writes=[nsrb])
                        else:
                            old, oldb = srun_state[(c, j)]
                            P.op("dve", lambda e, nsr=nsr, sp_sb=u["sp"], old=old: e.tensor_tensor(out=nsr[:], in0=old[:], in1=sp_sb[:], op=ALU.add),
                                 reads=[u["spb"], oldb], writes=[nsrb])
                        srun_state[(c, j)] = (nsr, nsrb)
                    w_sb, w_b = wr.next()
                    P.op("act", lambda e, w_sb=w_sb, acc=u["acc"]: e.activation(out=w_sb[:], in_=acc[:], func=AF.Exp, scale=-1.0),
                         reads=[u["accb"]], writes=[w_b])
                    u["w"], u["wb"] = w_sb, w_b
                elif k == 5:
                    A_sb, A_b = Ar.next()
                    P.op("dve", lambda e, A_sb=A_sb, e_sb=u["e"], w_sb=u["w"]: e.tensor_tensor(out=A_sb[:], in0=e_sb[:], in1=w_sb[:], op=ALU.mult),
                         reads=[u["eb"], u["wb"]], writes=[A_b])
                    u["A"], u["Ab"] = A_sb, A_b
                elif k == 6:
                    py = u["py"]
                    P.op("pe", lambda e, py=py, A_sb=u["A"], kb=kb, j=j, c=c, step=step: e.matmul(
                        py[:], lhsT=V[:, kb, c * 128 + 64 * j: c * 128 + 64 * j + 64], rhs=A_sb[:], start=(step == 0), stop=(kb == 0)),
                        reads=[u["Ab"], V_b[kb]], writes=[u["pyb"]])
                    if kb == 0:
                        row = c * 128 + 64 * j
                        if env:
                            outs.extend(env.emit_m(P, row, 64, ti, py[:], [u["pyb"]]))
                            return
                        yo, yo_b, yo_sem = yr[yri[0] % 4]
                        yri[0] += 1
                        P.op("act", lambda e, yo=yo, py=py: e.copy(yo[:], py[:]), reads=[u["pyb"]], writes=[yo_b])
                        if env:
                            pass
                        else:
                            outs.append(P.op("sp", lambda e, yo=yo, row=row, tsl=tsl: e.dma_start(out=y_o[row:row + 64, tsl], in_=yo[:]),
                                             reads=[yo_b], sem=yo_sem, inc=16))

            NS = 7
            SK = [0, 2, 3, 4, 6, 7, 8]
            for slot in range(len(units) + SK[-1]):
                for k in reversed(range(NS)):
                    ui = slot - SK[k]
                    if 0 <= ui < len(units):
                        stage(k, units[ui])
            if env:
                env.m_done(P, ti)
        if env:
            env.outs = outs
        else:
            P.finish(outs)
    return nc


CW = 31
HALO = 32


def build_conv(T=TOK, env=None):
    nc = env.nc if env else bass.Bass("TRN2", target_bir_lowering=False)
    NT = T // TT
    dr = env.dr if env else (lambda n, s, dt, kind: nc.dram_tensor(n, list(s), dt, kind=kind).ap())
    xT_d = dr("xT", [D, T], F32, "ExternalInput")
    xnh_d = dr("xnhT", [D, HALO + T], F32, "ExternalInput")
    flag_d = dr("flag", [128, 1], F32, "ExternalInput")
    pw1_d = dr("pw1_w", [D, 2 * D], F32, "ExternalInput")
    pw1b_d = dr("pw1_b", [128, 16], F32, "ExternalInput")
    dww_d = dr("dw_w", [128, CW * 8], F32, "ExternalInput")
    dwb_d = dr("dw_b", [128, 8], F32, "ExternalInput")
    lnw_d = dr("ln_w", [128, 8], F32, "ExternalInput")
    lnb_d = dr("ln_b", [128, 8], F32, "ExternalInput")
    pw2_d = dr("pw2_w", [D, D], F32, "ExternalInput")
    pw2b_d = dr("pw2_b", [128, 8], F32, "ExternalInput")
    x_o = dr("x_out", [D, T], F32, "ExternalOutput")
    with ExitStack() as st:
        if env:
            P = env.P
        else:
            P = Prog(nc, st)
            P.out_sem = P.newsem("outs")
        ones = P.sb("ones32", [128, 128], F32); ones_b = P.buf("ones")
        P.op("pool", lambda e: e.memset(ones[:], 1.0), writes=[ones_b])
        ident = P.sb("ident", [128, 128], F32); ident_b = P.buf("ident")
        P.op("pool", lambda e: e.memset(ident[:], 1.0), writes=[ident_b])
        P.op("pool", lambda e: e.affine_select(out=ident[:], in_=ident[:], pattern=[[-1, 128]],
                                               compare_op=ALU.is_equal, fill=0.0, base=0, channel_multiplier=1),
             reads=[ident_b], writes=[ident_b])
        psA = Rot([(P.ps(f"psA{i}", [128, 512]), P.buf(f"psA{i}")) for i in range(2)])
        psG = Rot([(P.ps(f"psG{i}", [128, 512]), P.buf(f"psG{i}")) for i in range(2)])
        psC = Rot([(P.ps(f"psC{i}", [128, 512]), P.buf(f"psC{i}")) for i in range(2)])
        psS = Rot([(P.ps(f"psS{i}", [128, 512]), P.buf(f"psS{i}")) for i in range(2)])

        def small(name, d_ap, shape):
            t = P.sb(name, shape, F32)
            b = P.buf(name, dma=True)
            P.dma("sp", t[:], d_ap, dst=b)
            return t, b
        flag, flag_b = small("flag", flag_d, [128, 1])
        pw1b, pw1b_b = small("pw1b", pw1b_d, [128, 16])
        dww, dww_b = small("dww", dww_d, [128, CW * 8])
        dwb, dwb_b = small("dwb", dwb_d, [128, 8])
        lnw, lnw_b = small("lnw", lnw_d, [128, 8])
        lnb, lnb_b = small("lnb", lnb_d, [128, 8])
        pw2b, pw2b_b = small("pw2b", pw2b_d, [128, 8])

        WB = P.sb("WB", [128, 32768], BF16)
        pw1 = WB[:, 0:16384].rearrange("p (a b) -> p a b", a=8)
        pw1_b = [P.buf(f"pw1_{k}", dma=True) for k in range(8)]
        pw1_v = pw1_d.rearrange("(c p) f -> p c f", p=128)
        for k in range(8):
            P.dma("pool", pw1[:, k, :], pw1_v[:, k, :], dst=pw1_b[k])
        pw2 = P.sb("pw2", [128, 8, D], BF16)
        pw2_b = [P.buf(f"pw2_{k}", dma=True) for k in range(4)]
        pw2_v = pw2_d.rearrange("(c p) f -> p c f", p=128)
        for k2 in range(4):
            P.dma("pool", pw2[:, 2 * k2:2 * k2 + 2, :], pw2_v[:, 2 * k2:2 * k2 + 2, :], dst=pw2_b[k2])

        h = P.sb("h", [128, 8, HALO + T], BF16)
        h_b = [[P.buf(f"h{c}_{t}") for t in range(NT + 1)] for c in range(8)]
        xnt = P.sb("xnt", [128, 8, HALO + TT], BF16); xnt_b = P.buf("xnt", dma=True)
        sgr = Rot([(P.sb(f"sgm{i}", [128, TT], F32), P.buf(f"sgm{i}")) for i in range(2)])
        if env:
            xn_halo = env.xn_halo
            xn_main = env.xn_main
        else:
            xnh_v = xnh_d.rearrange("(c p) t -> p c t", p=128)
            xn_halo = lambda: xnh_v[:, :, 0:HALO]
            xn_main = lambda t: xnh_v[:, :, HALO + t * TT:HALO + (t + 1) * TT]
        last_pw1 = None
        for t in range(NT):
            if t == 0:
                P.dma("pool", xnt[:, :, 0:HALO], xn_halo(), dst=xnt_b, extra=(env.xn_wait(NT - 1) if env else ()))
                P.dma("pool", xnt[:, :, HALO:HALO + TT], xn_main(0), dst=xnt_b)
                segs = [(0, HALO, 0), (HALO, TT, 1)]
            else:
                P.dma("pool", xnt[:, :, HALO:HALO + TT], xn_main(t), dst=xnt_b)
                segs = [(HALO, TT, t + 1)]
            for (off, n, hidx) in segs:
                col0 = 0 if hidx == 0 else HALO + (hidx - 1) * TT
                for c in range(8):
                    pa, pab = psA.next()
                    mm_group(P, pa[:, 0:n], pab, [(pw1[:, k, c * 128:(c + 1) * 128], xnt[:, k, off:off + n], [pw1_b[k], xnt_b]) for k in range(8)])
                    pg, pgb = psG.next()
                    last_pw1 = mm_group(P, pg[:, 0:n], pgb, [(pw1[:, k, D + c * 128:D + (c + 1) * 128], xnt[:, k, off:off + n], [pw1_b[k], xnt_b]) for k in range(8)])
                    sg_, sg_b = sgr.next()
                    P.op("act", lambda e, sg_=sg_, pg=pg, n=n, c=c: e.activation(out=sg_[:, 0:n], in_=pg[:, 0:n], func=AF.Sigmoid,
                                                                                bias=pw1b[:, 8 + c:9 + c], scale=1.0),
                         reads=[pgb, pw1b_b], writes=[sg_b])
                    P.op("dve", lambda e, sg_=sg_, pa=pa, n=n, c=c, col0=col0: e.scalar_tensor_tensor(
                        out=h[:, c, col0:col0 + n], in0=pa[:, 0:n], scalar=pw1b[:, c:c + 1], in1=sg_[:, 0:n], op0=ALU.add, op1=ALU.mult),
                        reads=[pab, sg_b, pw1b_b], writes=[h_b[c][hidx]])
                    if hidx == 0:
                        P.op("dve", lambda e, c=c: e.tensor_scalar(out=h[:, c, 0:HALO], in0=h[:, c, 0:HALO], scalar1=flag[:, 0:1], scalar2=None, op0=ALU.mult),
                             reads=[h_b[c][0], flag_b], writes=[h_b[c][0]])
        diag = WB[:, 0:CW * 8 * 128].rearrange("p (a b) -> p a b", a=CW * 8)
        diag_b = [P.buf(f"diag{c}") for c in range(8)]
        for c in range(8):
            diag_b[c].w = last_pw1
            ev = None
            for j in range(CW):
                eng = "dve"
                ev = P.op(eng, lambda e, c=c, j=j: e.tensor_scalar(out=diag[:, c * CW + j, :], in0=ident[:], scalar1=dww[:, j * 8 + c:j * 8 + c + 1], scalar2=None, op0=ALU.mult),
                          reads=[ident_b, dww_b], writes=[], extra=[last_pw1])
                diag_b[c].r.append(ev)
            diag_b[c].w = None
            diag_b[c].wlist = list(diag_b[c].r)
            diag_b[c].r = []
        cv = P.sb("cv", [128, 8, TT], F32); cv_b = [P.buf(f"cv{c}") for c in range(8)]
        scr = P.sb("scr", [128, 8, TT], F32); scr_b = [P.buf(f"scr{c}") for c in range(8)]
        u = P.sb("u", [128, 8, TT], BF16); u_b = [P.buf(f"u{c}") for c in range(8)]
        xt = P.sb("xt", [128, 8, TT], F32); xt_b = P.buf("xt", dma=True)
        xt_cb = [P.buf(f"xt{c}") for c in range(8)]
        st_sem = [P.newsem(f"st_x{c}") for c in range(8)]
        mu = P.sb("mu", [128, TT], F32); mu_b = P.buf("mu")
        rs = P.sb("rs", [128, TT], F32); rs_b = P.buf("rs")
        tmp = P.sb("tmp", [128, TT], F32); tmp_b = P.buf("tmp")
        xT_v = xT_d.rearrange("(c p) t -> p c t", p=128)
        x_ov = x_o.rearrange("(c p) t -> p c t", p=128)
        outs = []
        for t in range(NT):
            tsl = slice(t * TT, (t + 1) * TT)
            evl = P.op("sp", lambda e, tsl=tsl: e.dma_start(out=xt[:], in_=xT_v[:, :, tsl]), reads=[], writes=xt_cb, sem=P.semof(xt_b, "sp"), inc=16)
            for c in range(8):
                pc, pcb = psC.next()
                hb = [h_b[c][t], h_b[c][t + 1]]
                n = CW
                evm = None
                for j in range(CW):
                    evm = P.op("pe", lambda e, pc=pc, c=c, j=j, t=t: e.matmul(pc[:], lhsT=diag[:, c * CW + j, :], rhs=h[:, c, t * TT + 2 + j:t * TT + 2 + j + TT],
                                                                        start=(j == 0), stop=(j == CW - 1)),
                               reads=hb, writes=[pcb] if j == 0 else [], signal=(j == CW - 1), extra=diag_b[c].wlist)
                pcb.w = evm
                pcb.r = []
                P.op("act", lambda e, pc=pc, c=c: e.activation(out=cv[:, c, :], in_=pc[:], func=AF.Identity, bias=dwb[:, c:c + 1], scale=1.0),
                     reads=[pcb, dwb_b], writes=[cv_b[c]])
                P.op("act", lambda e, c=c: e.activation(out=scr[:, c, :], in_=cv[:, c, :], func=AF.Square), reads=[cv_b[c]], writes=[scr_b[c]])
            ps1, ps1b = psS.next()
            mm_group(P, ps1[:], ps1b, [(ones[:], cv[:, c, :], [cv_b[c], ones_b]) for c in range(8)])
            ps2, ps2b = psS.next()
            mm_group(P, ps2[:], ps2b, [(ones[:], scr[:, c, :], [scr_b[c], ones_b]) for c in range(8)])
            P.op("act", lambda e, ps1=ps1: e.activation(out=mu[:], in_=ps1[:], func=AF.Copy, scale=1.0 / D), reads=[ps1b], writes=[mu_b])
            P.op("dve", lambda e: e.tensor_tensor(out=tmp[:], in0=mu[:], in1=mu[:], op=ALU.mult), reads=[mu_b], writes=[tmp_b])
            P.op("dve", lambda e, ps2=ps2: e.scalar_tensor_tensor(out=rs[:], in0=ps2[:], scalar=1.0 / D, in1=tmp[:], op0=ALU.mult, op1=ALU.subtract),
                 reads=[ps2b, tmp_b], writes=[rs_b])
            P.op("act", lambda e: e.activation(out=rs[:], in_=rs[:], func=AF.Sqrt, bias=EPS, scale=1.0), reads=[rs_b], writes=[rs_b])
            P.op("dve", lambda e: e.reciprocal(rs[:], rs[:]), reads=[rs_b], writes=[rs_b])
            for c in range(8):
                P.op("dve", lambda e, c=c: e.tensor_tensor(out=cv[:, c, :], in0=cv[:, c, :], in1=mu[:], op=ALU.subtract), reads=[cv_b[c], mu_b], writes=[cv_b[c]])
                P.op("dve", lambda e, c=c: e.tensor_tensor(out=cv[:, c, :], in0=cv[:, c, :], in1=rs[:], op=ALU.mult), reads=[cv_b[c], rs_b], writes=[cv_b[c]])
                P.op("act", lambda e, c=c: e.activation(out=u[:, c, :], in_=cv[:, c, :], func=AF.Silu, bias=lnb[:, c:c + 1], scale=lnw[:, c:c + 1]),
                     reads=[cv_b[c], lnw_b, lnb_b], writes=[u_b[c]])
            for fo in range(8):
                po, pob = psA.next()
                mm_group(P, po[:], pob, [(pw2[:, k, fo * 128:(fo + 1) * 128], u[:, k, :], [pw2_b[k // 2], u_b[k]]) for k in range(8)])
                P.op("dve", lambda e, po=po, fo=fo: e.scalar_tensor_tensor(out=xt[:, fo, :], in0=po[:], scalar=pw2b[:, fo:fo + 1], in1=xt[:, fo, :], op0=ALU.add, op1=ALU.add),
                     reads=[pob, pw2b_b, xt_cb[fo]], writes=[xt_cb[fo]])
                outs.append(P.op("sp", lambda e, fo=fo, tsl=tsl: e.dma_start(out=x_ov[:, fo, tsl], in_=xt[:, fo, :]), reads=[xt_cb[fo]], sem=st_sem[fo], inc=16))
        if env:
            env.outs = outs
        else:
            P.finish(outs)
    return nc


def conv_inputs(xT, xnhT, flagv, pw1_w, pw1_b, dw_w, dw_b, ln_w, ln_b, pw2_w, pw2_b):
    f = lambda a: np.ascontiguousarray(a, dtype=np.float32)
    dww = np.asarray(dw_w, np.float32).reshape(CW, 8, 128).transpose(2, 0, 1).reshape(128, CW * 8)
    return {"xT": f(xT), "xnhT": f(xnhT), "flag": np.full((128, 1), flagv, np.float32),
            "pw1_w": f(pw1_w), "pw1_b": pcol(pw1_b), "dw_w": f(dww), "dw_b": pcol(dw_b),
            "ln_w": pcol(ln_w), "ln_b": pcol(ln_b), "pw2_w": f(pw2_w), "pw2_b": pcol(pw2_b)}


class Env:
    def __init__(self, nc, P, T):
        self.nc = nc
        self.P = P
        self.T = T
        self.io = {}
        self.outs = []
        self.flags_d = None
        self.mz2d = None
        self.fmy = 0
        self.m_pending = []
        self.m_evs = {}
        self.nocc = False
        self.ag_ev = {}
        self.rs_ev = {}

    def xn_wait(self, ti):
        nt = self.T // TT
        ev = self.ag_ev.get(ti % nt)
        return [ev] if ev is not None else []

    def m_wait(self, t):
        ev = self.rs_ev.get(t)
        return [ev] if ev is not None else []

    def dr(self, name, shape, dt, kind):
        return self.io.get(name)

    def phase_setup(self, P):
        self.flag = P.sb("flags", [128, 2], F32)
        self.flag_b = P.buf("flags", dma=True)
        P.dma("sp", self.flag[:], self.flags_d, dst=self.flag_b)
        self.stages = Rot([(P.sb(f"stg{i}", [128, TT], BF16), P.buf(f"stg{i}"), P.newsem(f"st_stg{i}")) for i in range(4)])

    def flagged_affine(self, P, gnw, gnw_b, gnb, gnb_b):
        self.gnwf = P.sb("gnwf", [128, 2, 8], F32); self.gnwf_b = P.buf("gnwf")
        self.gnbf = P.sb("gnbf", [128, 2, 8], F32); self.gnbf_b = P.buf("gnbf")
        for j in range(2):
            P.op("dve", lambda e, j=j: e.tensor_scalar(out=self.gnwf[:, j, :], in0=gnw[:], scalar1=self.flag[:, j:j + 1], scalar2=None, op0=ALU.mult),
                 reads=[gnw_b, self.flag_b], writes=[self.gnwf_b])
            P.op("dve", lambda e, j=j: e.tensor_scalar(out=self.gnbf[:, j, :], in0=gnb[:], scalar1=self.flag[:, j:j + 1], scalar2=None, op0=ALU.mult),
                 reads=[gnb_b, self.flag_b], writes=[self.gnbf_b])
        self.aff_tmp = Rot([(P.sb(f"afft{i}", [128, TT], F32), P.buf(f"afft{i}")) for i in range(3)])

    def emit_gn(self, P, c, ti, y_ap, y_bufs, sg_ap, sg_buf):
        nt = self.T // TT
        h, tl = ti // nt, ti % nt
        evs = []
        for j in range(2):
            tmp, tmp_b = self.aff_tmp.next()
            P.op("act", lambda e, tmp=tmp, j=j: e.activation(out=tmp[:], in_=y_ap, func=AF.Identity,
                                                            bias=self.gnbf[:, j, c:c + 1], scale=self.gnwf[:, j, c:c + 1]),
                 reads=list(y_bufs) + [self.gnwf_b, self.gnbf_b], writes=[tmp_b])
            stg, stg_b, stg_sem = self.stages.next()
            P.op("dve", lambda e, tmp=tmp, stg=stg: e.tensor_tensor(out=stg[:], in0=tmp[:], in1=sg_ap, op=ALU.mult),
                 reads=[tmp_b, sg_buf], writes=[stg_b])
            r0 = (h * 2 + j) * self.fmy + c * 128
            mz = self.mz2d[tl]
            evs.append(P.op("sp", lambda e, stg=stg, r0=r0, mz=mz: e.dma_start(out=mz[r0:r0 + 128, :], in_=stg[:]),
                            reads=[stg_b], sem=stg_sem, inc=16))
        self.m_pending.extend(evs)
        return evs

    def gather_tile(self, P, t, evs):
        if self.nocc:
            return
        self.ag_ev[t] = P.collective("AllGather", ALU.bypass, self.groups, self.xn_my[t], self.xn_full[t], extra=evs)

    def scatter_tile(self, P, tl, evs):
        if self.nocc:
            return
        self.rs_ev[tl] = P.collective("ReduceScatter", ALU.add, self.groups, self.mz2d[tl], self.mrs_out[tl], extra=evs)

    def m_done(self, P, ti):
        nt = self.T // TT
        h, tl = ti // nt, ti % nt
        self.m_evs.setdefault(tl, []).extend(self.m_pending)
        self.m_pending = []
        if h == 1:
            self.scatter_tile(P, tl, self.m_evs.pop(tl))

    def xn_src(self, ti):
        nt = self.T // TT
        rank, tl = ti // nt, ti % nt
        return self.xn_full[tl][rank * D:(rank + 1) * D, :].rearrange("(c p) t -> p c t", p=128)

    def xn_halo(self):
        nt = self.T // TT
        return self.xn_full[nt - 1][0:D, TT - HALO:TT].rearrange("(c p) t -> p c t", p=128)

    def xn_main(self, t):
        return self.xn_my[t].rearrange("(c p) t -> p c t", p=128)

    def xn_dst(self, t):
        return self.xn_my[t].rearrange("(c p) t -> p c t", p=128)

    def m_src(self, t):
        return self.mrs[t].rearrange("(c p) t -> p c t", p=128)

    def emit_m(self, P, row0, nrows, ti, src_ap, src_bufs):
        nt = self.T // TT
        h, tl = ti // nt, ti % nt
        evs = []
        for j in range(2):
            stg, stg_b, stg_sem = self.stages.next()
            P.op("act", lambda e, stg=stg, j=j: e.activation(out=stg[0:nrows, :], in_=src_ap, func=AF.Identity,
                                                            bias=0.0, scale=self.flag[0:nrows, j:j + 1]),
                 reads=list(src_bufs) + [self.flag_b], writes=[stg_b])
            r0 = (h * 2 + j) * self.fmy + row0
            mz = self.mz2d[tl]
            evs.append(P.op("sp", lambda e, stg=stg, r0=r0, mz=mz: e.dma_start(out=mz[r0:r0 + nrows, :], in_=stg[0:nrows, :]),
                            reads=[stg_b], sem=stg_sem, inc=16))
        self.m_pending.extend(evs)
        return evs


FUSED_INPUTS = None


def build_fused(T=TOK, groups=None):
    SEQ = 2 * T
    if groups is None:
        groups = [[2 * i, 2 * i + 1] for i in range(NCORES // 2)]
    nc = bass.Bass("TRN2", target_bir_lowering=False)
    ext = {}

    def inp(name, shape):
        ext[name] = nc.dram_tensor(name, list(shape), F32, kind="ExternalInput").ap()
        return ext[name]

    def internal(name, shape, dt):
        return nc.dram_tensor(name, list(shape), dt).ap()

    xT_in = inp("xT", [D, T])
    flags = inp("flags", [128, 2])
    flagc = inp("flagc", [128, 1])
    g_mix = [inp(f"g_mix{i}", [128, 8]) for i in range(4)]
    g_ffn = [inp(f"g_ffn{i}", [128, 8]) for i in range(4)]
    g_fin = inp("g_final", [128, 8])
    w1 = [inp(f"w1_{i}", [D, DFF]) for i in range(4)]
    w2 = [inp(f"w2_{i}", [DFF, D]) for i in range(4)]
    ret = []
    for j in range(2):
        ret.append(dict(wq=inp(f"r{j}_wq", [D, 512]), wk=inp(f"r{j}_wk", [D, 512]), wv=inp(f"r{j}_wv", [D, 1024]),
                        wg=inp(f"r{j}_wg", [D, 1024]), qg=inp(f"r{j}_qg", [128, 2]), kg=inp(f"r{j}_kg", [128, 2]),
                        gnw=inp(f"r{j}_gnw", [128, 8]), gnb=inp(f"r{j}_gnb", [128, 8]), w_out=inp(f"r{j}_wout", [2 * D, D])))
    rtab = dict(cos=inp("cos", [128, SEQ]), sin=inp("sin", [128, SEQ]), dt=inp("dt", [128, 2, 128]),
                qdec=inp("qdec", [128, 2, 512]), kdec=inp("kdec", [128, 2]), g128=inp("g128", [128, 2]))
    cv = dict(pw1_w=inp("pw1_w", [D, 2 * D]), pw1_b=inp("pw1_b", [128, 16]), dw_w=inp("dw_w", [128, CW * 8]),
              dw_b=inp("dw_b", [128, 8]), ln_w=inp("ln_w", [128, 8]), ln_b=inp("ln_b", [128, 8]),
              pw2_w=inp("pw2_w", [D, D]), pw2_b=inp("pw2_b", [128, 8]))
    sbw = dict(wq=inp("s_wq", [D, 512]), wk=inp("s_wk", [D, 512]), wv=inp("s_wv", [D, 512]),
               qg=inp("s_qg", [128, 1]), kg=inp("s_kg", [128, 1]), ltri=inp("ltri", [128, 128]), ustr=inp("ustr", [128, 128]),
               oblk=inp("oblk", [128, 128]), maskd=inp("maskd", [128, 4, 512]), w_out=inp("s_wout", [D, D]))
    out_d = nc.dram_tensor("out", [D, T], F32, kind="ExternalOutput").ap()
    xsp = internal("xsp", [D, T], F32)
    NT = T // TT
    xn_my = [internal(f"xn_my{t}", [D, TT], BF16) for t in range(NT)]
    xn_full = [internal(f"xn_full{t}", [2 * D, TT], BF16) for t in range(NT)]
    mz_ret = [internal(f"mz_ret{t}", [4 * 1024, TT], BF16) for t in range(NT)]
    mrs_ret = [internal(f"mrs_ret{t}", [2 * 1024, TT], BF16) for t in range(NT)]
    mz_sb = [internal(f"mz_sb{t}", [4 * 512, TT], BF16) for t in range(NT)]
    mrs_sb = [internal(f"mrs_sb{t}", [2 * 512, TT], BF16) for t in range(NT)]

    global FUSED_INPUTS
    FUSED_INPUTS = list(ext.keys())
    with ExitStack() as st:
        P = Prog(nc, st)
        P.out_sem = P.newsem("outs")
        P.enable_phases()
        env = Env(nc, P, T)
        env.flags_d = flags
        env.xn_full = xn_full
        env.xn_my = xn_my

        import os as _os
        nocc = bool(_os.environ.get("NOCC"))

        env.nocc = nocc
        env.groups = groups

        def gather():
            P.next_phase()

        def scatter(mz, mrs):
            P.next_phase()

        def tok_phase(i, x_src, m_src, w_out, fm, final=False):
            env.mrs = m_src
            env.io = {"xT": x_src, "mT": m_src, "w_out": w_out, "g_ffn": g_ffn[i], "w1": w1[i], "w2": w2[i],
                      "g_next": (g_fin if final else g_mix[i + 1]), "x_out": xsp, "xn_out": (out_d if final else xn_my)}
            build_tok("p2", T=T, fm=fm, env=env, final=final)

        def ret_phase(j):
            env.io = dict(xnT=None, **{k: ret[j][k] for k in ("wq", "wk", "wv", "wg", "qg", "kg", "gnw", "gnb")}, **rtab)
            env.mz2d, env.fmy, env.mrs_out = mz_ret, 1024, mrs_ret
            build_ret(SEQ=SEQ, env=env)
            scatter(mz_ret, mrs_ret)

        env.io = {"xT": xT_in, "g_next": g_mix[0], "xn_out": xn_my}
        build_tok("norm0", T=T, env=env, final=False)
        gather()
        ret_phase(0)
        tok_phase(0, xT_in, mrs_ret, ret[0]["w_out"], 2 * D)
        gather()
        env.io = dict(xT=xsp, x_out=xsp, flag=flagc, **cv)
        build_conv(T=T, env=env)
        P.next_phase()
        tok_phase(1, xsp, None, None, 0)
        gather()
        env.io = dict(xnT=None, **{k: sbw[k] for k in ("wq", "wk", "wv", "qg", "kg", "ltri", "ustr", "oblk", "maskd")})
        env.mz2d, env.fmy, env.mrs_out = mz_sb, 512, mrs_sb
        build_sb(SEQ=SEQ, env=env)
        scatter(mz_sb, mrs_sb)
        tok_phase(2, xsp, mrs_sb, sbw["w_out"], D)
        gather()
        ret_phase(1)
        tok_phase(3, xsp, mrs_ret, ret[1]["w_out"], 2 * D, final=True)
        P.next_phase()
        P.finish(env.outs)
    return nc


_PROGS = {}


def fused_inputs(T, x_b, rank, prm):
    A = lambda a: np.ascontiguousarray(a, dtype=np.float32)
    hp = rank
    m = {"xT": A(x_b[rank * T:(rank + 1) * T].T)}
    fl = np.zeros((128, 2), np.float32)
    fl[:, rank] = 1.0
    m["flags"] = fl
    m["flagc"] = np.full((128, 1), float(rank), np.float32)
    for i in range(4):
        m[f"g_mix{i}"] = pcol(prm["norm_mix"][i])
        m[f"g_ffn{i}"] = pcol(prm["norm_ffn"][i])
        m[f"w1_{i}"] = A(prm["ffn_w1"][i])
        m[f"w2_{i}"] = A(prm["ffn_w2"][i])
    m["g_final"] = pcol(prm["final_norm"])
    for j in range(2):
        w_in = np.asarray(prm["ret_w_in"][j], np.float32)
        m[f"r{j}_wq"] = A(w_in[:, hp * 512:(hp + 1) * 512])
        m[f"r{j}_wk"] = A(w_in[:, 1024 + hp * 512:1024 + (hp + 1) * 512])
        m[f"r{j}_wv"] = A(w_in[:, 2048 + hp * 1024:2048 + (hp + 1) * 1024])
        m[f"r{j}_wg"] = A(w_in[:, 4096 + hp * 1024:4096 + (hp + 1) * 1024])
        m[f"r{j}_qg"] = pcol(prm["ret_q_norm"][j])
        m[f"r{j}_kg"] = pcol(prm["ret_k_norm"][j])
        m[f"r{j}_gnw"] = pcol(np.asarray(prm["ret_gn_w"][j], np.float32)[hp * 1024:(hp + 1) * 1024])
        m[f"r{j}_gnb"] = pcol(np.asarray(prm["ret_gn_b"][j], np.float32)[hp * 1024:(hp + 1) * 1024])
        m[f"r{j}_wout"] = A(prm["ret_w_out"][j])
    tabs = ret_tables((2 * hp, 2 * hp + 1))
    m["cos"] = A(tabs["cos"][:, :2 * T]); m["sin"] = A(tabs["sin"][:, :2 * T])
    for k in ("dt", "qdec", "kdec", "g128"):
        m[k] = A(tabs[k])
    ci = conv_inputs(np.zeros((1, 1)), np.zeros((1, 1)), 0.0, prm["conv_pw1_w"][0], prm["conv_pw1_b"][0], prm["conv_dw_w"][0],
                     prm["conv_dw_b"][0], prm["conv_ln_w"][0], prm["conv_ln_b"][0], prm["conv_pw2_w"][0], prm["conv_pw2_b"][0])
    for k in ("pw1_w", "pw1_b", "dw_w", "dw_b", "ln_w", "ln_b", "pw2_w", "pw2_b"):
        m[k] = ci[k]
    sw = np.asarray(prm["sb_w_in"][0], np.float32)
    m["s_wq"] = A(sw[:, hp * 512:(hp + 1) * 512])
    m["s_wk"] = A(sw[:, 1024 + hp * 512:1024 + (hp + 1) * 512])
    m["s_wv"] = A(sw[:, 2048 + hp * 512:2048 + (hp + 1) * 512])
    m["s_qg"] = A(np.tile(np.asarray(prm["sb_q_norm"][0], np.float32), 2)[:, None])
    m["s_kg"] = A(np.tile(np.asarray(prm["sb_k_norm"][0], np.float32), 2)[:, None])
    m["s_wout"] = A(prm["sb_w_out"][0])
    for k, v in sb_tables().items():
        m[k] = A(v)
    return m


def kernel(**prm):
    x = np.asarray(prm["x"], np.float32)
    if "fused" not in _PROGS:
        _PROGS["fused"] = build_fused()
    nc = _PROGS["fused"]
    in_maps = [fused_inputs(TOK, x[c // 2], c % 2, prm) for c in range(NCORES)]
    res = run_bass_kernel_spmd(nc, in_maps, core_ids=list(range(NCORES)))
    out = np.empty((B, S, D), np.float32)
    for c in range(NCORES):
        out[c // 2, (c % 2) * TOK:(c % 2 + 1) * TOK] = res.results[c]["out"].T
    return out
```

```python
import math
from contextlib import ExitStack

import numpy as np
import ml_dtypes
import concourse.bass as bass
import concourse.mybir as mybir
from concourse.bass_utils import run_bass_kernel_spmd

F32 = mybir.dt.float32
BF16 = mybir.dt.bfloat16
AF = mybir.ActivationFunctionType
ALU = mybir.AluOpType

D = 1024
S = 4096
B = 4
DFF = 4096
EPS = 1e-6
NCORES = 8
TOK = 2048
TT = 512


class Ev:
    __slots__ = ("sem", "val")

    def __init__(self, sem, val):
        self.sem = sem
        self.val = val


class Buf:
    __slots__ = ("name", "w", "r", "sem", "wlist")

    def __init__(self, name, sem=None):
        self.name = name
        self.w = None
        self.r = []
        self.sem = sem


class Prog:
    ENGS = ("pe", "act", "dve", "pool", "sp")

    def __init__(self, nc, stack):
        self.nc = nc
        self.stack = stack
        self.q = {e: [] for e in self.ENGS}
        self.sems = {}
        self.cnt = {}
        self.waited = {}
        self.nsem = 0
        self.arena = None
        self.arena_off = 0
        self.psbanks = None
        self.ps_i = 0
        self.sempool = None
        self.sem_i = 0
        for e in self.ENGS:
            self.newsem("c_" + e)

    def newsem(self, name, kind="hw"):
        if self.sempool is not None:
            pool = self.sempool[kind]
            if self.sem_i[kind] < len(pool):
                nm = pool[self.sem_i[kind]]
            else:
                nm = f"q{kind}{len(pool)}"
                self.sems[nm] = self.stack.enter_context(self.nc.semaphore(nm))
                self.cnt[nm] = 0
                pool.append(nm)
            self.sem_i[kind] += 1
            return nm
        s = self.stack.enter_context(self.nc.semaphore(name))
        self.sems[name] = s
        self.cnt[name] = 0
        self.nsem += 1
        return name

    def buf(self, name, dma=False):
        return Buf(name, "LAZY" if dma else None)

    def semof(self, b, eng):
        if b.sem == "LAZY":
            b.sem = self.newsem("d_" + b.name, kind=("sw" if eng == "pool" else "hw"))
        return b.sem

    def enable_phases(self, arena_bytes=206 * 1024):
        self.arena = self.stack.enter_context(self.nc.sbuf_tensor("arena", [128, arena_bytes // 2], BF16))
        self.arena_bytes = arena_bytes
        self.psbanks = [self.stack.enter_context(self.nc.psum_tensor(f"pbank{i}", [128, 512], F32)) for i in range(8)]
        self.sempool = {"hw": [], "sw": []}
        self.sem_i = {"hw": 0, "sw": 0}

    def next_phase(self):
        for eng in self.ENGS:
            for name, c in self.cnt.items():
                if c > 0 and name != "cc":
                    self._wait(eng, Ev(name, c))
        import os as _os
        if _os.environ.get("ARENA_DEBUG"):
            print("phase arena max", getattr(self, "arena_max", 0), "off", self.arena_off)
        self.arena_off = 0
        self.ps_i = 0
        self.sem_i = {"hw": 0, "sw": 0}

    def sb(self, name, shape, dt):
        if self.arena is None:
            return self.stack.enter_context(self.nc.sbuf_tensor("s_" + name, list(shape), dt))
        shape = list(shape)
        n = 1
        for d in shape[1:]:
            n *= d
        esz = 4 if dt == F32 else 2
        nb = (n * esz + 31) // 32 * 32
        off = self.arena_off
        assert off + nb <= self.arena_bytes, f"arena overflow allocating {name}: {off}+{nb}"
        self.arena_off += nb
        self.arena_max = max(getattr(self, "arena_max", 0), self.arena_off)
        v = self.arena[0:shape[0], off // 2: off // 2 + n * esz // 2]
        if dt == F32:
            v = v.bitcast(F32)
        if len(shape) == 3:
            v = v.rearrange("p (a b) -> p a b", a=shape[1])
        elif len(shape) == 4:
            v = v.rearrange("p (a b c) -> p a b c", a=shape[1], b=shape[2])
        return v

    def ps(self, name, shape, dt=F32):
        if self.psbanks is None:
            return self.stack.enter_context(self.nc.psum_tensor("p_" + name, list(shape), dt))
        shape = list(shape)
        bank = self.psbanks[self.ps_i]
        self.ps_i += 1
        n = 1
        for d in shape[1:]:
            n *= d
        if dt == F32:
            v = bank[0:shape[0], 0:n]
        else:
            v = bank[0:shape[0], 0:n // 2].bitcast(dt)
        if len(shape) == 3:
            v = v.rearrange("p (a b) -> p a b", a=shape[1])
        return v

    def collective(self, kind, op, groups, in_ap, out_ap, extra=()):
        if "cc" not in self.sems:
            self.sems["cc"] = self.stack.enter_context(self.nc.semaphore("cc"))
            self.cnt["cc"] = 0
        return self.op("pool", lambda e: e.collective_compute(kind, op, replica_groups=groups, ins=[in_ap], outs=[out_ap]),
                       sem="cc", inc=1, extra=extra)

    def _wait(self, eng, ev):
        if ev is None:
            return
        if eng == "pe" and ev.sem == "c_pe":
            return
        key = (eng, ev.sem)
        if self.waited.get(key, 0) >= ev.val:
            return
        self.waited[key] = ev.val
        s = self.sems[ev.sem]
        v = ev.val
        self.q[eng].append(lambda e, s=s, v=v: e.wait_ge(s, v))

    def op(self, eng, fn, reads=(), writes=(), signal=True, sem=None, inc=1, extra=()):
        for b in reads:
            self._wait(eng, b.w)
        for b in writes:
            self._wait(eng, b.w)
            for ev in b.r:
                self._wait(eng, ev)
        for ev in extra:
            self._wait(eng, ev)
        name = sem or ("c_" + eng)
        if signal:
            self.cnt[name] += inc
            ev = Ev(name, self.cnt[name])
            s = self.sems[name]
            self.q[eng].append(lambda e, fn=fn, s=s, inc=inc: fn(e).then_inc(s, inc))
        else:
            ev = Ev(name, self.cnt[name] + inc)
            self.q[eng].append(lambda e, fn=fn: fn(e))
        for b in reads:
            b.r.append(ev)
        for b in writes:
            b.w = ev
            b.r = []
        return ev

    def dma(self, eng, out, in_, dst=None, src=None, extra=()):
        reads = [src] if src is not None else []
        writes = [dst] if dst is not None else []
        semname = self.semof(dst, eng) if (dst is not None and dst.sem) else None
        if semname is None:
            semname = self.out_sem
        return self.op(eng, lambda e: e.dma_start(out=out, in_=in_), reads, writes,
                       sem=semname, inc=16, extra=extra)

    def finish(self, final_evs):
        for ev in final_evs:
            self._wait("sp", ev)
        nc = self.nc
        with nc.Block() as block:
            def mk(name):
                def body(e):
                    for f in self.q[name]:
                        f(e)
                return body
            block.tensor(mk("pe"))
            block.scalar(mk("act"))
            block.vector(mk("dve"))
            block.gpsimd(mk("pool"))
            block.sync(mk("sp"))


def mm_group(P, out_ap, out_buf, terms):
    n = len(terms)
    ev = None
    for i, (l, r, rb) in enumerate(terms):
        ev = P.op("pe",
                  lambda e, l=l, r=r, i=i: e.matmul(out_ap, lhsT=l, rhs=r, start=(i == 0), stop=(i == n - 1)),
                  reads=rb, writes=[out_buf] if i == 0 else [], signal=(i == n - 1))
        if i > 0:
            pass
    out_buf.w = ev
    out_buf.r = []
    return ev


def pcol(v):
    v = np.asarray(v, dtype=np.float32)
    return np.ascontiguousarray(v.reshape(-1, 128).T)


class Rot:
    def __init__(self, items):
        self.items = items
        self.i = 0

    def next(self):
        it = self.items[self.i % len(self.items)]
        self.i += 1
        return it


class Ctx:
    pass


def setup_common(P, nc):
    C = Ctx()
    C.ones = P.sb("ones32", [128, 128], F32)
    C.ones_b = P.buf("ones")
    P.op("pool", lambda e: e.memset(C.ones[:], 1.0), writes=[C.ones_b])
    C.psA = Rot([(P.ps(f"psA{i}", [128, 512]), P.buf(f"psA{i}")) for i in range(3)])
    C.psB = Rot([(P.ps(f"psB{i}", [128, 512]), P.buf(f"psB{i}")) for i in range(3)])
    C.psS = Rot([(P.ps(f"psS{i}", [128, 512]), P.buf(f"psS{i}")) for i in range(2)])
    return C


def load_vec(P, name, dram_ap, nchunk):
    t = P.sb(name, [128, nchunk], F32)
    b = P.buf(name, dma=True)
    P.dma("sp", t[:], dram_ap, dst=b)
    return t, b


def rmsnorm_tile(P, C, x_sb, x_bufs, tsl, gain, gain_b, out_fn, scr, scr_b, rs, rs_b, nfeat=1024):
    nchunk = nfeat // 128
    for c in range(nchunk):
        P.op("act", lambda e, c=c: e.activation(out=scr[:, c, :], in_=x_sb[:, c, tsl], func=AF.Square),
             reads=[x_bufs[c]], writes=[scr_b[c]])
    ps, psb = C.psS.next()
    mm_group(P, ps[:], psb, [(C.ones[:], scr[:, c, :], [scr_b[c], C.ones_b]) for c in range(nchunk)])
    P.op("act", lambda e: e.activation(out=rs[:], in_=ps[:], func=AF.Sqrt, bias=EPS, scale=1.0 / nfeat),
         reads=[psb], writes=[rs_b])
    P.op("dve", lambda e: e.reciprocal(rs[:], rs[:]), reads=[rs_b], writes=[rs_b])
    for c in range(nchunk):
        oap, ob = out_fn(c)
        P.op("dve", lambda e, c=c, oap=oap: e.scalar_tensor_tensor(
            out=oap, in0=x_sb[:, c, tsl], scalar=gain[:, c:c + 1], in1=rs[:], op0=ALU.mult, op1=ALU.mult),
            reads=[x_bufs[c], gain_b, rs_b], writes=[ob])


def build_tok(mode, T=TOK, fm=0, env=None, final=True):
    nc = env.nc if env else bass.Bass("TRN2", target_bir_lowering=False)
    NT = T // TT
    dr = env.dr if env else (lambda n, s, dt, kind: nc.dram_tensor(n, list(s), dt, kind=kind).ap())
    xn_dt = F32 if (env is None or final) else BF16
    xT_d = dr("xT", [D, T], F32, "ExternalInput")
    gn_d = dr("g_next", [128, 8], F32, "ExternalInput")
    xn_o = dr("xn_out", [D, T], xn_dt, "ExternalOutput")
    if mode == "p2":
        if fm:
            mT_d = dr("mT", [fm, T], F32, "ExternalInput")
            wo_d = dr("w_out", [fm, D], F32, "ExternalInput")
        gf_d = dr("g_ffn", [128, 8], F32, "ExternalInput")
        w1_d = dr("w1", [D, DFF], F32, "ExternalInput")
        w2_d = dr("w2", [DFF, D], F32, "ExternalInput")
        x_o = dr("x_out", [D, T], F32, "ExternalOutput")
    with ExitStack() as st:
        if env:
            P = env.P
        else:
            P = Prog(nc, st)
            P.out_sem = P.newsem("outs")
        C = setup_common(P, nc)
        x_sb = P.sb("x_sb", [128, 8, T], F32)
        x_b = [[P.buf(f"x{c}_{t}") for t in range(NT)] for c in range(8)]
        xld = [P.buf(f"xld{t}", dma=True) for t in range(NT)]
        xT_v = xT_d.rearrange("(c p) t -> p c t", p=128)
        for t in range(NT):
            ev = P.dma("sp", x_sb[:, :, t * TT:(t + 1) * TT], xT_v[:, :, t * TT:(t + 1) * TT], dst=xld[t])
            for c in range(8):
                x_b[c][t].w = ev
        gnext, gnext_b = load_vec(P, "gnext", gn_d, 8)
        scr = P.sb("scr", [128, 8, TT], F32)
        scr_b = [P.buf(f"scr{c}") for c in range(8)]
        rs = P.sb("rs", [128, TT], F32)
        rs_b = P.buf("rs")
        outs = []
        xno = Rot([(P.sb(f"xno{i}", [128, 8, TT], xn_dt), [P.buf(f"xno{i}_{c}") for c in range(8)]) for i in range(2)])
        xn_ov = xn_o.rearrange("(c p) t -> p c t", p=128) if (env is None or final) else None
        xno_sems = [P.newsem(f"st_xno{i}") for i in range(2)]
        x_ov = x_o.rearrange("(c p) t -> p c t", p=128) if mode == "p2" else None
        tail_i = [0]

        def tile_tail(t):
            tsl = slice(t * TT, (t + 1) * TT)
            if mode == "p2" and (env is None or not final):
                for c in range(8):
                    outs.append(P.op("sp", lambda e, c=c, tsl=tsl: e.dma_start(out=x_ov[:, c, tsl], in_=x_sb[:, c, tsl]),
                                     reads=[x_b[c][t]], sem=P.out_sem, inc=16))
            o_sb, o_b = xno.next()
            osem = xno_sems[tail_i[0] % 2]
            tail_i[0] += 1
            rmsnorm_tile(P, C, x_sb, [x_b[c][t] for c in range(8)], tsl, gnext, gnext_b,
                         lambda c, o_sb=o_sb, o_b=o_b: (o_sb[:, c, :], o_b[c]), scr, scr_b, rs, rs_b)
            xn_dst = env.xn_dst(t) if (env and not final) else xn_ov[:, :, tsl]
            ev = P.op("sp", lambda e, o_sb=o_sb, xn_dst=xn_dst: e.dma_start(out=xn_dst, in_=o_sb[:]),
                      reads=o_b, sem=osem, inc=16)
            outs.append(ev)
            if env and not final:
                env.gather_tile(P, t, [ev])

        if mode == "p2":
            gffn, gffn_b = load_vec(P, "gffn", gf_d, 8)
            KM = fm // 128
            if fm:
              pass
            R1 = P.sb("R1", [128, 32768], BF16)

            def view(off, d0, d1):
                return R1[:, off:off + d0 * d1].rearrange("p (a b) -> p a b", a=d0)
            lastA = None
            if fm:
                wo_sb = view(0, KM, D)
                wo_b = [P.buf(f"wo{k}", dma=True) for k in range(KM // 4)]
                wo_v = wo_d.rearrange("(c p) f -> p c f", p=128)
                for k4 in range(KM // 4):
                    P.dma("pool", wo_sb[:, 4 * k4:4 * k4 + 4, :], wo_v[:, 4 * k4:4 * k4 + 4, :], dst=wo_b[k4])
                mr = Rot([(view(16384 + i * 8192, KM, TT), P.buf(f"mt{i}", dma=True)) for i in range(2)])
                if env:
                    m_src = env.m_src
                else:
                    mT_v = mT_d.rearrange("(c p) t -> p c t", p=128)
                    m_src = lambda t: mT_v[:, :, t * TT:(t + 1) * TT]
                for t in range(NT):
                    tsl = slice(t * TT, (t + 1) * TT)
                    m_t, m_tb = mr.next()
                    P.dma("pool", m_t, m_src(t), dst=m_tb, extra=(env.m_wait(t) if env else ()))
                    for fo in range(8):
                        ps, psb = C.psB.next()
                        lastA = mm_group(P, ps[:], psb,
                                         [(wo_sb[:, k, fo * 128:(fo + 1) * 128], m_t[:, k, :], [wo_b[k // 4], m_tb])
                                          for k in range(KM)])
                        P.op("dve", lambda e, ps=ps, fo=fo, tsl=tsl: e.tensor_tensor(
                            out=x_sb[:, fo, tsl], in0=x_sb[:, fo, tsl], in1=ps[:], op=ALU.add),
                            reads=[psb, x_b[fo][t]], writes=[x_b[fo][t]])
            xn_sb = view(0, 8, T)
            xn_b = [[P.buf(f"xn{c}_{t}") for t in range(NT)] for c in range(8)]
            for c in range(8):
                for t in range(NT):
                    xn_b[c][t].w = lastA
            for t in range(NT):
                tsl = slice(t * TT, (t + 1) * TT)
                rmsnorm_tile(P, C, x_sb, [x_b[c][t] for c in range(8)], tsl, gffn, gffn_b,
                             lambda c, t=t, tsl=tsl: (xn_sb[:, c, tsl], xn_b[c][t]), scr, scr_b, rs, rs_b)
            NG = DFF // 512
            w1r = Rot([(view(16384 + i * 4096, 8, 512), P.buf(f"w1g{i}", dma=True)) for i in range(2)])
            w2r = Rot([(view(24576 + i * 4096, 4, D), P.buf(f"w2g{i}", dma=True)) for i in range(2)])
            for (_, wb_) in w1r.items + w2r.items:
                wb_.w = lastA
            hr = Rot([(P.sb(f"h{i}", [128, 4, TT], BF16), [P.buf(f"h{i}_{j}") for j in range(4)]) for i in range(2)])
            sqr = Rot([(P.sb(f"sq{i}", [128, TT], F32), P.buf(f"sq{i}")) for i in range(2)])
            w1_v = w1_d.rearrange("(c p) f -> p c f", p=128)
            w2_v = w2_d.rearrange("(c p) f -> p c f", p=128)
            for g in range(NG):
                w1g, w1b = w1r.next()
                w2g, w2b = w2r.next()
                P.dma("pool", w1g, w1_v[:, :, g * 512:(g + 1) * 512], dst=w1b)
                P.dma("pool", w2g, w2_v[:, 4 * g:4 * g + 4, :], dst=w2b)
                for t in range(NT):
                    tsl = slice(t * TT, (t + 1) * TT)
                    h_sb, h_b = hr.next()
                    for j in range(4):
                        ps, psb = C.psA.next()
                        mm_group(P, ps[:], psb,
                                 [(w1g[:, k, j * 128:(j + 1) * 128], xn_sb[:, k, tsl], [w1b, xn_b[k][t]])
                                  for k in range(8)])
                        sq, sqb = sqr.next()
                        P.op("act", lambda e, sq=sq, ps=ps: e.activation(out=sq[:], in_=ps[:], func=AF.Square),
                             reads=[psb], writes=[sqb])
                        P.op("dve", lambda e, sq=sq, ps=ps, h_sb=h_sb, j=j: e.scalar_tensor_tensor(
                            out=h_sb[:, j, :], in0=ps[:], scalar=0.0, in1=sq[:], op0=ALU.is_gt, op1=ALU.mult),
                            reads=[psb, sqb], writes=[h_b[j]])
                    for fo in range(8):
                        ps, psb = C.psB.next()
                        mm_group(P, ps[:], psb,
                                 [(w2g[:, j, fo * 128:(fo + 1) * 128], h_sb[:, j, :], [w2b, h_b[j]])
                                  for j in range(4)])
                        P.op("dve", lambda e, ps=ps, fo=fo, tsl=tsl: e.tensor_tensor(
                            out=x_sb[:, fo, tsl], in0=x_sb[:, fo, tsl], in1=ps[:], op=ALU.add),
                            reads=[psb, x_b[fo][t]], writes=[x_b[fo][t]])
                    if g == NG - 1:
                        tile_tail(t)
        if mode != "p2":
            for t in range(NT):
                tile_tail(t)
        if env:
            env.outs = outs
        else:
            P.finish(outs)
    return nc


RET_G = [1.0 - 2.0 ** (-5.0 - h) for h in range(4)]


def ret_tables(heads):
    half = 128
    inv_freq = (10000.0 ** (-np.arange(half, dtype=np.float32) / np.float32(half))).astype(np.float32)
    ang = (np.arange(S, dtype=np.float32)[None, :] * inv_freq[:, None]).astype(np.float32)
    cos = np.cos(ang.astype(np.float64)).astype(np.float32)
    sin = np.sin(ang.astype(np.float64)).astype(np.float32)
    idx = np.arange(128)
    dt = np.zeros((128, 2, 128), np.float32)
    qdec = np.zeros((128, 2, 512), np.float32)
    kdec = np.zeros((128, 2), np.float32)
    g128 = np.zeros((128, 2), np.float32)
    for i, h in enumerate(heads):
        lg = math.log(RET_G[h])
        t = idx[None, :]
        s = idx[:, None]
        same = (t // 64) == (s // 64)
        later = (t // 64) > (s // 64)
        dmat = np.where(same, np.exp(lg * np.abs(t - s)), np.where(later, np.exp(lg * (t - s)), 0.0))
        dt[:, i, :] = dmat.astype(np.float32)
        qdec[:, i, :] = np.tile(np.exp(lg * (idx + 1.0)), 4)[None, :]
        kdec[:, i] = np.exp(lg * (127.0 - idx))
        g128[:, i] = math.exp(lg * 128.0)
    return dict(cos=cos, sin=sin, dt=dt, qdec=qdec, kdec=kdec, g128=g128)


def build_ret(SEQ=S, env=None):
    nc = env.nc if env else bass.Bass("TRN2", target_bir_lowering=False)
    NTI = SEQ // TT
    dr = env.dr if env else (lambda n, s, dt, kind: nc.dram_tensor(n, list(s), dt, kind=kind).ap())
    xn_d = dr("xnT", [D, SEQ], F32, "ExternalInput")
    wq_d = dr("wq", [D, 512], F32, "ExternalInput")
    wk_d = dr("wk", [D, 512], F32, "ExternalInput")
    wv_d = dr("wv", [D, 1024], F32, "ExternalInput")
    wg_d = dr("wg", [D, 1024], F32, "ExternalInput")
    qg_d = dr("qg", [128, 2], F32, "ExternalInput")
    kg_d = dr("kg", [128, 2], F32, "ExternalInput")
    gnw_d = dr("gnw", [128, 8], F32, "ExternalInput")
    gnb_d = dr("gnb", [128, 8], F32, "ExternalInput")
    cos_d = dr("cos", [128, SEQ], F32, "ExternalInput")
    sin_d = dr("sin", [128, SEQ], F32, "ExternalInput")
    dt_d = dr("dt", [128, 2, 128], F32, "ExternalInput")
    qdec_d = dr("qdec", [128, 2, 512], F32, "ExternalInput")
    kdec_d = dr("kdec", [128, 2], F32, "ExternalInput")
    g128_d = dr("g128", [128, 2], F32, "ExternalInput")
    m_o = dr("mT_out", [D, SEQ], F32, "ExternalOutput")
    with ExitStack() as st:
        if env:
            P = env.P
            env.phase_setup(P)
        else:
            P = Prog(nc, st)
            P.out_sem = P.newsem("outs")
        ones = P.sb("ones32", [128, 128], F32)
        ones_b = P.buf("ones")
        P.op("pool", lambda e: e.memset(ones[:], 1.0), writes=[ones_b])
        ident = P.sb("ident", [128, 128], BF16)
        ident_b = P.buf("ident")
        P.op("pool", lambda e: e.memset(ident[:], 1.0), writes=[ident_b])
        P.op("pool", lambda e: e.affine_select(out=ident[:], in_=ident[:], pattern=[[-1, 128]],
                                               compare_op=ALU.is_equal, fill=0.0, base=0, channel_multiplier=1),
             reads=[ident_b], writes=[ident_b])
        psA = Rot([(P.ps(f"psA{i}", [128, 512]), P.buf(f"psA{i}")) for i in range(2)])
        psS = Rot([(P.ps(f"psS{i}", [128, 512]), P.buf(f"psS{i}")) for i in range(1)])
        psT = Rot([(P.ps(f"psT{i}", [128, 4, 128], BF16), P.buf(f"psT{i}")) for i in range(1)])
        psSc = Rot([(P.ps(f"psSc{i}", [128, 128]), P.buf(f"psSc{i}")) for i in range(1)])
        psY = Rot([(P.ps(f"psY{i}", [128, 4, 128]), P.buf(f"psY{i}")) for i in range(1)])
        psSt = Rot([(P.ps(f"psSt{i}", [128, 512]), P.buf(f"psSt{i}")) for i in range(2)])

        def small(name, d_ap, shape):
            t = P.sb(name, shape, F32)
            b = P.buf(name, dma=True)
            P.dma("sp", t[:], d_ap, dst=b)
            return t, b
        qg, qg_b = small("qg", qg_d, [128, 2])
        kg, kg_b = small("kg", kg_d, [128, 2])
        gnw, gnw_b = small("gnw", gnw_d, [128, 8])
        gnb, gnb_b = small("gnb", gnb_d, [128, 8])
        if env:
            env.flagged_affine(P, gnw, gnw_b, gnb, gnb_b)
        dtt, dtt_b = small("dtt", dt_d, [128, 2, 128])
        qdec, qdec_b = small("qdec", qdec_d, [128, 2, 512])
        kdec, kdec_b = small("kdec", kdec_d, [128, 2])
        g128, g128_b = small("g128", g128_d, [128, 2])

        def wload(name, d_ap, ncol):
            t = P.sb(name, [128, 8, ncol], BF16)
            bs = []
            v = d_ap.rearrange("(c p) f -> p c f", p=128)
            for k2 in range(4):
                b = P.buf(f"{name}{k2}", dma=True)
                P.dma("pool", t[:, 2 * k2:2 * k2 + 2, :], v[:, 2 * k2:2 * k2 + 2, :], dst=b)
                bs.append(b)
            return t, bs
        wq, wq_b = wload("wq", wq_d, 512)
        wk, wk_b = wload("wk", wk_d, 512)
        wv, wv_b = wload("wv", wv_d, 1024)
        wg, wg_b = wload("wg", wg_d, 1024)

        xnr = Rot([(P.sb(f"xn{i}", [128, 8, TT], BF16), P.buf(f"xn{i}", dma=True)) for i in range(2)])
        csr = Rot([(P.sb(f"cs{i}", [128, 2, TT], F32), P.buf(f"cs{i}", dma=True)) for i in range(2)])
        rawq = P.sb("rawq", [128, 4, TT], F32); rawq_b = [P.buf(f"rawq{c}") for c in range(4)]
        rawk = P.sb("rawk", [128, 4, TT], F32); rawk_b = [P.buf(f"rawk{c}") for c in range(4)]
        rsq = P.sb("rsq", [128, 2, TT], F32); rsq_b = [P.buf(f"rsq{h}") for h in range(2)]
        rsk = P.sb("rsk", [128, 2, TT], F32); rsk_b = [P.buf(f"rsk{h}") for h in range(2)]
        raw, raw_b = rawq, rawq_b
        rs2, rs2_b = rsq, rsq_b
        mu2, mu2_b = rsk, rsk_b
        scr = P.sb("scr", [128, 4, TT], F32)
        scr_b = [P.buf(f"scr{c}") for c in range(4)]
        scrq, scrq_b = scr, scr_b
        tmpA = P.sb("tmpA", [128, TT], F32); tmpA_b = P.buf("tmpA")
        tmpB = P.sb("tmpB", [128, TT], F32); tmpB_b = P.buf("tmpB")
        def mkset(i):
            d = Ctx()
            d.QT = P.sb(f"QT{i}", [128, 4, TT], BF16); d.QT_b = [P.buf(f"QT{i}_{c}") for c in range(4)]
            d.QdT = P.sb(f"QdT{i}", [128, 4, TT], BF16); d.QdT_b = [P.buf(f"QdT{i}_{c}") for c in range(4)]
            d.KT = P.sb(f"KT{i}", [128, 4, TT], BF16); d.KT_b = [P.buf(f"KT{i}_{c}") for c in range(4)]
            d.Kd = P.sb(f"Kd{i}", [128, 4, 4, 128], BF16); d.Kd_b = [[P.buf(f"Kd{i}_{b}_{h}") for h in range(2)] for b in range(4)]
            d.Vt = P.sb(f"Vt{i}", [128, 4, 2, 512], BF16); d.Vt_b = [[P.buf(f"Vt{i}_{b}_{h}") for h in range(2)] for b in range(4)]
            return d
        sets = [mkset(0), mkset(1)]
        sg = P.sb("sg", [128, 8, TT], BF16); sg_b = [P.buf(f"sg{c}") for c in range(8)]
        y32 = P.sb("y32", [128, 2, 4, TT], F32)
        y_b = [[[P.buf(f"y{h}_{ec}_{b}") for b in range(4)] for ec in range(4)] for h in range(2)]
        PT = P.sb("PT", [128, 2, 128], BF16); PT_b = [P.buf(f"PT{h}") for h in range(2)]
        S32 = P.sb("S32", [128, 2, 2, 512], F32)
        Sbf = P.sb("Sbf", [128, 2, 2, 512], BF16)
        S_b = [[P.buf(f"S32_{h}_{d}") for d in range(2)] for h in range(2)]
        Sbf_b = [[P.buf(f"Sbf_{h}_{d}") for d in range(2)] for h in range(2)]
        for h in range(2):
            for d in range(2):
                P.op("pool", lambda e, h=h, d=d: e.memset(S32[:, h, d, :], 0.0), writes=[S_b[h][d]])
                P.op("pool", lambda e, h=h, d=d: e.memset(Sbf[:, h, d, :], 0.0), writes=[Sbf_b[h][d]])

        if env:
            xn_src = env.xn_src
        else:
            xn_v = xn_d.rearrange("(c p) t -> p c t", p=128)
            xn_src = lambda ti: xn_v[:, :, ti * TT:(ti + 1) * TT]
            m_ov = m_o.rearrange("(c p) t -> p c t", p=128)
            st_sem = [[P.newsem(f"st_y{h}_{ec}") for ec in range(4)] for h in range(2)]
        outs = []

        def qk_path(w, w_b, gain, gain_b, xn, xn_b, cs, cs_b, is_k, bs, part):
            QT, QT_b, QdT, QdT_b, KT, KT_b = bs.QT, bs.QT_b, bs.QdT, bs.QdT_b, bs.KT, bs.KT_b
            OT, OT_b = (KT, KT_b) if is_k else (QT, QT_b)
            raw, raw_b, scr, scr_b, rs2, rs2_b = (rawk, rawk_b, scrq, scrq_b, rsk, rsk_b) if is_k else (rawq, rawq_b, scrq, scrq_b, rsq, rsq_b)
            if part == "A":
                qk_A(w, w_b, xn, xn_b, is_k, raw, raw_b, scr, scr_b, rs2, rs2_b)
                return
            qk_B(gain, gain_b, cs, cs_b, is_k, raw, raw_b, rs2, rs2_b, OT, OT_b, QT, QT_b, QdT, QdT_b)

        def qk_A(w, w_b, xn, xn_b, is_k, raw, raw_b, scr, scr_b, rs2, rs2_b):
            for c in range(4):
                ps, psb = psA.next()
                mm_group(P, ps[:], psb, [(w[:, k, c * 128:(c + 1) * 128], xn[:, k, :], [w_b[k // 2], xn_b]) for k in range(8)])
                P.op("act", lambda e, c=c, ps=ps: e.copy(raw[:, c, :], ps[:]), reads=[psb], writes=[raw_b[c]])
                P.op("act", lambda e, c=c: e.activation(out=scr[:, c, :], in_=raw[:, c, :], func=AF.Square),
                     reads=[raw_b[c]], writes=[scr_b[c]])
            for h in range(2):
                ps, psb = psSt.next()
                mm_group(P, ps[:], psb, [(ones[:], scr[:, 2 * h + dc, :], [scr_b[2 * h + dc], ones_b]) for dc in range(2)])
                if is_k:
                    P.op("act", lambda e, ps=ps, h=h: e.activation(out=rs2[:, h, :], in_=ps[:], func=AF.Sqrt, bias=256.0 * EPS, scale=1.0),
                         reads=[psb], writes=[rs2_b[h]])
                else:
                    P.op("act", lambda e, ps=ps, h=h: e.activation(out=rs2[:, h, :], in_=ps[:], func=AF.Sqrt, bias=EPS, scale=1.0 / 256.0),
                         reads=[psb], writes=[rs2_b[h]])

        def qk_B(gain, gain_b, cs, cs_b, is_k, raw, raw_b, rs2, rs2_b, OT, OT_b, QT, QT_b, QdT, QdT_b):
            for h in range(2):
                P.op("dve", lambda e, h=h: e.reciprocal(rs2[:, h, :], rs2[:, h, :]), reads=[rs2_b[h]], writes=[rs2_b[h]])
                for dc in range(2):
                    c = 2 * h + dc
                    P.op("dve", lambda e, c=c, dc=dc, h=h: e.scalar_tensor_tensor(
                        out=raw[:, c, :], in0=raw[:, c, :], scalar=gain[:, dc:dc + 1], in1=rs2[:, h, :], op0=ALU.mult, op1=ALU.mult),
                        reads=[raw_b[c], gain_b, rs2_b[h]], writes=[raw_b[c]])
                c1, c2 = 2 * h, 2 * h + 1
                P.op("dve", lambda e, c1=c1: e.tensor_tensor(out=tmpA[:], in0=raw[:, c1, :], in1=cs[:, 0, :], op=ALU.mult),
                     reads=[raw_b[c1], cs_b], writes=[tmpA_b])
                P.op("dve", lambda e, c2=c2: e.tensor_tensor(out=tmpB[:], in0=raw[:, c2, :], in1=cs[:, 1, :], op=ALU.mult),
                     reads=[raw_b[c2], cs_b], writes=[tmpB_b])
                P.op("dve", lambda e, c1=c1: e.tensor_tensor(out=OT[:, c1, :], in0=tmpA[:], in1=tmpB[:], op=ALU.subtract),
                     reads=[tmpA_b, tmpB_b], writes=[OT_b[c1]])
                P.op("dve", lambda e, c1=c1: e.tensor_tensor(out=tmpA[:], in0=raw[:, c1, :], in1=cs[:, 1, :], op=ALU.mult),
                     reads=[raw_b[c1], cs_b], writes=[tmpA_b])
                P.op("dve", lambda e, c2=c2: e.tensor_tensor(out=tmpB[:], in0=raw[:, c2, :], in1=cs[:, 0, :], op=ALU.mult),
                     reads=[raw_b[c2], cs_b], writes=[tmpB_b])
                P.op("dve", lambda e, c2=c2: e.tensor_tensor(out=OT[:, c2, :], in0=tmpA[:], in1=tmpB[:], op=ALU.add),
                     reads=[tmpA_b, tmpB_b], writes=[OT_b[c2]])
                if not is_k:
                    for c in (c1, c2):
                        P.op("dve", lambda e, c=c, h=h: e.tensor_tensor(out=QdT[:, c, :], in0=QT[:, c, :], in1=qdec[:, h, :], op=ALU.mult),
                             reads=[QT_b[c], qdec_b], writes=[QdT_b[c]])

        tiles = {}

        def load(ti):
            tsl = slice(ti * TT, (ti + 1) * TT)
            xn, xn_b = xnr.next()
            P.dma("pool", xn[:], xn_src(ti), dst=xn_b, extra=(env.xn_wait(ti) if env else ()))
            cs, cs_b = csr.next()
            P.dma("sp", cs[:, 0, :], cos_d[:, tsl], dst=cs_b)
            P.dma("sp", cs[:, 1, :], sin_d[:, tsl], dst=cs_b)
            tiles[ti] = dict(xn=xn, xn_b=xn_b, cs=cs, cs_b=cs_b, bs=sets[ti % 2])

        def qpath(ti, part):
            t = tiles[ti]
            qk_path(wq, wq_b, qg, qg_b, t["xn"], t["xn_b"], t["cs"], t["cs_b"], False, t["bs"], part)

        def kpath(ti, part):
            t = tiles[ti]
            bs = t["bs"]
            qk_path(wk, wk_b, kg, kg_b, t["xn"], t["xn_b"], t["cs"], t["cs_b"], True, bs, part)
            if part == "A":
                return
            for b in range(4):
                bsl = slice(b * 128, (b + 1) * 128)
                pt, ptb = psT.next()
                ev = None
                for c in range(4):
                    ev = P.op("pe", lambda e, c=c, pt=pt, bsl=bsl, bs=bs: e.transpose(pt[:, c, :], bs.KT[:, c, bsl], ident[:]),
                              reads=[bs.KT_b[c], ident_b], writes=[ptb] if c == 0 else [], signal=(c == 3))
                ptb.w = ev
                ptb.r = []
                for h in range(2):
                    P.op("dve", lambda e, b=b, h=h, pt=pt, bs=bs: e.tensor_scalar(
                        out=bs.Kd[:, b, 2 * h:2 * h + 2, :], in0=pt[:, 2 * h:2 * h + 2, :], scalar1=kdec[:, h:h + 1], scalar2=None, op0=ALU.mult),
                        reads=[ptb, kdec_b], writes=[bs.Kd_b[b][h]])

        def vproj(ti):
            t = tiles[ti]
            xn, xn_b, bs = t["xn"], t["xn_b"], t["bs"]
            for b in range(4):
                for h in range(2):
                    ps, psb = psA.next()
                    mm_group(P, ps[:], psb, [(xn[:, k, b * 128:(b + 1) * 128], wv[:, k, h * 512:(h + 1) * 512], [xn_b, wv_b[k // 2]]) for k in range(8)])
                    P.op("act", lambda e, b=b, h=h, ps=ps, bs=bs: e.copy(bs.Vt[:, b, h, :], ps[:]), reads=[psb], writes=[bs.Vt_b[b][h]])

        def gproj(ti):
            t = tiles[ti]
            xn, xn_b = t["xn"], t["xn_b"]
            for c in range(8):
                ps, psb = psA.next()
                mm_group(P, ps[:], psb, [(wg[:, k, c * 128:(c + 1) * 128], xn[:, k, :], [wg_b[k // 2], xn_b]) for k in range(8)])
                P.op("act", lambda e, c=c, ps=ps: e.activation(out=sg[:, c, :], in_=ps[:], func=AF.Silu), reads=[psb], writes=[sg_b[c]])

        def block(ti, b):
            bs = tiles[ti]["bs"]
            KT, KT_b, QT, QT_b, QdT, QdT_b, Kd, Kd_b, Vt, Vt_b = bs.KT, bs.KT_b, bs.QT, bs.QT_b, bs.QdT, bs.QdT_b, bs.Kd, bs.Kd_b, bs.Vt, bs.Vt_b
            bsl = slice(b * 128, (b + 1) * 128)
            for h in range(2):
                sc, scb = psSc.next()
                mm_group(P, sc[:], scb, [(KT[:, 2 * h + dc, bsl], QT[:, 2 * h + dc, bsl], [KT_b[2 * h + dc], QT_b[2 * h + dc]]) for dc in range(2)])
                P.op("dve", lambda e, h=h, sc=sc: e.tensor_tensor(out=PT[:, h, :], in0=sc[:], in1=dtt[:, h, :], op=ALU.mult),
                     reads=[scb, dtt_b], writes=[PT_b[h]])
                py, pyb = psY.next()
                first = True
                ev = None
                for ec in range(4):
                    esl = slice(ec * 128, (ec + 1) * 128)
                    terms = [(Vt[:, b, h, esl], PT[:, h, :], [Vt_b[b][h], PT_b[h]])]
                    terms += [(Sbf[:, h, dc, esl], QdT[:, 2 * h + dc, bsl], [Sbf_b[h][dc], QdT_b[2 * h + dc]]) for dc in range(2)]
                    for i, (l, r, rb) in enumerate(terms):
                        ev = P.op("pe", lambda e, l=l, r=r, i=i, ec=ec, py=py: e.matmul(py[:, ec, :], lhsT=l, rhs=r, start=(i == 0), stop=(i == 2)),
                                  reads=rb, writes=[pyb] if first else [], signal=(ec == 3 and i == 2))
                        first = False
                pyb.w = ev
                pyb.r = []
                P.op("act", lambda e, h=h, bsl=bsl, py=py: e.copy(y32[:, h, :, bsl], py[:]),
                     reads=[pyb], writes=[y_b[h][ec][b] for ec in range(4)])
                for dc in range(2):
                    pst, pstb = psSt.next()
                    mm_group(P, pst[:], pstb, [(Kd[:, b, 2 * h + dc, :], Vt[:, b, h, :], [Kd_b[b][h], Vt_b[b][h]])])
                    P.op("dve", lambda e, h=h, dc=dc, pst=pst: e.scalar_tensor_tensor(
                        out=S32[:, h, dc, :], in0=S32[:, h, dc, :], scalar=g128[:, h:h + 1], in1=pst[:], op0=ALU.mult, op1=ALU.add),
                        reads=[pstb, g128_b, S_b[h][dc]], writes=[S_b[h][dc]])
                    P.op("act", lambda e, h=h, dc=dc: e.copy(Sbf[:, h, dc, :], S32[:, h, dc, :]),
                         reads=[S_b[h][dc]], writes=[Sbf_b[h][dc]])

        def gn_norm(ti):
            sq = [(scr, scr_b), (rawk, rawk_b)]
            for h in range(2):
                for ec in range(4):
                    P.op("act", lambda e, h=h, ec=ec: e.activation(out=sq[h][0][:, ec, :], in_=y32[:, h, ec, :], func=AF.Square),
                         reads=[y_b[h][ec][b] for b in range(4)], writes=[sq[h][1][ec]])
            pss = []
            for h in range(2):
                ps1, ps1b = psA.next()
                mm_group(P, ps1[:], ps1b, [(ones[:], y32[:, h, ec, :], [y_b[h][ec][b] for b in range(4)] + [ones_b]) for ec in range(4)])
                ps2, ps2b = psSt.next()
                mm_group(P, ps2[:], ps2b, [(ones[:], sq[h][0][:, ec, :], [sq[h][1][ec], ones_b]) for ec in range(4)])
                pss.append((ps1, ps1b, ps2, ps2b))
            for h in range(2):
                ps1, ps1b, ps2, ps2b = pss[h]
                P.op("act", lambda e, ps1=ps1, h=h: e.activation(out=mu2[:, h, :], in_=ps1[:], func=AF.Copy, scale=1.0 / 512.0),
                     reads=[ps1b], writes=[mu2_b[h]])
                P.op("dve", lambda e, h=h: e.tensor_tensor(out=tmpA[:], in0=mu2[:, h, :], in1=mu2[:, h, :], op=ALU.mult), reads=[mu2_b[h]], writes=[tmpA_b])
                P.op("dve", lambda e, ps2=ps2, h=h: e.scalar_tensor_tensor(out=rs2[:, h, :], in0=ps2[:], scalar=1.0 / 512.0, in1=tmpA[:], op0=ALU.mult, op1=ALU.subtract),
                     reads=[ps2b, tmpA_b], writes=[rs2_b[h]])
                P.op("act", lambda e, h=h: e.activation(out=rs2[:, h, :], in_=rs2[:, h, :], func=AF.Sqrt, bias=EPS, scale=1.0), reads=[rs2_b[h]], writes=[rs2_b[h]])
            for h in range(2):
                P.op("dve", lambda e, h=h: e.reciprocal(rs2[:, h, :], rs2[:, h, :]), reads=[rs2_b[h]], writes=[rs2_b[h]])
                for ec in range(4):
                    c = 4 * h + ec
                    yb = [y_b[h][ec][b] for b in range(4)]
                    P.op("dve", lambda e, h=h, ec=ec: e.tensor_tensor(out=y32[:, h, ec, :], in0=y32[:, h, ec, :], in1=mu2[:, h, :], op=ALU.subtract),
                         reads=yb + [mu2_b[h]], writes=yb)
                    P.op("dve", lambda e, h=h, ec=ec: e.tensor_tensor(out=y32[:, h, ec, :], in0=y32[:, h, ec, :], in1=rs2[:, h, :], op=ALU.mult),
                         reads=yb + [rs2_b[h]], writes=yb)

        def gn_out(ti):
            tsl = slice(ti * TT, (ti + 1) * TT)
            for h in range(2):
                for ec in range(4):
                    c = 4 * h + ec
                    yb = [y_b[h][ec][b] for b in range(4)]
                    if env:
                        outs.extend(env.emit_gn(P, c, ti, y32[:, h, ec, :], yb, sg[:, c, :], sg_b[c]))
                        continue
                    P.op("act", lambda e, h=h, ec=ec, c=c: e.activation(out=y32[:, h, ec, :], in_=y32[:, h, ec, :], func=AF.Identity,
                                                                      bias=gnb[:, c:c + 1], scale=gnw[:, c:c + 1]),
                         reads=yb + [gnw_b, gnb_b], writes=yb)
                    P.op("dve", lambda e, h=h, ec=ec, c=c: e.tensor_tensor(out=y32[:, h, ec, :], in0=y32[:, h, ec, :], in1=sg[:, c, :], op=ALU.mult),
                         reads=yb + [sg_b[c]], writes=yb)
                    if env:
                        pass
                    else:
                        outs.append(P.op("sp", lambda e, h=h, ec=ec, c=c, tsl=tsl: e.dma_start(out=m_ov[:, c, tsl], in_=y32[:, h, ec, :]),
                                         reads=yb, sem=st_sem[h][ec], inc=16))

        load(0)
        qpath(0, "A"); kpath(0, "A"); qpath(0, "B"); kpath(0, "B")
        vproj(0)
        for ti in range(NTI):
            nxt = ti + 1 < NTI
            if nxt:
                load(ti + 1)
            block(ti, 0)
            if nxt:
                qpath(ti + 1, "A")
            block(ti, 1)
            if nxt:
                qpath(ti + 1, "B")
                kpath(ti + 1, "A")
                vproj(ti + 1)
            block(ti, 2)
            if nxt:
                kpath(ti + 1, "B")
            gproj(ti)
            block(ti, 3)
            gn_norm(ti)
            gn_out(ti)
            if env:
                env.m_done(P, ti)
        if env:
            env.outs = outs
        else:
            P.finish(outs)
    return nc


def sb_tables():
    j = np.arange(128)
    ltri = (j[:, None] >= j[None, :]).astype(np.float32)
    ustr = (j[:, None] < j[None, :]).astype(np.float32)
    oblk = np.zeros((128, 128), np.float32)
    oblk[:64, :64] = 1.0
    oblk[64:, 64:] = 1.0
    t = np.arange(512)
    maskd = np.zeros((128, 4, 512), np.float32)
    for r in range(4):
        maskd[:, r, :] = ((128 * r + j)[:, None] < t[None, :]).astype(np.float32)
    return dict(ltri=ltri, ustr=ustr, oblk=oblk, maskd=maskd)


def build_sb(SEQ=S, env=None):
    nc = env.nc if env else bass.Bass("TRN2", target_bir_lowering=False)
    NTI = SEQ // TT
    NKB = SEQ // 128
    dr = env.dr if env else (lambda n, s, dt, kind: nc.dram_tensor(n, list(s), dt, kind=kind).ap())
    xn_d = dr("xnT", [D, SEQ], F32, "ExternalInput")
    wq_d = dr("wq", [D, 512], F32, "ExternalInput")
    wk_d = dr("wk", [D, 512], F32, "ExternalInput")
    wv_d = dr("wv", [D, 512], F32, "ExternalInput")
    qg_d = dr("qg", [128, 1], F32, "ExternalInput")
    kg_d = dr("kg", [128, 1], F32, "ExternalInput")
    ltri_d = dr("ltri", [128, 128], F32, "ExternalInput")
    ustr_d = dr("ustr", [128, 128], F32, "ExternalInput")
    oblk_d = dr("oblk", [128, 128], F32, "ExternalInput")
    maskd_d = dr("maskd", [128, 4, 512], F32, "ExternalInput")
    y_o = dr("yT_out", [512, SEQ], F32, "ExternalOutput")
    with ExitStack() as st:
        if env:
            P = env.P
            env.phase_setup(P)
        else:
            P = Prog(nc, st)
            P.out_sem = P.newsem("outs")
        psZ = Rot([(P.ps(f"psZ{i}", [128, 512]), P.buf(f"psZ{i}")) for i in range(2)])
        psAcc = Rot([(P.ps(f"psAcc{i}", [128, 512]), P.buf(f"psAcc{i}")) for i in range(2)])
        psY = Rot([(P.ps(f"psY{i}", [64, 512]), P.buf(f"psY{i}")) for i in range(4)])
        psS = psAcc

        def small(name, d_ap, shape, dt=F32, eng="sp"):
            t = P.sb(name, shape, dt)
            b = P.buf(name, dma=True)
            P.dma(eng, t[:], d_ap, dst=b)
            return t, b
        qg, qg_b = small("qg", qg_d, [128, 1])
        kg, kg_b = small("kg", kg_d, [128, 1])
        oblk, oblk_b = small("oblk", oblk_d, [128, 128])
        maskd, maskd_b = small("maskd", maskd_d, [128, 4, 512])
        ltri, ltri_b = small("ltri", ltri_d, [128, 128], BF16, "pool")
        ustr, ustr_b = small("ustr", ustr_d, [128, 128], BF16, "pool")

        def wload(name, d_ap, ncol):
            t = P.sb(name, [128, 8, ncol], BF16)
            bs = []
            v = d_ap.rearrange("(c p) f -> p c f", p=128)
            for k2 in range(4):
                b = P.buf(f"{name}{k2}", dma=True)
                P.dma("pool", t[:, 2 * k2:2 * k2 + 2, :], v[:, 2 * k2:2 * k2 + 2, :], dst=b)
                bs.append(b)
            return t, bs
        wq, wq_b = wload("wq", wq_d, 512)
        wk, wk_b = wload("wk", wk_d, 512)
        wv, wv_b = wload("wv", wv_d, 512)

        xnr = Rot([(P.sb(f"xn{i}", [128, 8, TT], BF16), P.buf(f"xn{i}", dma=True)) for i in range(2)])
        raw = P.sb("raw", [128, 4, TT], F32); raw_b = [P.buf(f"raw{c}") for c in range(4)]
        scr = P.sb("scr", [128, 4, TT], F32); scr_b = [P.buf(f"scr{c}") for c in range(4)]
        rs = P.sb("rs", [128, TT], F32); rs_b = P.buf("rs")
        QT = P.sb("QT", [128, 4, TT], BF16); QT_b = [P.buf(f"QT{c}") for c in range(4)]
        KT = P.sb("KT", [128, 4, SEQ], BF16); KT_b = [[P.buf(f"KT{c}_{t}") for t in range(NTI)] for c in range(4)]
        V = P.sb("V", [128, NKB, 512], BF16); V_b = [P.buf(f"V{kb}") for kb in range(NKB)]
        er = Rot([(P.sb(f"e{i}", [128, TT], F32), P.buf(f"e{i}")) for i in range(8)])
        wr = Rot([(P.sb(f"w{i}", [128, TT], F32), P.buf(f"w{i}")) for i in range(3)])
        spr = Rot([(P.sb(f"sp{i}", [128, TT], BF16), P.buf(f"sp{i}")) for i in range(5)])
        Ar = Rot([(P.sb(f"A{i}", [128, TT], BF16), P.buf(f"A{i}")) for i in range(4)])
        srun_rots = [Rot([(P.sb(f"srun{j}_{i}", [128, TT], BF16), P.buf(f"srun{j}_{i}")) for i in range(3)]) for j in range(2)]
        ones_bf = P.sb("ones_bf", [128, 128], BF16); ones_bf_b = P.buf("ones_bf")
        P.op("pool", lambda e: e.memset(ones_bf[:], 1.0), writes=[ones_bf_b])
        yr = [(P.sb(f"yo{i}", [64, TT], F32), P.buf(f"yo{i}"), P.newsem(f"st_yo{i}")) for i in range(4)]
        yri = [0]

        if env:
            xn_src = env.xn_src
        else:
            xn_v = xn_d.rearrange("(c p) t -> p c t", p=128)
            xn_src = lambda ti: xn_v[:, :, ti * TT:(ti + 1) * TT]
        outs = []

        def qk_proj(w, w_b, gain, gain_b, xn, xn_b, out_fn):
            for c in range(4):
                ps, psb = psZ.next()
                mm_group(P, ps[:], psb, [(w[:, k, c * 128:(c + 1) * 128], xn[:, k, :], [w_b[k // 2], xn_b]) for k in range(8)])
                P.op("act", lambda e, c=c, ps=ps: e.copy(raw[:, c, :], ps[:]), reads=[psb], writes=[raw_b[c]])
                P.op("act", lambda e, c=c: e.activation(out=scr[:, c, :], in_=raw[:, c, :], func=AF.Square),
                     reads=[raw_b[c]], writes=[scr_b[c]])
                ps2, ps2b = psS.next()
                mm_group(P, ps2[:], ps2b, [(oblk[:], scr[:, c, :], [scr_b[c], oblk_b])])
                P.op("act", lambda e, ps2=ps2: e.activation(out=rs[:], in_=ps2[:], func=AF.Sqrt, bias=EPS, scale=1.0 / 64.0),
                     reads=[ps2b], writes=[rs_b])
                P.op("dve", lambda e: e.reciprocal(rs[:], rs[:]), reads=[rs_b], writes=[rs_b])
                oap, ob = out_fn(c)
                P.op("dve", lambda e, c=c, oap=oap: e.scalar_tensor_tensor(
                    out=oap, in0=raw[:, c, :], scalar=gain[:, 0:1], in1=rs[:], op0=ALU.mult, op1=ALU.mult),
                    reads=[raw_b[c], gain_b, rs_b], writes=[ob])

        for ti in range(NTI):
            tsl = slice(ti * TT, (ti + 1) * TT)
            xn, xn_b = xnr.next()
            P.dma("pool", xn[:], xn_src(ti), dst=xn_b, extra=(env.xn_wait(ti) if env else ()))
            qk_proj(wq, wq_b, qg, qg_b, xn, xn_b, lambda c: (QT[:, c, :], QT_b[c]))
            qk_proj(wk, wk_b, kg, kg_b, xn, xn_b, lambda c, ti=ti, tsl=tsl: (KT[:, c, tsl], KT_b[c][ti]))
            for b in range(4):
                kb = 4 * ti + b
                ps, psb = psZ.next()
                mm_group(P, ps[:], psb, [(xn[:, k, b * 128:(b + 1) * 128], wv[:, k, :], [xn_b, wv_b[k // 2]]) for k in range(8)])
                P.op("act", lambda e, kb=kb, ps=ps: e.copy(V[:, kb, :], ps[:]), reads=[psb], writes=[V_b[kb]])
            nkb = 4 * ti + 4
            units = []
            for c in range(4):
                pys = [psY.next() for _ in range(2)]
                for step in range(nkb):
                    for j in range(2):
                        units.append(dict(c=c, j=j, step=step, kb=nkb - 1 - step, psl=slice(64 * j, 64 * j + 64),
                                          py=pys[j][0], pyb=pys[j][1]))
            srun_state = {}

            def stage(k, u):
                c, j, step, kb = u["c"], u["j"], u["step"], u["kb"]
                ksl = slice(kb * 128, (kb + 1) * 128)
                r = kb - 4 * ti
                if k == 0:
                    z, zb = psZ.next()
                    mm_group(P, z[:], zb, [(KT[u["psl"], c, ksl], QT[u["psl"], c, :], [KT_b[c][kb // 4], QT_b[c]])])
                    u["z"], u["zb"] = z, zb
                elif k == 1:
                    e_sb, e_b = er.next()
                    P.op("act", lambda e, e_sb=e_sb, z=u["z"]: e.activation(out=e_sb[:], in_=z[:], func=AF.Exp, scale=0.125),
                         reads=[u["zb"]], writes=[e_b])
                    if r >= 0:
                        P.op("dve", lambda e, e_sb=e_sb, r=r: e.tensor_tensor(out=e_sb[:], in0=e_sb[:], in1=maskd[:, r, :], op=ALU.mult),
                             reads=[e_b, maskd_b], writes=[e_b])
                    u["e"], u["eb"] = e_sb, e_b
                elif k == 2:
                    sp_sb, sp_b = spr.next()
                    P.op("act", lambda e, sp_sb=sp_sb, e_sb=u["e"]: e.activation(out=sp_sb[:], in_=e_sb[:], func=AF.Ln, bias=1.0, scale=1.0),
                         reads=[u["eb"]], writes=[sp_b])
                    u["sp"], u["spb"] = sp_sb, sp_b
                elif k == 3:
                    acc, accb = psAcc.next()
                    terms = [(ltri[:], u["sp"][:], [u["spb"], ltri_b])]
                    if step > 0:
                        srun, srunb = srun_state[(c, j)]
                        terms.append((ones_bf[:], srun[:], [srunb, ones_bf_b]))
                    mm_group(P, acc[:], accb, terms)
                    u["acc"], u["accb"] = acc, accb
                elif k == 4:
                    if kb > 0:
                        nsr, nsrb = srun_rots[j].next()
                        if step == 0:
                            P.op("dve", lambda e, nsr=nsr, sp_sb=u["sp"]: e.tensor_copy(nsr[:], sp_sb[:]),
                                 reads=[u["spb"]], writes=[nsrb])
                        else:
                            old, oldb = srun_state[(c, j)]
                            P.op("dve", lambda e, nsr=nsr, sp_sb=u["sp"], old=old: e.tensor_tensor(out=nsr[:], in0=old[:], in1=sp_sb[:], op=ALU.add),
                                 reads=[u["spb"], oldb], writes=[nsrb])
                        srun_state[(c, j)] = (nsr, nsrb)
                    w_sb, w_b = wr.next()
                    P.op("act", lambda e, w_sb=w_sb, acc=u["acc"]: e.activation(out=w_sb[:], in_=acc[:], func=AF.Exp, scale=-1.0),
                         reads=[u["accb"]], writes=[w_b])
                    u["w"], u["wb"] = w_sb, w_b
                elif k == 5:
                    A_sb, A_b = Ar.next()
                    P.op("dve", lambda e, A_sb=A_sb, e_sb=u["e"], w_sb=u["w"]: e.tensor_tensor(out=A_sb[:], in0=e_sb[:], in1=w_sb[:], op=ALU.mult),
                         reads=[u["eb"], u["wb"]], writes=[A_b])
                    u["A"], u["Ab"] = A_sb, A_b
                elif k == 6:
                    py = u["py"]
                    P.op("pe", lambda e, py=py, A_sb=u["A"], kb=kb, j=j, c=c, step=step: e.matmul(
                        py[:], lhsT=V[:, kb, c * 128 + 64 * j: c * 128 + 64 * j + 64], rhs=A_sb[:], start=(step == 0), stop=(kb == 0)),
                        reads=[u["Ab"], V_b[kb]], writes=[u["pyb"]])
                    if kb == 0:
                        row = c * 128 + 64 * j
                        if env:
                            outs.extend(env.emit_m(P, row, 64, ti, py[:], [u["pyb"]]))
                            return
                        yo, yo_b, yo_sem = yr[yri[0] % 4]
                        yri[0] += 1
                        P.op("act", lambda e, yo=yo, py=py: e.copy(yo[:], py[:]), reads=[u["pyb"]], writes=[yo_b])
                        if env:
                            pass
                        else:
                            outs.append(P.op("sp", lambda e, yo=yo, row=row, tsl=tsl: e.dma_start(out=y_o[row:row + 64, tsl], in_=yo[:]),
                                             reads=[yo_b], sem=yo_sem, inc=16))

            NS = 7
            SK = [0, 2, 3, 4, 6, 7, 8]
            for slot in range(len(units) + SK[-1]):
                for k in reversed(range(NS)):
                    ui = slot - SK[k]
                    if 0 <= ui < len(units):
                        stage(k, units[ui])
            if env:
                env.m_done(P, ti)
        if env:
            env.outs = outs
        else:
            P.finish(outs)
    return nc


CW = 31
HALO = 32


def build_conv(T=TOK, env=None):
    nc = env.nc if env else bass.Bass("TRN2", target_bir_lowering=False)
    NT = T // TT
    dr = env.dr if env else (lambda n, s, dt, kind: nc.dram_tensor(n, list(s), dt, kind=kind).ap())
    xT_d = dr("xT", [D, T], F32, "ExternalInput")
    xnh_d = dr("xnhT", [D, HALO + T], F32, "ExternalInput")
    flag_d = dr("flag", [128, 1], F32, "ExternalInput")
    pw1_d = dr("pw1_w", [D, 2 * D], F32, "ExternalInput")
    pw1b_d = dr("pw1_b", [128, 16], F32, "ExternalInput")
    dww_d = dr("dw_w", [128, CW * 8], F32, "ExternalInput")
    dwb_d = dr("dw_b", [128, 8], F32, "ExternalInput")
    lnw_d = dr("ln_w", [128, 8], F32, "ExternalInput")
    lnb_d = dr("ln_b", [128, 8], F32, "ExternalInput")
    pw2_d = dr("pw2_w", [D, D], F32, "ExternalInput")
    pw2b_d = dr("pw2_b", [128, 8], F32, "ExternalInput")
    x_o = dr("x_out", [D, T], F32, "ExternalOutput")
    with ExitStack() as st:
        if env:
            P = env.P
        else:
            P = Prog(nc, st)
            P.out_sem = P.newsem("outs")
        ones = P.sb("ones32", [128, 128], F32); ones_b = P.buf("ones")
        P.op("pool", lambda e: e.memset(ones[:], 1.0), writes=[ones_b])
        ident = P.sb("ident", [128, 128], F32); ident_b = P.buf("ident")
        P.op("pool", lambda e: e.memset(ident[:], 1.0), writes=[ident_b])
        P.op("pool", lambda e: e.affine_select(out=ident[:], in_=ident[:], pattern=[[-1, 128]],
                                               compare_op=ALU.is_equal, fill=0.0, base=0, channel_multiplier=1),
             reads=[ident_b], writes=[ident_b])
        psA = Rot([(P.ps(f"psA{i}", [128, 512]), P.buf(f"psA{i}")) for i in range(2)])
        psG = Rot([(P.ps(f"psG{i}", [128, 512]), P.buf(f"psG{i}")) for i in range(2)])
        psC = Rot([(P.ps(f"psC{i}", [128, 512]), P.buf(f"psC{i}")) for i in range(2)])
        psS = Rot([(P.ps(f"psS{i}", [128, 512]), P.buf(f"psS{i}")) for i in range(2)])

        def small(name, d_ap, shape):
            t = P.sb(name, shape, F32)
            b = P.buf(name, dma=True)
            P.dma("sp", t[:], d_ap, dst=b)
            return t, b
        flag, flag_b = small("flag", flag_d, [128, 1])
        pw1b, pw1b_b = small("pw1b", pw1b_d, [128, 16])
        dww, dww_b = small("dww", dww_d, [128, CW * 8])
        dwb, dwb_b = small("dwb", dwb_d, [128, 8])
        lnw, lnw_b = small("lnw", lnw_d, [128, 8])
        lnb, lnb_b = small("lnb", lnb_d, [128, 8])
        pw2b, pw2b_b = small("pw2b", pw2b_d, [128, 8])

        WB = P.sb("WB", [128, 32768], BF16)
        pw1 = WB[:, 0:16384].rearrange("p (a b) -> p a b", a=8)
        pw1_b = [P.buf(f"pw1_{k}", dma=True) for k in range(8)]
        pw1_v = pw1_d.rearrange("(c p) f -> p c f", p=128)
        for k in range(8):
            P.dma("pool", pw1[:, k, :], pw1_v[:, k, :], dst=pw1_b[k])
        pw2 = P.sb("pw2", [128, 8, D], BF16)
        pw2_b = [P.buf(f"pw2_{k}", dma=True) for k in range(4)]
        pw2_v = pw2_d.rearrange("(c p) f -> p c f", p=128)
        for k2 in range(4):
            P.dma("pool", pw2[:, 2 * k2:2 * k2 + 2, :], pw2_v[:, 2 * k2:2 * k2 + 2, :], dst=pw2_b[k2])

        h = P.sb("h", [128, 8, HALO + T], BF16)
        h_b = [[P.buf(f"h{c}_{t}") for t in range(NT + 1)] for c in range(8)]
        xnt = P.sb("xnt", [128, 8, HALO + TT], BF16); xnt_b = P.buf("xnt", dma=True)
        sgr = Rot([(P.sb(f"sgm{i}", [128, TT], F32), P.buf(f"sgm{i}")) for i in range(2)])
        if env:
            xn_halo = env.xn_halo
            xn_main = env.xn_main
        else:
            xnh_v = xnh_d.rearrange("(c p) t -> p c t", p=128)
            xn_halo = lambda: xnh_v[:, :, 0:HALO]
            xn_main = lambda t: xnh_v[:, :, HALO + t * TT:HALO + (t + 1) * TT]
        last_pw1 = None
        for t in range(NT):
            if t == 0:
                P.dma("pool", xnt[:, :, 0:HALO], xn_halo(), dst=xnt_b, extra=(env.xn_wait(NT - 1) if env else ()))
                P.dma("pool", xnt[:, :, HALO:HALO + TT], xn_main(0), dst=xnt_b)
                segs = [(0, HALO, 0), (HALO, TT, 1)]
            else:
                P.dma("pool", xnt[:, :, HALO:HALO + TT], xn_main(t), dst=xnt_b)
                segs = [(HALO, TT, t + 1)]
            for (off, n, hidx) in segs:
                col0 = 0 if hidx == 0 else HALO + (hidx - 1) * TT
                for c in range(8):
                    pa, pab = psA.next()
                    mm_group(P, pa[:, 0:n], pab, [(pw1[:, k, c * 128:(c + 1) * 128], xnt[:, k, off:off + n], [pw1_b[k], xnt_b]) for k in range(8)])
                    pg, pgb = psG.next()
                    last_pw1 = mm_group(P, pg[:, 0:n], pgb, [(pw1[:, k, D + c * 128:D + (c + 1) * 128], xnt[:, k, off:off + n], [pw1_b[k], xnt_b]) for k in range(8)])
                    sg_, sg_b = sgr.next()
                    P.op("act", lambda e, sg_=sg_, pg=pg, n=n, c=c: e.activation(out=sg_[:, 0:n], in_=pg[:, 0:n], func=AF.Sigmoid,
                                                                                bias=pw1b[:, 8 + c:9 + c], scale=1.0),
                         reads=[pgb, pw1b_b], writes=[sg_b])
                    P.op("dve", lambda e, sg_=sg_, pa=pa, n=n, c=c, col0=col0: e.scalar_tensor_tensor(
                        out=h[:, c, col0:col0 + n], in0=pa[:, 0:n], scalar=pw1b[:, c:c + 1], in1=sg_[:, 0:n], op0=ALU.add, op1=ALU.mult),
                        reads=[pab, sg_b, pw1b_b], writes=[h_b[c][hidx]])
                    if hidx == 0:
                        P.op("dve", lambda e, c=c: e.tensor_scalar(out=h[:, c, 0:HALO], in0=h[:, c, 0:HALO], scalar1=flag[:, 0:1], scalar2=None, op0=ALU.mult),
                             reads=[h_b[c][0], flag_b], writes=[h_b[c][0]])
        diag = WB[:, 0:CW * 8 * 128].rearrange("p (a b) -> p a b", a=CW * 8)
        diag_b = [P.buf(f"diag{c}") for c in range(8)]
        for c in range(8):
            diag_b[c].w = last_pw1
            ev = None
            for j in range(CW):
                eng = "dve"
                ev = P.op(eng, lambda e, c=c, j=j: e.tensor_scalar(out=diag[:, c * CW + j, :], in0=ident[:], scalar1=dww[:, j * 8 + c:j * 8 + c + 1], scalar2=None, op0=ALU.mult),
                          reads=[ident_b, dww_b], writes=[], extra=[last_pw1])
                diag_b[c].r.append(ev)
            diag_b[c].w = None
            diag_b[c].wlist = list(diag_b[c].r)
            diag_b[c].r = []
        cv = P.sb("cv", [128, 8, TT], F32); cv_b = [P.buf(f"cv{c}") for c in range(8)]
        scr = P.sb("scr", [128, 8, TT], F32); scr_b = [P.buf(f"scr{c}") for c in range(8)]
        u = P.sb("u", [128, 8, TT], BF16); u_b = [P.buf(f"u{c}") for c in range(8)]
        xt = P.sb("xt", [128, 8, TT], F32); xt_b = P.buf("xt", dma=True)
        xt_cb = [P.buf(f"xt{c}") for c in range(8)]
        st_sem = [P.newsem(f"st_x{c}") for c in range(8)]
        mu = P.sb("mu", [128, TT], F32); mu_b = P.buf("mu")
        rs = P.sb("rs", [128, TT], F32); rs_b = P.buf("rs")
        tmp = P.sb("tmp", [128, TT], F32); tmp_b = P.buf("tmp")
        xT_v = xT_d.rearrange("(c p) t -> p c t", p=128)
        x_ov = x_o.rearrange("(c p) t -> p c t", p=128)
        outs = []
        for t in range(NT):
            tsl = slice(t * TT, (t + 1) * TT)
            evl = P.op("sp", lambda e, tsl=tsl: e.dma_start(out=xt[:], in_=xT_v[:, :, tsl]), reads=[], writes=xt_cb, sem=P.semof(xt_b, "sp"), inc=16)
            for c in range(8):
                pc, pcb = psC.next()
                hb = [h_b[c][t], h_b[c][t + 1]]
                n = CW
                evm = None
                for j in range(CW):
                    evm = P.op("pe", lambda e, pc=pc, c=c, j=j, t=t: e.matmul(pc[:], lhsT=diag[:, c * CW + j, :], rhs=h[:, c, t * TT + 2 + j:t * TT + 2 + j + TT],
                                                                        start=(j == 0), stop=(j == CW - 1)),
                               reads=hb, writes=[pcb] if j == 0 else [], signal=(j == CW - 1), extra=diag_b[c].wlist)
                pcb.w = evm
                pcb.r = []
                P.op("act", lambda e, pc=pc, c=c: e.activation(out=cv[:, c, :], in_=pc[:], func=AF.Identity, bias=dwb[:, c:c + 1], scale=1.0),
                     reads=[pcb, dwb_b], writes=[cv_b[c]])
                P.op("act", lambda e, c=c: e.activation(out=scr[:, c, :], in_=cv[:, c, :], func=AF.Square), reads=[cv_b[c]], writes=[scr_b[c]])
            ps1, ps1b = psS.next()
            mm_group(P, ps1[:], ps1b, [(ones[:], cv[:, c, :], [cv_b[c], ones_b]) for c in range(8)])
            ps2, ps2b = psS.next()
            mm_group(P, ps2[:], ps2b, [(ones[:], scr[:, c, :], [scr_b[c], ones_b]) for c in range(8)])
            P.op("act", lambda e, ps1=ps1: e.activation(out=mu[:], in_=ps1[:], func=AF.Copy, scale=1.0 / D), reads=[ps1b], writes=[mu_b])
            P.op("dve", lambda e: e.tensor_tensor(out=tmp[:], in0=mu[:], in1=mu[:], op=ALU.mult), reads=[mu_b], writes=[tmp_b])
            P.op("dve", lambda e, ps2=ps2: e.scalar_tensor_tensor(out=rs[:], in0=ps2[:], scalar=1.0 / D, in1=tmp[:], op0=ALU.mult, op1=ALU.subtract),
                 reads=[ps2b, tmp_b], writes=[rs_b])
            P.op("act", lambda e: e.activation(out=rs[:], in_=rs[:], func=AF.Sqrt, bias=EPS, scale=1.0), reads=[rs_b], writes=[rs_b])
            P.op("dve", lambda e: e.reciprocal(rs[:], rs[:]), reads=[rs_b], writes=[rs_b])
            for c in range(8):
                P.op("dve", lambda e, c=c: e.tensor_tensor(out=cv[:, c, :], in0=cv[:, c, :], in1=mu[:], op=ALU.subtract), reads=[cv_b[c], mu_b], writes=[cv_b[c]])
                P.op("dve", lambda e, c=c: e.tensor_tensor(out=cv[:, c, :], in0=cv[:, c, :], in1=rs[:], op=ALU.mult), reads=[cv_b[c], rs_b], writes=[cv_b[c]])
                P.op("act", lambda e, c=c: e.activation(out=u[:, c, :], in_=cv[:, c, :], func=AF.Silu, bias=lnb[:, c:c + 1], scale=lnw[:, c:c + 1]),
                     reads=[cv_b[c], lnw_b, lnb_b], writes=[u_b[c]])
            for fo in range(8):
                po, pob = psA.next()
                mm_group(P, po[:], pob, [(pw2[:, k, fo * 128:(fo + 1) * 128], u[:, k, :], [pw2_b[k // 2], u_b[k]]) for k in range(8)])
                P.op("dve", lambda e, po=po, fo=fo: e.scalar_tensor_tensor(out=xt[:, fo, :], in0=po[:], scalar=pw2b[:, fo:fo + 1], in1=xt[:, fo, :], op0=ALU.add, op1=ALU.add),
                     reads=[pob, pw2b_b, xt_cb[fo]], writes=[xt_cb[fo]])
                outs.append(P.op("sp", lambda e, fo=fo, tsl=tsl: e.dma_start(out=x_ov[:, fo, tsl], in_=xt[:, fo, :]), reads=[xt_cb[fo]], sem=st_sem[fo], inc=16))
        if env:
            env.outs = outs
        else:
            P.finish(outs)
    return nc


def conv_inputs(xT, xnhT, flagv, pw1_w, pw1_b, dw_w, dw_b, ln_w, ln_b, pw2_w, pw2_b):
    f = lambda a: np.ascontiguousarray(a, dtype=np.float32)
    dww = np.asarray(dw_w, np.float32).reshape(CW, 8, 128).transpose(2, 0, 1).reshape(128, CW * 8)
    return {"xT": f(xT), "xnhT": f(xnhT), "flag": np.full((128, 1), flagv, np.float32),
            "pw1_w": f(pw1_w), "pw1_b": pcol(pw1_b), "dw_w": f(dww), "dw_b": pcol(dw_b),
            "ln_w": pcol(ln_w), "ln_b": pcol(ln_b), "pw2_w": f(pw2_w), "pw2_b": pcol(pw2_b)}


class Env:
    def __init__(self, nc, P, T):
        self.nc = nc
        self.P = P
        self.T = T
        self.io = {}
        self.outs = []
        self.flags_d = None
        self.mz2d = None
        self.fmy = 0
        self.m_pending = []
        self.m_evs = {}
        self.nocc = False
        self.ag_ev = {}
        self.rs_ev = {}

    def xn_wait(self, ti):
        nt = self.T // TT
        ev = self.ag_ev.get(ti % nt)
        return [ev] if ev is not None else []

    def m_wait(self, t):
        ev = self.rs_ev.get(t)
        return [ev] if ev is not None else []

    def dr(self, name, shape, dt, kind):
        return self.io.get(name)

    def phase_setup(self, P):
        self.flag = P.sb("flags", [128, 2], F32)
        self.flag_b = P.buf("flags", dma=True)
        P.dma("sp", self.flag[:], self.flags_d, dst=self.flag_b)
        self.stages = Rot([(P.sb(f"stg{i}", [128, TT], BF16), P.buf(f"stg{i}"), P.newsem(f"st_stg{i}")) for i in range(3)])

    def flagged_affine(self, P, gnw, gnw_b, gnb, gnb_b):
        self.gnwf = P.sb("gnwf", [128, 2, 8], F32); self.gnwf_b = P.buf("gnwf")
        self.gnbf = P.sb("gnbf", [128, 2, 8], F32); self.gnbf_b = P.buf("gnbf")
        for j in range(2):
            P.op("dve", lambda e, j=j: e.tensor_scalar(out=self.gnwf[:, j, :], in0=gnw[:], scalar1=self.flag[:, j:j + 1], scalar2=None, op0=ALU.mult),
                 reads=[gnw_b, self.flag_b], writes=[self.gnwf_b])
            P.op("dve", lambda e, j=j: e.tensor_scalar(out=self.gnbf[:, j, :], in0=gnb[:], scalar1=self.flag[:, j:j + 1], scalar2=None, op0=ALU.mult),
                 reads=[gnb_b, self.flag_b], writes=[self.gnbf_b])
        self.aff_tmp = Rot([(P.sb(f"afft{i}", [128, TT], F32), P.buf(f"afft{i}")) for i in range(2)])

    def emit_gn(self, P, c, ti, y_ap, y_bufs, sg_ap, sg_buf):
        nt = self.T // TT
        h, tl = ti // nt, ti % nt
        evs = []
        for j in range(2):
            tmp, tmp_b = self.aff_tmp.next()
            P.op("act", lambda e, tmp=tmp, j=j: e.activation(out=tmp[:], in_=y_ap, func=AF.Identity,
                                                            bias=self.gnbf[:, j, c:c + 1], scale=self.gnwf[:, j, c:c + 1]),
                 reads=list(y_bufs) + [self.gnwf_b, self.gnbf_b], writes=[tmp_b])
            stg, stg_b, stg_sem = self.stages.next()
            P.op("dve", lambda e, tmp=tmp, stg=stg: e.tensor_tensor(out=stg[:], in0=tmp[:], in1=sg_ap, op=ALU.mult),
                 reads=[tmp_b, sg_buf], writes=[stg_b])
            r0 = (h * 2 + j) * self.fmy + c * 128
            mz = self.mz2d[tl]
            evs.append(P.op("sp", lambda e, stg=stg, r0=r0, mz=mz: e.dma_start(out=mz[r0:r0 + 128, :], in_=stg[:]),
                            reads=[stg_b], sem=stg_sem, inc=16))
        self.m_pending.extend(evs)
        return evs

    def gather_tile(self, P, t, evs):
        if self.nocc:
            return
        self.ag_ev[t] = P.collective("AllGather", ALU.bypass, self.groups, self.xn_my[t], self.xn_full[t], extra=evs)

    def scatter_tile(self, P, tl, evs):
        if self.nocc:
            return
        self.rs_ev[tl] = P.collective("ReduceScatter", ALU.add, self.groups, self.mz2d[tl], self.mrs_out[tl], extra=evs)

    def m_done(self, P, ti):
        nt = self.T // TT
        h, tl = ti // nt, ti % nt
        self.m_evs.setdefault(tl, []).extend(self.m_pending)
        self.m_pending = []
        if h == 1:
            self.scatter_tile(P, tl, self.m_evs.pop(tl))

    def xn_src(self, ti):
        nt = self.T // TT
        rank, tl = ti // nt, ti % nt
        return self.xn_full[tl][rank * D:(rank + 1) * D, :].rearrange("(c p) t -> p c t", p=128)

    def xn_halo(self):
        nt = self.T // TT
        return self.xn_full[nt - 1][0:D, TT - HALO:TT].rearrange("(c p) t -> p c t", p=128)

    def xn_main(self, t):
        return self.xn_my[t].rearrange("(c p) t -> p c t", p=128)

    def xn_dst(self, t):
        return self.xn_my[t].rearrange("(c p) t -> p c t", p=128)

    def m_src(self, t):
        return self.mrs[t].rearrange("(c p) t -> p c t", p=128)

    def emit_m(self, P, row0, nrows, ti, src_ap, src_bufs):
        nt = self.T // TT
        h, tl = ti // nt, ti % nt
        evs = []
        for j in range(2):
            stg, stg_b, stg_sem = self.stages.next()
            P.op("act", lambda e, stg=stg, j=j: e.activation(out=stg[0:nrows, :], in_=src_ap, func=AF.Identity,
                                                            bias=0.0, scale=self.flag[0:nrows, j:j + 1]),
                 reads=list(src_bufs) + [self.flag_b], writes=[stg_b])
            r0 = (h * 2 + j) * self.fmy + row0
            mz = self.mz2d[tl]
            evs.append(P.op("sp", lambda e, stg=stg, r0=r0, mz=mz: e.dma_start(out=mz[r0:r0 + nrows, :], in_=stg[0:nrows, :]),
                            reads=[stg_b], sem=stg_sem, inc=16))
        self.m_pending.extend(evs)
        return evs


FUSED_INPUTS = None


def build_fused(T=TOK, groups=None):
    SEQ = 2 * T
    if groups is None:
        groups = [[2 * i, 2 * i + 1] for i in range(NCORES // 2)]
    nc = bass.Bass("TRN2", target_bir_lowering=False)
    ext = {}

    def inp(name, shape):
        ext[name] = nc.dram_tensor(name, list(shape), F32, kind="ExternalInput").ap()
        return ext[name]

    def internal(name, shape, dt):
        return nc.dram_tensor(name, list(shape), dt).ap()

    xT_in = inp("xT", [D, T])
    flags = inp("flags", [128, 2])
    flagc = inp("flagc", [128, 1])
    g_mix = [inp(f"g_mix{i}", [128, 8]) for i in range(4)]
    g_ffn = [inp(f"g_ffn{i}", [128, 8]) for i in range(4)]
    g_fin = inp("g_final", [128, 8])
    w1 = [inp(f"w1_{i}", [D, DFF]) for i in range(4)]
    w2 = [inp(f"w2_{i}", [DFF, D]) for i in range(4)]
    ret = []
    for j in range(2):
        ret.append(dict(wq=inp(f"r{j}_wq", [D, 512]), wk=inp(f"r{j}_wk", [D, 512]), wv=inp(f"r{j}_wv", [D, 1024]),
                        wg=inp(f"r{j}_wg", [D, 1024]), qg=inp(f"r{j}_qg", [128, 2]), kg=inp(f"r{j}_kg", [128, 2]),
                        gnw=inp(f"r{j}_gnw", [128, 8]), gnb=inp(f"r{j}_gnb", [128, 8]), w_out=inp(f"r{j}_wout", [2 * D, D])))
    rtab = dict(cos=inp("cos", [128, SEQ]), sin=inp("sin", [128, SEQ]), dt=inp("dt", [128, 2, 128]),
                qdec=inp("qdec", [128, 2, 512]), kdec=inp("kdec", [128, 2]), g128=inp("g128", [128, 2]))
    cv = dict(pw1_w=inp("pw1_w", [D, 2 * D]), pw1_b=inp("pw1_b", [128, 16]), dw_w=inp("dw_w", [128, CW * 8]),
              dw_b=inp("dw_b", [128, 8]), ln_w=inp("ln_w", [128, 8]), ln_b=inp("ln_b", [128, 8]),
              pw2_w=inp("pw2_w", [D, D]), pw2_b=inp("pw2_b", [128, 8]))
    sbw = dict(wq=inp("s_wq", [D, 512]), wk=inp("s_wk", [D, 512]), wv=inp("s_wv", [D, 512]),
               qg=inp("s_qg", [128, 1]), kg=inp("s_kg", [128, 1]), ltri=inp("ltri", [128, 128]), ustr=inp("ustr", [128, 128]),
               oblk=inp("oblk", [128, 128]), maskd=inp("maskd", [128, 4, 512]), w_out=inp("s_wout", [D, D]))
    out_d = nc.dram_tensor("out", [D, T], F32, kind="ExternalOutput").ap()
    xsp = internal("xsp", [D, T], F32)
    NT = T // TT
    xn_my = [internal(f"xn_my{t}", [D, TT], BF16) for t in range(NT)]
    xn_full = [internal(f"xn_full{t}", [2 * D, TT], BF16) for t in range(NT)]
    mz_ret = [internal(f"mz_ret{t}", [4 * 1024, TT], BF16) for t in range(NT)]
    mrs_ret = [internal(f"mrs_ret{t}", [2 * 1024, TT], BF16) for t in range(NT)]
    mz_sb = [internal(f"mz_sb{t}", [4 * 512, TT], BF16) for t in range(NT)]
    mrs_sb = [internal(f"mrs_sb{t}", [2 * 512, TT], BF16) for t in range(NT)]

    global FUSED_INPUTS
    FUSED_INPUTS = list(ext.keys())
    with ExitStack() as st:
        P = Prog(nc, st)
        P.out_sem = P.newsem("outs")
        P.enable_phases()
        env = Env(nc, P, T)
        env.flags_d = flags
        env.xn_full = xn_full
        env.xn_my = xn_my

        import os as _os
        nocc = bool(_os.environ.get("NOCC"))

        env.nocc = nocc
        env.groups = groups

        def gather():
            P.next_phase()

        def scatter(mz, mrs):
            P.next_phase()

        def tok_phase(i, x_src, m_src, w_out, fm, final=False):
            env.mrs = m_src
            env.io = {"xT": x_src, "mT": m_src, "w_out": w_out, "g_ffn": g_ffn[i], "w1": w1[i], "w2": w2[i],
                      "g_next": (g_fin if final else g_mix[i + 1]), "x_out": xsp, "xn_out": (out_d if final else xn_my)}
            build_tok("p2", T=T, fm=fm, env=env, final=final)

        def ret_phase(j):
            env.io = dict(xnT=None, **{k: ret[j][k] for k in ("wq", "wk", "wv", "wg", "qg", "kg", "gnw", "gnb")}, **rtab)
            env.mz2d, env.fmy, env.mrs_out = mz_ret, 1024, mrs_ret
            build_ret(SEQ=SEQ, env=env)
            scatter(mz_ret, mrs_ret)

        env.io = {"xT": xT_in, "g_next": g_mix[0], "xn_out": xn_my}
        build_tok("norm0", T=T, env=env, final=False)
        gather()
        ret_phase(0)
        tok_phase(0, xT_in, mrs_ret, ret[0]["w_out"], 2 * D)
        gather()
        env.io = dict(xT=xsp, x_out=xsp, flag=flagc, **cv)
        build_conv(T=T, env=env)
        P.next_phase()
        tok_phase(1, xsp, None, None, 0)
        gather()
        env.io = dict(xnT=None, **{k: sbw[k] for k in ("wq", "wk", "wv", "qg", "kg", "ltri", "ustr", "oblk", "maskd")})
        env.mz2d, env.fmy, env.mrs_out = mz_sb, 512, mrs_sb
        build_sb(SEQ=SEQ, env=env)
        scatter(mz_sb, mrs_sb)
        tok_phase(2, xsp, mrs_sb, sbw["w_out"], D)
        gather()
        ret_phase(1)
        tok_phase(3, xsp, mrs_ret, ret[1]["w_out"], 2 * D, final=True)
        P.next_phase()
        P.finish(env.outs)
    return nc


_PROGS = {}


def fused_inputs(T, x_b, rank, prm):
    A = lambda a: np.ascontiguousarray(a, dtype=np.float32)
    hp = rank
    m = {"xT": A(x_b[rank * T:(rank + 1) * T].T)}
    fl = np.zeros((128, 2), np.float32)
    fl[:, rank] = 1.0
    m["flags"] = fl
    m["flagc"] = np.full((128, 1), float(rank), np.float32)
    for i in range(4):
        m[f"g_mix{i}"] = pcol(prm["norm_mix"][i])
        m[f"g_ffn{i}"] = pcol(prm["norm_ffn"][i])
        m[f"w1_{i}"] = A(prm["ffn_w1"][i])
        m[f"w2_{i}"] = A(prm["ffn_w2"][i])
    m["g_final"] = pcol(prm["final_norm"])
    for j in range(2):
        w_in = np.asarray(prm["ret_w_in"][j], np.float32)
        m[f"r{j}_wq"] = A(w_in[:, hp * 512:(hp + 1) * 512])
        m[f"r{j}_wk"] = A(w_in[:, 1024 + hp * 512:1024 + (hp + 1) * 512])
        m[f"r{j}_wv"] = A(w_in[:, 2048 + hp * 1024:2048 + (hp + 1) * 1024])
        m[f"r{j}_wg"] = A(w_in[:, 4096 + hp * 1024:4096 + (hp + 1) * 1024])
        m[f"r{j}_qg"] = pcol(prm["ret_q_norm"][j])
        m[f"r{j}_kg"] = pcol(prm["ret_k_norm"][j])
        m[f"r{j}_gnw"] = pcol(np.asarray(prm["ret_gn_w"][j], np.float32)[hp * 1024:(hp + 1) * 1024])
        m[f"r{j}_gnb"] = pcol(np.asarray(prm["ret_gn_b"][j], np.float32)[hp * 1024:(hp + 1) * 1024])
        m[f"r{j}_wout"] = A(prm["ret_w_out"][j])
    tabs = ret_tables((2 * hp, 2 * hp + 1))
    m["cos"] = A(tabs["cos"][:, :2 * T]); m["sin"] = A(tabs["sin"][:, :2 * T])
    for k in ("dt", "qdec", "kdec", "g128"):
        m[k] = A(tabs[k])
    ci = conv_inputs(np.zeros((1, 1)), np.zeros((1, 1)), 0.0, prm["conv_pw1_w"][0], prm["conv_pw1_b"][0], prm["conv_dw_w"][0],
                     prm["conv_dw_b"][0], prm["conv_ln_w"][0], prm["conv_ln_b"][0], prm["conv_pw2_w"][0], prm["conv_pw2_b"][0])
    for k in ("pw1_w", "pw1_b", "dw_w", "dw_b", "ln_w", "ln_b", "pw2_w", "pw2_b"):
        m[k] = ci[k]
    sw = np.asarray(prm["sb_w_in"][0], np.float32)
    m["s_wq"] = A(sw[:, hp * 512:(hp + 1) * 512])
    m["s_wk"] = A(sw[:, 1024 + hp * 512:1024 + (hp + 1) * 512])
    m["s_wv"] = A(sw[:, 2048 + hp * 512:2048 + (hp + 1) * 512])
    m["s_qg"] = A(np.tile(np.asarray(prm["sb_q_norm"][0], np.float32), 2)[:, None])
    m["s_kg"] = A(np.tile(np.asarray(prm["sb_k_norm"][0], np.float32), 2)[:, None])
    m["s_wout"] = A(prm["sb_w_out"][0])
    for k, v in sb_tables().items():
        m[k] = A(v)
    return m


def kernel(**prm):
    x = np.asarray(prm["x"], np.float32)
    if "fused" not in _PROGS:
        _PROGS["fused"] = build_fused()
    nc = _PROGS["fused"]
    in_maps = [fused_inputs(TOK, x[c // 2], c % 2, prm) for c in range(NCORES)]
    res = run_bass_kernel_spmd(nc, in_maps, core_ids=list(range(NCORES)))
    out = np.empty((B, S, D), np.float32)
    for c in range(NCORES):
        out[c // 2, (c % 2) * TOK:(c % 2 + 1) * TOK] = res.results[c]["out"].T
    return out
```

```python
import math
from contextlib import ExitStack

import numpy as np
import ml_dtypes
import concourse.bass as bass
import concourse.mybir as mybir
from concourse.bass_utils import run_bass_kernel_spmd

F32 = mybir.dt.float32
BF16 = mybir.dt.bfloat16
AF = mybir.ActivationFunctionType
ALU = mybir.AluOpType

D = 1024
S = 4096
B = 4
DFF = 4096
EPS = 1e-6
NCORES = 8
TOK = 2048
TT = 512


class Ev:
    __slots__ = ("sem", "val")

    def __init__(self, sem, val):
        self.sem = sem
        self.val = val


class Buf:
    __slots__ = ("name", "w", "r", "sem", "wlist")

    def __init__(self, name, sem=None):
        self.name = name
        self.w = None
        self.r = []
        self.sem = sem


class Prog:
    ENGS = ("pe", "act", "dve", "pool", "sp")

    def __init__(self, nc, stack):
        self.nc = nc
        self.stack = stack
        self.q = {e: [] for e in self.ENGS}
        self.sems = {}
        self.cnt = {}
        self.waited = {}
        self.nsem = 0
        self.arena = None
        self.arena_off = 0
        self.psbanks = None
        self.ps_i = 0
        self.sempool = None
        self.sem_i = 0
        for e in self.ENGS:
            self.newsem("c_" + e)

    def newsem(self, name, kind="hw"):
        if self.sempool is not None:
            pool = self.sempool[kind]
            if self.sem_i[kind] < len(pool):
                nm = pool[self.sem_i[kind]]
            else:
                nm = f"q{kind}{len(pool)}"
                self.sems[nm] = self.stack.enter_context(self.nc.semaphore(nm))
                self.cnt[nm] = 0
                pool.append(nm)
            self.sem_i[kind] += 1
            return nm
        s = self.stack.enter_context(self.nc.semaphore(name))
        self.sems[name] = s
        self.cnt[name] = 0
        self.nsem += 1
        return name

    def buf(self, name, dma=False):
        return Buf(name, "LAZY" if dma else None)

    def semof(self, b, eng):
        if b.sem == "LAZY":
            b.sem = self.newsem("d_" + b.name, kind=("sw" if eng == "pool" else "hw"))
        return b.sem

    def enable_phases(self, arena_bytes=206 * 1024):
        self.arena = self.stack.enter_context(self.nc.sbuf_tensor("arena", [128, arena_bytes // 2], BF16))
        self.arena_bytes = arena_bytes
        self.psbanks = [self.stack.enter_context(self.nc.psum_tensor(f"pbank{i}", [128, 512], F32)) for i in range(8)]
        self.sempool = {"hw": [], "sw": []}
        self.sem_i = {"hw": 0, "sw": 0}

    def next_phase(self):
        for eng in self.ENGS:
            for name, c in self.cnt.items():
                if c > 0 and name != "cc":
                    self._wait(eng, Ev(name, c))
        import os as _os
        if _os.environ.get("ARENA_DEBUG"):
            print("phase arena max", getattr(self, "arena_max", 0), "off", self.arena_off)
        self.arena_off = 0
        self.ps_i = 0
        self.sem_i = {"hw": 0, "sw": 0}

    def sb(self, name, shape, dt):
        if self.arena is None:
            return self.stack.enter_context(self.nc.sbuf_tensor("s_" + name, list(shape), dt))
        shape = list(shape)
        n = 1
        for d in shape[1:]:
            n *= d
        esz = 4 if dt == F32 else 2
        nb = (n * esz + 31) // 32 * 32
        off = self.arena_off
        assert off + nb <= self.arena_bytes, f"arena overflow allocating {name}: {off}+{nb}"
        self.arena_off += nb
        self.arena_max = max(getattr(self, "arena_max", 0), self.arena_off)
        v = self.arena[0:shape[0], off // 2: off // 2 + n * esz // 2]
        if dt == F32:
            v = v.bitcast(F32)
        if len(shape) == 3:
            v = v.rearrange("p (a b) -> p a b", a=shape[1])
        elif len(shape) == 4:
            v = v.rearrange("p (a b c) -> p a b c", a=shape[1], b=shape[2])
        return v

    def ps(self, name, shape, dt=F32):
        if self.psbanks is None:
            return self.stack.enter_context(self.nc.psum_tensor("p_" + name, list(shape), dt))
        shape = list(shape)
        bank = self.psbanks[self.ps_i]
        self.ps_i += 1
        n = 1
        for d in shape[1:]:
            n *= d
        if dt == F32:
            v = bank[0:shape[0], 0:n]
        else:
            v = bank[0:shape[0], 0:n // 2].bitcast(dt)
        if len(shape) == 3:
            v = v.rearrange("p (a b) -> p a b", a=shape[1])
        return v

    def collective(self, kind, op, groups, in_ap, out_ap, extra=()):
        if "cc" not in self.sems:
            self.sems["cc"] = self.stack.enter_context(self.nc.semaphore("cc"))
            self.cnt["cc"] = 0
        return self.op("pool", lambda e: e.collective_compute(kind, op, replica_groups=groups, ins=[in_ap], outs=[out_ap]),
                       sem="cc", inc=1, extra=extra)

    def _wait(self, eng, ev):
        if ev is None:
            return
        if eng == "pe" and ev.sem == "c_pe":
            return
        key = (eng, ev.sem)
        if self.waited.get(key, 0) >= ev.val:
            return
        self.waited[key] = ev.val
        s = self.sems[ev.sem]
        v = ev.val
        self.q[eng].append(lambda e, s=s, v=v: e.wait_ge(s, v))

    def op(self, eng, fn, reads=(), writes=(), signal=True, sem=None, inc=1, extra=()):
        for b in reads:
            self._wait(eng, b.w)
        for b in writes:
            self._wait(eng, b.w)
            for ev in b.r:
                self._wait(eng, ev)
        for ev in extra:
            self._wait(eng, ev)
        name = sem or ("c_" + eng)
        if signal:
            self.cnt[name] += inc
            ev = Ev(name, self.cnt[name])
            s = self.sems[name]
            self.q[eng].append(lambda e, fn=fn, s=s, inc=inc: fn(e).then_inc(s, inc))
        else:
            ev = Ev(name, self.cnt[name] + inc)
            self.q[eng].append(lambda e, fn=fn: fn(e))
        for b in reads:
            b.r.append(ev)
        for b in writes:
            b.w = ev
            b.r = []
        return ev

    def dma(self, eng, out, in_, dst=None, src=None, extra=()):
        reads = [src] if src is not None else []
        writes = [dst] if dst is not None else []
        semname = self.semof(dst, eng) if (dst is not None and dst.sem) else None
        if semname is None:
            semname = self.out_sem
        return self.op(eng, lambda e: e.dma_start(out=out, in_=in_), reads, writes,
                       sem=semname, inc=16, extra=extra)

    def finish(self, final_evs):
        for ev in final_evs:
            self._wait("sp", ev)
        nc = self.nc
        with nc.Block() as block:
            def mk(name):
                def body(e):
                    for f in self.q[name]:
                        f(e)
                return body
            block.tensor(mk("pe"))
            block.scalar(mk("act"))
            block.vector(mk("dve"))
            block.gpsimd(mk("pool"))
            block.sync(mk("sp"))


def mm_group(P, out_ap, out_buf, terms):
    n = len(terms)
    ev = None
    for i, (l, r, rb) in enumerate(terms):
        ev = P.op("pe",
                  lambda e, l=l, r=r, i=i: e.matmul(out_ap, lhsT=l, rhs=r, start=(i == 0), stop=(i == n - 1)),
                  reads=rb, writes=[out_buf] if i == 0 else [], signal=(i == n - 1))
        if i > 0:
            pass
    out_buf.w = ev
    out_buf.r = []
    return ev


def pcol(v):
    v = np.asarray(v, dtype=np.float32)
    return np.ascontiguousarray(v.reshape(-1, 128).T)


class Rot:
    def __init__(self, items):
        self.items = items
        self.i = 0

    def next(self):
        it = self.items[self.i % len(self.items)]
        self.i += 1
        return it


class Ctx:
    pass


def setup_common(P, nc):
    C = Ctx()
    C.ones = P.sb("ones32", [128, 128], F32)
    C.ones_b = P.buf("ones")
    P.op("pool", lambda e: e.memset(C.ones[:], 1.0), writes=[C.ones_b])
    C.psA = Rot([(P.ps(f"psA{i}", [128, 512]), P.buf(f"psA{i}")) for i in range(3)])
    C.psB = Rot([(P.ps(f"psB{i}", [128, 512]), P.buf(f"psB{i}")) for i in range(3)])
    C.psS = Rot([(P.ps(f"psS{i}", [128, 512]), P.buf(f"psS{i}")) for i in range(2)])
    return C


def load_vec(P, name, dram_ap, nchunk):
    t = P.sb(name, [128, nchunk], F32)
    b = P.buf(name, dma=True)
    P.dma("sp", t[:], dram_ap, dst=b)
    return t, b


def rmsnorm_tile(P, C, x_sb, x_bufs, tsl, gain, gain_b, out_fn, scr, scr_b, rs, rs_b, nfeat=1024):
    nchunk = nfeat // 128
    for c in range(nchunk):
        P.op("act", lambda e, c=c: e.activation(out=scr[:, c, :], in_=x_sb[:, c, tsl], func=AF.Square),
             reads=[x_bufs[c]], writes=[scr_b[c]])
    ps, psb = C.psS.next()
    mm_group(P, ps[:], psb, [(C.ones[:], scr[:, c, :], [scr_b[c], C.ones_b]) for c in range(nchunk)])
    P.op("act", lambda e: e.activation(out=rs[:], in_=ps[:], func=AF.Ln, bias=EPS, scale=1.0 / nfeat),
         reads=[psb], writes=[rs_b])
    P.op("act", lambda e: e.activation(out=rs[:], in_=rs[:], func=AF.Exp, scale=-0.5), reads=[rs_b], writes=[rs_b])
    for c in range(nchunk):
        oap, ob = out_fn(c)
        P.op("dve", lambda e, c=c, oap=oap: e.scalar_tensor_tensor(
            out=oap, in0=x_sb[:, c, tsl], scalar=gain[:, c:c + 1], in1=rs[:], op0=ALU.mult, op1=ALU.mult),
            reads=[x_bufs[c], gain_b, rs_b], writes=[ob])


def build_tok(mode, T=TOK, fm=0, env=None, final=True):
    nc = env.nc if env else bass.Bass("TRN2", target_bir_lowering=False)
    NT = T // TT
    dr = env.dr if env else (lambda n, s, dt, kind: nc.dram_tensor(n, list(s), dt, kind=kind).ap())
    xn_dt = F32 if (env is None or final) else BF16
    xT_d = dr("xT", [D, T], F32, "ExternalInput")
    gn_d = dr("g_next", [128, 8], F32, "ExternalInput")
    xn_o = dr("xn_out", [D, T], xn_dt, "ExternalOutput")
    if mode == "p2":
        if fm:
            mT_d = dr("mT", [fm, T], F32, "ExternalInput")
            wo_d = dr("w_out", [fm, D], F32, "ExternalInput")
        gf_d = dr("g_ffn", [128, 8], F32, "ExternalInput")
        w1_d = dr("w1", [D, DFF], F32, "ExternalInput")
        w2_d = dr("w2", [DFF, D], F32, "ExternalInput")
        x_o = dr("x_out", [D, T], F32, "ExternalOutput")
    with ExitStack() as st:
        if env:
            P = env.P
        else:
            P = Prog(nc, st)
            P.out_sem = P.newsem("outs")
        C = setup_common(P, nc)
        x_sb = P.sb("x_sb", [128, 8, T], F32)
        x_b = [[P.buf(f"x{c}_{t}") for t in range(NT)] for c in range(8)]
        xld = [P.buf(f"xld{t}", dma=True) for t in range(NT)]
        xT_v = xT_d.rearrange("(c p) t -> p c t", p=128)
        for t in range(NT):
            ev = P.dma("sp", x_sb[:, :, t * TT:(t + 1) * TT], xT_v[:, :, t * TT:(t + 1) * TT], dst=xld[t])
            for c in range(8):
                x_b[c][t].w = ev
        gnext, gnext_b = load_vec(P, "gnext", gn_d, 8)
        scr = P.sb("scr", [128, 8, TT], F32)
        scr_b = [P.buf(f"scr{c}") for c in range(8)]
        rs = P.sb("rs", [128, TT], F32)
        rs_b = P.buf("rs")
        outs = []
        xno = Rot([(P.sb(f"xno{i}", [128, 8, TT], xn_dt), [P.buf(f"xno{i}_{c}") for c in range(8)]) for i in range(2)])
        xn_ov = xn_o.rearrange("(c p) t -> p c t", p=128) if (env is None or final) else None
        xno_sems = [P.newsem(f"st_xno{i}") for i in range(2)]
        x_ov = x_o.rearrange("(c p) t -> p c t", p=128) if mode == "p2" else None
        tail_i = [0]

        def tile_tail(t):
            tsl = slice(t * TT, (t + 1) * TT)
            if mode == "p2" and (env is None or not final):
                for c in range(8):
                    outs.append(P.op("sp", lambda e, c=c, tsl=tsl: e.dma_start(out=x_ov[:, c, tsl], in_=x_sb[:, c, tsl]),
                                     reads=[x_b[c][t]], sem=P.out_sem, inc=16))
            o_sb, o_b = xno.next()
            osem = xno_sems[tail_i[0] % 2]
            tail_i[0] += 1
            rmsnorm_tile(P, C, x_sb, [x_b[c][t] for c in range(8)], tsl, gnext, gnext_b,
                         lambda c, o_sb=o_sb, o_b=o_b: (o_sb[:, c, :], o_b[c]), scr, scr_b, rs, rs_b)
            xn_dst = env.xn_dst(t) if (env and not final) else xn_ov[:, :, tsl]
            ev = P.op("sp", lambda e, o_sb=o_sb, xn_dst=xn_dst: e.dma_start(out=xn_dst, in_=o_sb[:]),
                      reads=o_b, sem=osem, inc=16)
            outs.append(ev)
            if env and not final:
                env.gather_tile(P, t, [ev])

        if mode == "p2":
            gffn, gffn_b = load_vec(P, "gffn", gf_d, 8)
            KM = fm // 128
            if fm:
              pass
            R1 = P.sb("R1", [128, 32768], BF16)

            def view(off, d0, d1):
                return R1[:, off:off + d0 * d1].rearrange("p (a b) -> p a b", a=d0)
            lastA = None
            if fm:
                wo_sb = view(0, KM, D)
                wo_b = [P.buf(f"wo{k}", dma=True) for k in range(KM // 4)]
                wo_v = wo_d.rearrange("(c p) f -> p c f", p=128)
                for k4 in range(KM // 4):
                    P.dma("pool", wo_sb[:, 4 * k4:4 * k4 + 4, :], wo_v[:, 4 * k4:4 * k4 + 4, :], dst=wo_b[k4])
                mr = Rot([(view(16384 + i * 8192, KM, TT), P.buf(f"mt{i}", dma=True)) for i in range(2)])
                if env:
                    m_src = env.m_src
                else:
                    mT_v = mT_d.rearrange("(c p) t -> p c t", p=128)
                    m_src = lambda t: mT_v[:, :, t * TT:(t + 1) * TT]
                for t in range(NT):
                    tsl = slice(t * TT, (t + 1) * TT)
                    m_t, m_tb = mr.next()
                    P.dma("pool", m_t, m_src(t), dst=m_tb, extra=(env.m_wait(t) if env else ()))
                    for fo in range(8):
                        ps, psb = C.psB.next()
                        lastA = mm_group(P, ps[:], psb,
                                         [(wo_sb[:, k, fo * 128:(fo + 1) * 128], m_t[:, k, :], [wo_b[k // 4], m_tb])
                                          for k in range(KM)])
                        P.op("dve", lambda e, ps=ps, fo=fo, tsl=tsl: e.tensor_tensor(
                            out=x_sb[:, fo, tsl], in0=x_sb[:, fo, tsl], in1=ps[:], op=ALU.add),
                            reads=[psb, x_b[fo][t]], writes=[x_b[fo][t]])
            xn_sb = view(0, 8, T)
            xn_b = [[P.buf(f"xn{c}_{t}") for t in range(NT)] for c in range(8)]
            for c in range(8):
                for t in range(NT):
                    xn_b[c][t].w = lastA
            for t in range(NT):
                tsl = slice(t * TT, (t + 1) * TT)
                rmsnorm_tile(P, C, x_sb, [x_b[c][t] for c in range(8)], tsl, gffn, gffn_b,
                             lambda c, t=t, tsl=tsl: (xn_sb[:, c, tsl], xn_b[c][t]), scr, scr_b, rs, rs_b)
            NG = DFF // 512
            w1r = Rot([(view(16384 + i * 4096, 8, 512), P.buf(f"w1g{i}", dma=True)) for i in range(2)])
            w2r = Rot([(view(24576 + i * 4096, 4, D), P.buf(f"w2g{i}", dma=True)) for i in range(2)])
            for (_, wb_) in w1r.items + w2r.items:
                wb_.w = lastA
            hr = Rot([(P.sb(f"h{i}", [128, 4, TT], BF16), [P.buf(f"h{i}_{j}") for j in range(4)]) for i in range(2)])
            sqr = Rot([(P.sb(f"sq{i}", [128, TT], F32), P.buf(f"sq{i}")) for i in range(2)])
            w1_v = w1_d.rearrange("(c p) f -> p c f", p=128)
            w2_v = w2_d.rearrange("(c p) f -> p c f", p=128)
            for g in range(NG):
                w1g, w1b = w1r.next()
                w2g, w2b = w2r.next()
                P.dma("pool", w1g, w1_v[:, :, g * 512:(g + 1) * 512], dst=w1b)
                P.dma("pool", w2g, w2_v[:, 4 * g:4 * g + 4, :], dst=w2b)
                for t in range(NT):
                    tsl = slice(t * TT, (t + 1) * TT)
                    h_sb, h_b = hr.next()
                    for j in range(4):
                        ps, psb = C.psA.next()
                        mm_group(P, ps[:], psb,
                                 [(w1g[:, k, j * 128:(j + 1) * 128], xn_sb[:, k, tsl], [w1b, xn_b[k][t]])
                                  for k in range(8)])
                        sq, sqb = sqr.next()
                        P.op("act", lambda e, sq=sq, ps=ps: e.activation(out=sq[:], in_=ps[:], func=AF.Square),
                             reads=[psb], writes=[sqb])
                        P.op("dve", lambda e, sq=sq, ps=ps, h_sb=h_sb, j=j: e.scalar_tensor_tensor(
                            out=h_sb[:, j, :], in0=ps[:], scalar=0.0, in1=sq[:], op0=ALU.is_gt, op1=ALU.mult),
                            reads=[psb, sqb], writes=[h_b[j]])
                    for fo in range(8):
                        ps, psb = C.psB.next()
                        mm_group(P, ps[:], psb,
                                 [(w2g[:, j, fo * 128:(fo + 1) * 128], h_sb[:, j, :], [w2b, h_b[j]])
                                  for j in range(4)])
                        P.op("dve", lambda e, ps=ps, fo=fo, tsl=tsl: e.tensor_tensor(
                            out=x_sb[:, fo, tsl], in0=x_sb[:, fo, tsl], in1=ps[:], op=ALU.add),
                            reads=[psb, x_b[fo][t]], writes=[x_b[fo][t]])
                    if g == NG - 1:
                        tile_tail(t)
        if mode != "p2":
            for t in range(NT):
                tile_tail(t)
        if env:
            env.outs = outs
        else:
            P.finish(outs)
    return nc


RET_G = [1.0 - 2.0 ** (-5.0 - h) for h in range(4)]


def ret_tables(heads):
    half = 128
    inv_freq = (10000.0 ** (-np.arange(half, dtype=np.float32) / np.float32(half))).astype(np.float32)
    ang = (np.arange(S, dtype=np.float32)[None, :] * inv_freq[:, None]).astype(np.float32)
    cos = np.cos(ang.astype(np.float64)).astype(np.float32)
    sin = np.sin(ang.astype(np.float64)).astype(np.float32)
    idx = np.arange(128)
    dt = np.zeros((128, 2, 128), np.float32)
    qdec = np.zeros((128, 2, 512), np.float32)
    kdec = np.zeros((128, 2), np.float32)
    g128 = np.zeros((128, 2), np.float32)
    for i, h in enumerate(heads):
        lg = math.log(RET_G[h])
        t = idx[None, :]
        s = idx[:, None]
        same = (t // 64) == (s // 64)
        later = (t // 64) > (s // 64)
        dmat = np.where(same, np.exp(lg * np.abs(t - s)), np.where(later, np.exp(lg * (t - s)), 0.0))
        dt[:, i, :] = dmat.astype(np.float32)
        qdec[:, i, :] = np.tile(np.exp(lg * (idx + 1.0)), 4)[None, :]
        kdec[:, i] = np.exp(lg * (127.0 - idx))
        g128[:, i] = math.exp(lg * 128.0)
    return dict(cos=cos, sin=sin, dt=dt, qdec=qdec, kdec=kdec, g128=g128)


def build_ret(SEQ=S, env=None):
    nc = env.nc if env else bass.Bass("TRN2", target_bir_lowering=False)
    NTI = SEQ // TT
    dr = env.dr if env else (lambda n, s, dt, kind: nc.dram_tensor(n, list(s), dt, kind=kind).ap())
    xn_d = dr("xnT", [D, SEQ], F32, "ExternalInput")
    wq_d = dr("wq", [D, 512], F32, "ExternalInput")
    wk_d = dr("wk", [D, 512], F32, "ExternalInput")
    wv_d = dr("wv", [D, 1024], F32, "ExternalInput")
    wg_d = dr("wg", [D, 1024], F32, "ExternalInput")
    qg_d = dr("qg", [128, 2], F32, "ExternalInput")
    kg_d = dr("kg", [128, 2], F32, "ExternalInput")
    gnw_d = dr("gnw", [128, 8], F32, "ExternalInput")
    gnb_d = dr("gnb", [128, 8], F32, "ExternalInput")
    cos_d = dr("cos", [128, SEQ], F32, "ExternalInput")
    sin_d = dr("sin", [128, SEQ], F32, "ExternalInput")
    dt_d = dr("dt", [128, 2, 128], F32, "ExternalInput")
    qdec_d = dr("qdec", [128, 2, 512], F32, "ExternalInput")
    kdec_d = dr("kdec", [128, 2], F32, "ExternalInput")
    g128_d = dr("g128", [128, 2], F32, "ExternalInput")
    m_o = dr("mT_out", [D, SEQ], F32, "ExternalOutput")
    with ExitStack() as st:
        if env:
            P = env.P
            env.phase_setup(P)
        else:
            P = Prog(nc, st)
            P.out_sem = P.newsem("outs")
        ones = P.sb("ones32", [128, 128], F32)
        ones_b = P.buf("ones")
        P.op("pool", lambda e: e.memset(ones[:], 1.0), writes=[ones_b])
        ident = P.sb("ident", [128, 128], BF16)
        ident_b = P.buf("ident")
        P.op("pool", lambda e: e.memset(ident[:], 1.0), writes=[ident_b])
        P.op("pool", lambda e: e.affine_select(out=ident[:], in_=ident[:], pattern=[[-1, 128]],
                                               compare_op=ALU.is_equal, fill=0.0, base=0, channel_multiplier=1),
             reads=[ident_b], writes=[ident_b])
        psA = Rot([(P.ps(f"psA{i}", [128, 512]), P.buf(f"psA{i}")) for i in range(2)])
        psS = Rot([(P.ps(f"psS{i}", [128, 512]), P.buf(f"psS{i}")) for i in range(1)])
        psT = Rot([(P.ps(f"psT{i}", [128, 4, 128], BF16), P.buf(f"psT{i}")) for i in range(1)])
        psSc = Rot([(P.ps(f"psSc{i}", [128, 128]), P.buf(f"psSc{i}")) for i in range(1)])
        psY = Rot([(P.ps(f"psY{i}", [128, 4, 128]), P.buf(f"psY{i}")) for i in range(1)])
        psSt = Rot([(P.ps(f"psSt{i}", [128, 512]), P.buf(f"psSt{i}")) for i in range(2)])

        def small(name, d_ap, shape):
            t = P.sb(name, shape, F32)
            b = P.buf(name, dma=True)
            P.dma("sp", t[:], d_ap, dst=b)
            return t, b
        qg, qg_b = small("qg", qg_d, [128, 2])
        kg, kg_b = small("kg", kg_d, [128, 2])
        gnw, gnw_b = small("gnw", gnw_d, [128, 8])
        gnb, gnb_b = small("gnb", gnb_d, [128, 8])
        if env:
            env.flagged_affine(P, gnw, gnw_b, gnb, gnb_b)
        dtt, dtt_b = small("dtt", dt_d, [128, 2, 128])
        qdec, qdec_b = small("qdec", qdec_d, [128, 2, 512])
        kdec, kdec_b = small("kdec", kdec_d, [128, 2])
        g128, g128_b = small("g128", g128_d, [128, 2])

        def wload(name, d_ap, ncol):
            t = P.sb(name, [128, 8, ncol], BF16)
            bs = []
            v = d_ap.rearrange("(c p) f -> p c f", p=128)
            for k2 in range(4):
                b = P.buf(f"{name}{k2}", dma=True)
                P.dma("pool", t[:, 2 * k2:2 * k2 + 2, :], v[:, 2 * k2:2 * k2 + 2, :], dst=b)
                bs.append(b)
            return t, bs
        wq, wq_b = wload("wq", wq_d, 512)
        wk, wk_b = wload("wk", wk_d, 512)
        wv, wv_b = wload("wv", wv_d, 1024)
        wg, wg_b = wload("wg", wg_d, 1024)

        xnr = Rot([(P.sb(f"xn{i}", [128, 8, TT], BF16), P.buf(f"xn{i}", dma=True)) for i in range(2)])
        csr = Rot([(P.sb(f"cs{i}", [128, 2, TT], F32), P.buf(f"cs{i}", dma=True)) for i in range(2)])
        rawq = P.sb("rawq", [128, 4, TT], F32); rawq_b = [P.buf(f"rawq{c}") for c in range(4)]
        rawk = P.sb("rawk", [128, 4, TT], F32); rawk_b = [P.buf(f"rawk{c}") for c in range(4)]
        rsq = P.sb("rsq", [128, 2, TT], F32); rsq_b = [P.buf(f"rsq{h}") for h in range(2)]
        rsk = P.sb("rsk", [128, 2, TT], F32); rsk_b = [P.buf(f"rsk{h}") for h in range(2)]
        raw, raw_b = rawq, rawq_b
        rs2, rs2_b = rsq, rsq_b
        mu2, mu2_b = rsk, rsk_b
        scr = P.sb("scr", [128, 4, TT], F32)
        scr_b = [P.buf(f"scr{c}") for c in range(4)]
        scrq, scrq_b = scr, scr_b
        tmpA = P.sb("tmpA", [128, TT], F32); tmpA_b = P.buf("tmpA")
        tmpB = P.sb("tmpB", [128, TT], F32); tmpB_b = P.buf("tmpB")
        def mkset(i):
            d = Ctx()
            d.QT = P.sb(f"QT{i}", [128, 4, TT], BF16); d.QT_b = [P.buf(f"QT{i}_{c}") for c in range(4)]
            d.QdT = P.sb(f"QdT{i}", [128, 4, TT], BF16); d.QdT_b = [P.buf(f"QdT{i}_{c}") for c in range(4)]
            d.KT = P.sb(f"KT{i}", [128, 4, TT], BF16); d.KT_b = [P.buf(f"KT{i}_{c}") for c in range(4)]
            d.Kd = P.sb(f"Kd{i}", [128, 4, 4, 128], BF16); d.Kd_b = [[P.buf(f"Kd{i}_{b}_{h}") for h in range(2)] for b in range(4)]
            d.Vt = P.sb(f"Vt{i}", [128, 4, 2, 512], BF16); d.Vt_b = [[P.buf(f"Vt{i}_{b}_{h}") for h in range(2)] for b in range(4)]
            return d
        sets = [mkset(0), mkset(1)]
        sg = P.sb("sg", [128, 8, TT], BF16); sg_b = [P.buf(f"sg{c}") for c in range(8)]
        y32 = P.sb("y32", [128, 2, 4, TT], F32)
        y_b = [[[P.buf(f"y{h}_{ec}_{b}") for b in range(4)] for ec in range(4)] for h in range(2)]
        PT = P.sb("PT", [128, 2, 128], BF16); PT_b = [P.buf(f"PT{h}") for h in range(2)]
        S32 = P.sb("S32", [128, 2, 2, 512], F32)
        Sbf = P.sb("Sbf", [128, 2, 2, 512], BF16)
        S_b = [[P.buf(f"S32_{h}_{d}") for d in range(2)] for h in range(2)]
        Sbf_b = [[P.buf(f"Sbf_{h}_{d}") for d in range(2)] for h in range(2)]
        for h in range(2):
            for d in range(2):
                P.op("pool", lambda e, h=h, d=d: e.memset(S32[:, h, d, :], 0.0), writes=[S_b[h][d]])
                P.op("pool", lambda e, h=h, d=d: e.memset(Sbf[:, h, d, :], 0.0), writes=[Sbf_b[h][d]])

        if env:
            xn_src = env.xn_src
        else:
            xn_v = xn_d.rearrange("(c p) t -> p c t", p=128)
            xn_src = lambda ti: xn_v[:, :, ti * TT:(ti + 1) * TT]
            m_ov = m_o.rearrange("(c p) t -> p c t", p=128)
            st_sem = [[P.newsem(f"st_y{h}_{ec}") for ec in range(4)] for h in range(2)]
        outs = []

        def qk_path(w, w_b, gain, gain_b, xn, xn_b, cs, cs_b, is_k, bs, part):
            QT, QT_b, QdT, QdT_b, KT, KT_b = bs.QT, bs.QT_b, bs.QdT, bs.QdT_b, bs.KT, bs.KT_b
            OT, OT_b = (KT, KT_b) if is_k else (QT, QT_b)
            raw, raw_b, scr, scr_b, rs2, rs2_b = (rawk, rawk_b, scrq, scrq_b, rsk, rsk_b) if is_k else (rawq, rawq_b, scrq, scrq_b, rsq, rsq_b)
            if part == "A":
                qk_A(w, w_b, xn, xn_b, is_k, raw, raw_b, scr, scr_b, rs2, rs2_b)
                return
            qk_B(gain, gain_b, cs, cs_b, is_k, raw, raw_b, rs2, rs2_b, OT, OT_b, QT, QT_b, QdT, QdT_b)

        def qk_A(w, w_b, xn, xn_b, is_k, raw, raw_b, scr, scr_b, rs2, rs2_b):
            for c in range(4):
                ps, psb = psA.next()
                mm_group(P, ps[:], psb, [(w[:, k, c * 128:(c + 1) * 128], xn[:, k, :], [w_b[k // 2], xn_b]) for k in range(8)])
                P.op("act", lambda e, c=c, ps=ps: e.copy(raw[:, c, :], ps[:]), reads=[psb], writes=[raw_b[c]])
                P.op("act", lambda e, c=c: e.activation(out=scr[:, c, :], in_=raw[:, c, :], func=AF.Square),
                     reads=[raw_b[c]], writes=[scr_b[c]])
            for h in range(2):
                ps, psb = psSt.next()
                mm_group(P, ps[:], psb, [(ones[:], scr[:, 2 * h + dc, :], [scr_b[2 * h + dc], ones_b]) for dc in range(2)])
                if is_k:
                    P.op("act", lambda e, ps=ps, h=h: e.activation(out=rs2[:, h, :], in_=ps[:], func=AF.Ln, bias=256.0 * EPS, scale=1.0),
                         reads=[psb], writes=[rs2_b[h]])
                else:
                    P.op("act", lambda e, ps=ps, h=h: e.activation(out=rs2[:, h, :], in_=ps[:], func=AF.Ln, bias=EPS, scale=1.0 / 256.0),
                         reads=[psb], writes=[rs2_b[h]])

        def qk_B(gain, gain_b, cs, cs_b, is_k, raw, raw_b, rs2, rs2_b, OT, OT_b, QT, QT_b, QdT, QdT_b):
            for h in range(2):
                P.op("act", lambda e, h=h: e.activation(out=rs2[:, h, :], in_=rs2[:, h, :], func=AF.Exp, scale=-0.5), reads=[rs2_b[h]], writes=[rs2_b[h]])
                for dc in range(2):
                    c = 2 * h + dc
                    P.op("dve", lambda e, c=c, dc=dc, h=h: e.scalar_tensor_tensor(
                        out=raw[:, c, :], in0=raw[:, c, :], scalar=gain[:, dc:dc + 1], in1=rs2[:, h, :], op0=ALU.mult, op1=ALU.mult),
                        reads=[raw_b[c], gain_b, rs2_b[h]], writes=[raw_b[c]])
                c1, c2 = 2 * h, 2 * h + 1
                P.op("dve", lambda e, c1=c1: e.tensor_tensor(out=tmpA[:], in0=raw[:, c1, :], in1=cs[:, 0, :], op=ALU.mult),
                     reads=[raw_b[c1], cs_b], writes=[tmpA_b])
                P.op("dve", lambda e, c2=c2: e.tensor_tensor(out=tmpB[:], in0=raw[:, c2, :], in1=cs[:, 1, :], op=ALU.mult),
                     reads=[raw_b[c2], cs_b], writes=[tmpB_b])
                P.op("dve", lambda e, c1=c1: e.tensor_tensor(out=OT[:, c1, :], in0=tmpA[:], in1=tmpB[:], op=ALU.subtract),
                     reads=[tmpA_b, tmpB_b], writes=[OT_b[c1]])
                P.op("dve", lambda e, c1=c1: e.tensor_tensor(out=tmpA[:], in0=raw[:, c1, :], in1=cs[:, 1, :], op=ALU.mult),
                     reads=[raw_b[c1], cs_b], writes=[tmpA_b])
                P.op("dve", lambda e, c2=c2: e.tensor_tensor(out=tmpB[:], in0=raw[:, c2, :], in1=cs[:, 0, :], op=ALU.mult),
                     reads=[raw_b[c2], cs_b], writes=[tmpB_b])
                P.op("dve", lambda e, c2=c2: e.tensor_tensor(out=OT[:, c2, :], in0=tmpA[:], in1=tmpB[:], op=ALU.add),
                     reads=[tmpA_b, tmpB_b], writes=[OT_b[c2]])
                if not is_k:
                    for c in (c1, c2):
                        P.op("dve", lambda e, c=c, h=h: e.tensor_tensor(out=QdT[:, c, :], in0=QT[:, c, :], in1=qdec[:, h, :], op=ALU.mult),
                             reads=[QT_b[c], qdec_b], writes=[QdT_b[c]])

        tiles = {}

        def load(ti):
            tsl = slice(ti * TT, (ti + 1) * TT)
            xn, xn_b = xnr.next()
            P.dma("pool", xn[:], xn_src(ti), dst=xn_b, extra=(env.xn_wait(ti) if env else ()))
            cs, cs_b = csr.next()
            P.dma("sp", cs[:, 0, :], cos_d[:, tsl], dst=cs_b)
            P.dma("sp", cs[:, 1, :], sin_d[:, tsl], dst=cs_b)
            tiles[ti] = dict(xn=xn, xn_b=xn_b, cs=cs, cs_b=cs_b, bs=sets[ti % 2])

        def qpath(ti, part):
            t = tiles[ti]
            qk_path(wq, wq_b, qg, qg_b, t["xn"], t["xn_b"], t["cs"], t["cs_b"], False, t["bs"], part)

        def kpath(ti, part):
            t = tiles[ti]
            bs = t["bs"]
            qk_path(wk, wk_b, kg, kg_b, t["xn"], t["xn_b"], t["cs"], t["cs_b"], True, bs, part)
            if part == "A":
                return
            for b in range(4):
                bsl = slice(b * 128, (b + 1) * 128)
                pt, ptb = psT.next()
                ev = None
                for c in range(4):
                    ev = P.op("pe", lambda e, c=c, pt=pt, bsl=bsl, bs=bs: e.transpose(pt[:, c, :], bs.KT[:, c, bsl], ident[:]),
                              reads=[bs.KT_b[c], ident_b], writes=[ptb] if c == 0 else [], signal=(c == 3))
                ptb.w = ev
                ptb.r = []
                for h in range(2):
                    P.op("dve", lambda e, b=b, h=h, pt=pt, bs=bs: e.tensor_scalar(
                        out=bs.Kd[:, b, 2 * h:2 * h + 2, :], in0=pt[:, 2 * h:2 * h + 2, :], scalar1=kdec[:, h:h + 1], scalar2=None, op0=ALU.mult),
                        reads=[ptb, kdec_b], writes=[bs.Kd_b[b][h]])

        def vproj(ti):
            t = tiles[ti]
            xn, xn_b, bs = t["xn"], t["xn_b"], t["bs"]
            for b in range(4):
                for h in range(2):
                    ps, psb = psA.next()
                    mm_group(P, ps[:], psb, [(xn[:, k, b * 128:(b + 1) * 128], wv[:, k, h * 512:(h + 1) * 512], [xn_b, wv_b[k // 2]]) for k in range(8)])
                    P.op("act", lambda e, b=b, h=h, ps=ps, bs=bs: e.copy(bs.Vt[:, b, h, :], ps[:]), reads=[psb], writes=[bs.Vt_b[b][h]])

        def gproj(ti):
            t = tiles[ti]
            xn, xn_b = t["xn"], t["xn_b"]
            for c in range(8):
                ps, psb = psA.next()
                mm_group(P, ps[:], psb, [(wg[:, k, c * 128:(c + 1) * 128], xn[:, k, :], [wg_b[k // 2], xn_b]) for k in range(8)])
                P.op("act", lambda e, c=c, ps=ps: e.activation(out=sg[:, c, :], in_=ps[:], func=AF.Silu), reads=[psb], writes=[sg_b[c]])

        def block(ti, b):
            bs = tiles[ti]["bs"]
            KT, KT_b, QT, QT_b, QdT, QdT_b, Kd, Kd_b, Vt, Vt_b = bs.KT, bs.KT_b, bs.QT, bs.QT_b, bs.QdT, bs.QdT_b, bs.Kd, bs.Kd_b, bs.Vt, bs.Vt_b
            bsl = slice(b * 128, (b + 1) * 128)
            for h in range(2):
                sc, scb = psSc.next()
                mm_group(P, sc[:], scb, [(KT[:, 2 * h + dc, bsl], QT[:, 2 * h + dc, bsl], [KT_b[2 * h + dc], QT_b[2 * h + dc]]) for dc in range(2)])
                P.op("dve", lambda e, h=h, sc=sc: e.tensor_tensor(out=PT[:, h, :], in0=sc[:], in1=dtt[:, h, :], op=ALU.mult),
                     reads=[scb, dtt_b], writes=[PT_b[h]])
                py, pyb = psY.next()
                first = True
                ev = None
                for ec in range(4):
                    esl = slice(ec * 128, (ec + 1) * 128)
                    terms = [(Vt[:, b, h, esl], PT[:, h, :], [Vt_b[b][h], PT_b[h]])]
                    terms += [(Sbf[:, h, dc, esl], QdT[:, 2 * h + dc, bsl], [Sbf_b[h][dc], QdT_b[2 * h + dc]]) for dc in range(2)]
                    for i, (l, r, rb) in enumerate(terms):
                        ev = P.op("pe", lambda e, l=l, r=r, i=i, ec=ec, py=py: e.matmul(py[:, ec, :], lhsT=l, rhs=r, start=(i == 0), stop=(i == 2)),
                                  reads=rb, writes=[pyb] if first else [], signal=(ec == 3 and i == 2))
                        first = False
                pyb.w = ev
                pyb.r = []
                P.op("act", lambda e, h=h, bsl=bsl, py=py: e.copy(y32[:, h, :, bsl], py[:]),
                     reads=[pyb], writes=[y_b[h][ec][b] for ec in range(4)])
                for dc in range(2):
                    pst, pstb = psSt.next()
                    mm_group(P, pst[:], pstb, [(Kd[:, b, 2 * h + dc, :], Vt[:, b, h, :], [Kd_b[b][h], Vt_b[b][h]])])
                    P.op("dve", lambda e, h=h, dc=dc, pst=pst: e.scalar_tensor_tensor(
                        out=S32[:, h, dc, :], in0=S32[:, h, dc, :], scalar=g128[:, h:h + 1], in1=pst[:], op0=ALU.mult, op1=ALU.add),
                        reads=[pstb, g128_b, S_b[h][dc]], writes=[S_b[h][dc]])
                    P.op("act", lambda e, h=h, dc=dc: e.copy(Sbf[:, h, dc, :], S32[:, h, dc, :]),
                         reads=[S_b[h][dc]], writes=[Sbf_b[h][dc]])

        def gn_norm(ti):
            sq = [(scr, scr_b), (rawk, rawk_b)]
            for h in range(2):
                for ec in range(4):
                    P.op("act", lambda e, h=h, ec=ec: e.activation(out=sq[h][0][:, ec, :], in_=y32[:, h, ec, :], func=AF.Square),
                         reads=[y_b[h][ec][b] for b in range(4)], writes=[sq[h][1][ec]])
            pss = []
            for h in range(2):
                ps1, ps1b = psA.next()
                mm_group(P, ps1[:], ps1b, [(ones[:], y32[:, h, ec, :], [y_b[h][ec][b] for b in range(4)] + [ones_b]) for ec in range(4)])
                ps2, ps2b = psSt.next()
                mm_group(P, ps2[:], ps2b, [(ones[:], sq[h][0][:, ec, :], [sq[h][1][ec], ones_b]) for ec in range(4)])
                pss.append((ps1, ps1b, ps2, ps2b))
            for h in range(2):
                ps1, ps1b, ps2, ps2b = pss[h]
                P.op("act", lambda e, ps1=ps1, h=h: e.activation(out=mu2[:, h, :], in_=ps1[:], func=AF.Copy, scale=1.0 / 512.0),
                     reads=[ps1b], writes=[mu2_b[h]])
                P.op("dve", lambda e, h=h: e.tensor_tensor(out=tmpA[:], in0=mu2[:, h, :], in1=mu2[:, h, :], op=ALU.mult), reads=[mu2_b[h]], writes=[tmpA_b])
                P.op("dve", lambda e, ps2=ps2, h=h: e.scalar_tensor_tensor(out=rs2[:, h, :], in0=ps2[:], scalar=1.0 / 512.0, in1=tmpA[:], op0=ALU.mult, op1=ALU.subtract),
                     reads=[ps2b, tmpA_b], writes=[rs2_b[h]])
                P.op("act", lambda e, h=h: e.activation(out=rs2[:, h, :], in_=rs2[:, h, :], func=AF.Ln, bias=EPS, scale=1.0), reads=[rs2_b[h]], writes=[rs2_b[h]])
            for h in range(2):
                P.op("act", lambda e, h=h: e.activation(out=rs2[:, h, :], in_=rs2[:, h, :], func=AF.Exp, scale=-0.5), reads=[rs2_b[h]], writes=[rs2_b[h]])
                for ec in range(4):
                    c = 4 * h + ec
                    yb = [y_b[h][ec][b] for b in range(4)]
                    P.op("dve", lambda e, h=h, ec=ec: e.tensor_tensor(out=y32[:, h, ec, :], in0=y32[:, h, ec, :], in1=mu2[:, h, :], op=ALU.subtract),
                         reads=yb + [mu2_b[h]], writes=yb)
                    P.op("dve", lambda e, h=h, ec=ec: e.tensor_tensor(out=y32[:, h, ec, :], in0=y32[:, h, ec, :], in1=rs2[:, h, :], op=ALU.mult),
                         reads=yb + [rs2_b[h]], writes=yb)

        def gn_out(ti):
            tsl = slice(ti * TT, (ti + 1) * TT)
            for h in range(2):
                for ec in range(4):
                    c = 4 * h + ec
                    yb = [y_b[h][ec][b] for b in range(4)]
                    if env:
                        outs.extend(env.emit_gn(P, c, ti, y32[:, h, ec, :], yb, sg[:, c, :], sg_b[c]))
                        continue
                    P.op("act", lambda e, h=h, ec=ec, c=c: e.activation(out=y32[:, h, ec, :], in_=y32[:, h, ec, :], func=AF.Identity,
                                                                      bias=gnb[:, c:c + 1], scale=gnw[:, c:c + 1]),
                         reads=yb + [gnw_b, gnb_b], writes=yb)
                    P.op("dve", lambda e, h=h, ec=ec, c=c: e.tensor_tensor(out=y32[:, h, ec, :], in0=y32[:, h, ec, :], in1=sg[:, c, :], op=ALU.mult),
                         reads=yb + [sg_b[c]], writes=yb)
                    if env:
                        pass
                    else:
                        outs.append(P.op("sp", lambda e, h=h, ec=ec, c=c, tsl=tsl: e.dma_start(out=m_ov[:, c, tsl], in_=y32[:, h, ec, :]),
                                         reads=yb, sem=st_sem[h][ec], inc=16))

        load(0)
        qpath(0, "A"); kpath(0, "A"); qpath(0, "B"); kpath(0, "B")
        vproj(0)
        for ti in range(NTI):
            nxt = ti + 1 < NTI
            if nxt:
                load(ti + 1)
            block(ti, 0)
            if nxt:
                qpath(ti + 1, "A")
            block(ti, 1)
            if nxt:
                qpath(ti + 1, "B")
                kpath(ti + 1, "A")
                vproj(ti + 1)
            block(ti, 2)
            if nxt:
                kpath(ti + 1, "B")
            block(ti, 3)
            gn_norm(ti)
            gproj(ti)
            gn_out(ti)
            if env:
                env.m_done(P, ti)
        if env:
            env.outs = outs
        else:
            P.finish(outs)
    return nc


def sb_tables():
    j = np.arange(128)
    ltri = (j[:, None] >= j[None, :]).astype(np.float32)
    ustr = (j[:, None] < j[None, :]).astype(np.float32)
    oblk = np.zeros((128, 128), np.float32)
    oblk[:64, :64] = 1.0
    oblk[64:, 64:] = 1.0
    t = np.arange(512)
    maskd = np.zeros((128, 4, 512), np.float32)
    for r in range(4):
        maskd[:, r, :] = ((128 * r + j)[:, None] < t[None, :]).astype(np.float32)
    return dict(ltri=ltri, ustr=ustr, oblk=oblk, maskd=maskd)


def build_sb(SEQ=S, env=None):
    nc = env.nc if env else bass.Bass("TRN2", target_bir_lowering=False)
    NTI = SEQ // TT
    NKB = SEQ // 128
    dr = env.dr if env else (lambda n, s, dt, kind: nc.dram_tensor(n, list(s), dt, kind=kind).ap())
    xn_d = dr("xnT", [D, SEQ], F32, "ExternalInput")
    wq_d = dr("wq", [D, 512], F32, "ExternalInput")
    wk_d = dr("wk", [D, 512], F32, "ExternalInput")
    wv_d = dr("wv", [D, 512], F32, "ExternalInput")
    qg_d = dr("qg", [128, 1], F32, "ExternalInput")
    kg_d = dr("kg", [128, 1], F32, "ExternalInput")
    ltri_d = dr("ltri", [128, 128], F32, "ExternalInput")
    ustr_d = dr("ustr", [128, 128], F32, "ExternalInput")
    oblk_d = dr("oblk", [128, 128], F32, "ExternalInput")
    maskd_d = dr("maskd", [128, 4, 512], F32, "ExternalInput")
    y_o = dr("yT_out", [512, SEQ], F32, "ExternalOutput")
    with ExitStack() as st:
        if env:
            P = env.P
            env.phase_setup(P)
        else:
            P = Prog(nc, st)
            P.out_sem = P.newsem("outs")
        psZ = Rot([(P.ps(f"psZ{i}", [128, 512]), P.buf(f"psZ{i}")) for i in range(2)])
        psAcc = Rot([(P.ps(f"psAcc{i}", [128, 512]), P.buf(f"psAcc{i}")) for i in range(2)])
        psY = Rot([(P.ps(f"psY{i}", [64, 512]), P.buf(f"psY{i}")) for i in range(4)])
        psS = psAcc

        def small(name, d_ap, shape, dt=F32, eng="sp"):
            t = P.sb(name, shape, dt)
            b = P.buf(name, dma=True)
            P.dma(eng, t[:], d_ap, dst=b)
            return t, b
        qg, qg_b = small("qg", qg_d, [128, 1])
        kg, kg_b = small("kg", kg_d, [128, 1])
        oblk, oblk_b = small("oblk", oblk_d, [128, 128])
        maskd, maskd_b = small("maskd", maskd_d, [128, 4, 512])
        ltri, ltri_b = small("ltri", ltri_d, [128, 128], BF16, "pool")
        ustr, ustr_b = small("ustr", ustr_d, [128, 128], BF16, "pool")

        def wload(name, d_ap, ncol):
            t = P.sb(name, [128, 8, ncol], BF16)
            bs = []
            v = d_ap.rearrange("(c p) f -> p c f", p=128)
            for k2 in range(4):
                b = P.buf(f"{name}{k2}", dma=True)
                P.dma("pool", t[:, 2 * k2:2 * k2 + 2, :], v[:, 2 * k2:2 * k2 + 2, :], dst=b)
                bs.append(b)
            return t, bs
        wq, wq_b = wload("wq", wq_d, 512)
        wk, wk_b = wload("wk", wk_d, 512)
        wv, wv_b = wload("wv", wv_d, 512)

        xnr = Rot([(P.sb(f"xn{i}", [128, 8, TT], BF16), P.buf(f"xn{i}", dma=True)) for i in range(2)])
        raw = P.sb("raw", [128, 4, TT], F32); raw_b = [P.buf(f"raw{c}") for c in range(4)]
        scr = P.sb("scr", [128, 4, TT], F32); scr_b = [P.buf(f"scr{c}") for c in range(4)]
        rs = P.sb("rs", [128, TT], F32); rs_b = P.buf("rs")
        QT = P.sb("QT", [128, 4, TT], BF16); QT_b = [P.buf(f"QT{c}") for c in range(4)]
        KT = P.sb("KT", [128, 4, SEQ], BF16); KT_b = [[P.buf(f"KT{c}_{t}") for t in range(NTI)] for c in range(4)]
        V = P.sb("V", [128, NKB, 512], BF16); V_b = [P.buf(f"V{kb}") for kb in range(NKB)]
        er = Rot([(P.sb(f"e{i}", [128, TT], F32), P.buf(f"e{i}")) for i in range(8)])
        wr = Rot([(P.sb(f"w{i}", [128, TT], F32), P.buf(f"w{i}")) for i in range(3)])
        spr = Rot([(P.sb(f"sp{i}", [128, TT], BF16), P.buf(f"sp{i}")) for i in range(5)])
        Ar = Rot([(P.sb(f"A{i}", [128, TT], BF16), P.buf(f"A{i}")) for i in range(4)])
        srun_rots = [Rot([(P.sb(f"srun{j}_{i}", [128, TT], BF16), P.buf(f"srun{j}_{i}")) for i in range(3)]) for j in range(2)]
        ones_bf = P.sb("ones_bf", [128, 128], BF16); ones_bf_b = P.buf("ones_bf")
        P.op("pool", lambda e: e.memset(ones_bf[:], 1.0), writes=[ones_bf_b])
        yr = [(P.sb(f"yo{i}", [64, TT], F32), P.buf(f"yo{i}"), P.newsem(f"st_yo{i}")) for i in range(4)]
        yri = [0]

        if env:
            xn_src = env.xn_src
        else:
            xn_v = xn_d.rearrange("(c p) t -> p c t", p=128)
            xn_src = lambda ti: xn_v[:, :, ti * TT:(ti + 1) * TT]
        outs = []

        def qk_proj(w, w_b, gain, gain_b, xn, xn_b, out_fn):
            for c in range(4):
                ps, psb = psZ.next()
                mm_group(P, ps[:], psb, [(w[:, k, c * 128:(c + 1) * 128], xn[:, k, :], [w_b[k // 2], xn_b]) for k in range(8)])
                P.op("act", lambda e, c=c, ps=ps: e.copy(raw[:, c, :], ps[:]), reads=[psb], writes=[raw_b[c]])
                P.op("act", lambda e, c=c: e.activation(out=scr[:, c, :], in_=raw[:, c, :], func=AF.Square),
                     reads=[raw_b[c]], writes=[scr_b[c]])
                ps2, ps2b = psS.next()
                mm_group(P, ps2[:], ps2b, [(oblk[:], scr[:, c, :], [scr_b[c], oblk_b])])
                P.op("act", lambda e, ps2=ps2: e.activation(out=rs[:], in_=ps2[:], func=AF.Ln, bias=EPS, scale=1.0 / 64.0),
                     reads=[ps2b], writes=[rs_b])
                P.op("act", lambda e: e.activation(out=rs[:], in_=rs[:], func=AF.Exp, scale=-0.5), reads=[rs_b], writes=[rs_b])
                oap, ob = out_fn(c)
                P.op("dve", lambda e, c=c, oap=oap: e.scalar_tensor_tensor(
                    out=oap, in0=raw[:, c, :], scalar=gain[:, 0:1], in1=rs[:], op0=ALU.mult, op1=ALU.mult),
                    reads=[raw_b[c], gain_b, rs_b], writes=[ob])

        for ti in range(NTI):
            tsl = slice(ti * TT, (ti + 1) * TT)
            xn, xn_b = xnr.next()
            P.dma("pool", xn[:], xn_src(ti), dst=xn_b, extra=(env.xn_wait(ti) if env else ()))
            qk_proj(wq, wq_b, qg, qg_b, xn, xn_b, lambda c: (QT[:, c, :], QT_b[c]))
            qk_proj(wk, wk_b, kg, kg_b, xn, xn_b, lambda c, ti=ti, tsl=tsl: (KT[:, c, tsl], KT_b[c][ti]))
            for b in range(4):
                kb = 4 * ti + b
                ps, psb = psZ.next()
                mm_group(P, ps[:], psb, [(xn[:, k, b * 128:(b + 1) * 128], wv[:, k, :], [xn_b, wv_b[k // 2]]) for k in range(8)])
                P.op("act", lambda e, kb=kb, ps=ps: e.copy(V[:, kb, :], ps[:]), reads=[psb], writes=[V_b[kb]])
            nkb = 4 * ti + 4
            units = []
            for c in range(4):
                pys = [psY.next() for _ in range(2)]
                for step in range(nkb):
                    for j in range(2):
                        units.append(dict(c=c, j=j, step=step, kb=nkb - 1 - step, psl=slice(64 * j, 64 * j + 64),
                                          py=pys[j][0], pyb=pys[j][1]))
            srun_state = {}

            def stage(k, u):
                c, j, step, kb = u["c"], u["j"], u["step"], u["kb"]
                ksl = slice(kb * 128, (kb + 1) * 128)
                r = kb - 4 * ti
                if k == 0:
                    z, zb = psZ.next()
                    mm_group(P, z[:], zb, [(KT[u["psl"], c, ksl], QT[u["psl"], c, :], [KT_b[c][kb // 4], QT_b[c]])])
                    u["z"], u["zb"] = z, zb
                elif k == 1:
                    e_sb, e_b = er.next()
                    P.op("act", lambda e, e_sb=e_sb, z=u["z"]: e.activation(out=e_sb[:], in_=z[:], func=AF.Exp, scale=0.125),
                         reads=[u["zb"]], writes=[e_b])
                    if r >= 0:
                        P.op("dve", lambda e, e_sb=e_sb, r=r: e.tensor_tensor(out=e_sb[:], in0=e_sb[:], in1=maskd[:, r, :], op=ALU.mult),
                             reads=[e_b, maskd_b], writes=[e_b])
                    u["e"], u["eb"] = e_sb, e_b
                elif k == 2:
                    sp_sb, sp_b = spr.next()
                    P.op("act", lambda e, sp_sb=sp_sb, e_sb=u["e"]: e.activation(out=sp_sb[:], in_=e_sb[:], func=AF.Ln, bias=1.0, scale=1.0),
                         reads=[u["eb"]], writes=[sp_b])
                    u["sp"], u["spb"] = sp_sb, sp_b
                elif k == 3:
                    acc, accb = psAcc.next()
                    terms = [(ltri[:], u["sp"][:], [u["spb"], ltri_b])]
                    if step > 0:
                        srun, srunb = srun_state[(c, j)]
                        terms.append((ones_bf[:], srun[:], [srunb, ones_bf_b]))
                    mm_group(P, acc[:], accb, terms)
                    u["acc"], u["accb"] = acc, accb
                elif k == 4:
                    if kb > 0:
                        nsr, nsrb = srun_rots[j].next()
                        if step == 0:
                            P.op("dve", lambda e, nsr=nsr, sp_sb=u["sp"]: e.tensor_copy(nsr[:], sp_sb[:]),
                                 reads=[u["spb"]], writes=[nsrb])
                        else:
                            old, oldb = srun_state[(c, j)]
                            P.op("dve", lambda e, nsr=nsr, sp_sb=u["sp"], old=old: e.tensor_tensor(out=nsr[:], in0=old[:], in1=sp_sb[:], op=ALU.add),
                                 reads=[u["spb"], oldb], writes=[nsrb])
                        srun_state[(c, j)] = (nsr, nsrb)
                    w_sb, w_b = wr.next()
                    P.op("act", lambda e, w_sb=w_sb, acc=u["acc"]: e.activation(out=w_sb[:], in_=acc[:], func=AF.Exp, scale=-1.0),
                         reads=[u["accb"]], writes=[w_b])
                    u["w"], u["wb"] = w_sb, w_b
                elif k == 5:
                    A_sb, A_b = Ar.next()
                    P.op("dve", lambda e, A_sb=A_sb, e_sb=u["e"], w_sb=u["w"]: e.tensor_tensor(out=A_sb[:], in0=e_sb[:], in1=w_sb[:], op=ALU.mult),
                         reads=[u["eb"], u["wb"]], writes=[A_b])
                    u["A"], u["Ab"] = A_sb, A_b
                elif k == 6:
                    py = u["py"]
                    P.op("pe", lambda e, py=py, A_sb=u["A"], kb=kb, j=j, c=c, step=step: e.matmul(
                        py[:], lhsT=V[:, kb, c * 128 + 64 * j: c * 128 + 64 * j + 64], rhs=A_sb[:], start=(step == 0), stop=(kb == 0)),
                        reads=[u["Ab"], V_b[kb]], writes=[u["pyb"]])
                    if kb == 0:
                        row = c * 128 + 64 * j
                        if env:
                            outs.extend(env.emit_m(P, row, 64, ti, py[:], [u["pyb"]]))
                            return
                        yo, yo_b, yo_sem = yr[yri[0] % 4]
                        yri[0] += 1
                        P.op("act", lambda e, yo=yo, py=py: e.copy(yo[:], py[:]), reads=[u["pyb"]], writes=[yo_b])
                        if env:
                            pass
                        else:
                            outs.append(P.op("sp", lambda e, yo=yo, row=row, tsl=tsl: e.dma_start(out=y_o[row:row + 64, tsl], in_=yo[:]),
                                             reads=[yo_b], sem=yo_sem, inc=16))

            NS = 7
            SK = [0, 2, 3, 4, 6, 7, 8]
            for slot in range(len(units) + SK[-1]):
                for k in reversed(range(NS)):
                    ui = slot - SK[k]
                    if 0 <= ui < len(units):
                        stage(k, units[ui])
            if env:
                env.m_done(P, ti)
        if env:
            env.outs = outs
        else:
            P.finish(outs)
    return nc


CW = 31
HALO = 32


def build_conv(T=TOK, env=None):
    nc = env.nc if env else bass.Bass("TRN2", target_bir_lowering=False)
    NT = T // TT
    dr = env.dr if env else (lambda n, s, dt, kind: nc.dram_tensor(n, list(s), dt, kind=kind).ap())
    xT_d = dr("xT", [D, T], F32, "ExternalInput")
    xnh_d = dr("xnhT", [D, HALO + T], F32, "ExternalInput")
    flag_d = dr("flag", [128, 1], F32, "ExternalInput")
    pw1_d = dr("pw1_w", [D, 2 * D], F32, "ExternalInput")
    pw1b_d = dr("pw1_b", [128, 16], F32, "ExternalInput")
    dww_d = dr("dw_w", [128, CW * 8], F32, "ExternalInput")
    dwb_d = dr("dw_b", [128, 8], F32, "ExternalInput")
    lnw_d = dr("ln_w", [128, 8], F32, "ExternalInput")
    lnb_d = dr("ln_b", [128, 8], F32, "ExternalInput")
    pw2_d = dr("pw2_w", [D, D], F32, "ExternalInput")
    pw2b_d = dr("pw2_b", [128, 8], F32, "ExternalInput")
    x_o = dr("x_out", [D, T], F32, "ExternalOutput")
    with ExitStack() as st:
        if env:
            P = env.P
        else:
            P = Prog(nc, st)
            P.out_sem = P.newsem("outs")
        ones = P.sb("ones32", [128, 128], F32); ones_b = P.buf("ones")
        P.op("pool", lambda e: e.memset(ones[:], 1.0), writes=[ones_b])
        ident = P.sb("ident", [128, 128], F32); ident_b = P.buf("ident")
        P.op("pool", lambda e: e.memset(ident[:], 1.0), writes=[ident_b])
        P.op("pool", lambda e: e.affine_select(out=ident[:], in_=ident[:], pattern=[[-1, 128]],
                                               compare_op=ALU.is_equal, fill=0.0, base=0, channel_multiplier=1),
             reads=[ident_b], writes=[ident_b])
        psA = Rot([(P.ps(f"psA{i}", [128, 512]), P.buf(f"psA{i}")) for i in range(2)])
        psG = Rot([(P.ps(f"psG{i}", [128, 512]), P.buf(f"psG{i}")) for i in range(2)])
        psC = Rot([(P.ps(f"psC{i}", [128, 512]), P.buf(f"psC{i}")) for i in range(2)])
        psS = Rot([(P.ps(f"psS{i}", [128, 512]), P.buf(f"psS{i}")) for i in range(2)])

        def small(name, d_ap, shape):
            t = P.sb(name, shape, F32)
            b = P.buf(name, dma=True)
            P.dma("sp", t[:], d_ap, dst=b)
            return t, b
        flag, flag_b = small("flag", flag_d, [128, 1])
        pw1b, pw1b_b = small("pw1b", pw1b_d, [128, 16])
        dww, dww_b = small("dww", dww_d, [128, CW * 8])
        dwb, dwb_b = small("dwb", dwb_d, [128, 8])
        lnw, lnw_b = small("lnw", lnw_d, [128, 8])
        lnb, lnb_b = small("lnb", lnb_d, [128, 8])
        pw2b, pw2b_b = small("pw2b", pw2b_d, [128, 8])

        WB = P.sb("WB", [128, 32768], BF16)
        pw1 = WB[:, 0:16384].rearrange("p (a b) -> p a b", a=8)
        pw1_b = [P.buf(f"pw1_{k}", dma=True) for k in range(8)]
        pw1_v = pw1_d.rearrange("(c p) f -> p c f", p=128)
        for k in range(8):
            P.dma("pool", pw1[:, k, :], pw1_v[:, k, :], dst=pw1_b[k])
        pw2 = P.sb("pw2", [128, 8, D], BF16)
        pw2_b = [P.buf(f"pw2_{k}", dma=True) for k in range(4)]
        pw2_v = pw2_d.rearrange("(c p) f -> p c f", p=128)
        for k2 in range(4):
            P.dma("pool", pw2[:, 2 * k2:2 * k2 + 2, :], pw2_v[:, 2 * k2:2 * k2 + 2, :], dst=pw2_b[k2])

        h = P.sb("h", [128, 8, HALO + T], BF16)
        h_b = [[P.buf(f"h{c}_{t}") for t in range(NT + 1)] for c in range(8)]
        xnt = P.sb("xnt", [128, 8, HALO + TT], BF16); xnt_b = P.buf("xnt", dma=True)
        sgr = Rot([(P.sb(f"sgm{i}", [128, TT], F32), P.buf(f"sgm{i}")) for i in range(2)])
        if env:
            xn_halo = env.xn_halo
            xn_main = env.xn_main
        else:
            xnh_v = xnh_d.rearrange("(c p) t -> p c t", p=128)
            xn_halo = lambda: xnh_v[:, :, 0:HALO]
            xn_main = lambda t: xnh_v[:, :, HALO + t * TT:HALO + (t + 1) * TT]
        last_pw1 = None
        for t in range(NT):
            if t == 0:
                P.dma("pool", xnt[:, :, 0:HALO], xn_halo(), dst=xnt_b, extra=(env.xn_wait(NT - 1) if env else ()))
                P.dma("pool", xnt[:, :, HALO:HALO + TT], xn_main(0), dst=xnt_b)
                segs = [(0, HALO, 0), (HALO, TT, 1)]
            else:
                P.dma("pool", xnt[:, :, HALO:HALO + TT], xn_main(t), dst=xnt_b)
                segs = [(HALO, TT, t + 1)]
            for (off, n, hidx) in segs:
                col0 = 0 if hidx == 0 else HALO + (hidx - 1) * TT
                for c in range(8):
                    pa, pab = psA.next()
                    mm_group(P, pa[:, 0:n], pab, [(pw1[:, k, c * 128:(c + 1) * 128], xnt[:, k, off:off + n], [pw1_b[k], xnt_b]) for k in range(8)])
                    pg, pgb = psG.next()
                    last_pw1 = mm_group(P, pg[:, 0:n], pgb, [(pw1[:, k, D + c * 128:D + (c + 1) * 128], xnt[:, k, off:off + n], [pw1_b[k], xnt_b]) for k in range(8)])
                    sg_, sg_b = sgr.next()
                    P.op("act", lambda e, sg_=sg_, pg=pg, n=n, c=c: e.activation(out=sg_[:, 0:n], in_=pg[:, 0:n], func=AF.Sigmoid,
                                                                                bias=pw1b[:, 8 + c:9 + c], scale=1.0),
                         reads=[pgb, pw1b_b], writes=[sg_b])
                    P.op("dve", lambda e, sg_=sg_, pa=pa, n=n, c=c, col0=col0: e.scalar_tensor_tensor(
                        out=h[:, c, col0:col0 + n], in0=pa[:, 0:n], scalar=pw1b[:, c:c + 1], in1=sg_[:, 0:n], op0=ALU.add, op1=ALU.mult),
                        reads=[pab, sg_b, pw1b_b], writes=[h_b[c][hidx]])
                    if hidx == 0:
                        P.op("dve", lambda e, c=c: e.tensor_scalar(out=h[:, c, 0:HALO], in0=h[:, c, 0:HALO], scalar1=flag[:, 0:1], scalar2=None, op0=ALU.mult),
                             reads=[h_b[c][0], flag_b], writes=[h_b[c][0]])
        diag = WB[:, 0:CW * 8 * 128].rearrange("p (a b) -> p a b", a=CW * 8)
        diag_b = [P.buf(f"diag{c}") for c in range(8)]
        for c in range(8):
            diag_b[c].w = last_pw1
            ev = None
            for j in range(CW):
                eng = "dve"
                ev = P.op(eng, lambda e, c=c, j=j: e.tensor_scalar(out=diag[:, c * CW + j, :], in0=ident[:], scalar1=dww[:, j * 8 + c:j * 8 + c + 1], scalar2=None, op0=ALU.mult),
                          reads=[ident_b, dww_b], writes=[], extra=[last_pw1])
                diag_b[c].r.append(ev)
            diag_b[c].w = None
            diag_b[c].wlist = list(diag_b[c].r)
            diag_b[c].r = []
        cv = P.sb("cv", [128, 8, TT], F32); cv_b = [P.buf(f"cv{c}") for c in range(8)]
        scr = P.sb("scr", [128, 8, TT], F32); scr_b = [P.buf(f"scr{c}") for c in range(8)]
        u = P.sb("u", [128, 8, TT], BF16); u_b = [P.buf(f"u{c}") for c in range(8)]
        xt = P.sb("xt", [128, 8, TT], F32); xt_b = P.buf("xt", dma=True)
        xt_cb = [P.buf(f"xt{c}") for c in range(8)]
        st_sem = [P.newsem(f"st_x{c}") for c in range(8)]
        mu = P.sb("mu", [128, TT], F32); mu_b = P.buf("mu")
        rs = P.sb("rs", [128, TT], F32); rs_b = P.buf("rs")
        tmp = P.sb("tmp", [128, TT], F32); tmp_b = P.buf("tmp")
        xT_v = xT_d.rearrange("(c p) t -> p c t", p=128)
        x_ov = x_o.rearrange("(c p) t -> p c t", p=128)
        outs = []
        for t in range(NT):
            tsl = slice(t * TT, (t + 1) * TT)
            evl = P.op("sp", lambda e, tsl=tsl: e.dma_start(out=xt[:], in_=xT_v[:, :, tsl]), reads=[], writes=xt_cb, sem=P.semof(xt_b, "sp"), inc=16)
            for c in range(8):
                pc, pcb = psC.next()
                hb = [h_b[c][t], h_b[c][t + 1]]
                n = CW
                evm = None
                for j in range(CW):
                    evm = P.op("pe", lambda e, pc=pc, c=c, j=j, t=t: e.matmul(pc[:], lhsT=diag[:, c * CW + j, :], rhs=h[:, c, t * TT + 2 + j:t * TT + 2 + j + TT],
                                                                        start=(j == 0), stop=(j == CW - 1)),
                               reads=hb, writes=[pcb] if j == 0 else [], signal=(j == CW - 1), extra=diag_b[c].wlist)
                pcb.w = evm
                pcb.r = []
                P.op("act", lambda e, pc=pc, c=c: e.activation(out=cv[:, c, :], in_=pc[:], func=AF.Identity, bias=dwb[:, c:c + 1], scale=1.0),
                     reads=[pcb, dwb_b], writes=[cv_b[c]])
                P.op("act", lambda e, c=c: e.activation(out=scr[:, c, :], in_=cv[:, c, :], func=AF.Square), reads=[cv_b[c]], writes=[scr_b[c]])
            ps1, ps1b = psS.next()
            mm_group(P, ps1[:], ps1b, [(ones[:], cv[:, c, :], [cv_b[c], ones_b]) for c in range(8)])
            ps2, ps2b = psS.next()
            mm_group(P, ps2[:], ps2b, [(ones[:], scr[:, c, :], [scr_b[c], ones_b]) for c in range(8)])
            P.op("act", lambda e, ps1=ps1: e.activation(out=mu[:], in_=ps1[:], func=AF.Copy, scale=1.0 / D), reads=[ps1b], writes=[mu_b])
            P.op("dve", lambda e: e.tensor_tensor(out=tmp[:], in0=mu[:], in1=mu[:], op=ALU.mult), reads=[mu_b], writes=[tmp_b])
            P.op("dve", lambda e, ps2=ps2: e.scalar_tensor_tensor(out=rs[:], in0=ps2[:], scalar=1.0 / D, in1=tmp[:], op0=ALU.mult, op1=ALU.subtract),
                 reads=[ps2b, tmp_b], writes=[rs_b])
            P.op("act", lambda e: e.activation(out=rs[:], in_=rs[:], func=AF.Ln, bias=EPS, scale=1.0), reads=[rs_b], writes=[rs_b])
            P.op("act", lambda e: e.activation(out=rs[:], in_=rs[:], func=AF.Exp, scale=-0.5), reads=[rs_b], writes=[rs_b])
            for c in range(8):
                P.op("dve", lambda e, c=c: e.tensor_tensor(out=cv[:, c, :], in0=cv[:, c, :], in1=mu[:], op=ALU.subtract), reads=[cv_b[c], mu_b], writes=[cv_b[c]])
                P.op("dve", lambda e, c=c: e.tensor_tensor(out=cv[:, c, :], in0=cv[:, c, :], in1=rs[:], op=ALU.mult), reads=[cv_b[c], rs_b], writes=[cv_b[c]])
                P.op("act", lambda e, c=c: e.activation(out=u[:, c, :], in_=cv[:, c, :], func=AF.Silu, bias=lnb[:, c:c + 1], scale=lnw[:, c:c + 1]),
                     reads=[cv_b[c], lnw_b, lnb_b], writes=[u_b[c]])
            for fo in range(8):
                po, pob = psA.next()
                mm_group(P, po[:], pob, [(pw2[:, k, fo * 128:(fo + 1) * 128], u[:, k, :], [pw2_b[k // 2], u_b[k]]) for k in range(8)])
                P.op("dve", lambda e, po=po, fo=fo: e.scalar_tensor_tensor(out=xt[:, fo, :], in0=po[:], scalar=pw2b[:, fo:fo + 1], in1=xt[:, fo, :], op0=ALU.add, op1=ALU.add),
                     reads=[pob, pw2b_b, xt_cb[fo]], writes=[xt_cb[fo]])
                outs.append(P.op("sp", lambda e, fo=fo, tsl=tsl: e.dma_start(out=x_ov[:, fo, tsl], in_=xt[:, fo, :]), reads=[xt_cb[fo]], sem=st_sem[fo], inc=16))
        if env:
            env.outs = outs
        else:
            P.finish(outs)
    return nc


def conv_inputs(xT, xnhT, flagv, pw1_w, pw1_b, dw_w, dw_b, ln_w, ln_b, pw2_w, pw2_b):
    f = lambda a: np.ascontiguousarray(a, dtype=np.float32)
    dww = np.asarray(dw_w, np.float32).reshape(CW, 8, 128).transpose(2, 0, 1).reshape(128, CW * 8)
    return {"xT": f(xT), "xnhT": f(xnhT), "flag": np.full((128, 1), flagv, np.float32),
            "pw1_w": f(pw1_w), "pw1_b": pcol(pw1_b), "dw_w": f(dww), "dw_b": pcol(dw_b),
            "ln_w": pcol(ln_w), "ln_b": pcol(ln_b), "pw2_w": f(pw2_w), "pw2_b": pcol(pw2_b)}


class Env:
    def __init__(self, nc, P, T):
        self.nc = nc
        self.P = P
        self.T = T
        self.io = {}
        self.outs = []
        self.flags_d = None
        self.mz2d = None
        self.fmy = 0
        self.m_pending = []
        self.m_evs = {}
        self.nocc = False
        self.ag_ev = {}
        self.rs_ev = {}

    def xn_wait(self, ti):
        nt = self.T // TT
        ev = self.ag_ev.get(ti % nt)
        return [ev] if ev is not None else []

    def m_wait(self, t):
        ev = self.rs_ev.get(t)
        return [ev] if ev is not None else []

    def dr(self, name, shape, dt, kind):
        return self.io.get(name)

    def phase_setup(self, P):
        self.flag = P.sb("flags", [128, 2], F32)
        self.flag_b = P.buf("flags", dma=True)
        P.dma("sp", self.flag[:], self.flags_d, dst=self.flag_b)
        self.stages = Rot([(P.sb(f"stg{i}", [128, TT], BF16), P.buf(f"stg{i}"), P.newsem(f"st_stg{i}")) for i in range(3)])

    def flagged_affine(self, P, gnw, gnw_b, gnb, gnb_b):
        self.gnwf = P.sb("gnwf", [128, 2, 8], F32); self.gnwf_b = P.buf("gnwf")
        self.gnbf = P.sb("gnbf", [128, 2, 8], F32); self.gnbf_b = P.buf("gnbf")
        for j in range(2):
            P.op("dve", lambda e, j=j: e.tensor_scalar(out=self.gnwf[:, j, :], in0=gnw[:], scalar1=self.flag[:, j:j + 1], scalar2=None, op0=ALU.mult),
                 reads=[gnw_b, self.flag_b], writes=[self.gnwf_b])
            P.op("dve", lambda e, j=j: e.tensor_scalar(out=self.gnbf[:, j, :], in0=gnb[:], scalar1=self.flag[:, j:j + 1], scalar2=None, op0=ALU.mult),
                 reads=[gnb_b, self.flag_b], writes=[self.gnbf_b])
        self.aff_tmp = Rot([(P.sb(f"afft{i}", [128, TT], F32), P.buf(f"afft{i}")) for i in range(2)])

    def emit_gn(self, P, c, ti, y_ap, y_bufs, sg_ap, sg_buf):
        nt = self.T // TT
        h, tl = ti // nt, ti % nt
        evs = []
        for j in range(2):
            tmp, tmp_b = self.aff_tmp.next()
            P.op("act", lambda e, tmp=tmp, j=j: e.activation(out=tmp[:], in_=y_ap, func=AF.Identity,
                                                            bias=self.gnbf[:, j, c:c + 1], scale=self.gnwf[:, j, c:c + 1]),
                 reads=list(y_bufs) + [self.gnwf_b, self.gnbf_b], writes=[tmp_b])
            stg, stg_b, stg_sem = self.stages.next()
            P.op("dve", lambda e, tmp=tmp, stg=stg: e.tensor_tensor(out=stg[:], in0=tmp[:], in1=sg_ap, op=ALU.mult),
                 reads=[tmp_b, sg_buf], writes=[stg_b])
            r0 = (h * 2 + j) * self.fmy + c * 128
            mz = self.mz2d[tl]
            evs.append(P.op("sp", lambda e, stg=stg, r0=r0, mz=mz: e.dma_start(out=mz[r0:r0 + 128, :], in_=stg[:]),
                            reads=[stg_b], sem=stg_sem, inc=16))
        self.m_pending.extend(evs)
        return evs

    def gather_tile(self, P, t, evs):
        if self.nocc:
            return
        self.ag_ev[t] = P.collective("AllGather", ALU.bypass, self.groups, self.xn_my[t], self.xn_full[t], extra=evs)

    def scatter_tile(self, P, tl, evs):
        if self.nocc:
            return
        self.rs_ev[tl] = P.collective("ReduceScatter", ALU.add, self.groups, self.mz2d[tl], self.mrs_out[tl], extra=evs)

    def m_done(self, P, ti):
        nt = self.T // TT
        h, tl = ti // nt, ti % nt
        self.m_evs.setdefault(tl, []).extend(self.m_pending)
        self.m_pending = []
        if h == 1:
            self.scatter_tile(P, tl, self.m_evs.pop(tl))

    def xn_src(self, ti):
        nt = self.T // TT
        rank, tl = ti // nt, ti % nt
        return self.xn_full[tl][rank * D:(rank + 1) * D, :].rearrange("(c p) t -> p c t", p=128)

    def xn_halo(self):
        nt = self.T // TT
        return self.xn_full[nt - 1][0:D, TT - HALO:TT].rearrange("(c p) t -> p c t", p=128)

    def xn_main(self, t):
        return self.xn_my[t].rearrange("(c p) t -> p c t", p=128)

    def xn_dst(self, t):
        return self.xn_my[t].rearrange("(c p) t -> p c t", p=128)

    def m_src(self, t):
        return self.mrs[t].rearrange("(c p) t -> p c t", p=128)

    def emit_m(self, P, row0, nrows, ti, src_ap, src_bufs):
        nt = self.T // TT
        h, tl = ti // nt, ti % nt
        evs = []
        for j in range(2):
            stg, stg_b, stg_sem = self.stages.next()
            P.op("act", lambda e, stg=stg, j=j: e.activation(out=stg[0:nrows, :], in_=src_ap, func=AF.Identity,
                                                            bias=0.0, scale=self.flag[0:nrows, j:j + 1]),
                 reads=list(src_bufs) + [self.flag_b], writes=[stg_b])
            r0 = (h * 2 + j) * self.fmy + row0
            mz = self.mz2d[tl]
            evs.append(P.op("sp", lambda e, stg=stg, r0=r0, mz=mz: e.dma_start(out=mz[r0:r0 + nrows, :], in_=stg[0:nrows, :]),
                            reads=[stg_b], sem=stg_sem, inc=16))
        self.m_pending.extend(evs)
        return evs


FUSED_INPUTS = None


def build_fused(T=TOK, groups=None):
    SEQ = 2 * T
    if groups is None:
        groups = [[2 * i, 2 * i + 1] for i in range(NCORES // 2)]
    nc = bass.Bass("TRN2", target_bir_lowering=False)
    ext = {}

    def inp(name, shape):
        ext[name] = nc.dram_tensor(name, list(shape), F32, kind="ExternalInput").ap()
        return ext[name]

    def internal(name, shape, dt):
        return nc.dram_tensor(name, list(shape), dt).ap()

    xT_in = inp("xT", [D, T])
    flags = inp("flags", [128, 2])
    flagc = inp("flagc", [128, 1])
    g_mix = [inp(f"g_mix{i}", [128, 8]) for i in range(4)]
    g_ffn = [inp(f"g_ffn{i}", [128, 8]) for i in range(4)]
    g_fin = inp("g_final", [128, 8])
    w1 = [inp(f"w1_{i}", [D, DFF]) for i in range(4)]
    w2 = [inp(f"w2_{i}", [DFF, D]) for i in range(4)]
    ret = []
    for j in range(2):
        ret.append(dict(wq=inp(f"r{j}_wq", [D, 512]), wk=inp(f"r{j}_wk", [D, 512]), wv=inp(f"r{j}_wv", [D, 1024]),
                        wg=inp(f"r{j}_wg", [D, 1024]), qg=inp(f"r{j}_qg", [128, 2]), kg=inp(f"r{j}_kg", [128, 2]),
                        gnw=inp(f"r{j}_gnw", [128, 8]), gnb=inp(f"r{j}_gnb", [128, 8]), w_out=inp(f"r{j}_wout", [2 * D, D])))
    rtab = dict(cos=inp("cos", [128, SEQ]), sin=inp("sin", [128, SEQ]), dt=inp("dt", [128, 2, 128]),
                qdec=inp("qdec", [128, 2, 512]), kdec=inp("kdec", [128, 2]), g128=inp("g128", [128, 2]))
    cv = dict(pw1_w=inp("pw1_w", [D, 2 * D]), pw1_b=inp("pw1_b", [128, 16]), dw_w=inp("dw_w", [128, CW * 8]),
              dw_b=inp("dw_b", [128, 8]), ln_w=inp("ln_w", [128, 8]), ln_b=inp("ln_b", [128, 8]),
              pw2_w=inp("pw2_w", [D, D]), pw2_b=inp("pw2_b", [128, 8]))
    sbw = dict(wq=inp("s_wq", [D, 512]), wk=inp("s_wk", [D, 512]), wv=inp("s_wv", [D, 512]),
               qg=inp("s_qg", [128, 1]), kg=inp("s_kg", [128, 1]), ltri=inp("ltri", [128, 128]), ustr=inp("ustr", [128, 128]),
               oblk=inp("oblk", [128, 128]), maskd=inp("maskd", [128, 4, 512]), w_out=inp("s_wout", [D, D]))
    out_d = nc.dram_tensor("out", [D, T], F32, kind="ExternalOutput").ap()
    xsp = internal("xsp", [D, T], F32)
    NT = T // TT
    xn_my = [internal(f"xn_my{t}", [D, TT], BF16) for t in range(NT)]
    xn_full = [internal(f"xn_full{t}", [2 * D, TT], BF16) for t in range(NT)]
    mz_ret = [internal(f"mz_ret{t}", [4 * 1024, TT], BF16) for t in range(NT)]
    mrs_ret = [internal(f"mrs_ret{t}", [2 * 1024, TT], BF16) for t in range(NT)]
    mz_sb = [internal(f"mz_sb{t}", [4 * 512, TT], BF16) for t in range(NT)]
    mrs_sb = [internal(f"mrs_sb{t}", [2 * 512, TT], BF16) for t in range(NT)]

    global FUSED_INPUTS
    FUSED_INPUTS = list(ext.keys())
    with ExitStack() as st:
        P = Prog(nc, st)
        P.out_sem = P.newsem("outs")
        P.enable_phases()
        env = Env(nc, P, T)
        env.flags_d = flags
        env.xn_full = xn_full
        env.xn_my = xn_my

        import os as _os
        nocc = bool(_os.environ.get("NOCC"))

        env.nocc = nocc
        env.groups = groups

        def gather():
            P.next_phase()

        def scatter(mz, mrs):
            P.next_phase()

        def tok_phase(i, x_src, m_src, w_out, fm, final=False):
            env.mrs = m_src
            env.io = {"xT": x_src, "mT": m_src, "w_out": w_out, "g_ffn": g_ffn[i], "w1": w1[i], "w2": w2[i],
                      "g_next": (g_fin if final else g_mix[i + 1]), "x_out": xsp, "xn_out": (out_d if final else xn_my)}
            build_tok("p2", T=T, fm=fm, env=env, final=final)

        def ret_phase(j):
            env.io = dict(xnT=None, **{k: ret[j][k] for k in ("wq", "wk", "wv", "wg", "qg", "kg", "gnw", "gnb")}, **rtab)
            env.mz2d, env.fmy, env.mrs_out = mz_ret, 1024, mrs_ret
            build_ret(SEQ=SEQ, env=env)
            scatter(mz_ret, mrs_ret)

        env.io = {"xT": xT_in, "g_next": g_mix[0], "xn_out": xn_my}
        build_tok("norm0", T=T, env=env, final=False)
        gather()
        ret_phase(0)
        tok_phase(0, xT_in, mrs_ret, ret[0]["w_out"], 2 * D)
        gather()
        env.io = dict(xT=xsp, x_out=xsp, flag=flagc, **cv)
        build_conv(T=T, env=env)
        P.next_phase()
        tok_phase(1, xsp, None, None, 0)
        gather()
        env.io = dict(xnT=None, **{k: sbw[k] for k in ("wq", "wk", "wv", "qg", "kg", "ltri", "ustr", "oblk", "maskd")})
        env.mz2d, env.fmy, env.mrs_out = mz_sb, 512, mrs_sb
        build_sb(SEQ=SEQ, env=env)
        scatter(mz_sb, mrs_sb)
        tok_phase(2, xsp, mrs_sb, sbw["w_out"], D)
        gather()
        ret_phase(1)
        tok_phase(3, xsp, mrs_ret, ret[1]["w_out"], 2 * D, final=True)
        P.next_phase()
        P.finish(env.outs)
    return nc


_PROGS = {}


def fused_inputs(T, x_b, rank, prm):
    A = lambda a: np.ascontiguousarray(a, dtype=np.float32)
    hp = rank
    m = {"xT": A(x_b[rank * T:(rank + 1) * T].T)}
    fl = np.zeros((128, 2), np.float32)
    fl[:, rank] = 1.0
    m["flags"] = fl
    m["flagc"] = np.full((128, 1), float(rank), np.float32)
    for i in range(4):
        m[f"g_mix{i}"] = pcol(prm["norm_mix"][i])
        m[f"g_ffn{i}"] = pcol(prm["norm_ffn"][i])
        m[f"w1_{i}"] = A(prm["ffn_w1"][i])
        m[f"w2_{i}"] = A(prm["ffn_w2"][i])
    m["g_final"] = pcol(prm["final_norm"])
    for j in range(2):
        w_in = np.asarray(prm["ret_w_in"][j], np.float32)
        m[f"r{j}_wq"] = A(w_in[:, hp * 512:(hp + 1) * 512])
        m[f"r{j}_wk"] = A(w_in[:, 1024 + hp * 512:1024 + (hp + 1) * 512])
        m[f"r{j}_wv"] = A(w_in[:, 2048 + hp * 1024:2048 + (hp + 1) * 1024])
        m[f"r{j}_wg"] = A(w_in[:, 4096 + hp * 1024:4096 + (hp + 1) * 1024])
        m[f"r{j}_qg"] = pcol(prm["ret_q_norm"][j])
        m[f"r{j}_kg"] = pcol(prm["ret_k_norm"][j])
        m[f"r{j}_gnw"] = pcol(np.asarray(prm["ret_gn_w"][j], np.float32)[hp * 1024:(hp + 1) * 1024])
        m[f"r{j}_gnb"] = pcol(np.asarray(prm["ret_gn_b"][j], np.float32)[hp * 1024:(hp + 1) * 1024])
        m[f"r{j}_wout"] = A(prm["ret_w_out"][j])
    tabs = ret_tables((2 * hp, 2 * hp + 1))
    m["cos"] = A(tabs["cos"][:, :2 * T]); m["sin"] = A(tabs["sin"][:, :2 * T])
    for k in ("dt", "qdec", "kdec", "g128"):
        m[k] = A(tabs[k])
    ci = conv_inputs(np.zeros((1, 1)), np.zeros((1, 1)), 0.0, prm["conv_pw1_w"][0], prm["conv_pw1_b"][0], prm["conv_dw_w"][0],
                     prm["conv_dw_b"][0], prm["conv_ln_w"][0], prm["conv_ln_b"][0], prm["conv_pw2_w"][0], prm["conv_pw2_b"][0])
    for k in ("pw1_w", "pw1_b", "dw_w", "dw_b", "ln_w", "ln_b", "pw2_w", "pw2_b"):
        m[k] = ci[k]
    sw = np.asarray(prm["sb_w_in"][0], np.float32)
    m["s_wq"] = A(sw[:, hp * 512:(hp + 1) * 512])
    m["s_wk"] = A(sw[:, 1024 + hp * 512:1024 + (hp + 1) * 512])
    m["s_wv"] = A(sw[:, 2048 + hp * 512:2048 + (hp + 1) * 512])
    m["s_qg"] = A(np.tile(np.asarray(prm["sb_q_norm"][0], np.float32), 2)[:, None])
    m["s_kg"] = A(np.tile(np.asarray(prm["sb_k_norm"][0], np.float32), 2)[:, None])
    m["s_wout"] = A(prm["sb_w_out"][0])
    for k, v in sb_tables().items():
        m[k] = A(v)
    return m


def kernel(**prm):
    x = np.asarray(prm["x"], np.float32)
    if "fused" not in _PROGS:
        _PROGS["fused"] = build_fused()
    nc = _PROGS["fused"]
    in_maps = [fused_inputs(TOK, x[c // 2], c % 2, prm) for c in range(NCORES)]
    res = run_bass_kernel_spmd(nc, in_maps, core_ids=list(range(NCORES)))
    out = np.empty((B, S, D), np.float32)
    for c in range(NCORES):
        out[c // 2, (c % 2) * TOK:(c % 2 + 1) * TOK] = res.results[c]["out"].T
    return out
```
